# Optimizing a Trainium2 kernel written in Bass

```python
import math
import jax, jax.numpy as jnp
from jax import lax
import numpy as np

D_MODEL = 1024
BATCH = 8
SEQ = 2048
DEPTH = 2

GRID_W = 64
HEAD_DIM = 64
Q_BLOCK = 128
EPS = 1e-6
NEG_INF = -1e30
DIFF_HEADS = 4
DIFF_V_DIM = 2 * HEAD_DIM
DIFF_QK_W = DIFF_HEADS * 2 * HEAD_DIM
DIFF_V_W = DIFF_HEADS * DIFF_V_DIM
NA_HEADS = 8
NA_WIN_H = 8
NA_WIN_W = 16
NA_QCOL_BLOCK = 16
NA_KCOL_SPAN = NA_QCOL_BLOCK + NA_WIN_W
NA_W = NA_HEADS * HEAD_DIM
IN_PROJ_W = 2 * DIFF_QK_W + DIFF_V_W + 3 * NA_W
MIX_W = DIFF_V_W + NA_W
MLA_HEADS = 16
MLA_NOPE = 64
MLA_ROPE = 32
MLA_V = 64
MLA_Q_RANK = 512
MLA_KV_RANK = 256
ROPE_THETA = 10000.0
D_FF = 2816
CONV_W = 3
N_EVEN = (DEPTH + 1) // 2
N_ODD = DEPTH // 2

kernel_name = "hybrid_diffattn_natten_mla_convffn_encoder"


def rms_norm(x, g):
    xf = x.astype(jnp.float32)
    y = xf * lax.rsqrt(jnp.mean(xf * xf, axis=-1, keepdims=True) + EPS)
    return (y * g.astype(jnp.float32)).astype(x.dtype)


def alibi_slopes(n):
    return jnp.asarray(2.0 ** (-8.0 * np.arange(1, n + 1) / n), dtype=jnp.float32)


def rope_tables(s, dim):
    inv = ROPE_THETA ** (-jnp.arange(0, dim, 2, dtype=jnp.float32) / dim)
    ang = jnp.arange(s, dtype=jnp.float32)[:, None] * inv[None, :]
    return jnp.cos(ang), jnp.sin(ang)


def apply_rope(x, cos, sin):
    xf = x.astype(jnp.float32)
    half = x.shape[-1] // 2
    x1, x2 = xf[..., :half], xf[..., half:]
    return jnp.concatenate([x1 * cos - x2 * sin, x2 * cos + x1 * sin], axis=-1).astype(x.dtype)


def diff_attention(q, k, v, lam, slopes):
    b, s, h, _, dh = q.shape
    nblk = s // Q_BLOCK
    scale = dh ** -0.5
    pos = jnp.arange(s)
    qb = jnp.moveaxis(q.reshape(b, nblk, Q_BLOCK, h, 2, dh), 1, 0)

    def block(args):
        qi, i = args
        tq = i * Q_BLOCK + jnp.arange(Q_BLOCK)
        dist = jnp.abs(tq[:, None] - pos[None, :]).astype(jnp.float32)
        bias = -slopes[:, None, None] * dist[None]
        sc = jnp.einsum('bqhnd,bshnd->bhnqs', qi, k).astype(jnp.float32) * scale + bias[None, :, None]
        p = jax.nn.softmax(sc, axis=-1)
        a = (p[:, :, 0] - lam * p[:, :, 1]).astype(v.dtype)
        return jnp.einsum('bhqs,bshe->bqhe', a, v)

    o = lax.map(block, (qb, jnp.arange(nblk)))
    return jnp.moveaxis(o, 0, 1).reshape(b, s, h, v.shape[-1])


def neighbourhood_attention(q, k, v, rpb):
    b, s, h, dh = q.shape
    rows_n = s // GRID_W
    kh = min(NA_WIN_H, rows_n)
    nb = GRID_W // NA_QCOL_BLOCK
    scale = dh ** -0.5
    rows = np.arange(rows_n)
    row_start = np.clip(rows - kh // 2, 0, rows_n - kh)
    row_idx = row_start[:, None] + np.arange(kh)[None, :]
    cblk = np.clip(np.arange(nb) * NA_QCOL_BLOCK - NA_WIN_W // 2, 0, GRID_W - NA_KCOL_SPAN)
    col_idx = cblk[:, None] + np.arange(NA_KCOL_SPAN)[None, :]
    qcol = np.arange(nb)[:, None] * NA_QCOL_BLOCK + np.arange(NA_QCOL_BLOCK)[None, :]
    wstart = np.clip(qcol - NA_WIN_W // 2, 0, GRID_W - NA_WIN_W)
    kcol = col_idx[:, None, :]
    valid = (kcol >= wstart[..., None]) & (kcol < wstart[..., None] + NA_WIN_W)
    dr = row_idx - rows[:, None] + NA_WIN_H - 1
    dc = np.clip(kcol - qcol[..., None] + NA_WIN_W - 1, 0, 2 * NA_WIN_W - 2)
    bias = rpb[:, dr[:, :, None, None, None], dc[None, None]]
    bias = bias.transpose(1, 0, 3, 4, 2, 5).astype(jnp.float32)
    kg = k.reshape(b, rows_n, GRID_W, h, dh)
    vg = v.reshape(b, rows_n, GRID_W, h, dh)
    qrows = jnp.moveaxis(q.reshape(b, rows_n, nb, NA_QCOL_BLOCK, h, dh), 1, 0)

    def row(args):
        qr, rs, br = args
        kr = lax.dynamic_slice_in_dim(kg, rs, kh, axis=1)[:, :, col_idx]
        vr = lax.dynamic_slice_in_dim(vg, rs, kh, axis=1)[:, :, col_idx]
        sc = jnp.einsum('bcqhd,bkcjhd->bhcqkj', qr, kr).astype(jnp.float32) * scale + br[None]
        sc = jnp.where(valid[:, :, None, :], sc, NEG_INF)
        p = jax.nn.softmax(sc.reshape(sc.shape[:4] + (kh * NA_KCOL_SPAN,)), axis=-1)
        p = p.reshape(sc.shape).astype(v.dtype)
        return jnp.einsum('bhcqkj,bkcjhd->bcqhd', p, vr)

    o = lax.map(row, (qrows, jnp.asarray(row_start, dtype=jnp.int32), bias))
    return jnp.moveaxis(o, 0, 1).reshape(b, s, h * dh)


def diff_na_mixer(h, w_in, lq1, lk1, lq2, lk2, subln_g, rpb, w_out, layer):
    b, s, _ = h.shape
    proj = h @ w_in
    cuts = np.cumsum([DIFF_QK_W, DIFF_QK_W, DIFF_V_W, NA_W, NA_W]).tolist()
    qd, kd, vd, qn, kn, vn = jnp.split(proj, cuts, axis=-1)
    lam_init = 0.8 - 0.6 * math.exp(-0.3 * layer)
    f32 = lambda t: t.astype(jnp.float32)
    lam = jnp.exp(jnp.sum(f32(lq1) * f32(lk1))) - jnp.exp(jnp.sum(f32(lq2) * f32(lk2))) + lam_init
    a = diff_attention(qd.reshape(b, s, DIFF_HEADS, 2, HEAD_DIM),
                       kd.reshape(b, s, DIFF_HEADS, 2, HEAD_DIM),
                       vd.reshape(b, s, DIFF_HEADS, DIFF_V_DIM), lam, alibi_slopes(DIFF_HEADS))
    a = (rms_norm(a, subln_g) * (1.0 - lam_init)).reshape(b, s, DIFF_V_W)
    nb_out = neighbourhood_attention(qn.reshape(b, s, NA_HEADS, HEAD_DIM),
                                     kn.reshape(b, s, NA_HEADS, HEAD_DIM),
                                     vn.reshape(b, s, NA_HEADS, HEAD_DIM), rpb)
    return jnp.concatenate([a, nb_out], axis=-1) @ w_out


def mla_attention(h, w_dq, q_norm_g, w_uq, w_dkv, kv_norm_g, w_ukv, w_o):
    b, s, _ = h.shape
    nblk = s // Q_BLOCK
    scale = (MLA_NOPE + MLA_ROPE) ** -0.5
    cq = rms_norm(h @ w_dq, q_norm_g)
    q = (cq @ w_uq).reshape(b, s, MLA_HEADS, MLA_NOPE + MLA_ROPE)
    q_nope, q_rope = q[..., :MLA_NOPE], q[..., MLA_NOPE:]
    kv_a = h @ w_dkv
    ckv = rms_norm(kv_a[..., :MLA_KV_RANK], kv_norm_g)
    k_rope = kv_a[..., MLA_KV_RANK:]
    kv = (ckv @ w_ukv).reshape(b, s, MLA_HEADS, MLA_NOPE + MLA_V)
    k_nope, v = kv[..., :MLA_NOPE], kv[..., MLA_NOPE:]
    cos, sin = rope_tables(s, MLA_ROPE)
    q_rope = apply_rope(q_rope, cos[:, None, :], sin[:, None, :])
    k_rope = apply_rope(k_rope, cos, sin)
    qnb = jnp.moveaxis(q_nope.reshape(b, nblk, Q_BLOCK, MLA_HEADS, MLA_NOPE), 1, 0)
    qrb = jnp.moveaxis(q_rope.reshape(b, nblk, Q_BLOCK, MLA_HEADS, MLA_ROPE), 1, 0)

    def block(args):
        qn, qr = args
        sc = (jnp.einsum('bqhd,bshd->bhqs', qn, k_nope)
              + jnp.einsum('bqhr,bsr->bhqs', qr, k_rope)).astype(jnp.float32) * scale
        p = jax.nn.softmax(sc, axis=-1).astype(v.dtype)
        return jnp.einsum('bhqs,bshd->bqhd', p, v)

    o = lax.map(block, (qnb, qrb))
    return jnp.moveaxis(o, 0, 1).reshape(b, s, MLA_HEADS * MLA_V) @ w_o


def conv_ffn(h, w_gate, w_val, conv_w, conv_b, w_down):
    a = h @ w_gate
    a = lax.conv_general_dilated(a, conv_w[:, None, :].astype(a.dtype), window_strides=(1,),
                                 padding=((CONV_W // 2, CONV_W // 2),),
                                 dimension_numbers=('NWC', 'WIO', 'NWC'),
                                 feature_group_count=D_FF) + conv_b
    return (jax.nn.gelu(a, approximate=False) * (h @ w_val)) @ w_down


def setup_inputs(seed: int = 0) -> dict:
    key = jax.random.key(seed)
    ks = iter(jax.random.split(key, 32))
    f = jnp.float32

    def w(shape, fan_in):
        return jax.random.normal(next(ks), shape, f) * fan_in ** -0.5

    def gain(shape):
        return 1.0 + 0.05 * jax.random.normal(next(ks), shape, f)

    def small(shape, std):
        return std * jax.random.normal(next(ks), shape, f)

    return {
        "x": jax.random.normal(next(ks), (BATCH, SEQ, D_MODEL), f),
        "mix_norm_e": gain((N_EVEN, D_MODEL)),
        "w_in_e": w((N_EVEN, D_MODEL, IN_PROJ_W), D_MODEL),
        "diff_lq1": small((N_EVEN, HEAD_DIM), 0.1),
        "diff_lk1": small((N_EVEN, HEAD_DIM), 0.1),
        "diff_lq2": small((N_EVEN, HEAD_DIM), 0.1),
        "diff_lk2": small((N_EVEN, HEAD_DIM), 0.1),
        "diff_subln_g": gain((N_EVEN, DIFF_V_DIM)),
        "na_rpb": small((N_EVEN, NA_HEADS, 2 * NA_WIN_H - 1, 2 * NA_WIN_W - 1), 0.1),
        "w_out_e": w((N_EVEN, MIX_W, D_MODEL), MIX_W),
        "mix_norm_o": gain((N_ODD, D_MODEL)),
        "w_dq": w((N_ODD, D_MODEL, MLA_Q_RANK), D_MODEL),
        "q_norm_g": gain((N_ODD, MLA_Q_RANK)),
        "w_uq": w((N_ODD, MLA_Q_RANK, MLA_HEADS * (MLA_NOPE + MLA_ROPE)), MLA_Q_RANK),
        "w_dkv": w((N_ODD, D_MODEL, MLA_KV_RANK + MLA_ROPE), D_MODEL),
        "kv_norm_g": gain((N_ODD, MLA_KV_RANK)),
        "w_ukv": w((N_ODD, MLA_KV_RANK, MLA_HEADS * (MLA_NOPE + MLA_V)), MLA_KV_RANK),
        "w_o_mla": w((N_ODD, MLA_HEADS * MLA_V, D_MODEL), MLA_HEADS * MLA_V),
        "ffn_norm_g": gain((DEPTH, D_MODEL)),
        "w_ffn_gate": w((DEPTH, D_MODEL, D_FF), D_MODEL),
        "w_ffn_val": w((DEPTH, D_MODEL, D_FF), D_MODEL),
        "ffn_conv_w": w((DEPTH, CONV_W, D_FF), CONV_W),
        "ffn_conv_b": small((DEPTH, D_FF), 0.02),
        "w_ffn_down": w((DEPTH, D_FF, D_MODEL), D_FF),
        "final_norm_g": gain((D_MODEL,)),
    }


def reference(x, mix_norm_e, w_in_e, diff_lq1, diff_lk1, diff_lq2, diff_lk2, diff_subln_g, na_rpb,
              w_out_e, mix_norm_o, w_dq, q_norm_g, w_uq, w_dkv, kv_norm_g, w_ukv, w_o_mla,
              ffn_norm_g, w_ffn_gate, w_ffn_val, ffn_conv_w, ffn_conv_b, w_ffn_down, final_norm_g):
    for i in range(DEPTH):
        j = i // 2
        if i % 2 == 0:
            h = rms_norm(x, mix_norm_e[j])
            x = x + diff_na_mixer(h, w_in_e[j], diff_lq1[j], diff_lk1[j], diff_lq2[j], diff_lk2[j],
                                  diff_subln_g[j], na_rpb[j], w_out_e[j], i)
        else:
            h = rms_norm(x, mix_norm_o[j])
            x = x + mla_attention(h, w_dq[j], q_norm_g[j], w_uq[j], w_dkv[j], kv_norm_g[j],
                                  w_ukv[j], w_o_mla[j])
        h = rms_norm(x, ffn_norm_g[i])
        x = x + conv_ffn(h, w_ffn_gate[i], w_ffn_val[i], ffn_conv_w[i], ffn_conv_b[i], w_ffn_down[i])
    return rms_norm(x, final_norm_g)
```

```python
import math
import os
KDBG = int(os.environ.get("KDBG", "0"))
from contextlib import ExitStack
import numpy as np
import ml_dtypes
import concourse.bass as bass
import concourse.mybir as mybir
from concourse.bass_utils import run_bass_kernel_spmd

F32 = mybir.dt.float32
BF16 = mybir.dt.bfloat16
AF = mybir.ActivationFunctionType
ALU = mybir.AluOpType

T = 2048
D = 1024
DFF = 2816
NFC = 22
EPS = 1e-6
NEG = -30000.0
NSLOT = 4
LOOKAHEAD = 2
NBLK = 21


class Buf:
    __slots__ = ("name", "w", "r", "dsem", "dcnt", "excl")

    def __init__(self, name, excl=False):
        self.name = name
        self.excl = excl
        self.w = None
        self.r = {}
        self.dsem = None
        self.dcnt = 0


class Sched:
    ENG = ["pe", "act", "dve", "pool", "sp"]
    COMPUTE = ["pe", "act", "dve", "pool"]

    def __init__(self, nc, stack):
        self.nc = nc
        self.stack = stack
        self.ops = {e: [] for e in self.ENG}
        self.cnt = {e: 0 for e in self.ENG}
        self.sem = {e: stack.enter_context(nc.semaphore("s_" + e)) for e in self.ENG}
        self.seen = {e: {} for e in self.ENG}
        self.semobj = {e: self.sem[e] for e in self.ENG}
        self.nd = 0
        self.dma_bufs = []

    def _deps(self, eng, reads, writes):
        deps = {}

        def add(d):
            if d is None:
                return
            k, v = d
            if deps.get(k, 0) < v:
                deps[k] = v
        for b in reads:
            add(b.w)
        for b in writes:
            add(b.w)
            for d in b.r.items():
                add(d)
        waits = []
        seen = self.seen[eng]
        for k, v in deps.items():
            if k == "pe" and eng == "pe":
                continue
            if seen.get(k, 0) >= v:
                continue
            seen[k] = v
            waits.append((self.semobj[k], v))
        return waits

    def op(self, eng, fn, reads=(), writes=(), track=True):
        ex = [b for b in reads if b.excl]
        if ex:
            reads = [b for b in reads if not b.excl]
            writes = list(writes) + [b for b in ex if b not in writes]
        waits = self._deps(eng, reads, writes)
        idx = self.cnt[eng] + 1
        if track:
            self.cnt[eng] = idx
        ev = (eng, idx)
        for b in reads:
            if b.r.get(eng, 0) < idx:
                b.r[eng] = idx
        for b in writes:
            b.w = ev
            b.r = {}
        self.ops[eng].append((waits, fn, track, None))

    def dma(self, q, out, in_, reads=(), writes=()):
        waits = self._deps(q, reads, writes)
        bufs = list(writes) + list(reads)
        assert len(bufs) == 1
        b = bufs[0]
        if b.dsem is None:
            self.nd += 1
            key = "d%d" % self.nd
            b.dsem = key
            self.semobj[key] = self.stack.enter_context(self.nc.semaphore(key))
            self.dma_bufs.append(b)
        b.dcnt += 16
        ev = (b.dsem, b.dcnt)
        if writes:
            b.w = ev
            b.r = {}
        else:
            b.r[b.dsem] = b.dcnt
        self.ops[q].append((waits, L("dma_start", out=out, in_=in_), False,
                            (self.semobj[b.dsem], 16)))

    def barrier(self, dma=False):
        for e in self.ENG:
            waits = []
            if dma:
                for b in self.dma_bufs:
                    if self.seen[e].get(b.dsem, 0) < b.dcnt:
                        self.seen[e][b.dsem] = b.dcnt
                        waits.append((self.semobj[b.dsem], b.dcnt))
            for k in self.COMPUTE:
                v = self.cnt[k]
                if v > 0 and self.seen[e].get(k, 0) < v:
                    self.seen[e][k] = v
                    waits.append((self.semobj[k], v))
            if waits:
                self.ops[e].append((waits, None, False, None))

    def final_wait(self, eng):
        waits = [(self.semobj[b.dsem], b.dcnt) for b in self.dma_bufs]
        for k in self.COMPUTE:
            if self.cnt[k] > 0:
                waits.append((self.semobj[k], self.cnt[k]))
        self.ops[eng].append((waits, None, False, None))

    def emit(self):
        nc = self.nc
        handles = {"pe": "tensor", "act": "scalar", "dve": "vector", "pool": "gpsimd", "sp": "sync"}
        with nc.Block() as block:
            for e in self.ENG:
                ops = self.ops[e]
                own = self.sem[e]

                def body(engh, ops=ops, own=own):
                    for waits, fn, track, dinc in ops:
                        for s, v in waits:
                            engh.wait_ge(s, v)
                        if fn is None:
                            continue
                        ins = fn(engh)
                        if dinc is not None:
                            ins.then_inc(dinc[0], dinc[1])
                        elif track:
                            ins.then_inc(own, 1)
                getattr(block, handles[e])(body)


def L(method, *args, **kw):
    return lambda e: getattr(e, method)(*args, **kw)


def _na_tile_lists():
    lists = []
    tid = {}
    for m in range(16):
        if m <= 1:
            kbs = [0, 1, 2, 3]
        elif m >= 14:
            kbs = [12, 13, 14, 15]
        else:
            kbs = [m - 2, m - 1, m, m + 1, m + 2]
        cur = []
        for kb in kbs:
            key = ("int", kb - m) if 2 <= m <= 13 else ("edge", m, kb)
            if key not in tid:
                tid[key] = len(tid)
            cur.append((kb, tid[key]))
        lists.append(cur)
    return lists, tid


def _na_index_tables():
    lists, tid = _na_tile_lists()
    ntile = len(tid)
    dr_i = np.zeros((ntile, 128, 128), np.int64)
    dc_i = np.zeros((ntile, 128, 128), np.int64)
    valid = np.zeros((ntile, 128, 128), bool)
    done = set()
    for m in range(16):
        for kb, t in lists[m]:
            if t in done:
                continue
            done.add(t)
            kk = np.arange(128)
            kr = (2 * kb + kk // 64)[:, None]
            kc = (kk % 64)[:, None]
            qr = (2 * m + kk // 64)[None, :]
            qc = (kk % 64)[None, :]
            rs = np.clip(qr - 4, 0, 24)
            ws = np.clip(qc - 8, 0, 48)
            v = (kr >= rs) & (kr < rs + 8) & (kc >= ws) & (kc < ws + 16)
            dr = np.clip(kr - qr + 7, 0, 14)
            dc = np.clip(kc - qc + 15, 0, 30)
            dr_i[t] = np.broadcast_to(dr, (128, 128))
            dc_i[t] = np.broadcast_to(dc, (128, 128))
            valid[t] = v
    return dr_i, dc_i, valid


def _const_tables():
    c = {}
    c["ident"] = np.eye(128, dtype=np.float32)
    slopes = 2.0 ** (-8.0 * np.arange(1, 5) / 4)
    i = np.arange(T)
    a = (i // 128).astype(np.float32)
    r = (i % 128).astype(np.float32)
    augq = np.zeros((4, 4, T), np.float32)
    for h in range(4):
        s = slopes[h]
        augq[h, 0] = -s * 128.0 * a
        augq[h, 1] = -s * r
        augq[h, 2] = s
        augq[h, 3] = s
    c["augq"] = augq
    augk = np.zeros((2, 4, T), np.float32)
    for v, sg in enumerate([1.0, -1.0]):
        augk[v, 0] = sg
        augk[v, 1] = sg
        augk[v, 2] = sg * 128.0 * a
        augk[v, 3] = sg * r
    c["augk"] = augk
    jj = np.arange(128)[:, None]
    ii = np.arange(128)[None, :]
    cd = np.zeros((128, 4 * 128), np.float32)
    for h in range(4):
        cd[:, h * 128:(h + 1) * 128] = -2.0 * slopes[h] * np.maximum(jj - ii, 0)
    c["cdiag"] = cd
    inv = (10000.0 ** (-np.arange(0, 32, 2, dtype=np.float32) / np.float32(32))).astype(np.float32)
    ang = (np.arange(T, dtype=np.float32)[:, None] * inv[None, :]).astype(np.float32)
    cos = np.cos(ang.astype(np.float64)).astype(np.float32).T
    sin = np.sin(ang.astype(np.float64)).astype(np.float32).T
    rt = np.zeros((128, T), np.float32)
    rt[64:80] = cos
    rt[80:96] = cos
    rt[96:112] = -sin
    rt[112:128] = sin
    c["ropetab"] = rt
    return c


def _pm(v, nch):
    return np.ascontiguousarray(np.asarray(v, np.float32).reshape(nch, 128).T)


def _prep_shared(inp):
    g = {}
    c = _const_tables()
    g.update(c)
    f = lambda k: np.asarray(inp[k], np.float32)
    gains = np.zeros((128, 48), np.float32)
    gains[:, 0:8] = _pm(f("mix_norm_e")[0], 8)
    gains[:, 8:16] = _pm(f("ffn_norm_g")[0], 8)
    gains[:, 16:24] = _pm(f("mix_norm_o")[0], 8)
    gains[:, 24:32] = _pm(f("ffn_norm_g")[1], 8)
    gains[:, 32:40] = _pm(f("final_norm_g"), 8)
    gains[:, 40:44] = _pm(f("q_norm_g")[0], 4)
    gains[:, 44:46] = _pm(f("kv_norm_g")[0], 2)
    gains[:, 46] = f("diff_subln_g")[0]
    g["gains"] = gains
    cv = np.zeros((128, 2 * NFC * 4), np.float32)
    for l in range(2):
        for k in range(3):
            cv[:, l * 88 + k * 22: l * 88 + (k + 1) * 22] = _pm(f("ffn_conv_w")[l, k], NFC)
        cv[:, l * 88 + 66: l * 88 + 88] = _pm(f("ffn_conv_b")[l], NFC)
    g["convp"] = cv
    lam = np.zeros((128, 256), np.float32)
    for k, nm in enumerate(["diff_lq1", "diff_lk1", "diff_lq2", "diff_lk2"]):
        lam[:, k * 64:(k + 1) * 64] = np.broadcast_to(f(nm)[0][None, :], (128, 64))
    g["lamv"] = lam
    dr_i, dc_i, valid = _na_index_tables()
    rpb = f("na_rpb")[0]
    nb = np.empty((8, dr_i.shape[0], 128, 128), np.float32)
    for h in range(8):
        nb[h] = np.where(valid, rpb[h][dr_i, dc_i], np.float32(NEG))
    g["nbias"] = np.ascontiguousarray(nb.transpose(0, 2, 1, 3))
    g["w_in"] = f("w_in_e")[0]
    g["w_out"] = f("w_out_e")[0]
    g["w_gate"] = f("w_ffn_gate")
    g["w_val"] = f("w_ffn_val")
    g["w_down"] = f("w_ffn_down")
    g["w_dq"] = f("w_dq")[0]
    wuq = f("w_uq")[0]
    cols = []
    for h in range(16):
        b = h * 96
        cols += list(range(b, b + 96)) + list(range(b + 80, b + 96)) + list(range(b + 64, b + 80))
    g["w_uq"] = np.ascontiguousarray(wuq[:, cols])
    wdkv = f("w_dkv")[0]
    wd = np.zeros((1024, 384), np.float32)
    wd[:, 0:256] = wdkv[:, 0:256]
    wd[:, 320:352] = wdkv[:, 256:288]
    wd[:, 352:368] = wdkv[:, 272:288]
    wd[:, 368:384] = wdkv[:, 256:272]
    g["w_dkv"] = wd
    wukv = f("w_ukv")[0].reshape(256, 16, 128)
    g["w_uk"] = np.ascontiguousarray(wukv[:, :, 0:64].reshape(256, 1024))
    g["w_uv"] = np.ascontiguousarray(wukv[:, :, 64:128].reshape(256, 1024))
    g["w_o"] = f("w_o_mla")[0]
    return g


BF16_IN = ()
SHAPES = {
    "x": [T, D], "ident": [128, 128], "augq": [4, 4, T], "augk": [2, 4, T], "cdiag": [128, 512],
    "ropetab": [128, T], "gains": [128, 48], "convp": [128, 176], "lamv": [128, 256],
    "nbias": [8, 128, 21, 128], "w_in": [1024, 3072], "w_out": [1024, 1024],
    "w_gate": [2, 1024, DFF], "w_val": [2, 1024, DFF], "w_down": [2, DFF, 1024],
    "w_dq": [1024, 512], "w_uq": [512, 2048], "w_dkv": [1024, 384], "w_uk": [256, 1024],
    "w_uv": [256, 1024], "w_o": [1024, 1024],
}


def build(plan=None, nstage=99):
    nc = bass.Bass("TRN2", target_bir_lowering=False)
    dram = {k: nc.dram_tensor(k, shp, BF16 if k in BF16_IN else F32, kind="ExternalInput").ap() for k, shp in SHAPES.items()}
    y_out = nc.dram_tensor("y", [T, D], F32, kind="ExternalOutput").ap()
    requests = []
    with ExitStack() as st:
        S = Sched(nc, st)
        sb = lambda n, shp, dt: st.enter_context(nc.sbuf_tensor("sb_" + n, shp, dt))
        xT = sb("xT", [128, 8, T], F32)
        hT = sb("hT", [128, 8, T], BF16)
        wring = sb("wring", [128, NSLOT, 2048], BF16)
        AR = sb("arena", [128, NBLK * 2048], BF16)
        ident = sb("ident", [128, 128], F32)
        identb = sb("identb", [128, 128], BF16)
        onesb = sb("onesb", [128, 128], BF16)
        gains = sb("gains", [128, 48], F32)
        convp = sb("convp", [128, 176], F32)
        small = sb("small", [128, 16], F32)
        cdiag = sb("cdiag", [128, 512], BF16)
        rs_a = sb("rs_a", [128, 512], F32)
        rs_b = sb("rs_b", [128, 512], F32)
        sqr = sb("sqr", [128, 4, 512], BF16)
        PS = st.enter_context(nc.psum_tensor("PS", [128, 8 * 512], F32))

        XT = [[Buf("xT%d_%d" % (c, g)) for g in range(4)] for c in range(8)]
        HT = [[Buf("hT%d_%d" % (c, g)) for g in range(4)] for c in range(8)]
        WS = [Buf("ws%d" % i) for i in range(NSLOT)]
        PB = [Buf("pb%d" % i, excl=True) for i in range(8)]
        B_const = Buf("const")
        B_small = Buf("small")
        B_rsa, B_rsb = Buf("rsa"), Buf("rsb")
        SQ = [Buf("sq%d" % i) for i in range(4)]

        def bank(i):
            return PS[:, i * 512:(i + 1) * 512]

        def blk(k, n=1):
            return AR[:, k * 2048:(k + n) * 2048]

        def blkf(k, n=2):
            return AR[:, k * 2048:(k + n) * 2048].bitcast(F32)

        state = {"bank": 0, "wi": 0, "wissued": 0, "sq": 0}
        lamv = blkf(0, 1)[:, 0:256]

        def nb_():
            b = state["bank"]
            state["bank"] = (b + 1) % 8
            return b

        def tgs(g):
            return slice(g * 512, (g + 1) * 512)

        def wtile(fn):
            i = state["wi"]
            state["wi"] = i + 1
            requests.append(fn)
            seq = plan if plan is not None else requests
            hi = min(len(seq), i + 1 + (LOOKAHEAD if plan is not None else 0))
            while state["wissued"] < hi:
                j = state["wissued"]
                slot = j % NSLOT
                for (o, a) in seq[j](dram, wring[:, slot, :]):
                    S.dma("pool", o, a, writes=[WS[slot]])
                state["wissued"] = j + 1
            return wring[:, i % NSLOT, :], WS[i % NSLOT]

        def wreq_cols(name, kc, c0, ncols, lidx=None, r0=0):
            def fn(dr, slot):
                src = dr[name] if lidx is None else dr[name][lidx]
                src = src[r0:r0 + kc * 128, c0:c0 + ncols].rearrange("(k p) n -> p k n", p=128)
                dst = slot[:, 0:kc * ncols].rearrange("p (k n) -> p k n", k=kc)
                return [(dst, src)]
            return fn

        def wview(slot, kc, ncols):
            return slot[:, 0:kc * ncols].rearrange("p (k n) -> p k n", k=kc)

        S.dma("sp", ident[:], dram["ident"], writes=[B_const])
        gb_l = Buf("gains_l")
        S.dma("sp", gains[:], dram["gains"], writes=[gb_l])
        cb1, cb2, cb3 = Buf("c1"), Buf("c2"), Buf("c3")
        S.dma("sp", convp[:], dram["convp"], writes=[cb1])
        S.dma("sp", lamv, dram["lamv"], writes=[cb2])
        S.dma("pool", cdiag[:], dram["cdiag"], writes=[cb3])
        S.op("dve", L("tensor_copy", identb[:], ident[:]), reads=[B_const], writes=[B_const])
        S.op("dve", L("memset", onesb[:], 1.0), writes=[B_const])
        S.op("dve", L("memset", small[:, 0:1], EPS), writes=[B_small])
        lam_init = 0.8 - 0.6 * math.exp(-0.3 * 0)
        S.op("dve", L("tensor_tensor", lamv[:, 0:64], lamv[:, 0:64], lamv[:, 64:128], ALU.mult),
             reads=[cb2], writes=[cb2])
        S.op("dve", L("tensor_tensor", lamv[:, 128:192], lamv[:, 128:192], lamv[:, 192:256], ALU.mult),
             reads=[cb2], writes=[cb2])
        S.op("dve", L("reduce_sum", small[:, 4:5], lamv[:, 0:64], mybir.AxisListType.X),
             reads=[cb2], writes=[B_small])
        S.op("dve", L("reduce_sum", small[:, 5:6], lamv[:, 128:192], mybir.AxisListType.X),
             reads=[cb2], writes=[B_small])
        S.op("act", L("activation", small[:, 6:8], small[:, 4:6], AF.Exp), reads=[B_small], writes=[B_small])
        S.op("dve", L("scalar_tensor_tensor", small[:, 1:2], small[:, 7:8], -lam_init, small[:, 6:7],
                                                     ALU.add, ALU.subtract), reads=[B_small], writes=[B_small])
        S.op("dve", L("tensor_scalar", small[:, 2:3], gains[:, 46:47], 1.0 - lam_init, None, ALU.mult),
             reads=[B_small, gb_l], writes=[B_small])
        S.barrier(dma=True)

        def mm(out, lhsT, rhs, start, stop, reads, writes, track=None):
            if track is None:
                track = stop
            S.op("pe", L("matmul", out, lhsT, rhs, start=start, stop=stop), reads=reads, writes=writes,
                 track=track)

        def rstd_from_psum(pb, nfeat, dst, dstbuf):
            S.op("act", L("activation", rs_a[:], bank(pb), AF.Sqrt, scale=1.0 / nfeat, bias=small[:, 0:1]),
                 reads=[PB[pb], B_small], writes=[B_rsa])
            S.op("dve", L("reciprocal", dst, rs_a[:]), reads=[B_rsa], writes=[dstbuf])

        def norm_stage(goff):
            for g in range(4):
                pb = nb_()
                for c in range(8):
                    q = state["sq"]
                    state["sq"] = (q + 1) % 4
                    S.op("act", L("activation", sqr[:, q, :], xT[:, c, tgs(g)], AF.Square),
                         reads=[XT[c][g]], writes=[SQ[q]])
                    mm(bank(pb), onesb[:], sqr[:, q, :], c == 0, c == 7, [SQ[q], B_const], [PB[pb]], track=True)
                rstd_from_psum(pb, 1024.0, rs_b[:], B_rsb)
                for c in range(8):
                    S.op("dve", L("scalar_tensor_tensor", hT[:, c, tgs(g)], xT[:, c, tgs(g)], gains[:, goff + c:goff + c + 1], rs_b[:],
                        ALU.mult, ALU.mult), reads=[XT[c][g], B_rsb], writes=[HT[c][g]])

        def proj_fm(wv, wbuf, col0, rhs_fn, rhs_bufs_fn, nk, g, M=128):
            pb = nb_()
            for k in range(nk):
                mm(bank(pb)[0:M, :], wv[:, k, col0:col0 + M], rhs_fn(k, g), k == 0, k == nk - 1,
                   [wbuf] + rhs_bufs_fn(k, g), [PB[pb]])
            return pb

        hT_rhs = lambda k, g: hT[:, k, tgs(g)]
        hT_bufs = lambda k, g: [HT[k][g]]

        def resid_add(pb, n, g, eng="dve"):
            S.op(eng, L("tensor_tensor", xT[:, n, tgs(g)], bank(pb), xT[:, n, tgs(g)], ALU.add),
                 reads=[PB[pb], XT[n][g]], writes=[XT[n][g]])

        def out_proj(name, mix_fn, mix_bufs, krow0, nkc):
            for nt in range(4):
                slot, wb = wtile(wreq_cols(name, nkc, nt * 256, 256, r0=krow0))
                wv = wview(slot, nkc, 256)
                for nn in range(2):
                    n = nt * 2 + nn
                    for g in range(4):
                        pb = proj_fm(wv, wb, nn * 128, mix_fn, mix_bufs, nkc, g)
                        resid_add(pb, n, g)

        XS = [Buf("xs%d" % i) for i in range(4)]
        for g in range(4):
            for j in range(4):
                tt = g * 4 + j
                S.dma("sp", blkf(j, 1), dram["x"][tt * 128:(tt + 1) * 128, :], writes=[XS[j]])
            for c in range(8):
                pb = nb_()
                for j in range(4):
                    S.op("pe", L("transpose", bank(pb)[:, j * 128:(j + 1) * 128], blkf(j, 1)[:, c * 128:(c + 1) * 128], ident[:]),
                        reads=[XS[j], B_const], writes=[PB[pb]], track=(j == 3))
                eng = "act" if c % 2 == 0 else "dve"
                if eng == "act":
                    S.op("act", L("activation", xT[:, c, tgs(g)], bank(pb), AF.Copy),
                         reads=[PB[pb]], writes=[XT[c][g]])
                else:
                    S.op("dve", L("tensor_copy", xT[:, c, tgs(g)], bank(pb)),
                         reads=[PB[pb]], writes=[XT[c][g]])
        S.barrier()

        def ffn(layer, goff):
            norm_stage(goff)
            U = [Buf("u%d" % i) for i in range(11)]
            TB = [Buf("tb%d" % i) for i in range(2)]
            cp = layer * 88
            for rnd in range(2):
                for cc in range(11):
                    c = rnd * 11 + cc

                    def fn(dr, slot, c=c):
                        v = slot.rearrange("p (k n) -> p k n", k=8)
                        return ([(v[:, k, 0:128], dr["w_gate"][layer][k * 128:(k + 1) * 128, c * 128:(c + 1) * 128]) for k in range(8)]
                                + [(v[:, k, 128:256], dr["w_val"][layer][k * 128:(k + 1) * 128, c * 128:(c + 1) * 128]) for k in range(8)])
                    slot, wb = wtile(fn)
                    wv = wview(slot, 8, 256)
                    for g in range(4):
                        for k in range(8):
                            mm(bank(g), wv[:, k, 0:128], hT[:, k, tgs(g)], k == 0, k == 7, [wb, HT[k][g]], [PB[g]])
                    for g in range(4):
                        for k in range(8):
                            mm(bank(4 + g), wv[:, k, 128:256], hT[:, k, tgs(g)], k == 0, k == 7,
                               [wb, HT[k][g]], [PB[4 + g]])
                    tb = cc % 2
                    tv = blkf(11 + 2 * tb)
                    G = PS[:, 0:2048]
                    gb = [PB[0], PB[1], PB[2], PB[3]]
                    w0 = convp[:, cp + c:cp + c + 1]
                    w1 = convp[:, cp + 22 + c:cp + 22 + c + 1]
                    w2 = convp[:, cp + 44 + c:cp + 44 + c + 1]
                    bb = convp[:, cp + 66 + c:cp + 66 + c + 1]
                    S.op("act", L("activation", tv, G, AF.Identity, scale=w1, bias=bb),
                         reads=gb + [cb1], writes=[TB[tb]])
                    S.op("dve", L("scalar_tensor_tensor", tv[:, 1:T], PS[:, 0:T - 1], w0, tv[:, 1:T], ALU.mult, ALU.add),
                        reads=gb + [TB[tb]], writes=[TB[tb]])
                    S.op("dve", L("scalar_tensor_tensor", tv[:, 0:T - 1], PS[:, 1:T], w2, tv[:, 0:T - 1], ALU.mult, ALU.add),
                        reads=gb + [TB[tb]], writes=[TB[tb]])
                    S.op("act", L("activation", tv, tv, AF.Gelu), reads=[TB[tb]], writes=[TB[tb]])
                    for hh in range(2):
                        S.op("dve", L("tensor_tensor", blk(cc)[:, hh * 1024:(hh + 1) * 1024], tv[:, hh * 1024:(hh + 1) * 1024],
                            PS[:, 2048 + hh * 1024:2048 + (hh + 1) * 1024], ALU.mult),
                            reads=[TB[tb], PB[4 + 2 * hh], PB[5 + 2 * hh]], writes=[U[cc]])
                for n in range(8):
                    slot, wb = wtile(wreq_cols("w_down", 11, n * 128, 128, lidx=layer, r0=rnd * 1408))
                    wv = wview(slot, 11, 128)
                    for g in range(4):
                        pb = proj_fm(wv, wb, 0, lambda k, g: blk(k)[:, tgs(g)], lambda k, g: [U[k]], 11, g)
                        resid_add(pb, n, g)
            S.barrier()

        def softmax_epilogue_diff(h, I, pO, pD, MIX):
            r1, r2, t1, t2 = blkf(13, 2)[:, 0:512], blkf(13, 2)[:, 512:1024], blkf(13, 2)[:, 1024:1536], blkf(13, 2)[:, 1536:2048]
            EB = state.setdefault("EB", [Buf("eb%d" % i) for i in range(4)])
            S.op("dve", L("reciprocal", r1, bank(pD[0])), reads=[PB[pD[0]]], writes=[EB[0]])
            S.op("dve", L("reciprocal", r2, bank(pD[1])), reads=[PB[pD[1]]], writes=[EB[1]])
            S.op("dve", L("tensor_tensor", t1, bank(pO[0]), r1, ALU.mult), reads=[PB[pO[0]], EB[0]], writes=[EB[2]])
            S.op("dve", L("tensor_tensor", t2, bank(pO[1]), r2, ALU.mult), reads=[PB[pO[1]], EB[1]], writes=[EB[3]])
            S.op("dve", L("scalar_tensor_tensor", t1, t2, small[:, 1:2], t1, ALU.mult, ALU.add),
                 reads=[EB[2], EB[3], B_small], writes=[EB[2]])
            q = state["sq"]
            state["sq"] = (q + 1) % 4
            S.op("act", L("activation", sqr[:, q, :], t1, AF.Square), reads=[EB[2]], writes=[SQ[q]])
            pb = nb_()
            mm(bank(pb), onesb[:], sqr[:, q, :], True, True, [SQ[q], B_const], [PB[pb]])
            rstd_from_psum(pb, 128.0, r1, EB[0])
            S.op("dve", L("scalar_tensor_tensor", blk(h)[:, tgs(I)], t1, small[:, 2:3], r1, ALU.mult, ALU.mult),
                 reads=[EB[2], EB[0], B_small], writes=[MIX[h]])

        def diff_phase():
            MIX = [Buf("mix%d" % i) for i in range(4)]
            QB = [Buf("dq%d" % i) for i in range(2)]
            KB = [[Buf("dk%d%d" % (i, v)) for v in range(2)] for i in range(2)]
            VB = Buf("dv")
            PT = [Buf("pt%d" % i) for i in range(8)]
            Qt = [blk(4), blk(5)]
            Kt = [[blk(6), blk(7)], [blk(8), blk(9)]]
            Vt = blk(10).rearrange("p (t e) -> p t e", t=16)
            ptv = lambda i: blk(11, 2)[:, i * 512:(i + 1) * 512]
            pti = [0]
            for n in range(2):
                for v in range(2):
                    S.dma("pool", Kt[n][v][64:68, :], dram["augk"][v], writes=[KB[n][v]])
            for h in range(int(os.environ.get("KH0", "0")), int(os.environ.get("KH1", "4")) if KDBG in (0, 5) else 1):
                if KDBG == 1:
                    break
                if os.environ.get("KBAR", "1") == "1":
                    S.barrier()
                slot, wb = wtile(wreq_cols("w_in", 8, h * 128, 128))
                wq = wview(slot, 8, 128)
                for n in range(2):
                    S.dma("pool", Qt[n][64:68, :], dram["augq"][h], writes=[QB[n]])
                for g in range(4):
                    pb = proj_fm(wq, wb, 0, hT_rhs, hT_bufs, 8, g)
                    S.op("dve", L("tensor_scalar", Qt[0][0:64, tgs(g)], bank(pb)[0:64, :], 0.125, None, ALU.mult),
                         reads=[PB[pb]], writes=[QB[0]])
                    S.op("dve", L("tensor_scalar", Qt[1][0:64, tgs(g)], bank(pb)[64:128, :], 0.125, None, ALU.mult),
                         reads=[PB[pb]], writes=[QB[1]])
                slot, wb = wtile(wreq_cols("w_in", 8, 512 + h * 128, 128))
                wk = wview(slot, 8, 128)
                for g in range(4):
                    pb = proj_fm(wk, wb, 0, hT_rhs, hT_bufs, 8, g)
                    for n in range(2):
                        S.op("act", L("activation", Kt[n][0][0:64, tgs(g)], bank(pb)[64 * n:64 * n + 64, :], AF.Copy),
                             reads=[PB[pb]], writes=[KB[n][0]])
                        S.op("dve", L("tensor_copy", Kt[n][1][0:64, tgs(g)], bank(pb)[64 * n:64 * n + 64, :]),
                             reads=[PB[pb]], writes=[KB[n][1]])
                slot, wb = wtile(wreq_cols("w_in", 8, 1024 + h * 128, 128))
                wvv = wview(slot, 8, 128)
                for g in range(4):
                    pb = nb_()
                    for j in range(4):
                        tt = g * 4 + j
                        for k in range(8):
                            mm(bank(pb)[:, j * 128:(j + 1) * 128], hT[:, k, tt * 128:(tt + 1) * 128], wvv[:, k, :],
                               k == 0, k == 7, [wb, HT[k][g]], [PB[pb]], track=(k == 7 and j == 3))
                    S.op("act", L("activation", Vt[:, g * 4:(g + 1) * 4, :], bank(pb).rearrange("p (t e) -> p t e", t=4), AF.Copy),
                        reads=[PB[pb]], writes=[VB])
                for I in range(4 if KDBG in (0, 4, 5) else (0 if KDBG <= 2 else 1)):
                    pO = [nb_(), nb_()]
                    pD = [nb_(), nb_()]
                    for J in range(16):
                        for n in range(2):
                            ps = nb_()
                            while ps in pO or ps in pD:
                                ps = nb_()
                            if J < 4 * I or J >= 4 * I + 4:
                                v = 0 if J < 4 * I else 1
                                mm(bank(ps), Kt[n][v][0:68, J * 128:(J + 1) * 128], Qt[n][0:68, tgs(I)], True, True,
                                   [KB[n][v], QB[n]], [PB[ps]])
                            else:
                                for b in range(4):
                                    gb_ = 4 * I + b
                                    v = 0 if gb_ >= J else 1
                                    qs = slice(gb_ * 128, (gb_ + 1) * 128)
                                    osl = bank(ps)[:, b * 128:(b + 1) * 128]
                                    diag = gb_ == J
                                    mm(osl, Kt[n][v][0:68, J * 128:(J + 1) * 128], Qt[n][0:68, qs], True, not diag,
                                       [KB[n][v], QB[n]], [PB[ps]], track=(b == 3 and not diag))
                                    if diag:
                                        mm(osl, identb[:], cdiag[:, h * 128:(h + 1) * 128], False, True,
                                           [B_const, cb3], [PB[ps]], track=(b == 3))
                            pi = pti[0]
                            pti[0] = (pi + 1) % 8
                            S.op("act", L("activation", ptv(pi), bank(ps), AF.Exp),
                                 reads=[PB[ps]], writes=[PT[pi]])
                            mm(bank(pO[n]), Vt[:, J, :], ptv(pi), J == 0, J == 15, [VB, PT[pi]], [PB[pO[n]]], track=True)
                            mm(bank(pD[n]), onesb[:], ptv(pi), J == 0, J == 15, [B_const, PT[pi]], [PB[pD[n]]], track=True)
                    softmax_epilogue_diff(h, I, pO, pD, MIX)
            if KDBG == 0:
                out_proj("w_out", lambda k, g: blk(k)[:, tgs(g)], lambda k, g: [MIX[k]], 0, 4)
            S.barrier()

        def na_phase():
            lists, _ = _na_tile_lists()
            MIX = [Buf("nmix%d" % i) for i in range(4)]
            NQb, NKb = Buf("nq"), Buf("nk")
            VAb = [Buf("va0"), Buf("va1")]
            NBb = Buf("nbias")
            PT = [Buf("npt%d" % i) for i in range(4)]
            RD = [Buf("nrd%d" % i) for i in range(2)]
            NQ, NK = blk(4), blk(5)
            VA = [blk(6).rearrange("p (t e) -> p t e", t=16), blk(7).rearrange("p (t e) -> p t e", t=16)]
            NBt = blk(8, 3)
            ptv = lambda i: blk(11, 2)[:, i * 640:(i + 1) * 640]
            rdv = lambda i: blkf(13, 2)[:, i * 512:(i + 1) * 512]
            for a in range(2):
                S.op("dve", L("memset", VA[a][:, :, 64:128], 1.0), writes=[VAb[a]])
            pti = [0]
            for j in range(4):
                slot, wb = wtile(wreq_cols("w_in", 8, 1536 + j * 128, 128))
                wq = wview(slot, 8, 128)
                for g in range(4):
                    pb = proj_fm(wq, wb, 0, hT_rhs, hT_bufs, 8, g)
                    S.op("dve", L("tensor_scalar", NQ[:, tgs(g)], bank(pb), 0.125, None, ALU.mult),
                         reads=[PB[pb]], writes=[NQb])
                slot, wb = wtile(wreq_cols("w_in", 8, 2048 + j * 128, 128))
                wk = wview(slot, 8, 128)
                for g in range(4):
                    pb = proj_fm(wk, wb, 0, hT_rhs, hT_bufs, 8, g)
                    S.op("dve", L("tensor_copy", NK[:, tgs(g)], bank(pb)), reads=[PB[pb]], writes=[NKb])
                slot, wb = wtile(wreq_cols("w_in", 8, 2560 + j * 128, 128))
                wvv = wview(slot, 8, 128)
                for g in range(4):
                    pb = nb_()
                    for jj in range(4):
                        tt = g * 4 + jj
                        for k in range(8):
                            mm(bank(pb)[:, jj * 128:(jj + 1) * 128], hT[:, k, tt * 128:(tt + 1) * 128], wvv[:, k, :],
                               k == 0, k == 7, [wb, HT[k][g]], [PB[pb]], track=(k == 7 and jj == 3))
                    pv = bank(pb).rearrange("p (t e) -> p t e", t=4)
                    S.op("act", L("activation", VA[0][:, g * 4:(g + 1) * 4, 0:64], pv[:, :, 0:64], AF.Copy),
                         reads=[PB[pb]], writes=[VAb[0]])
                    S.op("dve", L("tensor_copy", VA[1][:, g * 4:(g + 1) * 4, 0:64], pv[:, :, 64:128]),
                         reads=[PB[pb]], writes=[VAb[1]])
                for a in range(2):
                    S.dma("pool", NBt[:, a * 2688:(a + 1) * 2688].rearrange("p (t q) -> p t q", t=21),
                          dram["nbias"][2 * j + a], writes=[NBb])
                for a in range(2):
                    p0 = 64 * a
                    for mg in range(4):
                        po = nb_()
                        for mi in range(4):
                            m = mg * 4 + mi
                            lst = lists[m]
                            pi = pti[0]
                            pti[0] = (pi + 1) % 3
                            psA = nb_()
                            while psA == po:
                                psA = nb_()
                            psB = None
                            if len(lst) == 5:
                                psB = nb_()
                                while psB == po:
                                    psB = nb_()
                            for idx, (kb, t) in enumerate(lst):
                                if idx < 4:
                                    osl, pbx = bank(psA)[:, idx * 128:(idx + 1) * 128], psA
                                else:
                                    osl, pbx = bank(psB)[:, 0:128], psB
                                last = (idx == 3) or (idx == 4)
                                mm(osl, NK[p0:p0 + 64, kb * 128:(kb + 1) * 128], NQ[p0:p0 + 64, m * 128:(m + 1) * 128],
                                   True, False, [NKb, NQb], [PB[pbx]], track=False)
                                mm(osl, identb[:], NBt[:, a * 2688 + t * 128: a * 2688 + (t + 1) * 128], False, True,
                                   [B_const, NBb], [PB[pbx]], track=last)
                            S.op("act", L("activation", ptv(pi)[:, 0:512], bank(psA), AF.Exp),
                                 reads=[PB[psA]], writes=[PT[pi]])
                            if psB is not None:
                                S.op("act", L("activation", ptv(pi)[:, 512:640], bank(psB)[:, 0:128], AF.Exp),
                                     reads=[PB[psB]], writes=[PT[pi]])
                            for idx, (kb, t) in enumerate(lst):
                                mm(bank(po)[:, mi * 128:(mi + 1) * 128], VA[a][:, kb, :], ptv(pi)[:, idx * 128:(idx + 1) * 128],
                                   idx == 0, idx == len(lst) - 1, [VAb[a], PT[pi]], [PB[po]],
                                   track=(idx == len(lst) - 1))
                        ri = mg % 2
                        S.op("dve", L("reciprocal", rdv(ri)[0:64, :], bank(po)[64:128, :]),
                             reads=[PB[po]], writes=[RD[ri]])
                        S.op("dve", L("tensor_tensor", blk(j)[p0:p0 + 64, tgs(mg)], bank(po)[0:64, :], rdv(ri)[0:64, :], ALU.mult),
                            reads=[PB[po], RD[ri]], writes=[MIX[j]])
            out_proj("w_out", lambda k, g: blk(k)[:, tgs(g)], lambda k, g: [MIX[k]], 512, 4)
            S.barrier()

        def mla_phase():
            CQ = [Buf("cq%d" % i) for i in range(4)]
            CKV = [Buf("ckv%d" % i) for i in range(2)]
            KRb = Buf("kr")
            RTb = Buf("ropetab")
            QHb = [Buf("qh0"), Buf("qh1")]
            KHb = [Buf("kh0"), Buf("kh1")]
            VAb = [Buf("mva0"), Buf("mva1")]
            MIX = [Buf("mmix%d" % i) for i in range(4)]
            PT = [Buf("mpt%d" % i) for i in range(4)]
            TMP = [Buf("mtmp%d" % i) for i in range(2)]
            cqv = lambda c: blk(4 + c)
            ckvv = lambda c: blk(8 + c)
            KR = blk(10)
            rt = blkf(11, 2)
            QH = [blk(13), blk(14)]
            KH = [blk(15), blk(16)]
            VA = [blk(17).rearrange("p (t e) -> p t e", t=16), blk(18).rearrange("p (t e) -> p t e", t=16)]
            ptv = lambda i: blk(19)[:, i * 512:(i + 1) * 512]
            tmpv = lambda i: blkf(20, 1)[:, i * 512:(i + 1) * 512]
            scale = 96.0 ** -0.5
            S.dma("sp", rt, dram["ropetab"], writes=[RTb])
            for a in range(2):
                S.op("dve", L("memset", VA[a][:, :, 64:128], 1.0), writes=[VAb[a]])

            def rope(dst, dstbuf, pb, g):
                S.op("dve", L("tensor_tensor", tmpv(0)[64:96, :], bank(pb)[64:96, :], rt[64:96, tgs(g)], ALU.mult),
                     reads=[PB[pb], RTb], writes=[TMP[0]])
                S.op("dve", L("tensor_tensor", tmpv(1)[64:96, :], bank(pb)[96:128, :], rt[96:128, tgs(g)], ALU.mult),
                     reads=[PB[pb], RTb], writes=[TMP[1]])
                S.op("dve", L("tensor_tensor", dst[64:96, tgs(g)], tmpv(0)[64:96, :], tmpv(1)[64:96, :], ALU.add),
                     reads=[TMP[0], TMP[1]], writes=[dstbuf])

            def latent(wtiles, nch, goff, dstv, dstb, nfeat, extra=None):
                for g in range(4):
                    pbs = []
                    for (wv, wb, ncol) in wtiles:
                        for cc in range(ncol // 128):
                            pbs.append(proj_fm(wv, wb, cc * 128, hT_rhs, hT_bufs, 8, g))
                    pss = nb_()
                    while pss in pbs:
                        pss = nb_()
                    for c in range(nch):
                        q = state["sq"]
                        state["sq"] = (q + 1) % 4
                        S.op("act", L("activation", sqr[:, q, :], bank(pbs[c]), AF.Square),
                             reads=[PB[pbs[c]]], writes=[SQ[q]])
                        mm(bank(pss), onesb[:], sqr[:, q, :], c == 0, c == nch - 1, [SQ[q], B_const], [PB[pss]], track=True)
                    rstd_from_psum(pss, nfeat, rs_b[:], B_rsb)
                    for c in range(nch):
                        S.op("dve", L("scalar_tensor_tensor", dstv(c)[:, tgs(g)], bank(pbs[c]), gains[:, goff + c:goff + c + 1], rs_b[:], ALU.mult, ALU.mult),
                            reads=[PB[pbs[c]], B_rsb], writes=[dstb[c]])
                    if extra is not None:
                        extra(pbs[nch], g)

            s0, b0 = wtile(wreq_cols("w_dq", 8, 0, 256))
            s1, b1 = wtile(wreq_cols("w_dq", 8, 256, 256))
            latent([(wview(s0, 8, 256), b0, 256), (wview(s1, 8, 256), b1, 256)], 4, 40, cqv, CQ, 512.0)
            s0, b0 = wtile(wreq_cols("w_dkv", 8, 0, 256))
            s1, b1 = wtile(wreq_cols("w_dkv", 8, 256, 128))
            latent([(wview(s0, 8, 256), b0, 256), (wview(s1, 8, 128), b1, 128)], 2, 44, ckvv, CKV, 256.0,
                   extra=lambda pb, g: rope(KR, KRb, pb, g))

            for half in range(2):
                for pr in range(4):
                    hp = half * 4 + pr
                    sq_, bq_ = wtile(wreq_cols("w_uq", 4, hp * 256, 256))
                    wq = wview(sq_, 4, 256)
                    qoff = 0
                    for a in range(2):
                        for g in range(4):
                            pb = proj_fm(wq, bq_, qoff + a * 128, lambda k, g: cqv(k)[:, tgs(g)], lambda k, g: [CQ[k]], 4, g)
                            S.op("act", L("activation", QH[a][0:64, tgs(g)], bank(pb)[0:64, :], AF.Copy),
                                 reads=[PB[pb]], writes=[QHb[a]])
                            rope(QH[a], QHb[a], pb, g)
                    sk_, bk_ = wtile(wreq_cols("w_uk", 2, hp * 128, 128))
                    wk = wview(sk_, 2, 128)
                    for g in range(4):
                        pb = proj_fm(wk, bk_, 0, lambda k, g: ckvv(k)[:, tgs(g)], lambda k, g: [CKV[k]], 2, g)
                        S.op("act", L("activation", KH[0][0:64, tgs(g)], bank(pb)[0:64, :], AF.Copy),
                             reads=[PB[pb]], writes=[KHb[0]])
                        S.op("dve", L("tensor_copy", KH[1][0:64, tgs(g)], bank(pb)[64:128, :]),
                             reads=[PB[pb]], writes=[KHb[1]])
                    for a in range(2):
                        S.op("dve", L("tensor_copy", KH[a][64:96, :], KR[64:96, :]), reads=[KRb], writes=[KHb[a]])
                    sv_, bv_ = wtile(wreq_cols("w_uv", 2, hp * 128, 128))
                    wv_ = wview(sv_, 2, 128)
                    for g in range(4):
                        pb = nb_()
                        for jj in range(4):
                            tt = g * 4 + jj
                            for k in range(2):
                                mm(bank(pb)[:, jj * 128:(jj + 1) * 128], ckvv(k)[:, tt * 128:(tt + 1) * 128],
                                   wv_[:, k, :], k == 0, k == 1, [bv_, CKV[k]], [PB[pb]],
                                   track=(k == 1 and jj == 3))
                        pv = bank(pb).rearrange("p (t e) -> p t e", t=4)
                        S.op("act", L("activation", VA[0][:, g * 4:(g + 1) * 4, 0:64], pv[:, :, 0:64], AF.Copy),
                             reads=[PB[pb]], writes=[VAb[0]])
                        S.op("dve", L("tensor_copy", VA[1][:, g * 4:(g + 1) * 4, 0:64], pv[:, :, 64:128]),
                             reads=[PB[pb]], writes=[VAb[1]])
                    pti = state.setdefault("mpti", [0])
                    for a in range(2):
                        for I in range(4):
                            po = nb_()
                            for J in range(16):
                                ps = nb_()
                                while ps == po:
                                    ps = nb_()
                                mm(bank(ps), KH[a][0:96, J * 128:(J + 1) * 128], QH[a][0:96, tgs(I)], True, True,
                                   [KHb[a], QHb[a]], [PB[ps]])
                                pi = pti[0]
                                pti[0] = (pi + 1) % 4
                                S.op("act", L("activation", ptv(pi), bank(ps), AF.Exp, scale=scale),
                                     reads=[PB[ps]], writes=[PT[pi]])
                                mm(bank(po), VA[a][:, J, :], ptv(pi), J == 0, J == 15, [VAb[a], PT[pi]], [PB[po]], track=True)
                            ri = I % 2
                            rsv = rs_a if ri == 0 else rs_b
                            rsb_ = B_rsa if ri == 0 else B_rsb
                            S.op("dve", L("reciprocal", rsv[0:64, :], bank(po)[64:128, :]),
                                 reads=[PB[po]], writes=[rsb_])
                            S.op("dve", L("tensor_tensor", blk(pr)[64 * a:64 * a + 64, tgs(I)], bank(po)[0:64, :], rsv[0:64, :], ALU.mult),
                                reads=[PB[po], rsb_], writes=[MIX[pr]])
                out_proj("w_o", lambda k, g: blk(k)[:, tgs(g)], lambda k, g: [MIX[k]], half * 512, 4)
            S.barrier()

        def final_stage(goff):
            YT = [Buf("yt%d" % i) for i in range(8)]
            OS = [Buf("os%d" % i) for i in range(2)]
            ytv = lambda c: blkf(2 * c, 2)[:, 0:512]
            osv = lambda i: blkf(16 + 2 * i, 2)[:, 0:1024]
            for g in range(4):
                pb = nb_()
                for c in range(8):
                    q = state["sq"]
                    state["sq"] = (q + 1) % 4
                    S.op("act", L("activation", sqr[:, q, :], xT[:, c, tgs(g)], AF.Square),
                         reads=[XT[c][g]], writes=[SQ[q]])
                    mm(bank(pb), onesb[:], sqr[:, q, :], c == 0, c == 7, [SQ[q], B_const], [PB[pb]], track=True)
                rstd_from_psum(pb, 1024.0, rs_b[:], B_rsb)
                for c in range(8):
                    if goff is None:
                        S.op("dve", L("tensor_copy", ytv(c), xT[:, c, tgs(g)]), reads=[XT[c][g]], writes=[YT[c]])
                    else:
                        S.op("dve", L("scalar_tensor_tensor", ytv(c), xT[:, c, tgs(g)], gains[:, goff + c:goff + c + 1], rs_b[:], ALU.mult, ALU.mult),
                            reads=[XT[c][g], B_rsb], writes=[YT[c]])
                for j in range(4):
                    tt = g * 4 + j
                    oi = tt % 2
                    p0, p1 = nb_(), nb_()
                    for c in range(8):
                        pbx = p0 if c < 4 else p1
                        S.op("pe", L("transpose", bank(pbx)[:, (c % 4) * 128:(c % 4 + 1) * 128], ytv(c)[:, j * 128:(j + 1) * 128], ident[:]),
                            reads=[YT[c], B_const], writes=[PB[pbx]], track=(c % 4 == 3))
                    S.op("act", L("activation", osv(oi)[:, 0:512], bank(p0), AF.Copy),
                         reads=[PB[p0]], writes=[OS[oi]])
                    S.op("dve", L("tensor_copy", osv(oi)[:, 512:1024], bank(p1)),
                         reads=[PB[p1]], writes=[OS[oi]])
                    S.dma("sp", y_out[tt * 128:(tt + 1) * 128, :], osv(oi), reads=[OS[oi]])

        if KDBG == 10:
            for _ in range(30):
                norm_stage(0)
        elif KDBG == 11:
            norm_stage(0)
            for i in range(40):
                slot, wb = wtile(wreq_cols("w_in", 8, (i % 24) * 128, 128))
                wq = wview(slot, 8, 128)
                pb = proj_fm(wq, wb, 0, hT_rhs, hT_bufs, 8, i % 4)
                S.op("dve", L("tensor_copy", rs_b[:], bank(pb)), reads=[PB[pb]], writes=[B_rsb])
        elif nstage >= 1:
            norm_stage(0)
            S.barrier()
            diff_phase()
        if nstage >= 2:
            na_phase()
        if nstage >= 3:
            ffn(0, 8)
        if nstage >= 4:
            norm_stage(16)
            S.barrier()
            mla_phase()
        if nstage >= 5:
            ffn(1, 24)
        final_stage(32 if nstage >= 5 else None)
        S.final_wait("sp")
        S.emit()
    return nc, requests


_CACHE = {}


def _get_program(nstage=99):
    if nstage not in _CACHE:
        _, reqs = build(None, nstage)
        nc, _ = build(reqs, nstage)
        _CACHE[nstage] = nc
    return _CACHE[nstage]


def kernel(**inputs):
    nstage = int(inputs.pop("_nstage", 99))
    cores = inputs.pop("_cores", None)
    shared = _prep_shared(inputs)
    x = np.asarray(inputs["x"], np.float32)
    nb = x.shape[0] if cores is None else cores
    nc = _get_program(nstage)
    in_maps = []
    for b in range(nb):
        m = dict(shared)
        m["x"] = np.ascontiguousarray(x[b])
        in_maps.append(m)
    res = run_bass_kernel_spmd(nc, in_maps, core_ids=list(range(nb)))
    out = np.stack([np.asarray(r["y"], np.float32) for r in res.results], axis=0)
    return out
```

```python
import math
import os
KDBG = int(os.environ.get("KDBG", "0"))
from contextlib import ExitStack
import numpy as np
import ml_dtypes
import concourse.bass as bass
import concourse.mybir as mybir
from concourse.bass_utils import run_bass_kernel_spmd

F32 = mybir.dt.float32
BF16 = mybir.dt.bfloat16
AF = mybir.ActivationFunctionType
ALU = mybir.AluOpType

T = 2048
D = 1024
DFF = 2816
NFC = 22
EPS = 1e-6
NEG = -30000.0
NSLOT = 4
LOOKAHEAD = 2
NBLK = 21


class Buf:
    __slots__ = ("name", "w", "r", "dsem", "dcnt", "excl")

    def __init__(self, name, excl=False):
        self.name = name
        self.excl = excl
        self.w = None
        self.r = {}
        self.dsem = None
        self.dcnt = 0


class Sched:
    ENG = ["pe", "act", "dve", "pool", "sp"]
    COMPUTE = ["pe", "act", "dve", "pool"]

    def __init__(self, nc, stack):
        self.nc = nc
        self.stack = stack
        self.ops = {e: [] for e in self.ENG}
        self.cnt = {e: 0 for e in self.ENG}
        self.sem = {e: stack.enter_context(nc.semaphore("s_" + e)) for e in self.ENG}
        self.seen = {e: {} for e in self.ENG}
        self.semobj = {e: self.sem[e] for e in self.ENG}
        self.nd = 0
        self.dma_bufs = []

    def _deps(self, eng, reads, writes):
        deps = {}

        def add(d):
            if d is None:
                return
            k, v = d
            if deps.get(k, 0) < v:
                deps[k] = v
        for b in reads:
            add(b.w)
        for b in writes:
            add(b.w)
            for d in b.r.items():
                add(d)
        waits = []
        seen = self.seen[eng]
        for k, v in deps.items():
            if k == "pe" and eng == "pe":
                continue
            if seen.get(k, 0) >= v:
                continue
            seen[k] = v
            waits.append((self.semobj[k], v))
        return waits

    def op(self, eng, fn, reads=(), writes=(), track=True):
        ex = [b for b in reads if b.excl]
        if ex:
            reads = [b for b in reads if not b.excl]
            writes = list(writes) + [b for b in ex if b not in writes]
        waits = self._deps(eng, reads, writes)
        idx = self.cnt[eng] + 1
        if track:
            self.cnt[eng] = idx
        ev = (eng, idx)
        for b in reads:
            if b.r.get(eng, 0) < idx:
                b.r[eng] = idx
        for b in writes:
            b.w = ev
            b.r = {}
        self.ops[eng].append((waits, fn, track, None))

    def dma(self, q, out, in_, reads=(), writes=()):
        waits = self._deps(q, reads, writes)
        bufs = list(writes) + list(reads)
        assert len(bufs) == 1
        b = bufs[0]
        if b.dsem is None:
            self.nd += 1
            key = "d%d" % self.nd
            b.dsem = key
            self.semobj[key] = self.stack.enter_context(self.nc.semaphore(key))
            self.dma_bufs.append(b)
        b.dcnt += 16
        ev = (b.dsem, b.dcnt)
        if writes:
            b.w = ev
            b.r = {}
        else:
            b.r[b.dsem] = b.dcnt
        self.ops[q].append((waits, L("dma_start", out=out, in_=in_), False,
                            (self.semobj[b.dsem], 16)))

    def barrier(self, dma=False):
        for e in self.ENG:
            waits = []
            if dma:
                for b in self.dma_bufs:
                    if self.seen[e].get(b.dsem, 0) < b.dcnt:
                        self.seen[e][b.dsem] = b.dcnt
                        waits.append((self.semobj[b.dsem], b.dcnt))
            for k in self.COMPUTE:
                v = self.cnt[k]
                if v > 0 and self.seen[e].get(k, 0) < v:
                    self.seen[e][k] = v
                    waits.append((self.semobj[k], v))
            if waits:
                self.ops[e].append((waits, None, False, None))

    def final_wait(self, eng):
        waits = [(self.semobj[b.dsem], b.dcnt) for b in self.dma_bufs]
        for k in self.COMPUTE:
            if self.cnt[k] > 0:
                waits.append((self.semobj[k], self.cnt[k]))
        self.ops[eng].append((waits, None, False, None))

    def emit(self):
        nc = self.nc
        handles = {"pe": "tensor", "act": "scalar", "dve": "vector", "pool": "gpsimd", "sp": "sync"}
        with nc.Block() as block:
            for e in self.ENG:
                ops = self.ops[e]
                own = self.sem[e]

                def body(engh, ops=ops, own=own):
                    for waits, fn, track, dinc in ops:
                        for s, v in waits:
                            engh.wait_ge(s, v)
                        if fn is None:
                            continue
                        ins = fn(engh)
                        if dinc is not None:
                            ins.then_inc(dinc[0], dinc[1])
                        elif track:
                            ins.then_inc(own, 1)
                getattr(block, handles[e])(body)


def L(method, *args, **kw):
    return lambda e: getattr(e, method)(*args, **kw)


def _na_tile_lists():
    lists = []
    tid = {}
    for m in range(16):
        if m <= 1:
            kbs = [0, 1, 2, 3]
        elif m >= 14:
            kbs = [12, 13, 14, 15]
        else:
            kbs = [m - 2, m - 1, m, m + 1, m + 2]
        cur = []
        for kb in kbs:
            key = ("int", kb - m) if 2 <= m <= 13 else ("edge", m, kb)
            if key not in tid:
                tid[key] = len(tid)
            cur.append((kb, tid[key]))
        lists.append(cur)
    return lists, tid


def _na_index_tables():
    lists, tid = _na_tile_lists()
    ntile = len(tid)
    dr_i = np.zeros((ntile, 128, 128), np.int64)
    dc_i = np.zeros((ntile, 128, 128), np.int64)
    valid = np.zeros((ntile, 128, 128), bool)
    done = set()
    for m in range(16):
        for kb, t in lists[m]:
            if t in done:
                continue
            done.add(t)
            kk = np.arange(128)
            kr = (2 * kb + kk // 64)[:, None]
            kc = (kk % 64)[:, None]
            qr = (2 * m + kk // 64)[None, :]
            qc = (kk % 64)[None, :]
            rs = np.clip(qr - 4, 0, 24)
            ws = np.clip(qc - 8, 0, 48)
            v = (kr >= rs) & (kr < rs + 8) & (kc >= ws) & (kc < ws + 16)
            dr = np.clip(kr - qr + 7, 0, 14)
            dc = np.clip(kc - qc + 15, 0, 30)
            dr_i[t] = np.broadcast_to(dr, (128, 128))
            dc_i[t] = np.broadcast_to(dc, (128, 128))
            valid[t] = v
    return dr_i, dc_i, valid


def _const_tables():
    c = {}
    c["ident"] = np.eye(128, dtype=np.float32)
    slopes = 2.0 ** (-8.0 * np.arange(1, 5) / 4)
    i = np.arange(T)
    a = (i // 128).astype(np.float32)
    r = (i % 128).astype(np.float32)
    augq = np.zeros((4, 4, T), np.float32)
    for h in range(4):
        s = slopes[h]
        augq[h, 0] = -s * 128.0 * a
        augq[h, 1] = -s * r
        augq[h, 2] = s
        augq[h, 3] = s
    c["augq"] = augq
    augk = np.zeros((2, 4, T), np.float32)
    for v, sg in enumerate([1.0, -1.0]):
        augk[v, 0] = sg
        augk[v, 1] = sg
        augk[v, 2] = sg * 128.0 * a
        augk[v, 3] = sg * r
    c["augk"] = augk
    jj = np.arange(128)[:, None]
    ii = np.arange(128)[None, :]
    cd = np.zeros((128, 4 * 128), np.float32)
    for h in range(4):
        cd[:, h * 128:(h + 1) * 128] = -2.0 * slopes[h] * np.maximum(jj - ii, 0)
    c["cdiag"] = cd
    inv = (10000.0 ** (-np.arange(0, 32, 2, dtype=np.float32) / np.float32(32))).astype(np.float32)
    ang = (np.arange(T, dtype=np.float32)[:, None] * inv[None, :]).astype(np.float32)
    cos = np.cos(ang.astype(np.float64)).astype(np.float32).T
    sin = np.sin(ang.astype(np.float64)).astype(np.float32).T
    rt = np.zeros((128, T), np.float32)
    rt[64:80] = cos
    rt[80:96] = cos
    rt[96:112] = -sin
    rt[112:128] = sin
    c["ropetab"] = rt
    return c


def _pm(v, nch):
    return np.ascontiguousarray(np.asarray(v, np.float32).reshape(nch, 128).T)


def _prep_shared(inp):
    g = {}
    c = _const_tables()
    g.update(c)
    f = lambda k: np.asarray(inp[k], np.float32)
    gains = np.zeros((128, 48), np.float32)
    gains[:, 0:8] = _pm(f("mix_norm_e")[0], 8)
    gains[:, 8:16] = _pm(f("ffn_norm_g")[0], 8)
    gains[:, 16:24] = _pm(f("mix_norm_o")[0], 8)
    gains[:, 24:32] = _pm(f("ffn_norm_g")[1], 8)
    gains[:, 32:40] = _pm(f("final_norm_g"), 8)
    gains[:, 40:44] = _pm(f("q_norm_g")[0], 4)
    gains[:, 44:46] = _pm(f("kv_norm_g")[0], 2)
    gains[:, 46] = f("diff_subln_g")[0]
    g["gains"] = gains
    cv = np.zeros((128, 2 * NFC * 4), np.float32)
    for l in range(2):
        for k in range(3):
            cv[:, l * 88 + k * 22: l * 88 + (k + 1) * 22] = _pm(f("ffn_conv_w")[l, k], NFC)
        cv[:, l * 88 + 66: l * 88 + 88] = _pm(f("ffn_conv_b")[l], NFC)
    g["convp"] = cv
    lam = np.zeros((128, 256), np.float32)
    for k, nm in enumerate(["diff_lq1", "diff_lk1", "diff_lq2", "diff_lk2"]):
        lam[:, k * 64:(k + 1) * 64] = np.broadcast_to(f(nm)[0][None, :], (128, 64))
    g["lamv"] = lam
    dr_i, dc_i, valid = _na_index_tables()
    rpb = f("na_rpb")[0]
    nb = np.empty((8, dr_i.shape[0], 128, 128), np.float32)
    for h in range(8):
        nb[h] = np.where(valid, rpb[h][dr_i, dc_i], np.float32(NEG))
    g["nbias"] = np.ascontiguousarray(nb.transpose(0, 2, 1, 3))
    g["w_in"] = f("w_in_e")[0]
    g["w_out"] = f("w_out_e")[0]
    g["w_gate"] = f("w_ffn_gate")
    g["w_val"] = f("w_ffn_val")
    g["w_down"] = f("w_ffn_down")
    g["w_dq"] = f("w_dq")[0]
    wuq = f("w_uq")[0]
    cols = []
    for h in range(16):
        b = h * 96
        cols += list(range(b, b + 96)) + list(range(b + 80, b + 96)) + list(range(b + 64, b + 80))
    g["w_uq"] = np.ascontiguousarray(wuq[:, cols])
    wdkv = f("w_dkv")[0]
    wd = np.zeros((1024, 384), np.float32)
    wd[:, 0:256] = wdkv[:, 0:256]
    wd[:, 320:352] = wdkv[:, 256:288]
    wd[:, 352:368] = wdkv[:, 272:288]
    wd[:, 368:384] = wdkv[:, 256:272]
    g["w_dkv"] = wd
    wukv = f("w_ukv")[0].reshape(256, 16, 128)
    g["w_uk"] = np.ascontiguousarray(wukv[:, :, 0:64].reshape(256, 1024))
    g["w_uv"] = np.ascontiguousarray(wukv[:, :, 64:128].reshape(256, 1024))
    g["w_o"] = f("w_o_mla")[0]
    return g


BF16_IN = ()
SHAPES = {
    "x": [T, D], "ident": [128, 128], "augq": [4, 4, T], "augk": [2, 4, T], "cdiag": [128, 512],
    "ropetab": [128, T], "gains": [128, 48], "convp": [128, 176], "lamv": [128, 256],
    "nbias": [8, 128, 21, 128],
}


def _desc_size(d):
    if d[0] == "gv":
        return 2048
    return d[2] * d[4]


def _pack_weights(g, plan):
    pack = np.zeros((len(plan), 128, 2048), np.float32)
    for i, d in enumerate(plan):
        if d[0] == "gv":
            _, layer, c = d
            wg = g["w_gate"][layer][:, c * 128:(c + 1) * 128].reshape(8, 128, 128)
            wv = g["w_val"][layer][:, c * 128:(c + 1) * 128].reshape(8, 128, 128)
            t = np.concatenate([wg, wv], axis=2)
            pack[i] = t.transpose(1, 0, 2).reshape(128, 2048)
        else:
            _, name, kc, c0, ncols, lidx, r0 = d
            w = g[name] if lidx is None else g[name][lidx]
            t = w[r0:r0 + kc * 128, c0:c0 + ncols].reshape(kc, 128, ncols)
            pack[i, :, 0:kc * ncols] = t.transpose(1, 0, 2).reshape(128, kc * ncols)
    return pack


def build(plan=None, nstage=99):
    nc = bass.Bass("TRN2", target_bir_lowering=False)
    dram = {k: nc.dram_tensor(k, shp, BF16 if k in BF16_IN else F32, kind="ExternalInput").ap() for k, shp in SHAPES.items()}
    if plan is not None:
        dram["wpack"] = nc.dram_tensor("wpack", [len(plan), 128, 2048], F32, kind="ExternalInput").ap()
    y_out = nc.dram_tensor("y", [T, D], F32, kind="ExternalOutput").ap()
    requests = []
    with ExitStack() as st:
        S = Sched(nc, st)
        sb = lambda n, shp, dt: st.enter_context(nc.sbuf_tensor("sb_" + n, shp, dt))
        xT = sb("xT", [128, 8, T], F32)
        hT = sb("hT", [128, 8, T], BF16)
        wring = sb("wring", [128, NSLOT, 2048], BF16)
        AR = sb("arena", [128, NBLK * 2048], BF16)
        ident = sb("ident", [128, 128], F32)
        identb = sb("identb", [128, 128], BF16)
        onesb = sb("onesb", [128, 128], BF16)
        gains = sb("gains", [128, 48], F32)
        convp = sb("convp", [128, 176], F32)
        small = sb("small", [128, 16], F32)
        cdiag = sb("cdiag", [128, 512], BF16)
        rs_a = sb("rs_a", [128, 512], F32)
        rs_b = sb("rs_b", [128, 512], F32)
        sqr = sb("sqr", [128, 4, 512], BF16)
        PS = st.enter_context(nc.psum_tensor("PS", [128, 8 * 512], F32))

        XT = [[Buf("xT%d_%d" % (c, g)) for g in range(4)] for c in range(8)]
        HT = [[Buf("hT%d_%d" % (c, g)) for g in range(4)] for c in range(8)]
        WS = [Buf("ws%d" % i) for i in range(NSLOT)]
        PB = [Buf("pb%d" % i, excl=True) for i in range(8)]
        B_const = Buf("const")
        B_small = Buf("small")
        B_rsa, B_rsb = Buf("rsa"), Buf("rsb")
        SQ = [Buf("sq%d" % i) for i in range(4)]

        def bank(i):
            return PS[:, i * 512:(i + 1) * 512]

        def blk(k, n=1):
            return AR[:, k * 2048:(k + n) * 2048]

        def blkf(k, n=2):
            return AR[:, k * 2048:(k + n) * 2048].bitcast(F32)

        state = {"bank": 0, "wi": 0, "wissued": 0, "sq": 0, "sb": 0}
        lamv = blkf(0, 1)[:, 0:256]

        def nb_():
            b = state["bank"]
            state["bank"] = (b + 1) % 8
            return b

        def tgs(g):
            return slice(g * 512, (g + 1) * 512)

        def wtile(desc):
            i = state["wi"]
            state["wi"] = i + 1
            requests.append(desc)
            if plan is not None:
                hi = min(len(plan), i + 1 + LOOKAHEAD)
                while state["wissued"] < hi:
                    j = state["wissued"]
                    slot = j % NSLOT
                    X = _desc_size(plan[j])
                    S.dma("pool", wring[:, slot, 0:X], dram["wpack"][j][:, 0:X], writes=[WS[slot]])
                    state["wissued"] = j + 1
            return wring[:, i % NSLOT, :], WS[i % NSLOT]

        def wreq_cols(name, kc, c0, ncols, lidx=None, r0=0):
            return ("cols", name, kc, c0, ncols, lidx, r0)

        def wview(slot, kc, ncols):
            return slot[:, 0:kc * ncols].rearrange("p (k n) -> p k n", k=kc)

        S.dma("sp", ident[:], dram["ident"], writes=[B_const])
        gb_l = Buf("gains_l")
        S.dma("sp", gains[:], dram["gains"], writes=[gb_l])
        cb1, cb2, cb3 = Buf("c1"), Buf("c2"), Buf("c3")
        S.dma("sp", convp[:], dram["convp"], writes=[cb1])
        S.dma("sp", lamv, dram["lamv"], writes=[cb2])
        S.dma("pool", cdiag[:], dram["cdiag"], writes=[cb3])
        S.op("dve", L("tensor_copy", identb[:], ident[:]), reads=[B_const], writes=[B_const])
        S.op("dve", L("memset", onesb[:], 1.0), writes=[B_const])
        S.op("dve", L("memset", small[:, 0:1], EPS), writes=[B_small])
        lam_init = 0.8 - 0.6 * math.exp(-0.3 * 0)
        S.op("dve", L("tensor_tensor", lamv[:, 0:64], lamv[:, 0:64], lamv[:, 64:128], ALU.mult),
             reads=[cb2], writes=[cb2])
        S.op("dve", L("tensor_tensor", lamv[:, 128:192], lamv[:, 128:192], lamv[:, 192:256], ALU.mult),
             reads=[cb2], writes=[cb2])
        S.op("dve", L("reduce_sum", small[:, 4:5], lamv[:, 0:64], mybir.AxisListType.X),
             reads=[cb2], writes=[B_small])
        S.op("dve", L("reduce_sum", small[:, 5:6], lamv[:, 128:192], mybir.AxisListType.X),
             reads=[cb2], writes=[B_small])
        S.op("act", L("activation", small[:, 6:8], small[:, 4:6], AF.Exp), reads=[B_small], writes=[B_small])
        S.op("dve", L("scalar_tensor_tensor", small[:, 1:2], small[:, 7:8], -lam_init, small[:, 6:7],
                                                     ALU.add, ALU.subtract), reads=[B_small], writes=[B_small])
        S.op("dve", L("tensor_scalar", small[:, 2:3], gains[:, 46:47], 1.0 - lam_init, None, ALU.mult),
             reads=[B_small, gb_l], writes=[B_small])
        S.barrier(dma=True)

        def mm(out, lhsT, rhs, start, stop, reads, writes, track=None):
            if track is None:
                track = stop
            S.op("pe", L("matmul", out, lhsT, rhs, start=start, stop=stop), reads=reads, writes=writes,
                 track=track)

        def rstd_from_psum(pb, nfeat, dst, dstbuf):
            S.op("act", L("activation", rs_a[:], bank(pb), AF.Sqrt, scale=1.0 / nfeat, bias=small[:, 0:1]),
                 reads=[PB[pb], B_small], writes=[B_rsa])
            S.op("dve", L("reciprocal", dst, rs_a[:]), reads=[B_rsa], writes=[dstbuf])

        def norm_stage(goff):
            for g in range(4):
                pb = nb_()
                for c in range(8):
                    q = state["sq"]
                    state["sq"] = (q + 1) % 4
                    S.op("act", L("activation", sqr[:, q, :], xT[:, c, tgs(g)], AF.Square),
                         reads=[XT[c][g]], writes=[SQ[q]])
                    mm(bank(pb), onesb[:], sqr[:, q, :], c == 0, c == 7, [SQ[q], B_const], [PB[pb]], track=True)
                rstd_from_psum(pb, 1024.0, rs_b[:], B_rsb)
                for c in range(8):
                    S.op("dve", L("scalar_tensor_tensor", hT[:, c, tgs(g)], xT[:, c, tgs(g)], gains[:, goff + c:goff + c + 1], rs_b[:],
                        ALU.mult, ALU.mult), reads=[XT[c][g], B_rsb], writes=[HT[c][g]])

        def proj_fm(wv, wbuf, col0, rhs_fn, rhs_bufs_fn, nk, g, M=128):
            pb = nb_()
            for k in range(nk):
                mm(bank(pb)[0:M, :], wv[:, k, col0:col0 + M], rhs_fn(k, g), k == 0, k == nk - 1,
                   [wbuf] + rhs_bufs_fn(k, g), [PB[pb]])
            return pb

        hT_rhs = lambda k, g: hT[:, k, tgs(g)]
        hT_bufs = lambda k, g: [HT[k][g]]

        def resid_add(pb, n, g, eng="dve"):
            S.op(eng, L("tensor_tensor", xT[:, n, tgs(g)], bank(pb), xT[:, n, tgs(g)], ALU.add),
                 reads=[PB[pb], XT[n][g]], writes=[XT[n][g]])

        def out_proj(name, mix_fn, mix_bufs, krow0, nkc):
            for nt in range(4):
                slot, wb = wtile(wreq_cols(name, nkc, nt * 256, 256, r0=krow0))
                wv = wview(slot, nkc, 256)
                for nn in range(2):
                    n = nt * 2 + nn
                    for g in range(4):
                        pb = proj_fm(wv, wb, nn * 128, mix_fn, mix_bufs, nkc, g)
                        resid_add(pb, n, g)

        XS = [Buf("xs%d" % i) for i in range(4)]
        for g in range(4):
            for j in range(4):
                tt = g * 4 + j
                S.dma("sp", blkf(j, 1), dram["x"][tt * 128:(tt + 1) * 128, :], writes=[XS[j]])
            for c in range(8):
                pb = nb_()
                for j in range(4):
                    S.op("pe", L("transpose", bank(pb)[:, j * 128:(j + 1) * 128], blkf(j, 1)[:, c * 128:(c + 1) * 128], ident[:]),
                        reads=[XS[j], B_const], writes=[PB[pb]], track=(j == 3))
                eng = "act" if c % 2 == 0 else "dve"
                if eng == "act":
                    S.op("act", L("activation", xT[:, c, tgs(g)], bank(pb), AF.Copy),
                         reads=[PB[pb]], writes=[XT[c][g]])
                else:
                    S.op("dve", L("tensor_copy", xT[:, c, tgs(g)], bank(pb)),
                         reads=[PB[pb]], writes=[XT[c][g]])
        S.barrier()

        def ffn(layer, goff):
            norm_stage(goff)
            U = [Buf("u%d" % i) for i in range(11)]
            TB = [Buf("tb%d" % i) for i in range(2)]
            cp = layer * 88
            for rnd in range(2):
                for cc in range(11):
                    c = rnd * 11 + cc

                    slot, wb = wtile(("gv", layer, c))
                    wv = wview(slot, 8, 256)
                    for g in range(4):
                        for k in range(8):
                            mm(bank(g), wv[:, k, 0:128], hT[:, k, tgs(g)], k == 0, k == 7, [wb, HT[k][g]], [PB[g]])
                    for g in range(4):
                        for k in range(8):
                            mm(bank(4 + g), wv[:, k, 128:256], hT[:, k, tgs(g)], k == 0, k == 7,
                               [wb, HT[k][g]], [PB[4 + g]])
                    tb = cc % 2
                    tv = blkf(11 + 2 * tb)
                    G = PS[:, 0:2048]
                    gb = [PB[0], PB[1], PB[2], PB[3]]
                    w0 = convp[:, cp + c:cp + c + 1]
                    w1 = convp[:, cp + 22 + c:cp + 22 + c + 1]
                    w2 = convp[:, cp + 44 + c:cp + 44 + c + 1]
                    bb = convp[:, cp + 66 + c:cp + 66 + c + 1]
                    S.op("act", L("activation", tv, G, AF.Identity, scale=w1, bias=bb),
                         reads=gb + [cb1], writes=[TB[tb]])
                    S.op("dve", L("scalar_tensor_tensor", tv[:, 1:T], PS[:, 0:T - 1], w0, tv[:, 1:T], ALU.mult, ALU.add),
                        reads=gb + [TB[tb]], writes=[TB[tb]])
                    S.op("dve", L("scalar_tensor_tensor", tv[:, 0:T - 1], PS[:, 1:T], w2, tv[:, 0:T - 1], ALU.mult, ALU.add),
                        reads=gb + [TB[tb]], writes=[TB[tb]])
                    S.op("act", L("activation", tv, tv, AF.Gelu), reads=[TB[tb]], writes=[TB[tb]])
                    for hh in range(2):
                        S.op("dve", L("tensor_tensor", blk(cc)[:, hh * 1024:(hh + 1) * 1024], tv[:, hh * 1024:(hh + 1) * 1024],
                            PS[:, 2048 + hh * 1024:2048 + (hh + 1) * 1024], ALU.mult),
                            reads=[TB[tb], PB[4 + 2 * hh], PB[5 + 2 * hh]], writes=[U[cc]])
                for n in range(8):
                    slot, wb = wtile(wreq_cols("w_down", 11, n * 128, 128, lidx=layer, r0=rnd * 1408))
                    wv = wview(slot, 11, 128)
                    for g in range(4):
                        pb = proj_fm(wv, wb, 0, lambda k, g: blk(k)[:, tgs(g)], lambda k, g: [U[k]], 11, g)
                        resid_add(pb, n, g)
            S.barrier()

        def softmax_epilogue_diff(h, I, pO, pD, MIX):
            r1, r2, t1, t2 = blkf(13, 2)[:, 0:512], blkf(13, 2)[:, 512:1024], blkf(13, 2)[:, 1024:1536], blkf(13, 2)[:, 1536:2048]
            EB = state.setdefault("EB", [Buf("eb%d" % i) for i in range(4)])
            S.op("dve", L("tensor_copy", r1, bank(pD[0])), reads=[PB[pD[0]]], writes=[EB[0]])
            S.op("dve", L("tensor_copy", r2, bank(pD[1])), reads=[PB[pD[1]]], writes=[EB[1]])
            S.op("dve", L("tensor_copy", t1, bank(pO[0])), reads=[PB[pO[0]]], writes=[EB[2]])
            S.op("dve", L("tensor_copy", t2, bank(pO[1])), reads=[PB[pO[1]]], writes=[EB[3]])
            S.op("dve", L("reciprocal", r1, r1), reads=[EB[0]], writes=[EB[0]])
            S.op("dve", L("reciprocal", r2, r2), reads=[EB[1]], writes=[EB[1]])
            S.op("dve", L("tensor_tensor", t1, t1, r1, ALU.mult), reads=[EB[0], EB[2]], writes=[EB[2]])
            S.op("dve", L("tensor_tensor", t2, t2, r2, ALU.mult), reads=[EB[1], EB[3]], writes=[EB[3]])
            S.op("dve", L("scalar_tensor_tensor", t1, t2, small[:, 1:2], t1, ALU.mult, ALU.add),
                 reads=[EB[2], EB[3], B_small], writes=[EB[2]])
            q = state["sq"]
            state["sq"] = (q + 1) % 4
            S.op("act", L("activation", sqr[:, q, :], t1, AF.Square), reads=[EB[2]], writes=[SQ[q]])
            pb = 4 + state["sb"] % 4
            state["sb"] += 1
            mm(bank(pb), onesb[:], sqr[:, q, :], True, True, [SQ[q], B_const], [PB[pb]])
            rstd_from_psum(pb, 128.0, r1, EB[0])
            S.op("dve", L("scalar_tensor_tensor", blk(h)[:, tgs(I)], t1, small[:, 2:3], r1, ALU.mult, ALU.mult),
                 reads=[EB[2], EB[0], B_small], writes=[MIX[h]])

        def diff_phase():
            MIX = [Buf("mix%d" % i) for i in range(4)]
            QB = [Buf("dq%d" % i) for i in range(2)]
            KB = [[Buf("dk%d%d" % (i, v)) for v in range(2)] for i in range(2)]
            VB = Buf("dv")
            PT = [Buf("pt%d" % i) for i in range(8)]
            Qt = [blk(4), blk(5)]
            Kt = [[blk(6), blk(7)], [blk(8), blk(9)]]
            Vt = blk(10).rearrange("p (t e) -> p t e", t=16)
            ptv = lambda i: blk(11, 2)[:, i * 512:(i + 1) * 512]
            pti = [0]
            for n in range(2):
                for v in range(2):
                    S.dma("pool", Kt[n][v][64:68, :], dram["augk"][v], writes=[KB[n][v]])
            for h in range(int(os.environ.get("KH0", "0")), int(os.environ.get("KH1", "4")) if KDBG in (0, 5) else 1):
                if KDBG == 1:
                    break
                if os.environ.get("KBAR", "0") == "1":
                    S.barrier()
                slot, wb = wtile(wreq_cols("w_in", 8, h * 128, 128))
                wq = wview(slot, 8, 128)
                for n in range(2):
                    S.dma("pool", Qt[n][64:68, :], dram["augq"][h], writes=[QB[n]])
                for g in range(4):
                    pb = proj_fm(wq, wb, 0, hT_rhs, hT_bufs, 8, g)
                    S.op("dve", L("tensor_scalar", Qt[0][0:64, tgs(g)], bank(pb)[0:64, :], 0.125, None, ALU.mult),
                         reads=[PB[pb]], writes=[QB[0]])
                    S.op("dve", L("tensor_scalar", Qt[1][0:64, tgs(g)], bank(pb)[64:128, :], 0.125, None, ALU.mult),
                         reads=[PB[pb]], writes=[QB[1]])
                slot, wb = wtile(wreq_cols("w_in", 8, 512 + h * 128, 128))
                wk = wview(slot, 8, 128)
                for g in range(4):
                    pb = proj_fm(wk, wb, 0, hT_rhs, hT_bufs, 8, g)
                    for n in range(2):
                        S.op("act", L("activation", Kt[n][0][0:64, tgs(g)], bank(pb)[64 * n:64 * n + 64, :], AF.Copy),
                             reads=[PB[pb]], writes=[KB[n][0]])
                        S.op("dve", L("tensor_copy", Kt[n][1][0:64, tgs(g)], bank(pb)[64 * n:64 * n + 64, :]),
                             reads=[PB[pb]], writes=[KB[n][1]])
                slot, wb = wtile(wreq_cols("w_in", 8, 1024 + h * 128, 128))
                wvv = wview(slot, 8, 128)
                for g in range(4):
                    pb = nb_()
                    for j in range(4):
                        tt = g * 4 + j
                        for k in range(8):
                            mm(bank(pb)[:, j * 128:(j + 1) * 128], hT[:, k, tt * 128:(tt + 1) * 128], wvv[:, k, :],
                               k == 0, k == 7, [wb, HT[k][g]], [PB[pb]], track=(k == 7 and j == 3))
                    S.op("act", L("activation", Vt[:, g * 4:(g + 1) * 4, :], bank(pb).rearrange("p (t e) -> p t e", t=4), AF.Copy),
                        reads=[PB[pb]], writes=[VB])
                pO, pD = [0, 1], [2, 3]
                tiles = [(I, J, n) for I in range(4) for J in range(16) for n in range(2)]
                info = {}

                def stage1(t):
                    I, J, n = t
                    ps = 4 + state["sb"] % 4
                    state["sb"] += 1
                    if J < 4 * I or J >= 4 * I + 4:
                        v = 0 if J < 4 * I else 1
                        mm(bank(ps), Kt[n][v][0:68, J * 128:(J + 1) * 128], Qt[n][0:68, tgs(I)], True, True,
                           [KB[n][v], QB[n]], [PB[ps]])
                    else:
                        for b in range(4):
                            gb_ = 4 * I + b
                            v = 0 if gb_ >= J else 1
                            qs = slice(gb_ * 128, (gb_ + 1) * 128)
                            osl = bank(ps)[:, b * 128:(b + 1) * 128]
                            diag = gb_ == J
                            mm(osl, Kt[n][v][0:68, J * 128:(J + 1) * 128], Qt[n][0:68, qs], True, not diag,
                               [KB[n][v], QB[n]], [PB[ps]], track=(b == 3 and not diag))
                            if diag:
                                mm(osl, identb[:], cdiag[:, h * 128:(h + 1) * 128], False, True,
                                   [B_const, cb3], [PB[ps]], track=(b == 3))
                    pi = pti[0]
                    pti[0] = (pi + 1) % 8
                    S.op("act", L("activation", ptv(pi), bank(ps), AF.Exp), reads=[PB[ps]], writes=[PT[pi]])
                    info[t] = pi

                def stage2(t):
                    I, J, n = t
                    pi = info.pop(t)
                    mm(bank(pO[n]), Vt[:, J, :], ptv(pi), J == 0, J == 15, [VB, PT[pi]], [PB[pO[n]]], track=True)
                    mm(bank(pD[n]), onesb[:], ptv(pi), J == 0, J == 15, [B_const, PT[pi]], [PB[pD[n]]], track=True)
                    if J == 15 and n == 1:
                        softmax_epilogue_diff(h, I, pO, pD, MIX)

                DEPTH = 2
                for i in range(len(tiles) + DEPTH):
                    if i < len(tiles):
                        stage1(tiles[i])
                    if i >= DEPTH:
                        stage2(tiles[i - DEPTH])
            if KDBG == 0:
                out_proj("w_out", lambda k, g: blk(k)[:, tgs(g)], lambda k, g: [MIX[k]], 0, 4)
            S.barrier()

        def na_phase():
            lists, _ = _na_tile_lists()
            MIX = [Buf("nmix%d" % i) for i in range(4)]
            NQb, NKb = Buf("nq"), Buf("nk")
            VAb = [Buf("va0"), Buf("va1")]
            NBb = Buf("nbias")
            PT = [Buf("npt%d" % i) for i in range(4)]
            RD = [Buf("nrd%d" % i) for i in range(2)]
            NQ, NK = blk(4), blk(5)
            VA = [blk(6).rearrange("p (t e) -> p t e", t=16), blk(7).rearrange("p (t e) -> p t e", t=16)]
            NBt = blk(8, 3)
            ptv = lambda i: blk(11, 2)[:, i * 640:(i + 1) * 640]
            rdv = lambda i: blkf(13, 2)[:, i * 512:(i + 1) * 512]
            for a in range(2):
                S.op("dve", L("memset", VA[a][:, :, 64:128], 1.0), writes=[VAb[a]])
            pti = [0]
            for j in range(4):
                slot, wb = wtile(wreq_cols("w_in", 8, 1536 + j * 128, 128))
                wq = wview(slot, 8, 128)
                for g in range(4):
                    pb = proj_fm(wq, wb, 0, hT_rhs, hT_bufs, 8, g)
                    S.op("dve", L("tensor_scalar", NQ[:, tgs(g)], bank(pb), 0.125, None, ALU.mult),
                         reads=[PB[pb]], writes=[NQb])
                slot, wb = wtile(wreq_cols("w_in", 8, 2048 + j * 128, 128))
                wk = wview(slot, 8, 128)
                for g in range(4):
                    pb = proj_fm(wk, wb, 0, hT_rhs, hT_bufs, 8, g)
                    S.op("dve", L("tensor_copy", NK[:, tgs(g)], bank(pb)), reads=[PB[pb]], writes=[NKb])
                slot, wb = wtile(wreq_cols("w_in", 8, 2560 + j * 128, 128))
                wvv = wview(slot, 8, 128)
                for g in range(4):
                    pb = nb_()
                    for jj in range(4):
                        tt = g * 4 + jj
                        for k in range(8):
                            mm(bank(pb)[:, jj * 128:(jj + 1) * 128], hT[:, k, tt * 128:(tt + 1) * 128], wvv[:, k, :],
                               k == 0, k == 7, [wb, HT[k][g]], [PB[pb]], track=(k == 7 and jj == 3))
                    pv = bank(pb).rearrange("p (t e) -> p t e", t=4)
                    S.op("act", L("activation", VA[0][:, g * 4:(g + 1) * 4, 0:64], pv[:, :, 0:64], AF.Copy),
                         reads=[PB[pb]], writes=[VAb[0]])
                    S.op("dve", L("tensor_copy", VA[1][:, g * 4:(g + 1) * 4, 0:64], pv[:, :, 64:128]),
                         reads=[PB[pb]], writes=[VAb[1]])
                for a in range(2):
                    S.dma("pool", NBt[:, a * 2688:(a + 1) * 2688].rearrange("p (t q) -> p t q", t=21),
                          dram["nbias"][2 * j + a], writes=[NBb])
                for a in range(2):
                    p0 = 64 * a
                    for mg in range(4):
                        po = nb_()
                        for mi in range(4):
                            m = mg * 4 + mi
                            lst = lists[m]
                            pi = pti[0]
                            pti[0] = (pi + 1) % 3
                            psA = nb_()
                            while psA == po:
                                psA = nb_()
                            psB = None
                            if len(lst) == 5:
                                psB = nb_()
                                while psB == po:
                                    psB = nb_()
                            for idx, (kb, t) in enumerate(lst):
                                if idx < 4:
                                    osl, pbx = bank(psA)[:, idx * 128:(idx + 1) * 128], psA
                                else:
                                    osl, pbx = bank(psB)[:, 0:128], psB
                                last = (idx == 3) or (idx == 4)
                                mm(osl, NK[p0:p0 + 64, kb * 128:(kb + 1) * 128], NQ[p0:p0 + 64, m * 128:(m + 1) * 128],
                                   True, False, [NKb, NQb], [PB[pbx]], track=False)
                                mm(osl, identb[:], NBt[:, a * 2688 + t * 128: a * 2688 + (t + 1) * 128], False, True,
                                   [B_const, NBb], [PB[pbx]], track=last)
                            S.op("act", L("activation", ptv(pi)[:, 0:512], bank(psA), AF.Exp),
                                 reads=[PB[psA]], writes=[PT[pi]])
                            if psB is not None:
                                S.op("act", L("activation", ptv(pi)[:, 512:640], bank(psB)[:, 0:128], AF.Exp),
                                     reads=[PB[psB]], writes=[PT[pi]])
                            for idx, (kb, t) in enumerate(lst):
                                mm(bank(po)[:, mi * 128:(mi + 1) * 128], VA[a][:, kb, :], ptv(pi)[:, idx * 128:(idx + 1) * 128],
                                   idx == 0, idx == len(lst) - 1, [VAb[a], PT[pi]], [PB[po]],
                                   track=(idx == len(lst) - 1))
                        ri = mg % 2
                        S.op("dve", L("reciprocal", rdv(ri)[0:64, :], bank(po)[64:128, :]),
                             reads=[PB[po]], writes=[RD[ri]])
                        S.op("dve", L("tensor_tensor", blk(j)[p0:p0 + 64, tgs(mg)], bank(po)[0:64, :], rdv(ri)[0:64, :], ALU.mult),
                            reads=[PB[po], RD[ri]], writes=[MIX[j]])
            out_proj("w_out", lambda k, g: blk(k)[:, tgs(g)], lambda k, g: [MIX[k]], 512, 4)
            S.barrier()

        def mla_phase():
            CQ = [Buf("cq%d" % i) for i in range(4)]
            CKV = [Buf("ckv%d" % i) for i in range(2)]
            KRb = Buf("kr")
            RTb = Buf("ropetab")
            QHb = [Buf("qh0"), Buf("qh1")]
            KHb = [Buf("kh0"), Buf("kh1")]
            VAb = [Buf("mva0"), Buf("mva1")]
            MIX = [Buf("mmix%d" % i) for i in range(4)]
            PT = [Buf("mpt%d" % i) for i in range(8)]
            TMP = [Buf("mtmp%d" % i) for i in range(2)]
            cqv = lambda c: blk(4 + c)
            ckvv = lambda c: blk(8 + c)
            KR = blk(10)
            rt = blkf(11, 2)
            QH = [blk(13), blk(14)]
            KH = [blk(15), blk(16)]
            VA = [blk(17).rearrange("p (t e) -> p t e", t=16), blk(18).rearrange("p (t e) -> p t e", t=16)]
            ptv = lambda i: blk(10 if i < 4 else 19)[:, (i % 4) * 512:(i % 4 + 1) * 512]
            tmpv = lambda i: blkf(20, 1)[:, i * 512:(i + 1) * 512]
            scale = 96.0 ** -0.5
            S.dma("sp", rt, dram["ropetab"], writes=[RTb])
            for a in range(2):
                S.op("dve", L("memset", VA[a][:, :, 64:128], 1.0), writes=[VAb[a]])

            def rope(dst, dstbuf, pb, g):
                S.op("dve", L("tensor_tensor", tmpv(0)[64:96, :], bank(pb)[64:96, :], rt[64:96, tgs(g)], ALU.mult),
                     reads=[PB[pb], RTb], writes=[TMP[0]])
                S.op("dve", L("tensor_tensor", tmpv(1)[64:96, :], bank(pb)[96:128, :], rt[96:128, tgs(g)], ALU.mult),
                     reads=[PB[pb], RTb], writes=[TMP[1]])
                S.op("dve", L("tensor_tensor", dst[64:96, tgs(g)], tmpv(0)[64:96, :], tmpv(1)[64:96, :], ALU.add),
                     reads=[TMP[0], TMP[1]], writes=[dstbuf])

            def latent(wtiles, nch, goff, dstv, dstb, nfeat, extra=None):
                for g in range(4):
                    pbs = []
                    for (wv, wb, ncol) in wtiles:
                        for cc in range(ncol // 128):
                            pbs.append(proj_fm(wv, wb, cc * 128, hT_rhs, hT_bufs, 8, g))
                    pss = nb_()
                    while pss in pbs:
                        pss = nb_()
                    for c in range(nch):
                        q = state["sq"]
                        state["sq"] = (q + 1) % 4
                        S.op("act", L("activation", sqr[:, q, :], bank(pbs[c]), AF.Square),
                             reads=[PB[pbs[c]]], writes=[SQ[q]])
                        mm(bank(pss), onesb[:], sqr[:, q, :], c == 0, c == nch - 1, [SQ[q], B_const], [PB[pss]], track=True)
                    rstd_from_psum(pss, nfeat, rs_b[:], B_rsb)
                    for c in range(nch):
                        S.op("dve", L("scalar_tensor_tensor", dstv(c)[:, tgs(g)], bank(pbs[c]), gains[:, goff + c:goff + c + 1], rs_b[:], ALU.mult, ALU.mult),
                            reads=[PB[pbs[c]], B_rsb], writes=[dstb[c]])
                    if extra is not None:
                        extra(pbs[nch], g)

            s0, b0 = wtile(wreq_cols("w_dq", 8, 0, 256))
            s1, b1 = wtile(wreq_cols("w_dq", 8, 256, 256))
            latent([(wview(s0, 8, 256), b0, 256), (wview(s1, 8, 256), b1, 256)], 4, 40, cqv, CQ, 512.0)
            s0, b0 = wtile(wreq_cols("w_dkv", 8, 0, 256))
            s1, b1 = wtile(wreq_cols("w_dkv", 8, 256, 128))
            latent([(wview(s0, 8, 256), b0, 256), (wview(s1, 8, 128), b1, 128)], 2, 44, ckvv, CKV, 256.0,
                   extra=lambda pb, g: rope(KH[0], KHb[0], pb, g))
            S.op("dve", L("tensor_copy", KH[1][64:96, :], KH[0][64:96, :]), reads=[KHb[0]], writes=[KHb[1]])

            for half in range(2):
                for pr in range(4):
                    hp = half * 4 + pr
                    sq_, bq_ = wtile(wreq_cols("w_uq", 4, hp * 256, 256))
                    wq = wview(sq_, 4, 256)
                    qoff = 0
                    for a in range(2):
                        for g in range(4):
                            pb = proj_fm(wq, bq_, qoff + a * 128, lambda k, g: cqv(k)[:, tgs(g)], lambda k, g: [CQ[k]], 4, g)
                            S.op("act", L("activation", QH[a][0:64, tgs(g)], bank(pb)[0:64, :], AF.Copy),
                                 reads=[PB[pb]], writes=[QHb[a]])
                            rope(QH[a], QHb[a], pb, g)
                    sk_, bk_ = wtile(wreq_cols("w_uk", 2, hp * 128, 128))
                    wk = wview(sk_, 2, 128)
                    for g in range(4):
                        pb = proj_fm(wk, bk_, 0, lambda k, g: ckvv(k)[:, tgs(g)], lambda k, g: [CKV[k]], 2, g)
                        S.op("act", L("activation", KH[0][0:64, tgs(g)], bank(pb)[0:64, :], AF.Copy),
                             reads=[PB[pb]], writes=[KHb[0]])
                        S.op("dve", L("tensor_copy", KH[1][0:64, tgs(g)], bank(pb)[64:128, :]),
                             reads=[PB[pb]], writes=[KHb[1]])
                    sv_, bv_ = wtile(wreq_cols("w_uv", 2, hp * 128, 128))
                    wv_ = wview(sv_, 2, 128)
                    for g in range(4):
                        pb = nb_()
                        for jj in range(4):
                            tt = g * 4 + jj
                            for k in range(2):
                                mm(bank(pb)[:, jj * 128:(jj + 1) * 128], ckvv(k)[:, tt * 128:(tt + 1) * 128],
                                   wv_[:, k, :], k == 0, k == 1, [bv_, CKV[k]], [PB[pb]],
                                   track=(k == 1 and jj == 3))
                        pv = bank(pb).rearrange("p (t e) -> p t e", t=4)
                        S.op("act", L("activation", VA[0][:, g * 4:(g + 1) * 4, 0:64], pv[:, :, 0:64], AF.Copy),
                             reads=[PB[pb]], writes=[VAb[0]])
                        S.op("dve", L("tensor_copy", VA[1][:, g * 4:(g + 1) * 4, 0:64], pv[:, :, 64:128]),
                             reads=[PB[pb]], writes=[VAb[1]])
                    pti = state.setdefault("mpti", [0])
                    tiles = [(a, I, J) for a in range(2) for I in range(4) for J in range(16)]
                    info = {}

                    def stage1(t):
                        a, I, J = t
                        ps = 2 + state["sb"] % 6
                        state["sb"] += 1
                        mm(bank(ps), KH[a][0:96, J * 128:(J + 1) * 128], QH[a][0:96, tgs(I)], True, True,
                           [KHb[a], QHb[a]], [PB[ps]])
                        pi = pti[0]
                        pti[0] = (pi + 1) % 8
                        S.op("act", L("activation", ptv(pi), bank(ps), AF.Exp, scale=scale),
                             reads=[PB[ps]], writes=[PT[pi]])
                        info[t] = pi

                    def stage2(t, pr=pr):
                        a, I, J = t
                        pi = info.pop(t)
                        po = I % 2
                        mm(bank(po), VA[a][:, J, :], ptv(pi), J == 0, J == 15, [VAb[a], PT[pi]], [PB[po]], track=True)
                        if J == 15:
                            rsv = rs_a if po == 0 else rs_b
                            rsb_ = B_rsa if po == 0 else B_rsb
                            S.op("dve", L("reciprocal", rsv[0:64, :], bank(po)[64:128, :]),
                                 reads=[PB[po]], writes=[rsb_])
                            S.op("dve", L("tensor_tensor", blk(pr)[64 * a:64 * a + 64, tgs(I)], bank(po)[0:64, :], rsv[0:64, :], ALU.mult),
                                 reads=[PB[po], rsb_], writes=[MIX[pr]])

                    DEPTH = 3
                    for i in range(len(tiles) + DEPTH):
                        if i < len(tiles):
                            stage1(tiles[i])
                        if i >= DEPTH:
                            stage2(tiles[i - DEPTH])
                out_proj("w_o", lambda k, g: blk(k)[:, tgs(g)], lambda k, g: [MIX[k]], half * 512, 4)
            S.barrier()

        def final_stage(goff):
            YT = [Buf("yt%d" % i) for i in range(8)]
            OS = [Buf("os%d" % i) for i in range(2)]
            ytv = lambda c: blkf(2 * c, 2)[:, 0:512]
            osv = lambda i: blkf(16 + 2 * i, 2)[:, 0:1024]
            for g in range(4):
                pb = nb_()
                for c in range(8):
                    q = state["sq"]
                    state["sq"] = (q + 1) % 4
                    S.op("act", L("activation", sqr[:, q, :], xT[:, c, tgs(g)], AF.Square),
                         reads=[XT[c][g]], writes=[SQ[q]])
                    mm(bank(pb), onesb[:], sqr[:, q, :], c == 0, c == 7, [SQ[q], B_const], [PB[pb]], track=True)
                rstd_from_psum(pb, 1024.0, rs_b[:], B_rsb)
                for c in range(8):
                    if goff is None:
                        S.op("dve", L("tensor_copy", ytv(c), xT[:, c, tgs(g)]), reads=[XT[c][g]], writes=[YT[c]])
                    else:
                        S.op("dve", L("scalar_tensor_tensor", ytv(c), xT[:, c, tgs(g)], gains[:, goff + c:goff + c + 1], rs_b[:], ALU.mult, ALU.mult),
                            reads=[XT[c][g], B_rsb], writes=[YT[c]])
                for j in range(4):
                    tt = g * 4 + j
                    oi = tt % 2
                    p0, p1 = nb_(), nb_()
                    for c in range(8):
                        pbx = p0 if c < 4 else p1
                        S.op("pe", L("transpose", bank(pbx)[:, (c % 4) * 128:(c % 4 + 1) * 128], ytv(c)[:, j * 128:(j + 1) * 128], ident[:]),
                            reads=[YT[c], B_const], writes=[PB[pbx]], track=(c % 4 == 3))
                    S.op("act", L("activation", osv(oi)[:, 0:512], bank(p0), AF.Copy),
                         reads=[PB[p0]], writes=[OS[oi]])
                    S.op("dve", L("tensor_copy", osv(oi)[:, 512:1024], bank(p1)),
                         reads=[PB[p1]], writes=[OS[oi]])
                    S.dma("sp", y_out[tt * 128:(tt + 1) * 128, :], osv(oi), reads=[OS[oi]])

        if KDBG == 10:
            for _ in range(30):
                norm_stage(0)
        elif KDBG == 11:
            norm_stage(0)
            for i in range(40):
                slot, wb = wtile(wreq_cols("w_in", 8, (i % 24) * 128, 128))
                wq = wview(slot, 8, 128)
                pb = proj_fm(wq, wb, 0, hT_rhs, hT_bufs, 8, i % 4)
                S.op("dve", L("tensor_copy", rs_b[:], bank(pb)), reads=[PB[pb]], writes=[B_rsb])
        elif nstage >= 1:
            norm_stage(0)
            S.barrier()
            diff_phase()
        if nstage >= 2:
            na_phase()
        if nstage >= 3:
            ffn(0, 8)
        if nstage >= 4:
            norm_stage(16)
            S.barrier()
            mla_phase()
        if nstage >= 5:
            ffn(1, 24)
        final_stage(32 if nstage >= 5 else None)
        S.final_wait("sp")
        S.emit()
    return nc, requests


_CACHE = {}


def _get_program(nstage=99):
    if nstage not in _CACHE:
        _, reqs = build(None, nstage)
        nc, _ = build(reqs, nstage)
        _CACHE[nstage] = (nc, reqs)
    return _CACHE[nstage]


def kernel(**inputs):
    nstage = int(inputs.pop("_nstage", 99))
    cores = inputs.pop("_cores", None)
    shared = _prep_shared(inputs)
    x = np.asarray(inputs["x"], np.float32)
    nb = x.shape[0] if cores is None else cores
    nc, plan = _get_program(nstage)
    dev = {k: shared[k] for k in SHAPES if k != "x"}
    dev["wpack"] = _pack_weights(shared, plan)
    in_maps = []
    for b in range(nb):
        m = dict(dev)
        m["x"] = np.ascontiguousarray(x[b])
        in_maps.append(m)
    res = run_bass_kernel_spmd(nc, in_maps, core_ids=list(range(nb)))
    out = np.stack([np.asarray(r["y"], np.float32) for r in res.results], axis=0)
    return out
```

```python
import math
import os
KDBG = int(os.environ.get("KDBG", "0"))
from contextlib import ExitStack
import numpy as np
import ml_dtypes
import concourse.bass as bass
import concourse.mybir as mybir
from concourse.bass_utils import run_bass_kernel_spmd

F32 = mybir.dt.float32
BF16 = mybir.dt.bfloat16
AF = mybir.ActivationFunctionType
ALU = mybir.AluOpType

T = 2048
D = 1024
DFF = 2816
NFC = 22
EPS = 1e-6
NEG = -30000.0
NSLOT = 4
LOOKAHEAD = 2
NBLK = 21


class Buf:
    __slots__ = ("name", "w", "r", "dsem", "dcnt", "excl")

    def __init__(self, name, excl=False):
        self.name = name
        self.excl = excl
        self.w = None
        self.r = {}
        self.dsem = None
        self.dcnt = 0


class Sched:
    ENG = ["pe", "act", "dve", "pool", "sp"]
    COMPUTE = ["pe", "act", "dve", "pool"]

    def __init__(self, nc, stack):
        self.nc = nc
        self.stack = stack
        self.ops = {e: [] for e in self.ENG}
        self.cnt = {e: 0 for e in self.ENG}
        self.sem = {e: stack.enter_context(nc.semaphore("s_" + e)) for e in self.ENG}
        self.seen = {e: {} for e in self.ENG}
        self.semobj = {e: self.sem[e] for e in self.ENG}
        self.nd = 0
        self.dma_bufs = []

    def _deps(self, eng, reads, writes):
        deps = {}

        def add(d):
            if d is None:
                return
            k, v = d
            if deps.get(k, 0) < v:
                deps[k] = v
        for b in reads:
            add(b.w)
        for b in writes:
            add(b.w)
            for d in b.r.items():
                add(d)
        waits = []
        seen = self.seen[eng]
        for k, v in deps.items():
            if k == "pe" and eng == "pe":
                continue
            if seen.get(k, 0) >= v:
                continue
            seen[k] = v
            waits.append((self.semobj[k], v))
        return waits

    def op(self, eng, fn, reads=(), writes=(), track=True):
        ex = [b for b in reads if b.excl]
        if ex:
            reads = [b for b in reads if not b.excl]
            writes = list(writes) + [b for b in ex if b not in writes]
        waits = self._deps(eng, reads, writes)
        idx = self.cnt[eng] + 1
        if track:
            self.cnt[eng] = idx
        ev = (eng, idx)
        for b in reads:
            if b.r.get(eng, 0) < idx:
                b.r[eng] = idx
        for b in writes:
            b.w = ev
            b.r = {}
        self.ops[eng].append((waits, fn, track, None))

    def dma(self, q, out, in_, reads=(), writes=()):
        waits = self._deps(q, reads, writes)
        bufs = list(writes) + list(reads)
        assert len(bufs) == 1
        b = bufs[0]
        if b.dsem is None:
            self.nd += 1
            key = "d%d" % self.nd
            b.dsem = key
            self.semobj[key] = self.stack.enter_context(self.nc.semaphore(key))
            self.dma_bufs.append(b)
        b.dcnt += 16
        ev = (b.dsem, b.dcnt)
        if writes:
            b.w = ev
            b.r = {}
        else:
            b.r[b.dsem] = b.dcnt
        self.ops[q].append((waits, L("dma_start", out=out, in_=in_), False,
                            (self.semobj[b.dsem], 16)))

    def barrier(self, dma=False):
        for e in self.ENG:
            waits = []
            if dma:
                for b in self.dma_bufs:
                    if self.seen[e].get(b.dsem, 0) < b.dcnt:
                        self.seen[e][b.dsem] = b.dcnt
                        waits.append((self.semobj[b.dsem], b.dcnt))
            for k in self.COMPUTE:
                v = self.cnt[k]
                if v > 0 and self.seen[e].get(k, 0) < v:
                    self.seen[e][k] = v
                    waits.append((self.semobj[k], v))
            if waits:
                self.ops[e].append((waits, None, False, None))

    def final_wait(self, eng):
        waits = [(self.semobj[b.dsem], b.dcnt) for b in self.dma_bufs]
        for k in self.COMPUTE:
            if self.cnt[k] > 0:
                waits.append((self.semobj[k], self.cnt[k]))
        self.ops[eng].append((waits, None, False, None))

    def emit(self):
        nc = self.nc
        handles = {"pe": "tensor", "act": "scalar", "dve": "vector", "pool": "gpsimd", "sp": "sync"}
        with nc.Block() as block:
            for e in self.ENG:
                ops = self.ops[e]
                own = self.sem[e]

                def body(engh, ops=ops, own=own):
                    for waits, fn, track, dinc in ops:
                        for s, v in waits:
                            engh.wait_ge(s, v)
                        if fn is None:
                            continue
                        ins = fn(engh)
                        if dinc is not None:
                            ins.then_inc(dinc[0], dinc[1])
                        elif track:
                            ins.then_inc(own, 1)
                getattr(block, handles[e])(body)


def L(method, *args, **kw):
    return lambda e: getattr(e, method)(*args, **kw)


def _na_tile_lists():
    lists = []
    tid = {}
    for m in range(16):
        if m <= 1:
            kbs = [0, 1, 2, 3]
        elif m >= 14:
            kbs = [12, 13, 14, 15]
        else:
            kbs = [m - 2, m - 1, m, m + 1, m + 2]
        cur = []
        for kb in kbs:
            key = ("int", kb - m) if 2 <= m <= 13 else ("edge", m, kb)
            if key not in tid:
                tid[key] = len(tid)
            cur.append((kb, tid[key]))
        lists.append(cur)
    return lists, tid


def _na_index_tables():
    lists, tid = _na_tile_lists()
    ntile = len(tid)
    dr_i = np.zeros((ntile, 128, 128), np.int64)
    dc_i = np.zeros((ntile, 128, 128), np.int64)
    valid = np.zeros((ntile, 128, 128), bool)
    done = set()
    for m in range(16):
        for kb, t in lists[m]:
            if t in done:
                continue
            done.add(t)
            kk = np.arange(128)
            kr = (2 * kb + kk // 64)[:, None]
            kc = (kk % 64)[:, None]
            qr = (2 * m + kk // 64)[None, :]
            qc = (kk % 64)[None, :]
            rs = np.clip(qr - 4, 0, 24)
            ws = np.clip(qc - 8, 0, 48)
            v = (kr >= rs) & (kr < rs + 8) & (kc >= ws) & (kc < ws + 16)
            dr = np.clip(kr - qr + 7, 0, 14)
            dc = np.clip(kc - qc + 15, 0, 30)
            dr_i[t] = np.broadcast_to(dr, (128, 128))
            dc_i[t] = np.broadcast_to(dc, (128, 128))
            valid[t] = v
    return dr_i, dc_i, valid


def _const_tables():
    c = {}
    c["ident"] = np.eye(128, dtype=np.float32)
    slopes = 2.0 ** (-8.0 * np.arange(1, 5) / 4)
    i = np.arange(T)
    a = (i // 128).astype(np.float32)
    r = (i % 128).astype(np.float32)
    augq = np.zeros((4, 4, T), np.float32)
    for h in range(4):
        s = slopes[h]
        augq[h, 0] = -s * 128.0 * a
        augq[h, 1] = -s * r
        augq[h, 2] = s
        augq[h, 3] = s
    c["augq"] = augq
    augk = np.zeros((2, 4, T), np.float32)
    for v, sg in enumerate([1.0, -1.0]):
        augk[v, 0] = sg
        augk[v, 1] = sg
        augk[v, 2] = sg * 128.0 * a
        augk[v, 3] = sg * r
    c["augk"] = augk
    jj = np.arange(128)[:, None]
    ii = np.arange(128)[None, :]
    cd = np.zeros((128, 4 * 128), np.float32)
    for h in range(4):
        cd[:, h * 128:(h + 1) * 128] = -2.0 * slopes[h] * np.maximum(jj - ii, 0)
    c["cdiag"] = cd
    inv = (10000.0 ** (-np.arange(0, 32, 2, dtype=np.float32) / np.float32(32))).astype(np.float32)
    ang = (np.arange(T, dtype=np.float32)[:, None] * inv[None, :]).astype(np.float32)
    cos = np.cos(ang.astype(np.float64)).astype(np.float32).T
    sin = np.sin(ang.astype(np.float64)).astype(np.float32).T
    rt = np.zeros((128, T), np.float32)
    rt[64:80] = cos
    rt[80:96] = cos
    rt[96:112] = -sin
    rt[112:128] = sin
    c["ropetab"] = rt
    return c


def _pm(v, nch):
    return np.ascontiguousarray(np.asarray(v, np.float32).reshape(nch, 128).T)


def _prep_shared(inp):
    g = {}
    c = _const_tables()
    g.update(c)
    f = lambda k: np.asarray(inp[k], np.float32)
    gains = np.zeros((128, 48), np.float32)
    gains[:, 0:8] = _pm(f("mix_norm_e")[0], 8)
    gains[:, 8:16] = _pm(f("ffn_norm_g")[0], 8)
    gains[:, 16:24] = _pm(f("mix_norm_o")[0], 8)
    gains[:, 24:32] = _pm(f("ffn_norm_g")[1], 8)
    gains[:, 32:40] = _pm(f("final_norm_g"), 8)
    gains[:, 40:44] = _pm(f("q_norm_g")[0], 4)
    gains[:, 44:46] = _pm(f("kv_norm_g")[0], 2)
    gains[:, 46] = f("diff_subln_g")[0]
    g["gains"] = gains
    cv = np.zeros((128, 2 * NFC * 4), np.float32)
    for l in range(2):
        for k in range(3):
            cv[:, l * 88 + k * 22: l * 88 + (k + 1) * 22] = _pm(f("ffn_conv_w")[l, k], NFC)
        cv[:, l * 88 + 66: l * 88 + 88] = _pm(f("ffn_conv_b")[l], NFC)
    g["convp"] = cv
    lam = np.zeros((128, 256), np.float32)
    for k, nm in enumerate(["diff_lq1", "diff_lk1", "diff_lq2", "diff_lk2"]):
        lam[:, k * 64:(k + 1) * 64] = np.broadcast_to(f(nm)[0][None, :], (128, 64))
    g["lamv"] = lam
    dr_i, dc_i, valid = _na_index_tables()
    rpb = f("na_rpb")[0]
    nb = np.empty((8, dr_i.shape[0], 128, 128), np.float32)
    for h in range(8):
        nb[h] = np.where(valid, rpb[h][dr_i, dc_i], np.float32(NEG))
    g["nbias"] = np.ascontiguousarray(nb.transpose(0, 2, 1, 3))
    g["w_in"] = f("w_in_e")[0]
    g["w_out"] = f("w_out_e")[0]
    g["w_gate"] = f("w_ffn_gate")
    g["w_val"] = f("w_ffn_val")
    g["w_down"] = f("w_ffn_down")
    g["w_dq"] = f("w_dq")[0]
    wuq = f("w_uq")[0]
    cols = []
    for h in range(16):
        b = h * 96
        cols += list(range(b, b + 96)) + list(range(b + 80, b + 96)) + list(range(b + 64, b + 80))
    g["w_uq"] = np.ascontiguousarray(wuq[:, cols])
    wdkv = f("w_dkv")[0]
    wd = np.zeros((1024, 384), np.float32)
    wd[:, 0:256] = wdkv[:, 0:256]
    wd[:, 320:352] = wdkv[:, 256:288]
    wd[:, 352:368] = wdkv[:, 272:288]
    wd[:, 368:384] = wdkv[:, 256:272]
    g["w_dkv"] = wd
    wukv = f("w_ukv")[0].reshape(256, 16, 128)
    g["w_uk"] = np.ascontiguousarray(wukv[:, :, 0:64].reshape(256, 1024))
    g["w_uv"] = np.ascontiguousarray(wukv[:, :, 64:128].reshape(256, 1024))
    g["w_o"] = f("w_o_mla")[0]
    return g


BF16_IN = ()
SHAPES = {
    "x": [T, D], "ident": [128, 128], "augq": [4, 4, T], "augk": [2, 4, T], "cdiag": [128, 512],
    "ropetab": [128, T], "gains": [128, 48], "convp": [128, 176], "lamv": [128, 256],
    "nbias": [8, 128, 21, 128],
}


def _desc_size(d):
    if d[0] == "gv":
        return 2048
    return d[2] * d[4]


def _pack_weights(g, plan):
    pack = np.zeros((len(plan), 128, 2048), np.float32)
    for i, d in enumerate(plan):
        if d[0] == "gv":
            _, layer, c = d
            wg = g["w_gate"][layer][:, c * 128:(c + 1) * 128].reshape(8, 128, 128)
            wv = g["w_val"][layer][:, c * 128:(c + 1) * 128].reshape(8, 128, 128)
            t = np.concatenate([wg, wv], axis=2)
            pack[i] = t.transpose(1, 0, 2).reshape(128, 2048)
        else:
            _, name, kc, c0, ncols, lidx, r0 = d
            w = g[name] if lidx is None else g[name][lidx]
            t = w[r0:r0 + kc * 128, c0:c0 + ncols].reshape(kc, 128, ncols)
            pack[i, :, 0:kc * ncols] = t.transpose(1, 0, 2).reshape(128, kc * ncols)
    return pack


def build(plan=None, nstage=99):
    nc = bass.Bass("TRN2", target_bir_lowering=False)
    dram = {k: nc.dram_tensor(k, shp, BF16 if k in BF16_IN else F32, kind="ExternalInput").ap() for k, shp in SHAPES.items()}
    if plan is not None:
        dram["wpack"] = nc.dram_tensor("wpack", [len(plan), 128, 2048], F32, kind="ExternalInput").ap()
    y_out = nc.dram_tensor("y", [T, D], F32, kind="ExternalOutput").ap()
    requests = []
    with ExitStack() as st:
        S = Sched(nc, st)
        sb = lambda n, shp, dt: st.enter_context(nc.sbuf_tensor("sb_" + n, shp, dt))
        xT = sb("xT", [128, 8, T], F32)
        hT = sb("hT", [128, 8, T], BF16)
        wring = sb("wring", [128, NSLOT, 2048], BF16)
        AR = sb("arena", [128, NBLK * 2048], BF16)
        ident = sb("ident", [128, 128], F32)
        identb = sb("identb", [128, 128], BF16)
        onesb = sb("onesb", [128, 128], BF16)
        gains = sb("gains", [128, 48], F32)
        convp = sb("convp", [128, 176], F32)
        small = sb("small", [128, 16], F32)
        cdiag = sb("cdiag", [128, 512], BF16)
        rs_a = sb("rs_a", [128, 512], F32)
        rs_b = sb("rs_b", [128, 512], F32)
        sqr = sb("sqr", [128, 4, 512], BF16)
        PS = st.enter_context(nc.psum_tensor("PS", [128, 8 * 512], F32))

        XT = [[Buf("xT%d_%d" % (c, g)) for g in range(4)] for c in range(8)]
        HT = [[Buf("hT%d_%d" % (c, g)) for g in range(4)] for c in range(8)]
        WS = [Buf("ws%d" % i) for i in range(NSLOT)]
        PB = [Buf("pb%d" % i, excl=True) for i in range(8)]
        B_const = Buf("const")
        B_small = Buf("small")
        B_rsa, B_rsb = Buf("rsa"), Buf("rsb")
        SQ = [Buf("sq%d" % i) for i in range(4)]

        def bank(i):
            return PS[:, i * 512:(i + 1) * 512]

        def blk(k, n=1):
            return AR[:, k * 2048:(k + n) * 2048]

        def blkf(k, n=2):
            return AR[:, k * 2048:(k + n) * 2048].bitcast(F32)

        state = {"bank": 0, "wi": 0, "wissued": 0, "sq": 0, "sb": 0, "tick": 0}
        pending = []
        lamv = blkf(0, 1)[:, 0:256]

        def nb_():
            b = state["bank"]
            state["bank"] = (b + 1) % 8
            return b

        def tgs(g):
            return slice(g * 512, (g + 1) * 512)

        def wtile(desc):
            i = state["wi"]
            state["wi"] = i + 1
            requests.append(desc)
            if plan is not None:
                hi = min(len(plan), i + 1 + LOOKAHEAD)
                while state["wissued"] < hi:
                    j = state["wissued"]
                    slot = j % NSLOT
                    X = _desc_size(plan[j])
                    S.dma("pool", wring[:, slot, 0:X], dram["wpack"][j][:, 0:X], writes=[WS[slot]])
                    state["wissued"] = j + 1
            return wring[:, i % NSLOT, :], WS[i % NSLOT]

        def wreq_cols(name, kc, c0, ncols, lidx=None, r0=0):
            return ("cols", name, kc, c0, ncols, lidx, r0)

        def wview(slot, kc, ncols):
            return slot[:, 0:kc * ncols].rearrange("p (k n) -> p k n", k=kc)

        S.dma("sp", ident[:], dram["ident"], writes=[B_const])
        gb_l = Buf("gains_l")
        S.dma("sp", gains[:], dram["gains"], writes=[gb_l])
        cb1, cb2, cb3 = Buf("c1"), Buf("c2"), Buf("c3")
        S.dma("sp", convp[:], dram["convp"], writes=[cb1])
        S.dma("sp", lamv, dram["lamv"], writes=[cb2])
        S.dma("pool", cdiag[:], dram["cdiag"], writes=[cb3])
        S.op("dve", L("tensor_copy", identb[:], ident[:]), reads=[B_const], writes=[B_const])
        S.op("dve", L("memset", onesb[:], 1.0), writes=[B_const])
        S.op("dve", L("memset", small[:, 0:1], EPS), writes=[B_small])
        lam_init = 0.8 - 0.6 * math.exp(-0.3 * 0)
        S.op("dve", L("tensor_tensor", lamv[:, 0:64], lamv[:, 0:64], lamv[:, 64:128], ALU.mult),
             reads=[cb2], writes=[cb2])
        S.op("dve", L("tensor_tensor", lamv[:, 128:192], lamv[:, 128:192], lamv[:, 192:256], ALU.mult),
             reads=[cb2], writes=[cb2])
        S.op("dve", L("reduce_sum", small[:, 4:5], lamv[:, 0:64], mybir.AxisListType.X),
             reads=[cb2], writes=[B_small])
        S.op("dve", L("reduce_sum", small[:, 5:6], lamv[:, 128:192], mybir.AxisListType.X),
             reads=[cb2], writes=[B_small])
        S.op("act", L("activation", small[:, 6:8], small[:, 4:6], AF.Exp), reads=[B_small], writes=[B_small])
        S.op("dve", L("scalar_tensor_tensor", small[:, 1:2], small[:, 7:8], -lam_init, small[:, 6:7],
                                                     ALU.add, ALU.subtract), reads=[B_small], writes=[B_small])
        S.op("dve", L("tensor_scalar", small[:, 2:3], gains[:, 46:47], 1.0 - lam_init, None, ALU.mult),
             reads=[B_small, gb_l], writes=[B_small])
        S.barrier(dma=True)

        def mm(out, lhsT, rhs, start, stop, reads, writes, track=None):
            if track is None:
                track = stop
            S.op("pe", L("matmul", out, lhsT, rhs, start=start, stop=stop), reads=reads, writes=writes,
                 track=track)

        def rstd_from_psum(pb, nfeat, dst, dstbuf):
            S.op("act", L("activation", rs_a[:], bank(pb), AF.Sqrt, scale=1.0 / nfeat, bias=small[:, 0:1]),
                 reads=[PB[pb], B_small], writes=[B_rsa])
            S.op("dve", L("reciprocal", dst, rs_a[:]), reads=[B_rsa], writes=[dstbuf])

        def norm_stage(goff):
            for g in range(4):
                pb = nb_()
                for c in range(8):
                    q = state["sq"]
                    state["sq"] = (q + 1) % 4
                    S.op("act", L("activation", sqr[:, q, :], xT[:, c, tgs(g)], AF.Square),
                         reads=[XT[c][g]], writes=[SQ[q]])
                    mm(bank(pb), onesb[:], sqr[:, q, :], c == 0, c == 7, [SQ[q], B_const], [PB[pb]], track=True)
                rstd_from_psum(pb, 1024.0, rs_b[:], B_rsb)
                for c in range(8):
                    S.op("dve", L("scalar_tensor_tensor", hT[:, c, tgs(g)], xT[:, c, tgs(g)], gains[:, goff + c:goff + c + 1], rs_b[:],
                        ALU.mult, ALU.mult), reads=[XT[c][g], B_rsb], writes=[HT[c][g]])

        def proj_fm(wv, wbuf, col0, rhs_fn, rhs_bufs_fn, nk, g, M=128):
            pb = nb_()
            for k in range(nk):
                mm(bank(pb)[0:M, :], wv[:, k, col0:col0 + M], rhs_fn(k, g), k == 0, k == nk - 1,
                   [wbuf] + rhs_bufs_fn(k, g), [PB[pb]])
            return pb

        hT_rhs = lambda k, g: hT[:, k, tgs(g)]
        hT_bufs = lambda k, g: [HT[k][g]]

        def resid_add(pb, n, g, eng="dve"):
            S.op(eng, L("tensor_tensor", xT[:, n, tgs(g)], bank(pb), xT[:, n, tgs(g)], ALU.add),
                 reads=[PB[pb], XT[n][g]], writes=[XT[n][g]])

        def out_proj(name, mix_fn, mix_bufs, krow0, nkc):
            for nt in range(4):
                slot, wb = wtile(wreq_cols(name, nkc, nt * 256, 256, r0=krow0))
                wv = wview(slot, nkc, 256)
                for nn in range(2):
                    n = nt * 2 + nn
                    for g in range(4):
                        pb = proj_fm(wv, wb, nn * 128, mix_fn, mix_bufs, nkc, g)
                        resid_add(pb, n, g)

        XS = [Buf("xs%d" % i) for i in range(4)]
        for g in range(4):
            for j in range(4):
                tt = g * 4 + j
                S.dma("sp", blkf(j, 1), dram["x"][tt * 128:(tt + 1) * 128, :], writes=[XS[j]])
            for c in range(8):
                pb = nb_()
                for j in range(4):
                    S.op("pe", L("transpose", bank(pb)[:, j * 128:(j + 1) * 128], blkf(j, 1)[:, c * 128:(c + 1) * 128], ident[:]),
                        reads=[XS[j], B_const], writes=[PB[pb]], track=(j == 3))
                eng = "act" if c % 2 == 0 else "dve"
                if eng == "act":
                    S.op("act", L("activation", xT[:, c, tgs(g)], bank(pb), AF.Copy),
                         reads=[PB[pb]], writes=[XT[c][g]])
                else:
                    S.op("dve", L("tensor_copy", xT[:, c, tgs(g)], bank(pb)),
                         reads=[PB[pb]], writes=[XT[c][g]])
        S.barrier()

        def ffn(layer, goff):
            norm_stage(goff)
            U = [Buf("u%d" % i) for i in range(11)]
            TB = [Buf("tb%d" % i) for i in range(2)]
            cp = layer * 88
            for rnd in range(2):
                for cc in range(11):
                    c = rnd * 11 + cc

                    slot, wb = wtile(("gv", layer, c))
                    wv = wview(slot, 8, 256)
                    for g in range(4):
                        for k in range(8):
                            mm(bank(g), wv[:, k, 0:128], hT[:, k, tgs(g)], k == 0, k == 7, [wb, HT[k][g]], [PB[g]])
                    for g in range(4):
                        for k in range(8):
                            mm(bank(4 + g), wv[:, k, 128:256], hT[:, k, tgs(g)], k == 0, k == 7,
                               [wb, HT[k][g]], [PB[4 + g]])
                    tb = cc % 2
                    tv = blkf(11 + 2 * tb)
                    G = PS[:, 0:2048]
                    gb = [PB[0], PB[1], PB[2], PB[3]]
                    w0 = convp[:, cp + c:cp + c + 1]
                    w1 = convp[:, cp + 22 + c:cp + 22 + c + 1]
                    w2 = convp[:, cp + 44 + c:cp + 44 + c + 1]
                    bb = convp[:, cp + 66 + c:cp + 66 + c + 1]
                    S.op("act", L("activation", tv, G, AF.Identity, scale=w1, bias=bb),
                         reads=gb + [cb1], writes=[TB[tb]])
                    S.op("dve", L("scalar_tensor_tensor", tv[:, 1:T], PS[:, 0:T - 1], w0, tv[:, 1:T], ALU.mult, ALU.add),
                        reads=gb + [TB[tb]], writes=[TB[tb]])
                    S.op("dve", L("scalar_tensor_tensor", tv[:, 0:T - 1], PS[:, 1:T], w2, tv[:, 0:T - 1], ALU.mult, ALU.add),
                        reads=gb + [TB[tb]], writes=[TB[tb]])
                    S.op("act", L("activation", tv, tv, AF.Gelu), reads=[TB[tb]], writes=[TB[tb]])
                    for hh in range(2):
                        S.op("dve", L("tensor_tensor", blk(cc)[:, hh * 1024:(hh + 1) * 1024], tv[:, hh * 1024:(hh + 1) * 1024],
                            PS[:, 2048 + hh * 1024:2048 + (hh + 1) * 1024], ALU.mult),
                            reads=[TB[tb], PB[4 + 2 * hh], PB[5 + 2 * hh]], writes=[U[cc]])
                for n in range(8):
                    slot, wb = wtile(wreq_cols("w_down", 11, n * 128, 128, lidx=layer, r0=rnd * 1408))
                    wv = wview(slot, 11, 128)
                    for g in range(4):
                        pb = proj_fm(wv, wb, 0, lambda k, g: blk(k)[:, tgs(g)], lambda k, g: [U[k]], 11, g)
                        resid_add(pb, n, g)
            S.barrier()

        def softmax_epilogue_diff(h, I, pO, pD, MIX):
            r1, r2, t2 = blkf(13, 2)[:, 0:512], blkf(13, 2)[:, 512:1024], blkf(13, 2)[:, 1024:1536]
            t1 = blkf(15, 2)[:, I * 512:(I + 1) * 512]
            EB = state.setdefault("EB", [Buf("eb%d" % i) for i in range(3)])
            A1 = state.setdefault("A1", [Buf("a1_%d" % i) for i in range(4)])
            S.op("dve", L("tensor_copy", r1, bank(pD[0])), reads=[PB[pD[0]]], writes=[EB[0]])
            S.op("dve", L("tensor_copy", r2, bank(pD[1])), reads=[PB[pD[1]]], writes=[EB[1]])
            S.op("dve", L("tensor_copy", t1, bank(pO[0])), reads=[PB[pO[0]]], writes=[A1[I]])
            S.op("dve", L("tensor_copy", t2, bank(pO[1])), reads=[PB[pO[1]]], writes=[EB[2]])
            S.op("dve", L("reciprocal", r1, r1), reads=[EB[0]], writes=[EB[0]])
            S.op("dve", L("reciprocal", r2, r2), reads=[EB[1]], writes=[EB[1]])
            S.op("dve", L("tensor_tensor", t1, t1, r1, ALU.mult), reads=[EB[0], A1[I]], writes=[A1[I]])
            S.op("dve", L("tensor_tensor", t2, t2, r2, ALU.mult), reads=[EB[1], EB[2]], writes=[EB[2]])
            S.op("dve", L("scalar_tensor_tensor", t1, t2, small[:, 1:2], t1, ALU.mult, ALU.add),
                 reads=[A1[I], EB[2], B_small], writes=[A1[I]])

        def subln_head(h, MIX):
            A1 = state["A1"]
            RS = state.setdefault("RS", [Buf("rs_%d" % i) for i in range(4)])
            pbs = []
            for I in range(4):
                t1 = blkf(15, 2)[:, I * 512:(I + 1) * 512]
                S.op("act", L("activation", sqr[:, I, :], t1, AF.Square), reads=[A1[I]], writes=[SQ[I]])
                pb = 4 + state["sb"] % 4
                state["sb"] += 1
                mm(bank(pb), onesb[:], sqr[:, I, :], True, True, [SQ[I], B_const], [PB[pb]])
                pbs.append(pb)
            for I in range(4):
                rsd = blkf(17, 2)[:, I * 512:(I + 1) * 512]
                S.op("act", L("activation", rsd, bank(pbs[I]), AF.Sqrt, scale=1.0 / 128.0, bias=small[:, 0:1]),
                     reads=[PB[pbs[I]], B_small], writes=[RS[I]])
            for I in range(4):
                t1 = blkf(15, 2)[:, I * 512:(I + 1) * 512]
                rsd = blkf(17, 2)[:, I * 512:(I + 1) * 512]
                S.op("dve", L("reciprocal", rsd, rsd), reads=[RS[I]], writes=[RS[I]])
                S.op("dve", L("scalar_tensor_tensor", blk(h)[:, tgs(I)], t1, small[:, 2:3], rsd, ALU.mult, ALU.mult),
                     reads=[A1[I], RS[I], B_small], writes=[MIX[h]])

        def diff_phase():
            MIX = [Buf("mix%d" % i) for i in range(4)]
            QB = [Buf("dq%d" % i) for i in range(2)]
            KB = [[Buf("dk%d%d" % (i, v)) for v in range(2)] for i in range(2)]
            VB = Buf("dv")
            PT = [Buf("pt%d" % i) for i in range(8)]
            Qt = [blk(4), blk(5)]
            Kt = [[blk(6), blk(7)], [blk(8), blk(9)]]
            Vt = blk(10).rearrange("p (t e) -> p t e", t=16)
            ptv = lambda i: blk(11, 2)[:, i * 512:(i + 1) * 512]
            pti = [0]
            for n in range(2):
                for v in range(2):
                    S.dma("pool", Kt[n][v][64:68, :], dram["augk"][v], writes=[KB[n][v]])
            for h in range(int(os.environ.get("KH0", "0")), int(os.environ.get("KH1", "4")) if KDBG in (0, 5) else 1):
                if KDBG == 1:
                    break
                if os.environ.get("KBAR", "0") == "1":
                    S.barrier()
                slot, wb = wtile(wreq_cols("w_in", 8, h * 128, 128))
                wq = wview(slot, 8, 128)
                for n in range(2):
                    S.dma("pool", Qt[n][64:68, :], dram["augq"][h], writes=[QB[n]])
                for g in range(4):
                    pb = proj_fm(wq, wb, 0, hT_rhs, hT_bufs, 8, g)
                    S.op("dve", L("tensor_scalar", Qt[0][0:64, tgs(g)], bank(pb)[0:64, :], 0.125, None, ALU.mult),
                         reads=[PB[pb]], writes=[QB[0]])
                    S.op("dve", L("tensor_scalar", Qt[1][0:64, tgs(g)], bank(pb)[64:128, :], 0.125, None, ALU.mult),
                         reads=[PB[pb]], writes=[QB[1]])
                slot, wb = wtile(wreq_cols("w_in", 8, 512 + h * 128, 128))
                wk = wview(slot, 8, 128)
                for g in range(4):
                    pb = proj_fm(wk, wb, 0, hT_rhs, hT_bufs, 8, g)
                    for n in range(2):
                        S.op("act", L("activation", Kt[n][0][0:64, tgs(g)], bank(pb)[64 * n:64 * n + 64, :], AF.Copy),
                             reads=[PB[pb]], writes=[KB[n][0]])
                        S.op("dve", L("tensor_copy", Kt[n][1][0:64, tgs(g)], bank(pb)[64 * n:64 * n + 64, :]),
                             reads=[PB[pb]], writes=[KB[n][1]])
                slot, wb = wtile(wreq_cols("w_in", 8, 1024 + h * 128, 128))
                wvv = wview(slot, 8, 128)
                for g in range(4):
                    pb = nb_()
                    for j in range(4):
                        tt = g * 4 + j
                        for k in range(8):
                            mm(bank(pb)[:, j * 128:(j + 1) * 128], hT[:, k, tt * 128:(tt + 1) * 128], wvv[:, k, :],
                               k == 0, k == 7, [wb, HT[k][g]], [PB[pb]], track=(k == 7 and j == 3))
                    S.op("act", L("activation", Vt[:, g * 4:(g + 1) * 4, :], bank(pb).rearrange("p (t e) -> p t e", t=4), AF.Copy),
                        reads=[PB[pb]], writes=[VB])
                pO, pD = [0, 1], [2, 3]
                tiles = [(I, J, n) for I in range(4) for J in range(16) for n in range(2)]
                info = {}

                def stage1(t):
                    I, J, n = t
                    ps = 4 + state["sb"] % 4
                    state["sb"] += 1
                    if J < 4 * I or J >= 4 * I + 4:
                        v = 0 if J < 4 * I else 1
                        mm(bank(ps), Kt[n][v][0:68, J * 128:(J + 1) * 128], Qt[n][0:68, tgs(I)], True, True,
                           [KB[n][v], QB[n]], [PB[ps]])
                    else:
                        for b in range(4):
                            gb_ = 4 * I + b
                            v = 0 if gb_ >= J else 1
                            qs = slice(gb_ * 128, (gb_ + 1) * 128)
                            osl = bank(ps)[:, b * 128:(b + 1) * 128]
                            diag = gb_ == J
                            mm(osl, Kt[n][v][0:68, J * 128:(J + 1) * 128], Qt[n][0:68, qs], True, not diag,
                               [KB[n][v], QB[n]], [PB[ps]], track=(b == 3 and not diag))
                            if diag:
                                mm(osl, identb[:], cdiag[:, h * 128:(h + 1) * 128], False, True,
                                   [B_const, cb3], [PB[ps]], track=(b == 3))
                    pi = pti[0]
                    pti[0] = (pi + 1) % 8
                    S.op("act", L("activation", ptv(pi), bank(ps), AF.Exp), reads=[PB[ps]], writes=[PT[pi]])
                    info[t] = pi

                def stage2(t):
                    I, J, n = t
                    pi = info.pop(t)
                    mm(bank(pO[n]), Vt[:, J, :], ptv(pi), J == 0, J == 15, [VB, PT[pi]], [PB[pO[n]]], track=True)
                    mm(bank(pD[n]), onesb[:], ptv(pi), J == 0, J == 15, [B_const, PT[pi]], [PB[pD[n]]], track=True)
                    if J == 15 and n == 1:
                        softmax_epilogue_diff(h, I, pO, pD, MIX)
                        if I == 3:
                            pending.append((state["tick"] + 24, lambda h=h: subln_head(h, MIX)))

                DEPTH = 2
                for i in range(len(tiles) + DEPTH):
                    state["tick"] += 1
                    while pending and pending[0][0] <= state["tick"]:
                        pending.pop(0)[1]()
                    if i < len(tiles):
                        stage1(tiles[i])
                    if i >= DEPTH:
                        stage2(tiles[i - DEPTH])
            while pending:
                pending.pop(0)[1]()
            if KDBG == 0:
                out_proj("w_out", lambda k, g: blk(k)[:, tgs(g)], lambda k, g: [MIX[k]], 0, 4)
            S.barrier()

        def na_phase():
            lists, _ = _na_tile_lists()
            MIX = [Buf("nmix%d" % i) for i in range(4)]
            NQb, NKb = Buf("nq"), Buf("nk")
            VAb = [Buf("va0"), Buf("va1")]
            NBb = Buf("nbias")
            PT = [Buf("npt%d" % i) for i in range(6)]
            RD = [Buf("nrd%d" % i) for i in range(2)]
            NQ, NK = blk(4), blk(5)
            VA = [blk(6).rearrange("p (t e) -> p t e", t=16), blk(7).rearrange("p (t e) -> p t e", t=16)]
            NBt = blk(8, 3)
            ptv = lambda i: blk(11, 2)[:, i * 640:(i + 1) * 640]
            rdv = lambda i: blkf(13, 2)[:, i * 512:(i + 1) * 512]
            for a in range(2):
                S.op("dve", L("memset", VA[a][:, :, 64:128], 1.0), writes=[VAb[a]])
            pti = [0]
            for j in range(4):
                slot, wb = wtile(wreq_cols("w_in", 8, 1536 + j * 128, 128))
                wq = wview(slot, 8, 128)
                for g in range(4):
                    pb = proj_fm(wq, wb, 0, hT_rhs, hT_bufs, 8, g)
                    S.op("dve", L("tensor_scalar", NQ[:, tgs(g)], bank(pb), 0.125, None, ALU.mult),
                         reads=[PB[pb]], writes=[NQb])
                slot, wb = wtile(wreq_cols("w_in", 8, 2048 + j * 128, 128))
                wk = wview(slot, 8, 128)
                for g in range(4):
                    pb = proj_fm(wk, wb, 0, hT_rhs, hT_bufs, 8, g)
                    S.op("dve", L("tensor_copy", NK[:, tgs(g)], bank(pb)), reads=[PB[pb]], writes=[NKb])
                slot, wb = wtile(wreq_cols("w_in", 8, 2560 + j * 128, 128))
                wvv = wview(slot, 8, 128)
                for g in range(4):
                    pb = nb_()
                    for jj in range(4):
                        tt = g * 4 + jj
                        for k in range(8):
                            mm(bank(pb)[:, jj * 128:(jj + 1) * 128], hT[:, k, tt * 128:(tt + 1) * 128], wvv[:, k, :],
                               k == 0, k == 7, [wb, HT[k][g]], [PB[pb]], track=(k == 7 and jj == 3))
                    pv = bank(pb).rearrange("p (t e) -> p t e", t=4)
                    S.op("act", L("activation", VA[0][:, g * 4:(g + 1) * 4, 0:64], pv[:, :, 0:64], AF.Copy),
                         reads=[PB[pb]], writes=[VAb[0]])
                    S.op("dve", L("tensor_copy", VA[1][:, g * 4:(g + 1) * 4, 0:64], pv[:, :, 64:128]),
                         reads=[PB[pb]], writes=[VAb[1]])
                for a in range(2):
                    S.dma("pool", NBt[:, a * 2688:(a + 1) * 2688].rearrange("p (t q) -> p t q", t=21),
                          dram["nbias"][2 * j + a], writes=[NBb])
                tiles = [(a, mg, mi) for a in range(2) for mg in range(4) for mi in range(4)]
                info = {}

                def stage1(t):
                    a, mg, mi = t
                    p0 = 64 * a
                    m = mg * 4 + mi
                    lst = lists[m]
                    pi = pti[0]
                    pti[0] = (pi + 1) % 6
                    psA = 2 + state["sb"] % 6
                    state["sb"] += 1
                    psB = None
                    if len(lst) == 5:
                        psB = 2 + state["sb"] % 6
                        state["sb"] += 1
                    for idx, (kb, tl) in enumerate(lst):
                        if idx < 4:
                            osl, pbx = bank(psA)[:, idx * 128:(idx + 1) * 128], psA
                        else:
                            osl, pbx = bank(psB)[:, 0:128], psB
                        last = (idx == 3) or (idx == 4)
                        mm(osl, NK[p0:p0 + 64, kb * 128:(kb + 1) * 128], NQ[p0:p0 + 64, m * 128:(m + 1) * 128],
                           True, False, [NKb, NQb], [PB[pbx]], track=False)
                        mm(osl, identb[:], NBt[:, a * 2688 + tl * 128: a * 2688 + (tl + 1) * 128], False, True,
                           [B_const, NBb], [PB[pbx]], track=last)
                    S.op("act", L("activation", ptv(pi)[:, 0:512], bank(psA), AF.Exp),
                         reads=[PB[psA]], writes=[PT[pi]])
                    if psB is not None:
                        S.op("act", L("activation", ptv(pi)[:, 512:640], bank(psB)[:, 0:128], AF.Exp),
                             reads=[PB[psB]], writes=[PT[pi]])
                    info[t] = pi

                def stage2(t, j=j):
                    a, mg, mi = t
                    p0 = 64 * a
                    m = mg * 4 + mi
                    lst = lists[m]
                    pi = info.pop(t)
                    po = (a * 4 + mg) % 2
                    for idx, (kb, tl) in enumerate(lst):
                        mm(bank(po)[:, mi * 128:(mi + 1) * 128], VA[a][:, kb, :], ptv(pi)[:, idx * 128:(idx + 1) * 128],
                           idx == 0, idx == len(lst) - 1, [VAb[a], PT[pi]], [PB[po]],
                           track=(idx == len(lst) - 1))
                    if mi == 3:
                        ri = mg % 2
                        S.op("dve", L("reciprocal", rdv(ri)[0:64, :], bank(po)[64:128, :]),
                             reads=[PB[po]], writes=[RD[ri]])
                        S.op("dve", L("tensor_tensor", blk(j)[p0:p0 + 64, tgs(mg)], bank(po)[0:64, :], rdv(ri)[0:64, :], ALU.mult),
                             reads=[PB[po], RD[ri]], writes=[MIX[j]])

                DEPTH = 2
                for i in range(len(tiles) + DEPTH):
                    if i < len(tiles):
                        stage1(tiles[i])
                    if i >= DEPTH:
                        stage2(tiles[i - DEPTH])
            out_proj("w_out", lambda k, g: blk(k)[:, tgs(g)], lambda k, g: [MIX[k]], 512, 4)
            S.barrier()

        def mla_phase():
            CQ = [Buf("cq%d" % i) for i in range(4)]
            CKV = [Buf("ckv%d" % i) for i in range(2)]
            KRb = Buf("kr")
            RTb = Buf("ropetab")
            QHb = [Buf("qh0"), Buf("qh1")]
            KHb = [Buf("kh0"), Buf("kh1")]
            VAb = [Buf("mva0"), Buf("mva1")]
            MIX = [Buf("mmix%d" % i) for i in range(4)]
            PT = [Buf("mpt%d" % i) for i in range(8)]
            TMP = [Buf("mtmp%d" % i) for i in range(2)]
            cqv = lambda c: blk(4 + c)
            ckvv = lambda c: blk(8 + c)
            KR = blk(10)
            rt = blkf(11, 2)
            QH = [blk(13), blk(14)]
            KH = [blk(15), blk(16)]
            VA = [blk(17).rearrange("p (t e) -> p t e", t=16), blk(18).rearrange("p (t e) -> p t e", t=16)]
            ptv = lambda i: blk(10 if i < 4 else 19)[:, (i % 4) * 512:(i % 4 + 1) * 512]
            tmpv = lambda i: blkf(20, 1)[:, i * 512:(i + 1) * 512]
            scale = 96.0 ** -0.5
            S.dma("sp", rt, dram["ropetab"], writes=[RTb])
            for a in range(2):
                S.op("dve", L("memset", VA[a][:, :, 64:128], 1.0), writes=[VAb[a]])

            def rope(dst, dstbuf, pb, g):
                S.op("dve", L("tensor_tensor", tmpv(0)[64:96, :], bank(pb)[64:96, :], rt[64:96, tgs(g)], ALU.mult),
                     reads=[PB[pb], RTb], writes=[TMP[0]])
                S.op("dve", L("tensor_tensor", tmpv(1)[64:96, :], bank(pb)[96:128, :], rt[96:128, tgs(g)], ALU.mult),
                     reads=[PB[pb], RTb], writes=[TMP[1]])
                S.op("dve", L("tensor_tensor", dst[64:96, tgs(g)], tmpv(0)[64:96, :], tmpv(1)[64:96, :], ALU.add),
                     reads=[TMP[0], TMP[1]], writes=[dstbuf])

            def latent(wtiles, nch, goff, dstv, dstb, nfeat, extra=None):
                for g in range(4):
                    pbs = []
                    for (wv, wb, ncol) in wtiles:
                        for cc in range(ncol // 128):
                            pbs.append(proj_fm(wv, wb, cc * 128, hT_rhs, hT_bufs, 8, g))
                    pss = nb_()
                    while pss in pbs:
                        pss = nb_()
                    for c in range(nch):
                        q = state["sq"]
                        state["sq"] = (q + 1) % 4
                        S.op("act", L("activation", sqr[:, q, :], bank(pbs[c]), AF.Square),
                             reads=[PB[pbs[c]]], writes=[SQ[q]])
                        mm(bank(pss), onesb[:], sqr[:, q, :], c == 0, c == nch - 1, [SQ[q], B_const], [PB[pss]], track=True)
                    rstd_from_psum(pss, nfeat, rs_b[:], B_rsb)
                    for c in range(nch):
                        S.op("dve", L("scalar_tensor_tensor", dstv(c)[:, tgs(g)], bank(pbs[c]), gains[:, goff + c:goff + c + 1], rs_b[:], ALU.mult, ALU.mult),
                            reads=[PB[pbs[c]], B_rsb], writes=[dstb[c]])
                    if extra is not None:
                        extra(pbs[nch], g)

            s0, b0 = wtile(wreq_cols("w_dq", 8, 0, 256))
            s1, b1 = wtile(wreq_cols("w_dq", 8, 256, 256))
            latent([(wview(s0, 8, 256), b0, 256), (wview(s1, 8, 256), b1, 256)], 4, 40, cqv, CQ, 512.0)
            s0, b0 = wtile(wreq_cols("w_dkv", 8, 0, 256))
            s1, b1 = wtile(wreq_cols("w_dkv", 8, 256, 128))
            latent([(wview(s0, 8, 256), b0, 256), (wview(s1, 8, 128), b1, 128)], 2, 44, ckvv, CKV, 256.0,
                   extra=lambda pb, g: rope(KH[0], KHb[0], pb, g))
            S.op("dve", L("tensor_copy", KH[1][64:96, :], KH[0][64:96, :]), reads=[KHb[0]], writes=[KHb[1]])

            for half in range(2):
                for pr in range(4):
                    hp = half * 4 + pr
                    sq_, bq_ = wtile(wreq_cols("w_uq", 4, hp * 256, 256))
                    wq = wview(sq_, 4, 256)
                    qoff = 0
                    for a in range(2):
                        for g in range(4):
                            pb = proj_fm(wq, bq_, qoff + a * 128, lambda k, g: cqv(k)[:, tgs(g)], lambda k, g: [CQ[k]], 4, g)
                            S.op("act", L("activation", QH[a][0:64, tgs(g)], bank(pb)[0:64, :], AF.Copy),
                                 reads=[PB[pb]], writes=[QHb[a]])
                            rope(QH[a], QHb[a], pb, g)
                    sk_, bk_ = wtile(wreq_cols("w_uk", 2, hp * 128, 128))
                    wk = wview(sk_, 2, 128)
                    for g in range(4):
                        pb = proj_fm(wk, bk_, 0, lambda k, g: ckvv(k)[:, tgs(g)], lambda k, g: [CKV[k]], 2, g)
                        S.op("act", L("activation", KH[0][0:64, tgs(g)], bank(pb)[0:64, :], AF.Copy),
                             reads=[PB[pb]], writes=[KHb[0]])
                        S.op("dve", L("tensor_copy", KH[1][0:64, tgs(g)], bank(pb)[64:128, :]),
                             reads=[PB[pb]], writes=[KHb[1]])
                    sv_, bv_ = wtile(wreq_cols("w_uv", 2, hp * 128, 128))
                    wv_ = wview(sv_, 2, 128)
                    for g in range(4):
                        pb = nb_()
                        for jj in range(4):
                            tt = g * 4 + jj
                            for k in range(2):
                                mm(bank(pb)[:, jj * 128:(jj + 1) * 128], ckvv(k)[:, tt * 128:(tt + 1) * 128],
                                   wv_[:, k, :], k == 0, k == 1, [bv_, CKV[k]], [PB[pb]],
                                   track=(k == 1 and jj == 3))
                        pv = bank(pb).rearrange("p (t e) -> p t e", t=4)
                        S.op("act", L("activation", VA[0][:, g * 4:(g + 1) * 4, 0:64], pv[:, :, 0:64], AF.Copy),
                             reads=[PB[pb]], writes=[VAb[0]])
                        S.op("dve", L("tensor_copy", VA[1][:, g * 4:(g + 1) * 4, 0:64], pv[:, :, 64:128]),
                             reads=[PB[pb]], writes=[VAb[1]])
                    pti = state.setdefault("mpti", [0])
                    tiles = [(a, I, J) for a in range(2) for I in range(4) for J in range(16)]
                    info = {}

                    def stage1(t):
                        a, I, J = t
                        ps = 2 + state["sb"] % 6
                        state["sb"] += 1
                        mm(bank(ps), KH[a][0:96, J * 128:(J + 1) * 128], QH[a][0:96, tgs(I)], True, True,
                           [KHb[a], QHb[a]], [PB[ps]])
                        pi = pti[0]
                        pti[0] = (pi + 1) % 8
                        S.op("act", L("activation", ptv(pi), bank(ps), AF.Exp, scale=scale),
                             reads=[PB[ps]], writes=[PT[pi]])
                        info[t] = pi

                    def stage2(t, pr=pr):
                        a, I, J = t
                        pi = info.pop(t)
                        po = I % 2
                        mm(bank(po), VA[a][:, J, :], ptv(pi), J == 0, J == 15, [VAb[a], PT[pi]], [PB[po]], track=True)
                        if J == 15:
                            rsv = rs_a if po == 0 else rs_b
                            rsb_ = B_rsa if po == 0 else B_rsb
                            S.op("dve", L("reciprocal", rsv[0:64, :], bank(po)[64:128, :]),
                                 reads=[PB[po]], writes=[rsb_])
                            S.op("dve", L("tensor_tensor", blk(pr)[64 * a:64 * a + 64, tgs(I)], bank(po)[0:64, :], rsv[0:64, :], ALU.mult),
                                 reads=[PB[po], rsb_], writes=[MIX[pr]])

                    DEPTH = 3
                    for i in range(len(tiles) + DEPTH):
                        if i < len(tiles):
                            stage1(tiles[i])
                        if i >= DEPTH:
                            stage2(tiles[i - DEPTH])
                out_proj("w_o", lambda k, g: blk(k)[:, tgs(g)], lambda k, g: [MIX[k]], half * 512, 4)
            S.barrier()

        def final_stage(goff):
            YT = [Buf("yt%d" % i) for i in range(8)]
            OS = [Buf("os%d" % i) for i in range(2)]
            ytv = lambda c: blkf(2 * c, 2)[:, 0:512]
            osv = lambda i: blkf(16 + 2 * i, 2)[:, 0:1024]
            for g in range(4):
                pb = nb_()
                for c in range(8):
                    q = state["sq"]
                    state["sq"] = (q + 1) % 4
                    S.op("act", L("activation", sqr[:, q, :], xT[:, c, tgs(g)], AF.Square),
                         reads=[XT[c][g]], writes=[SQ[q]])
                    mm(bank(pb), onesb[:], sqr[:, q, :], c == 0, c == 7, [SQ[q], B_const], [PB[pb]], track=True)
                rstd_from_psum(pb, 1024.0, rs_b[:], B_rsb)
                for c in range(8):
                    if goff is None:
                        S.op("dve", L("tensor_copy", ytv(c), xT[:, c, tgs(g)]), reads=[XT[c][g]], writes=[YT[c]])
                    else:
                        S.op("dve", L("scalar_tensor_tensor", ytv(c), xT[:, c, tgs(g)], gains[:, goff + c:goff + c + 1], rs_b[:], ALU.mult, ALU.mult),
                            reads=[XT[c][g], B_rsb], writes=[YT[c]])
                for j in range(4):
                    tt = g * 4 + j
                    oi = tt % 2
                    p0, p1 = nb_(), nb_()
                    for c in range(8):
                        pbx = p0 if c < 4 else p1
                        S.op("pe", L("transpose", bank(pbx)[:, (c % 4) * 128:(c % 4 + 1) * 128], ytv(c)[:, j * 128:(j + 1) * 128], ident[:]),
                            reads=[YT[c], B_const], writes=[PB[pbx]], track=(c % 4 == 3))
                    S.op("act", L("activation", osv(oi)[:, 0:512], bank(p0), AF.Copy),
                         reads=[PB[p0]], writes=[OS[oi]])
                    S.op("dve", L("tensor_copy", osv(oi)[:, 512:1024], bank(p1)),
                         reads=[PB[p1]], writes=[OS[oi]])
                    S.dma("sp", y_out[tt * 128:(tt + 1) * 128, :], osv(oi), reads=[OS[oi]])

        if KDBG == 10:
            for _ in range(30):
                norm_stage(0)
        elif KDBG == 11:
            norm_stage(0)
            for i in range(40):
                slot, wb = wtile(wreq_cols("w_in", 8, (i % 24) * 128, 128))
                wq = wview(slot, 8, 128)
                pb = proj_fm(wq, wb, 0, hT_rhs, hT_bufs, 8, i % 4)
                S.op("dve", L("tensor_copy", rs_b[:], bank(pb)), reads=[PB[pb]], writes=[B_rsb])
        elif nstage >= 1:
            norm_stage(0)
            S.barrier()
            diff_phase()
        if nstage >= 2:
            na_phase()
        if nstage >= 3:
            ffn(0, 8)
        if nstage >= 4:
            norm_stage(16)
            S.barrier()
            mla_phase()
        if nstage >= 5:
            ffn(1, 24)
        final_stage(32 if nstage >= 5 else None)
        S.final_wait("sp")
        S.emit()
    return nc, requests


_CACHE = {}


def _get_program(nstage=99):
    if nstage not in _CACHE:
        _, reqs = build(None, nstage)
        nc, _ = build(reqs, nstage)
        _CACHE[nstage] = (nc, reqs)
    return _CACHE[nstage]


def kernel(**inputs):
    nstage = int(inputs.pop("_nstage", 99))
    cores = inputs.pop("_cores", None)
    shared = _prep_shared(inputs)
    x = np.asarray(inputs["x"], np.float32)
    nb = x.shape[0] if cores is None else cores
    nc, plan = _get_program(nstage)
    dev = {k: shared[k] for k in SHAPES if k != "x"}
    dev["wpack"] = _pack_weights(shared, plan)
    in_maps = []
    for b in range(nb):
        m = dict(dev)
        m["x"] = np.ascontiguousarray(x[b])
        in_maps.append(m)
    res = run_bass_kernel_spmd(nc, in_maps, core_ids=list(range(nb)))
    out = np.stack([np.asarray(r["y"], np.float32) for r in res.results], axis=0)
    return out
```

```python
import math
import os
KDBG = int(os.environ.get("KDBG", "0"))
from contextlib import ExitStack
import numpy as np
import ml_dtypes
import concourse.bass as bass
import concourse.mybir as mybir
from concourse.bass_utils import run_bass_kernel_spmd

F32 = mybir.dt.float32
BF16 = mybir.dt.bfloat16
AF = mybir.ActivationFunctionType
ALU = mybir.AluOpType

T = 2048
D = 1024
DFF = 2816
NFC = 22
EPS = 1e-6
NEG = -30000.0
NSLOT = 4
LOOKAHEAD = 2
NBLK = 21


class Buf:
    __slots__ = ("name", "w", "r", "dsem", "dcnt", "excl")

    def __init__(self, name, excl=False):
        self.name = name
        self.excl = excl
        self.w = None
        self.r = {}
        self.dsem = None
        self.dcnt = 0


class Sched:
    ENG = ["pe", "act", "dve", "pool", "sp"]
    COMPUTE = ["pe", "act", "dve", "pool"]

    def __init__(self, nc, stack):
        self.nc = nc
        self.stack = stack
        self.ops = {e: [] for e in self.ENG}
        self.cnt = {e: 0 for e in self.ENG}
        self.sem = {e: stack.enter_context(nc.semaphore("s_" + e)) for e in self.ENG}
        self.seen = {e: {} for e in self.ENG}
        self.semobj = {e: self.sem[e] for e in self.ENG}
        self.nd = 0
        self.dma_bufs = []

    def _deps(self, eng, reads, writes):
        deps = {}

        def add(d):
            if d is None:
                return
            k, v = d
            if deps.get(k, 0) < v:
                deps[k] = v
        for b in reads:
            add(b.w)
        for b in writes:
            add(b.w)
            for d in b.r.items():
                add(d)
        waits = []
        seen = self.seen[eng]
        for k, v in deps.items():
            if k == "pe" and eng == "pe":
                continue
            if seen.get(k, 0) >= v:
                continue
            seen[k] = v
            waits.append((self.semobj[k], v))
        return waits

    def op(self, eng, fn, reads=(), writes=(), track=True):
        ex = [b for b in reads if b.excl]
        if ex:
            reads = [b for b in reads if not b.excl]
            writes = list(writes) + [b for b in ex if b not in writes]
        waits = self._deps(eng, reads, writes)
        idx = self.cnt[eng] + 1
        if track:
            self.cnt[eng] = idx
        ev = (eng, idx)
        for b in reads:
            if b.r.get(eng, 0) < idx:
                b.r[eng] = idx
        for b in writes:
            b.w = ev
            b.r = {}
        self.ops[eng].append((waits, fn, track, None))

    def dma(self, q, out, in_, reads=(), writes=()):
        waits = self._deps(q, reads, writes)
        bufs = list(writes) + list(reads)
        assert len(bufs) == 1
        b = bufs[0]
        if b.dsem is None:
            self.nd += 1
            key = "d%d" % self.nd
            b.dsem = key
            self.semobj[key] = self.stack.enter_context(self.nc.semaphore(key))
            self.dma_bufs.append(b)
        b.dcnt += 16
        ev = (b.dsem, b.dcnt)
        if writes:
            b.w = ev
            b.r = {}
        else:
            b.r[b.dsem] = b.dcnt
        self.ops[q].append((waits, L("dma_start", out=out, in_=in_), False,
                            (self.semobj[b.dsem], 16)))

    def barrier(self, dma=False):
        for e in self.ENG:
            waits = []
            if dma:
                for b in self.dma_bufs:
                    if self.seen[e].get(b.dsem, 0) < b.dcnt:
                        self.seen[e][b.dsem] = b.dcnt
                        waits.append((self.semobj[b.dsem], b.dcnt))
            for k in self.COMPUTE:
                v = self.cnt[k]
                if v > 0 and self.seen[e].get(k, 0) < v:
                    self.seen[e][k] = v
                    waits.append((self.semobj[k], v))
            if waits:
                self.ops[e].append((waits, None, False, None))

    def final_wait(self, eng):
        waits = [(self.semobj[b.dsem], b.dcnt) for b in self.dma_bufs]
        for k in self.COMPUTE:
            if self.cnt[k] > 0:
                waits.append((self.semobj[k], self.cnt[k]))
        self.ops[eng].append((waits, None, False, None))

    def emit(self):
        nc = self.nc
        handles = {"pe": "tensor", "act": "scalar", "dve": "vector", "pool": "gpsimd", "sp": "sync"}
        with nc.Block() as block:
            for e in self.ENG:
                ops = self.ops[e]
                own = self.sem[e]

                def body(engh, ops=ops, own=own):
                    for waits, fn, track, dinc in ops:
                        for s, v in waits:
                            engh.wait_ge(s, v)
                        if fn is None:
                            continue
                        ins = fn(engh)
                        if dinc is not None:
                            ins.then_inc(dinc[0], dinc[1])
                        elif track:
                            ins.then_inc(own, 1)
                getattr(block, handles[e])(body)


def L(method, *args, **kw):
    return lambda e: getattr(e, method)(*args, **kw)


def _na_tile_lists():
    lists = []
    tid = {}
    for m in range(16):
        if m <= 1:
            kbs = [0, 1, 2, 3]
        elif m >= 14:
            kbs = [12, 13, 14, 15]
        else:
            kbs = [m - 2, m - 1, m, m + 1, m + 2]
        cur = []
        for kb in kbs:
            key = ("int", kb - m) if 2 <= m <= 13 else ("edge", m, kb)
            if key not in tid:
                tid[key] = len(tid)
            cur.append((kb, tid[key]))
        lists.append(cur)
    return lists, tid


def _na_index_tables():
    lists, tid = _na_tile_lists()
    ntile = len(tid)
    dr_i = np.zeros((ntile, 128, 128), np.int64)
    dc_i = np.zeros((ntile, 128, 128), np.int64)
    valid = np.zeros((ntile, 128, 128), bool)
    done = set()
    for m in range(16):
        for kb, t in lists[m]:
            if t in done:
                continue
            done.add(t)
            kk = np.arange(128)
            kr = (2 * kb + kk // 64)[:, None]
            kc = (kk % 64)[:, None]
            qr = (2 * m + kk // 64)[None, :]
            qc = (kk % 64)[None, :]
            rs = np.clip(qr - 4, 0, 24)
            ws = np.clip(qc - 8, 0, 48)
            v = (kr >= rs) & (kr < rs + 8) & (kc >= ws) & (kc < ws + 16)
            dr = np.clip(kr - qr + 7, 0, 14)
            dc = np.clip(kc - qc + 15, 0, 30)
            dr_i[t] = np.broadcast_to(dr, (128, 128))
            dc_i[t] = np.broadcast_to(dc, (128, 128))
            valid[t] = v
    return dr_i, dc_i, valid


def _const_tables():
    c = {}
    c["ident"] = np.eye(128, dtype=np.float32)
    slopes = 2.0 ** (-8.0 * np.arange(1, 5) / 4)
    i = np.arange(T)
    a = (i // 128).astype(np.float32)
    r = (i % 128).astype(np.float32)
    augq = np.zeros((4, 4, T), np.float32)
    for h in range(4):
        s = slopes[h]
        augq[h, 0] = -s * 128.0 * a
        augq[h, 1] = -s * r
        augq[h, 2] = s
        augq[h, 3] = s
    c["augq"] = augq
    augk = np.zeros((2, 4, T), np.float32)
    for v, sg in enumerate([1.0, -1.0]):
        augk[v, 0] = sg
        augk[v, 1] = sg
        augk[v, 2] = sg * 128.0 * a
        augk[v, 3] = sg * r
    c["augk"] = augk
    jj = np.arange(128)[:, None]
    ii = np.arange(128)[None, :]
    cd = np.zeros((128, 4 * 128), np.float32)
    for h in range(4):
        cd[:, h * 128:(h + 1) * 128] = -2.0 * slopes[h] * np.maximum(jj - ii, 0)
    c["cdiag"] = cd
    inv = (10000.0 ** (-np.arange(0, 32, 2, dtype=np.float32) / np.float32(32))).astype(np.float32)
    ang = (np.arange(T, dtype=np.float32)[:, None] * inv[None, :]).astype(np.float32)
    cos = np.cos(ang.astype(np.float64)).astype(np.float32).T
    sin = np.sin(ang.astype(np.float64)).astype(np.float32).T
    rt = np.zeros((128, T), np.float32)
    rt[64:80] = cos
    rt[80:96] = cos
    rt[96:112] = -sin
    rt[112:128] = sin
    c["ropetab"] = rt
    return c


def _pm(v, nch):
    return np.ascontiguousarray(np.asarray(v, np.float32).reshape(nch, 128).T)


def _prep_shared(inp):
    g = {}
    c = _const_tables()
    g.update(c)
    f = lambda k: np.asarray(inp[k], np.float32)
    gains = np.zeros((128, 48), np.float32)
    gains[:, 0:8] = _pm(f("mix_norm_e")[0], 8)
    gains[:, 8:16] = _pm(f("ffn_norm_g")[0], 8)
    gains[:, 16:24] = _pm(f("mix_norm_o")[0], 8)
    gains[:, 24:32] = _pm(f("ffn_norm_g")[1], 8)
    gains[:, 32:40] = _pm(f("final_norm_g"), 8)
    gains[:, 40:44] = _pm(f("q_norm_g")[0], 4)
    gains[:, 44:46] = _pm(f("kv_norm_g")[0], 2)
    gains[:, 46] = f("diff_subln_g")[0]
    g["gains"] = gains
    cv = np.zeros((128, 2 * NFC * 4), np.float32)
    for l in range(2):
        for k in range(3):
            cv[:, l * 88 + k * 22: l * 88 + (k + 1) * 22] = _pm(f("ffn_conv_w")[l, k], NFC)
        cv[:, l * 88 + 66: l * 88 + 88] = _pm(f("ffn_conv_b")[l], NFC)
    g["convp"] = cv
    lam = np.zeros((128, 256), np.float32)
    for k, nm in enumerate(["diff_lq1", "diff_lk1", "diff_lq2", "diff_lk2"]):
        lam[:, k * 64:(k + 1) * 64] = np.broadcast_to(f(nm)[0][None, :], (128, 64))
    g["lamv"] = lam
    dr_i, dc_i, valid = _na_index_tables()
    rpb = f("na_rpb")[0]
    nb = np.empty((8, dr_i.shape[0], 128, 128), np.float32)
    for h in range(8):
        nb[h] = np.where(valid, rpb[h][dr_i, dc_i], np.float32(NEG))
    g["nbias"] = np.ascontiguousarray(nb.transpose(0, 2, 1, 3))
    g["w_in"] = f("w_in_e")[0]
    g["w_out"] = f("w_out_e")[0]
    g["w_gate"] = f("w_ffn_gate")
    g["w_val"] = f("w_ffn_val")
    g["w_down"] = f("w_ffn_down")
    g["w_dq"] = f("w_dq")[0]
    wuq = f("w_uq")[0]
    cols = []
    for h in range(16):
        b = h * 96
        cols += list(range(b, b + 96)) + list(range(b + 80, b + 96)) + list(range(b + 64, b + 80))
    g["w_uq"] = np.ascontiguousarray(wuq[:, cols])
    wdkv = f("w_dkv")[0]
    wd = np.zeros((1024, 384), np.float32)
    wd[:, 0:256] = wdkv[:, 0:256]
    wd[:, 320:352] = wdkv[:, 256:288]
    wd[:, 352:368] = wdkv[:, 272:288]
    wd[:, 368:384] = wdkv[:, 256:272]
    g["w_dkv"] = wd
    wukv = f("w_ukv")[0].reshape(256, 16, 128)
    g["w_uk"] = np.ascontiguousarray(wukv[:, :, 0:64].reshape(256, 1024))
    g["w_uv"] = np.ascontiguousarray(wukv[:, :, 64:128].reshape(256, 1024))
    g["w_o"] = f("w_o_mla")[0]
    return g


BF16_IN = ()
SHAPES = {
    "x": [T, D], "ident": [128, 128], "augq": [4, 4, T], "augk": [2, 4, T], "cdiag": [128, 512],
    "ropetab": [128, T], "gains": [128, 48], "convp": [128, 176], "lamv": [128, 256],
    "nbias": [8, 128, 21, 128],
}


def _desc_size(d):
    if d[0] == "gv":
        return 2048
    return d[2] * d[4]


def _pack_weights(g, plan):
    pack = np.zeros((len(plan), 128, 2048), np.float32)
    for i, d in enumerate(plan):
        if d[0] == "gv":
            _, layer, c = d
            wg = g["w_gate"][layer][:, c * 128:(c + 1) * 128].reshape(8, 128, 128)
            wv = g["w_val"][layer][:, c * 128:(c + 1) * 128].reshape(8, 128, 128)
            t = np.concatenate([wg, wv], axis=2)
            pack[i] = t.transpose(1, 0, 2).reshape(128, 2048)
        else:
            _, name, kc, c0, ncols, lidx, r0 = d
            w = g[name] if lidx is None else g[name][lidx]
            t = w[r0:r0 + kc * 128, c0:c0 + ncols].reshape(kc, 128, ncols)
            pack[i, :, 0:kc * ncols] = t.transpose(1, 0, 2).reshape(128, kc * ncols)
    return pack


def build(plan=None, nstage=99):
    nc = bass.Bass("TRN2", target_bir_lowering=False)
    dram = {k: nc.dram_tensor(k, shp, BF16 if k in BF16_IN else F32, kind="ExternalInput").ap() for k, shp in SHAPES.items()}
    if plan is not None:
        dram["wpack"] = nc.dram_tensor("wpack", [len(plan), 128, 2048], F32, kind="ExternalInput").ap()
    y_out = nc.dram_tensor("y", [T, D], F32, kind="ExternalOutput").ap()
    requests = []
    with ExitStack() as st:
        S = Sched(nc, st)
        sb = lambda n, shp, dt: st.enter_context(nc.sbuf_tensor("sb_" + n, shp, dt))
        xT = sb("xT", [128, 8, T], F32)
        hT = sb("hT", [128, 8, T], BF16)
        wring = sb("wring", [128, NSLOT, 2048], BF16)
        AR = sb("arena", [128, NBLK * 2048], BF16)
        ident = sb("ident", [128, 128], F32)
        identb = sb("identb", [128, 128], BF16)
        onesb = sb("onesb", [128, 128], BF16)
        gains = sb("gains", [128, 48], F32)
        convp = sb("convp", [128, 176], F32)
        small = sb("small", [128, 16], F32)
        cdiag = sb("cdiag", [128, 512], BF16)
        rs_a = sb("rs_a", [128, 512], F32)
        rs_b = sb("rs_b", [128, 512], F32)
        sqr = sb("sqr", [128, 4, 512], BF16)
        PS = st.enter_context(nc.psum_tensor("PS", [128, 8 * 512], F32))

        XT = [[Buf("xT%d_%d" % (c, g)) for g in range(4)] for c in range(8)]
        HT = [[Buf("hT%d_%d" % (c, g)) for g in range(4)] for c in range(8)]
        WS = [Buf("ws%d" % i) for i in range(NSLOT)]
        PB = [Buf("pb%d" % i, excl=True) for i in range(8)]
        B_const = Buf("const")
        B_small = Buf("small")
        B_rsa, B_rsb = Buf("rsa"), Buf("rsb")
        SQ = [Buf("sq%d" % i) for i in range(4)]

        def bank(i):
            return PS[:, i * 512:(i + 1) * 512]

        def blk(k, n=1):
            return AR[:, k * 2048:(k + n) * 2048]

        def blkf(k, n=2):
            return AR[:, k * 2048:(k + n) * 2048].bitcast(F32)

        state = {"bank": 0, "wi": 0, "wissued": 0, "sq": 0, "sb": 0, "tick": 0}
        pending = []
        lamv = blkf(0, 1)[:, 0:256]

        def nb_():
            b = state["bank"]
            state["bank"] = (b + 1) % 8
            return b

        def tgs(g):
            return slice(g * 512, (g + 1) * 512)

        def wtile(desc):
            i = state["wi"]
            state["wi"] = i + 1
            requests.append(desc)
            if plan is not None:
                hi = min(len(plan), i + 1 + LOOKAHEAD)
                while state["wissued"] < hi:
                    j = state["wissued"]
                    slot = j % NSLOT
                    X = _desc_size(plan[j])
                    S.dma("pool", wring[:, slot, 0:X], dram["wpack"][j][:, 0:X], writes=[WS[slot]])
                    state["wissued"] = j + 1
            return wring[:, i % NSLOT, :], WS[i % NSLOT]

        def wreq_cols(name, kc, c0, ncols, lidx=None, r0=0):
            return ("cols", name, kc, c0, ncols, lidx, r0)

        def wview(slot, kc, ncols):
            return slot[:, 0:kc * ncols].rearrange("p (k n) -> p k n", k=kc)

        S.dma("sp", ident[:], dram["ident"], writes=[B_const])
        gb_l = Buf("gains_l")
        S.dma("sp", gains[:], dram["gains"], writes=[gb_l])
        cb1, cb2, cb3 = Buf("c1"), Buf("c2"), Buf("c3")
        S.dma("sp", convp[:], dram["convp"], writes=[cb1])
        S.dma("sp", lamv, dram["lamv"], writes=[cb2])
        S.dma("pool", cdiag[:], dram["cdiag"], writes=[cb3])
        S.op("dve", L("tensor_copy", identb[:], ident[:]), reads=[B_const], writes=[B_const])
        S.op("dve", L("memset", onesb[:], 1.0), writes=[B_const])
        S.op("dve", L("memset", small[:, 0:1], EPS), writes=[B_small])
        lam_init = 0.8 - 0.6 * math.exp(-0.3 * 0)
        S.op("dve", L("tensor_tensor", lamv[:, 0:64], lamv[:, 0:64], lamv[:, 64:128], ALU.mult),
             reads=[cb2], writes=[cb2])
        S.op("dve", L("tensor_tensor", lamv[:, 128:192], lamv[:, 128:192], lamv[:, 192:256], ALU.mult),
             reads=[cb2], writes=[cb2])
        S.op("dve", L("reduce_sum", small[:, 4:5], lamv[:, 0:64], mybir.AxisListType.X),
             reads=[cb2], writes=[B_small])
        S.op("dve", L("reduce_sum", small[:, 5:6], lamv[:, 128:192], mybir.AxisListType.X),
             reads=[cb2], writes=[B_small])
        S.op("act", L("activation", small[:, 6:8], small[:, 4:6], AF.Exp), reads=[B_small], writes=[B_small])
        S.op("dve", L("scalar_tensor_tensor", small[:, 1:2], small[:, 7:8], -lam_init, small[:, 6:7],
                                                     ALU.add, ALU.subtract), reads=[B_small], writes=[B_small])
        S.op("dve", L("tensor_scalar", small[:, 2:3], gains[:, 46:47], 1.0 - lam_init, None, ALU.mult),
             reads=[B_small, gb_l], writes=[B_small])
        S.barrier(dma=True)

        def mm(out, lhsT, rhs, start, stop, reads, writes, track=None):
            if track is None:
                track = stop
            S.op("pe", L("matmul", out, lhsT, rhs, start=start, stop=stop), reads=reads, writes=writes,
                 track=track)

        def rstd_from_psum(pb, nfeat, dst, dstbuf):
            S.op("act", L("activation", rs_a[:], bank(pb), AF.Sqrt, scale=1.0 / nfeat, bias=small[:, 0:1]),
                 reads=[PB[pb], B_small], writes=[B_rsa])
            S.op("dve", L("reciprocal", dst, rs_a[:]), reads=[B_rsa], writes=[dstbuf])

        def norm_stage(goff):
            for g in range(4):
                pb = nb_()
                for c in range(8):
                    q = state["sq"]
                    state["sq"] = (q + 1) % 4
                    S.op("act", L("activation", sqr[:, q, :], xT[:, c, tgs(g)], AF.Square),
                         reads=[XT[c][g]], writes=[SQ[q]])
                    mm(bank(pb), onesb[:], sqr[:, q, :], c == 0, c == 7, [SQ[q], B_const], [PB[pb]], track=True)
                rstd_from_psum(pb, 1024.0, rs_b[:], B_rsb)
                for c in range(8):
                    S.op("dve", L("scalar_tensor_tensor", hT[:, c, tgs(g)], xT[:, c, tgs(g)], gains[:, goff + c:goff + c + 1], rs_b[:],
                        ALU.mult, ALU.mult), reads=[XT[c][g], B_rsb], writes=[HT[c][g]])

        def proj_fm(wv, wbuf, col0, rhs_fn, rhs_bufs_fn, nk, g, M=128):
            pb = nb_()
            for k in range(nk):
                mm(bank(pb)[0:M, :], wv[:, k, col0:col0 + M], rhs_fn(k, g), k == 0, k == nk - 1,
                   [wbuf] + rhs_bufs_fn(k, g), [PB[pb]])
            return pb

        hT_rhs = lambda k, g: hT[:, k, tgs(g)]
        hT_bufs = lambda k, g: [HT[k][g]]

        def resid_add(pb, n, g, eng="dve"):
            S.op(eng, L("tensor_tensor", xT[:, n, tgs(g)], bank(pb), xT[:, n, tgs(g)], ALU.add),
                 reads=[PB[pb], XT[n][g]], writes=[XT[n][g]])

        def out_proj(name, mix_fn, mix_bufs, krow0, nkc):
            for nt in range(4):
                slot, wb = wtile(wreq_cols(name, nkc, nt * 256, 256, r0=krow0))
                wv = wview(slot, nkc, 256)
                for nn in range(2):
                    n = nt * 2 + nn
                    for g in range(4):
                        pb = proj_fm(wv, wb, nn * 128, mix_fn, mix_bufs, nkc, g)
                        resid_add(pb, n, g)

        XS = [Buf("xs%d" % i) for i in range(16)]
        for tt in range(16):
            S.dma("sp", blkf(tt, 1), dram["x"][tt * 128:(tt + 1) * 128, :], writes=[XS[tt]])
        for g in range(4):
            for c in range(8):
                pb = nb_()
                for j in range(4):
                    tt = g * 4 + j
                    S.op("pe", L("transpose", bank(pb)[:, j * 128:(j + 1) * 128], blkf(tt, 1)[:, c * 128:(c + 1) * 128], ident[:]),
                        reads=[XS[tt], B_const], writes=[PB[pb]], track=(j == 3))
                eng = "act" if c % 2 == 0 else "dve"
                if eng == "act":
                    S.op("act", L("activation", xT[:, c, tgs(g)], bank(pb), AF.Copy),
                         reads=[PB[pb]], writes=[XT[c][g]])
                else:
                    S.op("dve", L("tensor_copy", xT[:, c, tgs(g)], bank(pb)),
                         reads=[PB[pb]], writes=[XT[c][g]])
        S.barrier()

        def ffn(layer, goff):
            norm_stage(goff)
            U = [Buf("u%d" % i) for i in range(11)]
            TB = [Buf("tb%d" % i) for i in range(2)]
            cp = layer * 88
            for rnd in range(2):
                for cc in range(11):
                    c = rnd * 11 + cc

                    slot, wb = wtile(("gv", layer, c))
                    wv = wview(slot, 8, 256)
                    for g in range(4):
                        for k in range(8):
                            mm(bank(g), wv[:, k, 0:128], hT[:, k, tgs(g)], k == 0, k == 7, [wb, HT[k][g]], [PB[g]])
                    for g in range(4):
                        for k in range(8):
                            mm(bank(4 + g), wv[:, k, 128:256], hT[:, k, tgs(g)], k == 0, k == 7,
                               [wb, HT[k][g]], [PB[4 + g]])
                    tb = cc % 2
                    tv = blkf(11 + 2 * tb)
                    G = PS[:, 0:2048]
                    gb = [PB[0], PB[1], PB[2], PB[3]]
                    w0 = convp[:, cp + c:cp + c + 1]
                    w1 = convp[:, cp + 22 + c:cp + 22 + c + 1]
                    w2 = convp[:, cp + 44 + c:cp + 44 + c + 1]
                    bb = convp[:, cp + 66 + c:cp + 66 + c + 1]
                    S.op("act", L("activation", tv, G, AF.Identity, scale=w1, bias=bb),
                         reads=gb + [cb1], writes=[TB[tb]])
                    S.op("dve", L("scalar_tensor_tensor", tv[:, 1:T], PS[:, 0:T - 1], w0, tv[:, 1:T], ALU.mult, ALU.add),
                        reads=gb + [TB[tb]], writes=[TB[tb]])
                    S.op("dve", L("scalar_tensor_tensor", tv[:, 0:T - 1], PS[:, 1:T], w2, tv[:, 0:T - 1], ALU.mult, ALU.add),
                        reads=gb + [TB[tb]], writes=[TB[tb]])
                    S.op("act", L("activation", tv, tv, AF.Gelu), reads=[TB[tb]], writes=[TB[tb]])
                    for hh in range(2):
                        S.op("dve", L("tensor_tensor", blk(cc)[:, hh * 1024:(hh + 1) * 1024], tv[:, hh * 1024:(hh + 1) * 1024],
                            PS[:, 2048 + hh * 1024:2048 + (hh + 1) * 1024], ALU.mult),
                            reads=[TB[tb], PB[4 + 2 * hh], PB[5 + 2 * hh]], writes=[U[cc]])
                for n in range(8):
                    slot, wb = wtile(wreq_cols("w_down", 11, n * 128, 128, lidx=layer, r0=rnd * 1408))
                    wv = wview(slot, 11, 128)
                    for g in range(4):
                        pb = proj_fm(wv, wb, 0, lambda k, g: blk(k)[:, tgs(g)], lambda k, g: [U[k]], 11, g)
                        resid_add(pb, n, g)
            S.barrier()

        def softmax_epilogue_diff(h, I, pO, pD, MIX):
            r1, r2, t2 = blkf(13, 2)[:, 0:512], blkf(13, 2)[:, 512:1024], blkf(13, 2)[:, 1024:1536]
            t1 = blkf(15, 2)[:, I * 512:(I + 1) * 512]
            EB = state.setdefault("EB", [Buf("eb%d" % i) for i in range(3)])
            A1 = state.setdefault("A1", [Buf("a1_%d" % i) for i in range(4)])
            S.op("dve", L("tensor_copy", r1, bank(pD[0])), reads=[PB[pD[0]]], writes=[EB[0]])
            S.op("dve", L("tensor_copy", r2, bank(pD[1])), reads=[PB[pD[1]]], writes=[EB[1]])
            S.op("dve", L("tensor_copy", t1, bank(pO[0])), reads=[PB[pO[0]]], writes=[A1[I]])
            S.op("dve", L("tensor_copy", t2, bank(pO[1])), reads=[PB[pO[1]]], writes=[EB[2]])
            S.op("dve", L("reciprocal", r1, r1), reads=[EB[0]], writes=[EB[0]])
            S.op("dve", L("reciprocal", r2, r2), reads=[EB[1]], writes=[EB[1]])
            S.op("dve", L("tensor_tensor", t1, t1, r1, ALU.mult), reads=[EB[0], A1[I]], writes=[A1[I]])
            S.op("dve", L("tensor_tensor", t2, t2, r2, ALU.mult), reads=[EB[1], EB[2]], writes=[EB[2]])
            S.op("dve", L("scalar_tensor_tensor", t1, t2, small[:, 1:2], t1, ALU.mult, ALU.add),
                 reads=[A1[I], EB[2], B_small], writes=[A1[I]])

        def subln_head(h, MIX):
            A1 = state["A1"]
            RS = state.setdefault("RS", [Buf("rs_%d" % i) for i in range(4)])
            pbs = []
            for I in range(4):
                t1 = blkf(15, 2)[:, I * 512:(I + 1) * 512]
                S.op("act", L("activation", sqr[:, I, :], t1, AF.Square), reads=[A1[I]], writes=[SQ[I]])
                pb = 4 + state["sb"] % 4
                state["sb"] += 1
                mm(bank(pb), onesb[:], sqr[:, I, :], True, True, [SQ[I], B_const], [PB[pb]])
                pbs.append(pb)
            for I in range(4):
                rsd = blkf(17, 2)[:, I * 512:(I + 1) * 512]
                S.op("act", L("activation", rsd, bank(pbs[I]), AF.Sqrt, scale=1.0 / 128.0, bias=small[:, 0:1]),
                     reads=[PB[pbs[I]], B_small], writes=[RS[I]])
            for I in range(4):
                t1 = blkf(15, 2)[:, I * 512:(I + 1) * 512]
                rsd = blkf(17, 2)[:, I * 512:(I + 1) * 512]
                S.op("dve", L("reciprocal", rsd, rsd), reads=[RS[I]], writes=[RS[I]])
                S.op("dve", L("scalar_tensor_tensor", blk(h)[:, tgs(I)], t1, small[:, 2:3], rsd, ALU.mult, ALU.mult),
                     reads=[A1[I], RS[I], B_small], writes=[MIX[h]])

        def diff_phase():
            MIX = [Buf("mix%d" % i) for i in range(4)]
            QB = [Buf("dq%d" % i) for i in range(2)]
            KB = [[Buf("dk%d%d" % (i, v)) for v in range(2)] for i in range(2)]
            VB = Buf("dv")
            PT = [Buf("pt%d" % i) for i in range(8)]
            Qt = [blk(4), blk(5)]
            Kt = [[blk(6), blk(7)], [blk(8), blk(9)]]
            Vt = blk(10).rearrange("p (t e) -> p t e", t=16)
            ptv = lambda i: blk(11, 2)[:, i * 512:(i + 1) * 512]
            pti = [0]
            for n in range(2):
                for v in range(2):
                    S.dma("pool", Kt[n][v][64:68, :], dram["augk"][v], writes=[KB[n][v]])
            for h in range(int(os.environ.get("KH0", "0")), int(os.environ.get("KH1", "4")) if KDBG in (0, 5) else 1):
                if KDBG == 1:
                    break
                if os.environ.get("KBAR", "0") == "1":
                    S.barrier()
                slot, wb = wtile(wreq_cols("w_in", 8, h * 128, 128))
                wq = wview(slot, 8, 128)
                for n in range(2):
                    S.dma("pool", Qt[n][64:68, :], dram["augq"][h], writes=[QB[n]])
                for g in range(4):
                    pb = proj_fm(wq, wb, 0, hT_rhs, hT_bufs, 8, g)
                    S.op("dve", L("tensor_scalar", Qt[0][0:64, tgs(g)], bank(pb)[0:64, :], 0.125, None, ALU.mult),
                         reads=[PB[pb]], writes=[QB[0]])
                    S.op("dve", L("tensor_scalar", Qt[1][0:64, tgs(g)], bank(pb)[64:128, :], 0.125, None, ALU.mult),
                         reads=[PB[pb]], writes=[QB[1]])
                slot, wb = wtile(wreq_cols("w_in", 8, 512 + h * 128, 128))
                wk = wview(slot, 8, 128)
                for g in range(4):
                    pb = proj_fm(wk, wb, 0, hT_rhs, hT_bufs, 8, g)
                    for n in range(2):
                        S.op("act", L("activation", Kt[n][0][0:64, tgs(g)], bank(pb)[64 * n:64 * n + 64, :], AF.Copy),
                             reads=[PB[pb]], writes=[KB[n][0]])
                        S.op("dve", L("tensor_copy", Kt[n][1][0:64, tgs(g)], bank(pb)[64 * n:64 * n + 64, :]),
                             reads=[PB[pb]], writes=[KB[n][1]])
                slot, wb = wtile(wreq_cols("w_in", 8, 1024 + h * 128, 128))
                wvv = wview(slot, 8, 128)
                for g in range(4):
                    pb = nb_()
                    for j in range(4):
                        tt = g * 4 + j
                        for k in range(8):
                            mm(bank(pb)[:, j * 128:(j + 1) * 128], hT[:, k, tt * 128:(tt + 1) * 128], wvv[:, k, :],
                               k == 0, k == 7, [wb, HT[k][g]], [PB[pb]], track=(k == 7 and j == 3))
                    S.op("act", L("activation", Vt[:, g * 4:(g + 1) * 4, :], bank(pb).rearrange("p (t e) -> p t e", t=4), AF.Copy),
                        reads=[PB[pb]], writes=[VB])
                pO, pD = [0, 1], [2, 3]
                tiles = [(I, J, n) for I in range(4) for J in range(16) for n in range(2)]
                info = {}

                def stage1(t):
                    I, J, n = t
                    ps = 4 + state["sb"] % 4
                    state["sb"] += 1
                    if J < 4 * I or J >= 4 * I + 4:
                        v = 0 if J < 4 * I else 1
                        mm(bank(ps), Kt[n][v][0:68, J * 128:(J + 1) * 128], Qt[n][0:68, tgs(I)], True, True,
                           [KB[n][v], QB[n]], [PB[ps]])
                    else:
                        for b in range(4):
                            gb_ = 4 * I + b
                            v = 0 if gb_ >= J else 1
                            qs = slice(gb_ * 128, (gb_ + 1) * 128)
                            osl = bank(ps)[:, b * 128:(b + 1) * 128]
                            diag = gb_ == J
                            mm(osl, Kt[n][v][0:68, J * 128:(J + 1) * 128], Qt[n][0:68, qs], True, not diag,
                               [KB[n][v], QB[n]], [PB[ps]], track=(b == 3 and not diag))
                            if diag:
                                mm(osl, identb[:], cdiag[:, h * 128:(h + 1) * 128], False, True,
                                   [B_const, cb3], [PB[ps]], track=(b == 3))
                    pi = pti[0]
                    pti[0] = (pi + 1) % 8
                    S.op("act", L("activation", ptv(pi), bank(ps), AF.Exp), reads=[PB[ps]], writes=[PT[pi]])
                    info[t] = pi

                def stage2(t):
                    I, J, n = t
                    pi = info.pop(t)
                    mm(bank(pO[n]), Vt[:, J, :], ptv(pi), J == 0, J == 15, [VB, PT[pi]], [PB[pO[n]]], track=True)
                    mm(bank(pD[n]), onesb[:], ptv(pi), J == 0, J == 15, [B_const, PT[pi]], [PB[pD[n]]], track=True)
                    if J == 15 and n == 1:
                        softmax_epilogue_diff(h, I, pO, pD, MIX)
                        if I == 3:
                            pending.append((state["tick"] + 24, lambda h=h: subln_head(h, MIX)))

                DEPTH = 2
                for i in range(len(tiles) + DEPTH):
                    state["tick"] += 1
                    while pending and pending[0][0] <= state["tick"]:
                        pending.pop(0)[1]()
                    if i < len(tiles):
                        stage1(tiles[i])
                    if i >= DEPTH:
                        stage2(tiles[i - DEPTH])
            while pending:
                pending.pop(0)[1]()
            if KDBG == 0:
                out_proj("w_out", lambda k, g: blk(k)[:, tgs(g)], lambda k, g: [MIX[k]], 0, 4)
            S.barrier()

        def na_phase():
            lists, _ = _na_tile_lists()
            MIX = [Buf("nmix%d" % i) for i in range(4)]
            NQb, NKb = Buf("nq"), Buf("nk")
            VAb = [Buf("va0"), Buf("va1")]
            NBb = Buf("nbias")
            PT = [Buf("npt%d" % i) for i in range(6)]
            RD = [Buf("nrd%d" % i) for i in range(2)]
            NQ, NK = blk(4), blk(5)
            VA = [blk(6).rearrange("p (t e) -> p t e", t=16), blk(7).rearrange("p (t e) -> p t e", t=16)]
            NBt = blk(8, 3)
            ptv = lambda i: blk(11, 2)[:, i * 640:(i + 1) * 640]
            rdv = lambda i: blkf(13, 2)[:, i * 512:(i + 1) * 512]
            for a in range(2):
                S.op("dve", L("memset", VA[a][:, :, 64:128], 1.0), writes=[VAb[a]])
            pti = [0]
            for j in range(4):
                slot, wb = wtile(wreq_cols("w_in", 8, 1536 + j * 128, 128))
                wq = wview(slot, 8, 128)
                for g in range(4):
                    pb = proj_fm(wq, wb, 0, hT_rhs, hT_bufs, 8, g)
                    S.op("dve", L("tensor_scalar", NQ[:, tgs(g)], bank(pb), 0.125, None, ALU.mult),
                         reads=[PB[pb]], writes=[NQb])
                slot, wb = wtile(wreq_cols("w_in", 8, 2048 + j * 128, 128))
                wk = wview(slot, 8, 128)
                for g in range(4):
                    pb = proj_fm(wk, wb, 0, hT_rhs, hT_bufs, 8, g)
                    S.op("dve", L("tensor_copy", NK[:, tgs(g)], bank(pb)), reads=[PB[pb]], writes=[NKb])
                slot, wb = wtile(wreq_cols("w_in", 8, 2560 + j * 128, 128))
                wvv = wview(slot, 8, 128)
                for g in range(4):
                    pb = nb_()
                    for jj in range(4):
                        tt = g * 4 + jj
                        for k in range(8):
                            mm(bank(pb)[:, jj * 128:(jj + 1) * 128], hT[:, k, tt * 128:(tt + 1) * 128], wvv[:, k, :],
                               k == 0, k == 7, [wb, HT[k][g]], [PB[pb]], track=(k == 7 and jj == 3))
                    pv = bank(pb).rearrange("p (t e) -> p t e", t=4)
                    S.op("act", L("activation", VA[0][:, g * 4:(g + 1) * 4, 0:64], pv[:, :, 0:64], AF.Copy),
                         reads=[PB[pb]], writes=[VAb[0]])
                    S.op("dve", L("tensor_copy", VA[1][:, g * 4:(g + 1) * 4, 0:64], pv[:, :, 64:128]),
                         reads=[PB[pb]], writes=[VAb[1]])
                for a in range(2):
                    S.dma("pool", NBt[:, a * 2688:(a + 1) * 2688].rearrange("p (t q) -> p t q", t=21),
                          dram["nbias"][2 * j + a], writes=[NBb])
                tiles = [(a, mg, mi) for a in range(2) for mg in range(4) for mi in range(4)]
                info = {}

                def stage1(t):
                    a, mg, mi = t
                    p0 = 64 * a
                    m = mg * 4 + mi
                    lst = lists[m]
                    pi = pti[0]
                    pti[0] = (pi + 1) % 6
                    psA = 2 + state["sb"] % 6
                    state["sb"] += 1
                    psB = None
                    if len(lst) == 5:
                        psB = 2 + state["sb"] % 6
                        state["sb"] += 1
                    for idx, (kb, tl) in enumerate(lst):
                        if idx < 4:
                            osl, pbx = bank(psA)[:, idx * 128:(idx + 1) * 128], psA
                        else:
                            osl, pbx = bank(psB)[:, 0:128], psB
                        last = (idx == 3) or (idx == 4)
                        mm(osl, NK[p0:p0 + 64, kb * 128:(kb + 1) * 128], NQ[p0:p0 + 64, m * 128:(m + 1) * 128],
                           True, False, [NKb, NQb], [PB[pbx]], track=False)
                        mm(osl, identb[:], NBt[:, a * 2688 + tl * 128: a * 2688 + (tl + 1) * 128], False, True,
                           [B_const, NBb], [PB[pbx]], track=last)
                    S.op("act", L("activation", ptv(pi)[:, 0:512], bank(psA), AF.Exp),
                         reads=[PB[psA]], writes=[PT[pi]])
                    if psB is not None:
                        S.op("act", L("activation", ptv(pi)[:, 512:640], bank(psB)[:, 0:128], AF.Exp),
                             reads=[PB[psB]], writes=[PT[pi]])
                    info[t] = pi

                def stage2(t, j=j):
                    a, mg, mi = t
                    p0 = 64 * a
                    m = mg * 4 + mi
                    lst = lists[m]
                    pi = info.pop(t)
                    po = (a * 4 + mg) % 2
                    for idx, (kb, tl) in enumerate(lst):
                        mm(bank(po)[:, mi * 128:(mi + 1) * 128], VA[a][:, kb, :], ptv(pi)[:, idx * 128:(idx + 1) * 128],
                           idx == 0, idx == len(lst) - 1, [VAb[a], PT[pi]], [PB[po]],
                           track=(idx == len(lst) - 1))
                    if mi == 3:
                        ri = mg % 2
                        S.op("dve", L("reciprocal", rdv(ri)[0:64, :], bank(po)[64:128, :]),
                             reads=[PB[po]], writes=[RD[ri]])
                        S.op("dve", L("tensor_tensor", blk(j)[p0:p0 + 64, tgs(mg)], bank(po)[0:64, :], rdv(ri)[0:64, :], ALU.mult),
                             reads=[PB[po], RD[ri]], writes=[MIX[j]])

                DEPTH = 2
                for i in range(len(tiles) + DEPTH):
                    if i < len(tiles):
                        stage1(tiles[i])
                    if i >= DEPTH:
                        stage2(tiles[i - DEPTH])
            out_proj("w_out", lambda k, g: blk(k)[:, tgs(g)], lambda k, g: [MIX[k]], 512, 4)
            S.barrier()

        def mla_phase():
            CQ = [Buf("cq%d" % i) for i in range(4)]
            CKV = [Buf("ckv%d" % i) for i in range(2)]
            KRb = Buf("kr")
            RTb = Buf("ropetab")
            QHb = [Buf("qh0"), Buf("qh1")]
            KHb = [Buf("kh0"), Buf("kh1")]
            VAb = [Buf("mva0"), Buf("mva1")]
            MIX = [Buf("mmix%d" % i) for i in range(4)]
            PT = [Buf("mpt%d" % i) for i in range(8)]
            TMP = [Buf("mtmp%d" % i) for i in range(2)]
            cqv = lambda c: blk(4 + c)
            ckvv = lambda c: blk(8 + c)
            KR = blk(10)
            rt = blkf(11, 2)
            QH = [blk(13), blk(14)]
            KH = [blk(15), blk(16)]
            VA = [blk(17).rearrange("p (t e) -> p t e", t=16), blk(18).rearrange("p (t e) -> p t e", t=16)]
            ptv = lambda i: blk(10 if i < 4 else 19)[:, (i % 4) * 512:(i % 4 + 1) * 512]
            tmpv = lambda i: blkf(20, 1)[:, i * 512:(i + 1) * 512]
            scale = 96.0 ** -0.5
            S.dma("sp", rt, dram["ropetab"], writes=[RTb])
            for a in range(2):
                S.op("dve", L("memset", VA[a][:, :, 64:128], 1.0), writes=[VAb[a]])

            def rope(dst, dstbuf, pb, g):
                S.op("dve", L("tensor_tensor", tmpv(0)[64:96, :], bank(pb)[64:96, :], rt[64:96, tgs(g)], ALU.mult),
                     reads=[PB[pb], RTb], writes=[TMP[0]])
                S.op("dve", L("tensor_tensor", tmpv(1)[64:96, :], bank(pb)[96:128, :], rt[96:128, tgs(g)], ALU.mult),
                     reads=[PB[pb], RTb], writes=[TMP[1]])
                S.op("dve", L("tensor_tensor", dst[64:96, tgs(g)], tmpv(0)[64:96, :], tmpv(1)[64:96, :], ALU.add),
                     reads=[TMP[0], TMP[1]], writes=[dstbuf])

            def latent(wtiles, nch, goff, dstv, dstb, nfeat, extra=None):
                for g in range(4):
                    pbs = []
                    for (wv, wb, ncol) in wtiles:
                        for cc in range(ncol // 128):
                            pbs.append(proj_fm(wv, wb, cc * 128, hT_rhs, hT_bufs, 8, g))
                    pss = nb_()
                    while pss in pbs:
                        pss = nb_()
                    for c in range(nch):
                        q = state["sq"]
                        state["sq"] = (q + 1) % 4
                        S.op("act", L("activation", sqr[:, q, :], bank(pbs[c]), AF.Square),
                             reads=[PB[pbs[c]]], writes=[SQ[q]])
                        mm(bank(pss), onesb[:], sqr[:, q, :], c == 0, c == nch - 1, [SQ[q], B_const], [PB[pss]], track=True)
                    rstd_from_psum(pss, nfeat, rs_b[:], B_rsb)
                    for c in range(nch):
                        S.op("dve", L("scalar_tensor_tensor", dstv(c)[:, tgs(g)], bank(pbs[c]), gains[:, goff + c:goff + c + 1], rs_b[:], ALU.mult, ALU.mult),
                            reads=[PB[pbs[c]], B_rsb], writes=[dstb[c]])
                    if extra is not None:
                        extra(pbs[nch], g)

            s0, b0 = wtile(wreq_cols("w_dq", 8, 0, 256))
            s1, b1 = wtile(wreq_cols("w_dq", 8, 256, 256))
            latent([(wview(s0, 8, 256), b0, 256), (wview(s1, 8, 256), b1, 256)], 4, 40, cqv, CQ, 512.0)
            s0, b0 = wtile(wreq_cols("w_dkv", 8, 0, 256))
            s1, b1 = wtile(wreq_cols("w_dkv", 8, 256, 128))
            latent([(wview(s0, 8, 256), b0, 256), (wview(s1, 8, 128), b1, 128)], 2, 44, ckvv, CKV, 256.0,
                   extra=lambda pb, g: rope(KH[0], KHb[0], pb, g))
            S.op("dve", L("tensor_copy", KH[1][64:96, :], KH[0][64:96, :]), reads=[KHb[0]], writes=[KHb[1]])

            def sbank():
                ps = 4 + state["sb"] % 4
                state["sb"] += 1
                return ps

            def ibank():
                state["ib"] = state.get("ib", 0) + 1
                return 2 + state["ib"] % 2

            def head_items(hd):
                s_ = hd % 2
                st = {}
                items = []

                def q_item(g):
                    if g == 0:
                        sl, st["bq"] = wtile(wreq_cols("w_uq", 4, hd * 128, 128))
                        st["wq"] = wview(sl, 4, 128)
                    pb = ibank()
                    for k in range(4):
                        mm(bank(pb), st["wq"][:, k, :], cqv(k)[:, tgs(g)], k == 0, k == 3, [st["bq"], CQ[k]], [PB[pb]])
                    S.op("act", L("activation", QH[s_][0:64, tgs(g)], bank(pb)[0:64, :], AF.Copy),
                         reads=[PB[pb]], writes=[QHb[s_]])
                    rope(QH[s_], QHb[s_], pb, g)

                def k_item(g):
                    if g == 0:
                        sl, st["bk"] = wtile(wreq_cols("w_uk", 2, hd * 64, 64))
                        st["wk"] = wview(sl, 2, 64)
                    pb = ibank()
                    for k in range(2):
                        mm(bank(pb)[0:64, :], st["wk"][:, k, :], ckvv(k)[:, tgs(g)], k == 0, k == 1, [st["bk"], CKV[k]], [PB[pb]])
                    S.op("dve", L("tensor_copy", KH[s_][0:64, tgs(g)], bank(pb)[0:64, :]), reads=[PB[pb]], writes=[KHb[s_]])

                def v_item(g):
                    if g == 0:
                        sl, st["bv"] = wtile(wreq_cols("w_uv", 2, hd * 64, 64))
                        st["wv"] = wview(sl, 2, 64)
                    pb = ibank()
                    for jj in range(4):
                        tt = g * 4 + jj
                        for k in range(2):
                            mm(bank(pb)[:, jj * 64:(jj + 1) * 64], ckvv(k)[:, tt * 128:(tt + 1) * 128],
                               st["wv"][:, k, :], k == 0, k == 1, [st["bv"], CKV[k]], [PB[pb]],
                               track=(k == 1 and jj == 3))
                    pv = bank(pb)[:, 0:256].rearrange("p (t e) -> p t e", t=4)
                    S.op("dve", L("tensor_copy", VA[s_][:, g * 4:(g + 1) * 4, 0:64], pv), reads=[PB[pb]], writes=[VAb[s_]])

                for g in range(4):
                    items.append(lambda g=g: q_item(g))
                for g in range(4):
                    items.append(lambda g=g: k_item(g))
                for g in range(4):
                    items.append(lambda g=g: v_item(g))
                return items

            pti = state.setdefault("mpti", [0])
            for it in head_items(0):
                it()
            for hd in range(16):
                s_ = hd % 2
                pr = (hd // 2) % 4
                a = hd % 2
                nxt = head_items(hd + 1) if hd < 15 else []
                tiles = [(I, J) for I in range(4) for J in range(16)]
                info = {}

                def stage1(t):
                    I, J = t
                    ps = sbank()
                    mm(bank(ps), KH[s_][0:96, J * 128:(J + 1) * 128], QH[s_][0:96, tgs(I)], True, True,
                       [KHb[s_], QHb[s_]], [PB[ps]])
                    pi = pti[0]
                    pti[0] = (pi + 1) % 8
                    S.op("act", L("activation", ptv(pi), bank(ps), AF.Exp, scale=scale),
                         reads=[PB[ps]], writes=[PT[pi]])
                    info[t] = pi

                def stage2(t):
                    I, J = t
                    pi = info.pop(t)
                    po = I % 2
                    mm(bank(po), VA[s_][:, J, :], ptv(pi), J == 0, J == 15, [VAb[s_], PT[pi]], [PB[po]], track=True)
                    if J == 15:
                        rsv = rs_a if po == 0 else rs_b
                        rsb_ = B_rsa if po == 0 else B_rsb
                        S.op("dve", L("reciprocal", rsv[0:64, :], bank(po)[64:128, :]),
                             reads=[PB[po]], writes=[rsb_])
                        S.op("dve", L("tensor_tensor", blk(pr)[64 * a:64 * a + 64, tgs(I)], bank(po)[0:64, :], rsv[0:64, :], ALU.mult),
                             reads=[PB[po], rsb_], writes=[MIX[pr]])

                DEPTH = 3
                for i in range(len(tiles) + DEPTH):
                    if i < len(tiles):
                        stage1(tiles[i])
                    if i >= DEPTH:
                        stage2(tiles[i - DEPTH])
                    if nxt and i >= 2 and (i - 2) % 5 == 0:
                        nxt.pop(0)()
                while nxt:
                    nxt.pop(0)()
                if hd % 8 == 7:
                    half = hd // 8
                    out_proj("w_o", lambda k, g: blk(k)[:, tgs(g)], lambda k, g: [MIX[k]], half * 512, 4)
            S.barrier()

        def final_stage(goff):
            YT = [Buf("yt%d" % i) for i in range(8)]
            OS = [Buf("os%d" % i) for i in range(2)]
            ytv = lambda c: blkf(2 * c, 2)[:, 0:512]
            osv = lambda i: blkf(16 + 2 * i, 2)[:, 0:1024]
            for g in range(4):
                pb = nb_()
                for c in range(8):
                    q = state["sq"]
                    state["sq"] = (q + 1) % 4
                    S.op("act", L("activation", sqr[:, q, :], xT[:, c, tgs(g)], AF.Square),
                         reads=[XT[c][g]], writes=[SQ[q]])
                    mm(bank(pb), onesb[:], sqr[:, q, :], c == 0, c == 7, [SQ[q], B_const], [PB[pb]], track=True)
                rstd_from_psum(pb, 1024.0, rs_b[:], B_rsb)
                for c in range(8):
                    if goff is None:
                        S.op("dve", L("tensor_copy", ytv(c), xT[:, c, tgs(g)]), reads=[XT[c][g]], writes=[YT[c]])
                    else:
                        S.op("dve", L("scalar_tensor_tensor", ytv(c), xT[:, c, tgs(g)], gains[:, goff + c:goff + c + 1], rs_b[:], ALU.mult, ALU.mult),
                            reads=[XT[c][g], B_rsb], writes=[YT[c]])
                for j in range(4):
                    tt = g * 4 + j
                    oi = tt % 2
                    p0, p1 = nb_(), nb_()
                    for c in range(8):
                        pbx = p0 if c < 4 else p1
                        S.op("pe", L("transpose", bank(pbx)[:, (c % 4) * 128:(c % 4 + 1) * 128], ytv(c)[:, j * 128:(j + 1) * 128], ident[:]),
                            reads=[YT[c], B_const], writes=[PB[pbx]], track=(c % 4 == 3))
                    S.op("act", L("activation", osv(oi)[:, 0:512], bank(p0), AF.Copy),
                         reads=[PB[p0]], writes=[OS[oi]])
                    S.op("dve", L("tensor_copy", osv(oi)[:, 512:1024], bank(p1)),
                         reads=[PB[p1]], writes=[OS[oi]])
                    S.dma("sp", y_out[tt * 128:(tt + 1) * 128, :], osv(oi), reads=[OS[oi]])

        if KDBG == 10:
            for _ in range(30):
                norm_stage(0)
        elif KDBG == 11:
            norm_stage(0)
            for i in range(40):
                slot, wb = wtile(wreq_cols("w_in", 8, (i % 24) * 128, 128))
                wq = wview(slot, 8, 128)
                pb = proj_fm(wq, wb, 0, hT_rhs, hT_bufs, 8, i % 4)
                S.op("dve", L("tensor_copy", rs_b[:], bank(pb)), reads=[PB[pb]], writes=[B_rsb])
        elif nstage >= 1:
            norm_stage(0)
            S.barrier()
            diff_phase()
        if nstage >= 2:
            na_phase()
        if nstage >= 3:
            ffn(0, 8)
        if nstage >= 4:
            norm_stage(16)
            S.barrier()
            mla_phase()
        if nstage >= 5:
            ffn(1, 24)
        final_stage(32 if nstage >= 5 else None)
        S.final_wait("sp")
        S.emit()
    return nc, requests


_CACHE = {}


def _get_program(nstage=99):
    if nstage not in _CACHE:
        _, reqs = build(None, nstage)
        nc, _ = build(reqs, nstage)
        _CACHE[nstage] = (nc, reqs)
    return _CACHE[nstage]


def kernel(**inputs):
    nstage = int(inputs.pop("_nstage", 99))
    cores = inputs.pop("_cores", None)
    shared = _prep_shared(inputs)
    x = np.asarray(inputs["x"], np.float32)
    nb = x.shape[0] if cores is None else cores
    nc, plan = _get_program(nstage)
    dev = {k: shared[k] for k in SHAPES if k != "x"}
    dev["wpack"] = _pack_weights(shared, plan)
    in_maps = []
    for b in range(nb):
        m = dict(dev)
        m["x"] = np.ascontiguousarray(x[b])
        in_maps.append(m)
    res = run_bass_kernel_spmd(nc, in_maps, core_ids=list(range(nb)))
    out = np.stack([np.asarray(r["y"], np.float32) for r in res.results], axis=0)
    return out
```

```python
import math
import os
KDBG = int(os.environ.get("KDBG", "0"))
from contextlib import ExitStack
import numpy as np
import ml_dtypes
import concourse.bass as bass
import concourse.mybir as mybir
from concourse.bass_utils import run_bass_kernel_spmd

F32 = mybir.dt.float32
BF16 = mybir.dt.bfloat16
AF = mybir.ActivationFunctionType
ALU = mybir.AluOpType

T = 2048
D = 1024
DFF = 2816
NFC = 22
EPS = 1e-6
NEG = -30000.0
NSLOT = 4
LOOKAHEAD = 2
NBLK = 21


class Buf:
    __slots__ = ("name", "w", "r", "dsem", "dcnt", "excl")

    def __init__(self, name, excl=False):
        self.name = name
        self.excl = excl
        self.w = None
        self.r = {}
        self.dsem = None
        self.dcnt = 0


class Sched:
    ENG = ["pe", "act", "dve", "pool", "sp"]
    COMPUTE = ["pe", "act", "dve", "pool"]

    def __init__(self, nc, stack):
        self.nc = nc
        self.stack = stack
        self.ops = {e: [] for e in self.ENG}
        self.cnt = {e: 0 for e in self.ENG}
        self.sem = {e: stack.enter_context(nc.semaphore("s_" + e)) for e in self.ENG}
        self.seen = {e: {} for e in self.ENG}
        self.semobj = {e: self.sem[e] for e in self.ENG}
        self.nd = 0
        self.dma_bufs = []

    def _deps(self, eng, reads, writes):
        deps = {}

        def add(d):
            if d is None:
                return
            k, v = d
            if deps.get(k, 0) < v:
                deps[k] = v
        for b in reads:
            add(b.w)
        for b in writes:
            add(b.w)
            for d in b.r.items():
                add(d)
        waits = []
        seen = self.seen[eng]
        for k, v in deps.items():
            if k == "pe" and eng == "pe":
                continue
            if seen.get(k, 0) >= v:
                continue
            seen[k] = v
            waits.append((self.semobj[k], v))
        return waits

    def op(self, eng, fn, reads=(), writes=(), track=True):
        ex = [b for b in reads if b.excl]
        if ex:
            reads = [b for b in reads if not b.excl]
            writes = list(writes) + [b for b in ex if b not in writes]
        waits = self._deps(eng, reads, writes)
        idx = self.cnt[eng] + 1
        if track:
            self.cnt[eng] = idx
        ev = (eng, idx)
        for b in reads:
            if b.r.get(eng, 0) < idx:
                b.r[eng] = idx
        for b in writes:
            b.w = ev
            b.r = {}
        self.ops[eng].append((waits, fn, track, None))

    def dma(self, q, out, in_, reads=(), writes=()):
        waits = self._deps(q, reads, writes)
        bufs = list(writes) + list(reads)
        assert len(bufs) == 1
        b = bufs[0]
        if b.dsem is None:
            self.nd += 1
            key = "d%d" % self.nd
            b.dsem = key
            self.semobj[key] = self.stack.enter_context(self.nc.semaphore(key))
            self.dma_bufs.append(b)
        b.dcnt += 16
        ev = (b.dsem, b.dcnt)
        if writes:
            b.w = ev
            b.r = {}
        else:
            b.r[b.dsem] = b.dcnt
        self.ops[q].append((waits, L("dma_start", out=out, in_=in_), False,
                            (self.semobj[b.dsem], 16)))

    def barrier(self, dma=False):
        for e in self.ENG:
            waits = []
            if dma:
                for b in self.dma_bufs:
                    if self.seen[e].get(b.dsem, 0) < b.dcnt:
                        self.seen[e][b.dsem] = b.dcnt
                        waits.append((self.semobj[b.dsem], b.dcnt))
            for k in self.COMPUTE:
                v = self.cnt[k]
                if v > 0 and self.seen[e].get(k, 0) < v:
                    self.seen[e][k] = v
                    waits.append((self.semobj[k], v))
            if waits:
                self.ops[e].append((waits, None, False, None))

    def final_wait(self, eng):
        waits = [(self.semobj[b.dsem], b.dcnt) for b in self.dma_bufs]
        for k in self.COMPUTE:
            if self.cnt[k] > 0:
                waits.append((self.semobj[k], self.cnt[k]))
        self.ops[eng].append((waits, None, False, None))

    def emit(self):
        nc = self.nc
        handles = {"pe": "tensor", "act": "scalar", "dve": "vector", "pool": "gpsimd", "sp": "sync"}
        with nc.Block() as block:
            for e in self.ENG:
                ops = self.ops[e]
                own = self.sem[e]

                def body(engh, ops=ops, own=own):
                    for waits, fn, track, dinc in ops:
                        for s, v in waits:
                            engh.wait_ge(s, v)
                        if fn is None:
                            continue
                        ins = fn(engh)
                        if dinc is not None:
                            ins.then_inc(dinc[0], dinc[1])
                        elif track:
                            ins.then_inc(own, 1)
                getattr(block, handles[e])(body)


def L(method, *args, **kw):
    return lambda e: getattr(e, method)(*args, **kw)


def _na_tile_lists():
    lists = []
    tid = {}
    for m in range(16):
        if m <= 1:
            kbs = [0, 1, 2, 3]
        elif m >= 14:
            kbs = [12, 13, 14, 15]
        else:
            kbs = [m - 2, m - 1, m, m + 1, m + 2]
        cur = []
        for kb in kbs:
            key = ("int", kb - m) if 2 <= m <= 13 else ("edge", m, kb)
            if key not in tid:
                tid[key] = len(tid)
            cur.append((kb, tid[key]))
        lists.append(cur)
    return lists, tid


def _na_index_tables():
    lists, tid = _na_tile_lists()
    ntile = len(tid)
    dr_i = np.zeros((ntile, 128, 128), np.int64)
    dc_i = np.zeros((ntile, 128, 128), np.int64)
    valid = np.zeros((ntile, 128, 128), bool)
    done = set()
    for m in range(16):
        for kb, t in lists[m]:
            if t in done:
                continue
            done.add(t)
            kk = np.arange(128)
            kr = (2 * kb + kk // 64)[:, None]
            kc = (kk % 64)[:, None]
            qr = (2 * m + kk // 64)[None, :]
            qc = (kk % 64)[None, :]
            rs = np.clip(qr - 4, 0, 24)
            ws = np.clip(qc - 8, 0, 48)
            v = (kr >= rs) & (kr < rs + 8) & (kc >= ws) & (kc < ws + 16)
            dr = np.clip(kr - qr + 7, 0, 14)
            dc = np.clip(kc - qc + 15, 0, 30)
            dr_i[t] = np.broadcast_to(dr, (128, 128))
            dc_i[t] = np.broadcast_to(dc, (128, 128))
            valid[t] = v
    return dr_i, dc_i, valid


def _na_kb_layout():
    lists, tid = _na_tile_lists()
    layout = []
    off = 0
    shared = None
    for kb in range(16):
        ms = [m for m in range(16) if kb in [k for k, _ in lists[m]]]
        tl = [dict(lists[m])[kb] for m in ms]
        if 4 <= kb <= 11:
            if shared is None:
                shared = off
                off += 128 * len(ms)
            o = shared
        else:
            o = off
            off += 128 * len(ms)
        layout.append((kb, o, ms, tl))
    return layout, off


def _const_tables():
    c = {}
    c["ident"] = np.eye(128, dtype=np.float32)
    slopes = 2.0 ** (-8.0 * np.arange(1, 5) / 4)
    i = np.arange(T)
    a = (i // 128).astype(np.float32)
    r = (i % 128).astype(np.float32)
    augq = np.zeros((4, 4, T), np.float32)
    for h in range(4):
        s = slopes[h]
        augq[h, 0] = -s * 128.0 * a
        augq[h, 1] = -s * r
        augq[h, 2] = s
        augq[h, 3] = s
    c["augq"] = augq
    augk = np.zeros((2, 4, T), np.float32)
    for v, sg in enumerate([1.0, -1.0]):
        augk[v, 0] = sg
        augk[v, 1] = sg
        augk[v, 2] = sg * 128.0 * a
        augk[v, 3] = sg * r
    c["augk"] = augk
    jj = np.arange(128)[:, None]
    ii = np.arange(128)[None, :]
    cd = np.zeros((128, 4 * 128), np.float32)
    for h in range(4):
        cd[:, h * 128:(h + 1) * 128] = -2.0 * slopes[h] * np.maximum(jj - ii, 0)
    c["cdiag"] = cd
    inv = (10000.0 ** (-np.arange(0, 32, 2, dtype=np.float32) / np.float32(32))).astype(np.float32)
    ang = (np.arange(T, dtype=np.float32)[:, None] * inv[None, :]).astype(np.float32)
    cos = np.cos(ang.astype(np.float64)).astype(np.float32).T
    sin = np.sin(ang.astype(np.float64)).astype(np.float32).T
    rt = np.zeros((128, T), np.float32)
    rt[64:80] = cos
    rt[80:96] = cos
    rt[96:112] = -sin
    rt[112:128] = sin
    c["ropetab"] = rt
    return c


def _pm(v, nch):
    return np.ascontiguousarray(np.asarray(v, np.float32).reshape(nch, 128).T)


def _prep_shared(inp):
    g = {}
    c = _const_tables()
    g.update(c)
    f = lambda k: np.asarray(inp[k], np.float32)
    gains = np.zeros((128, 48), np.float32)
    gains[:, 0:8] = _pm(f("mix_norm_e")[0], 8)
    gains[:, 8:16] = _pm(f("ffn_norm_g")[0], 8)
    gains[:, 16:24] = _pm(f("mix_norm_o")[0], 8)
    gains[:, 24:32] = _pm(f("ffn_norm_g")[1], 8)
    gains[:, 32:40] = _pm(f("final_norm_g"), 8)
    gains[:, 40:44] = _pm(f("q_norm_g")[0], 4)
    gains[:, 44:46] = _pm(f("kv_norm_g")[0], 2)
    gains[:, 46] = f("diff_subln_g")[0]
    g["gains"] = gains
    cv = np.zeros((128, 2 * NFC * 4), np.float32)
    for l in range(2):
        for k in range(3):
            cv[:, l * 88 + k * 22: l * 88 + (k + 1) * 22] = _pm(f("ffn_conv_w")[l, k], NFC)
        cv[:, l * 88 + 66: l * 88 + 88] = _pm(f("ffn_conv_b")[l], NFC)
    g["convp"] = cv
    lam = np.zeros((128, 256), np.float32)
    for k, nm in enumerate(["diff_lq1", "diff_lk1", "diff_lq2", "diff_lk2"]):
        lam[:, k * 64:(k + 1) * 64] = np.broadcast_to(f(nm)[0][None, :], (128, 64))
    g["lamv"] = lam
    dr_i, dc_i, valid = _na_index_tables()
    rpb = f("na_rpb")[0]
    nb = np.empty((8, dr_i.shape[0], 128, 128), np.float32)
    for h in range(8):
        nb[h] = np.where(valid, rpb[h][dr_i, dc_i], np.float32(NEG))
    layout, ncols = _na_kb_layout()
    nb2 = np.empty((8, 128, ncols), np.float32)
    for kb, o, ms, tl in layout:
        for i, t in enumerate(tl):
            nb2[:, :, o + i * 128:o + (i + 1) * 128] = nb[:, t]
    g["nbias"] = nb2
    g["w_in"] = f("w_in_e")[0]
    g["w_out"] = f("w_out_e")[0]
    g["w_gate"] = f("w_ffn_gate")
    g["w_val"] = f("w_ffn_val")
    g["w_down"] = f("w_ffn_down")
    g["w_dq"] = f("w_dq")[0]
    wuq = f("w_uq")[0]
    cols = []
    for h in range(16):
        b = h * 96
        cols += list(range(b, b + 96)) + list(range(b + 80, b + 96)) + list(range(b + 64, b + 80))
    g["w_uq"] = np.ascontiguousarray(wuq[:, cols])
    wdkv = f("w_dkv")[0]
    wd = np.zeros((1024, 384), np.float32)
    wd[:, 0:256] = wdkv[:, 0:256]
    wd[:, 320:352] = wdkv[:, 256:288]
    wd[:, 352:368] = wdkv[:, 272:288]
    wd[:, 368:384] = wdkv[:, 256:272]
    g["w_dkv"] = wd
    wukv = f("w_ukv")[0].reshape(256, 16, 128)
    g["w_uk"] = np.ascontiguousarray(wukv[:, :, 0:64].reshape(256, 1024))
    g["w_uv"] = np.ascontiguousarray(wukv[:, :, 64:128].reshape(256, 1024))
    g["w_o"] = f("w_o_mla")[0]
    return g


BF16_IN = ()
SHAPES = {
    "x": [T, D], "ident": [128, 128], "augq": [4, 4, T], "augk": [2, 4, T], "cdiag": [128, 512],
    "ropetab": [128, T], "gains": [128, 48], "convp": [128, 176], "lamv": [128, 256],
    "nbias": [8, 128, 5248],
}


def _desc_size(d):
    if d[0] == "gv":
        return 2048
    return d[2] * d[4]


def _pack_weights(g, plan):
    pack = np.zeros((len(plan), 128, 2048), np.float32)
    for i, d in enumerate(plan):
        if d[0] == "gv":
            _, layer, c = d
            wg = g["w_gate"][layer][:, c * 128:(c + 1) * 128].reshape(8, 128, 128)
            wv = g["w_val"][layer][:, c * 128:(c + 1) * 128].reshape(8, 128, 128)
            t = np.concatenate([wg, wv], axis=2)
            pack[i] = t.transpose(1, 0, 2).reshape(128, 2048)
        else:
            _, name, kc, c0, ncols, lidx, r0 = d
            w = g[name] if lidx is None else g[name][lidx]
            t = w[r0:r0 + kc * 128, c0:c0 + ncols].reshape(kc, 128, ncols)
            pack[i, :, 0:kc * ncols] = t.transpose(1, 0, 2).reshape(128, kc * ncols)
    return pack


def build(plan=None, nstage=99):
    nc = bass.Bass("TRN2", target_bir_lowering=False)
    dram = {k: nc.dram_tensor(k, shp, BF16 if k in BF16_IN else F32, kind="ExternalInput").ap() for k, shp in SHAPES.items()}
    if plan is not None:
        dram["wpack"] = nc.dram_tensor("wpack", [len(plan), 128, 2048], F32, kind="ExternalInput").ap()
    y_out = nc.dram_tensor("y", [T, D], F32, kind="ExternalOutput").ap()
    requests = []
    with ExitStack() as st:
        S = Sched(nc, st)
        sb = lambda n, shp, dt: st.enter_context(nc.sbuf_tensor("sb_" + n, shp, dt))
        xT = sb("xT", [128, 8, T], F32)
        hT = sb("hT", [128, 8, T], BF16)
        wring = sb("wring", [128, NSLOT, 2048], BF16)
        AR = sb("arena", [128, NBLK * 2048], BF16)
        ident = sb("ident", [128, 128], F32)
        identb = sb("identb", [128, 128], BF16)
        onesb = sb("onesb", [128, 128], BF16)
        zerob = sb("zerob", [128, 128], BF16)
        gains = sb("gains", [128, 48], F32)
        convp = sb("convp", [128, 176], F32)
        small = sb("small", [128, 16], F32)
        cdiag = sb("cdiag", [128, 512], BF16)
        rs_a = sb("rs_a", [128, 512], F32)
        rs_b = sb("rs_b", [128, 512], F32)
        sqr = sb("sqr", [128, 4, 512], BF16)
        PS = st.enter_context(nc.psum_tensor("PS", [128, 8 * 512], F32))

        XT = [[Buf("xT%d_%d" % (c, g)) for g in range(4)] for c in range(8)]
        HT = [[Buf("hT%d_%d" % (c, g)) for g in range(4)] for c in range(8)]
        WS = [Buf("ws%d" % i) for i in range(NSLOT)]
        PB = [Buf("pb%d" % i, excl=True) for i in range(8)]
        B_const = Buf("const")
        B_small = Buf("small")
        B_rsa, B_rsb = Buf("rsa"), Buf("rsb")
        SQ = [Buf("sq%d" % i) for i in range(4)]

        def bank(i):
            return PS[:, i * 512:(i + 1) * 512]

        def blk(k, n=1):
            return AR[:, k * 2048:(k + n) * 2048]

        def blkf(k, n=2):
            return AR[:, k * 2048:(k + n) * 2048].bitcast(F32)

        state = {"bank": 0, "wi": 0, "wissued": 0, "sq": 0, "sb": 0, "tick": 0}
        pending = []
        lamv = blkf(0, 1)[:, 0:256]

        def nb_():
            b = state["bank"]
            state["bank"] = (b + 1) % 8
            return b

        def tgs(g):
            return slice(g * 512, (g + 1) * 512)

        def wtile(desc):
            i = state["wi"]
            state["wi"] = i + 1
            requests.append(desc)
            if plan is not None:
                hi = min(len(plan), i + 1 + LOOKAHEAD)
                while state["wissued"] < hi:
                    j = state["wissued"]
                    slot = j % NSLOT
                    X = _desc_size(plan[j])
                    S.dma("pool", wring[:, slot, 0:X], dram["wpack"][j][:, 0:X], writes=[WS[slot]])
                    state["wissued"] = j + 1
            return wring[:, i % NSLOT, :], WS[i % NSLOT]

        def wreq_cols(name, kc, c0, ncols, lidx=None, r0=0):
            return ("cols", name, kc, c0, ncols, lidx, r0)

        def wview(slot, kc, ncols):
            return slot[:, 0:kc * ncols].rearrange("p (k n) -> p k n", k=kc)

        S.dma("sp", ident[:], dram["ident"], writes=[B_const])
        gb_l = Buf("gains_l")
        S.dma("sp", gains[:], dram["gains"], writes=[gb_l])
        cb1, cb2, cb3 = Buf("c1"), Buf("c2"), Buf("c3")
        S.dma("sp", convp[:], dram["convp"], writes=[cb1])
        S.dma("sp", lamv, dram["lamv"], writes=[cb2])
        S.dma("pool", cdiag[:], dram["cdiag"], writes=[cb3])
        S.op("dve", L("tensor_copy", identb[:], ident[:]), reads=[B_const], writes=[B_const])
        S.op("dve", L("memset", onesb[:], 1.0), writes=[B_const])
        S.op("dve", L("memset", zerob[:], 0.0), writes=[B_const])
        S.op("dve", L("memset", small[:, 0:1], EPS), writes=[B_small])
        lam_init = 0.8 - 0.6 * math.exp(-0.3 * 0)
        S.op("dve", L("tensor_tensor", lamv[:, 0:64], lamv[:, 0:64], lamv[:, 64:128], ALU.mult),
             reads=[cb2], writes=[cb2])
        S.op("dve", L("tensor_tensor", lamv[:, 128:192], lamv[:, 128:192], lamv[:, 192:256], ALU.mult),
             reads=[cb2], writes=[cb2])
        S.op("dve", L("reduce_sum", small[:, 4:5], lamv[:, 0:64], mybir.AxisListType.X),
             reads=[cb2], writes=[B_small])
        S.op("dve", L("reduce_sum", small[:, 5:6], lamv[:, 128:192], mybir.AxisListType.X),
             reads=[cb2], writes=[B_small])
        S.op("act", L("activation", small[:, 6:8], small[:, 4:6], AF.Exp), reads=[B_small], writes=[B_small])
        S.op("dve", L("scalar_tensor_tensor", small[:, 1:2], small[:, 7:8], -lam_init, small[:, 6:7],
                                                     ALU.add, ALU.subtract), reads=[B_small], writes=[B_small])
        S.op("dve", L("tensor_scalar", small[:, 2:3], gains[:, 46:47], 1.0 - lam_init, None, ALU.mult),
             reads=[B_small, gb_l], writes=[B_small])
        S.barrier(dma=True)

        def mm(out, lhsT, rhs, start, stop, reads, writes, track=None, sgc=False):
            if track is None:
                track = stop
            if sgc:
                S.op("pe", L("matmul", out, lhsT, rhs, start=start, stop=stop, skip_group_check=True),
                     reads=reads, writes=writes, track=track)
            else:
                S.op("pe", L("matmul", out, lhsT, rhs, start=start, stop=stop), reads=reads, writes=writes,
                     track=track)

        def rstd_from_psum(pb, nfeat, dst, dstbuf):
            S.op("act", L("activation", rs_a[:], bank(pb), AF.Sqrt, scale=1.0 / nfeat, bias=small[:, 0:1]),
                 reads=[PB[pb], B_small], writes=[B_rsa])
            S.op("dve", L("reciprocal", dst, rs_a[:]), reads=[B_rsa], writes=[dstbuf])

        def norm_stage(goff):
            for g in range(4):
                pb = nb_()
                for c in range(8):
                    q = state["sq"]
                    state["sq"] = (q + 1) % 4
                    S.op("act", L("activation", sqr[:, q, :], xT[:, c, tgs(g)], AF.Square),
                         reads=[XT[c][g]], writes=[SQ[q]])
                    mm(bank(pb), onesb[:], sqr[:, q, :], c == 0, c == 7, [SQ[q], B_const], [PB[pb]], track=True)
                rstd_from_psum(pb, 1024.0, rs_b[:], B_rsb)
                for c in range(8):
                    S.op("dve", L("scalar_tensor_tensor", hT[:, c, tgs(g)], xT[:, c, tgs(g)], gains[:, goff + c:goff + c + 1], rs_b[:],
                        ALU.mult, ALU.mult), reads=[XT[c][g], B_rsb], writes=[HT[c][g]])

        def proj_fm(wv, wbuf, col0, rhs_fn, rhs_bufs_fn, nk, g, M=128):
            pb = nb_()
            for k in range(nk):
                mm(bank(pb)[0:M, :], wv[:, k, col0:col0 + M], rhs_fn(k, g), k == 0, k == nk - 1,
                   [wbuf] + rhs_bufs_fn(k, g), [PB[pb]])
            return pb

        hT_rhs = lambda k, g: hT[:, k, tgs(g)]
        hT_bufs = lambda k, g: [HT[k][g]]

        def resid_add(pb, n, g, eng="dve"):
            S.op(eng, L("tensor_tensor", xT[:, n, tgs(g)], bank(pb), xT[:, n, tgs(g)], ALU.add),
                 reads=[PB[pb], XT[n][g]], writes=[XT[n][g]])

        def out_proj(name, mix_fn, mix_bufs, krow0, nkc):
            for nt in range(4):
                slot, wb = wtile(wreq_cols(name, nkc, nt * 256, 256, r0=krow0))
                wv = wview(slot, nkc, 256)
                for nn in range(2):
                    n = nt * 2 + nn
                    for g in range(4):
                        pb = proj_fm(wv, wb, nn * 128, mix_fn, mix_bufs, nkc, g)
                        resid_add(pb, n, g)

        XS = [Buf("xs%d" % i) for i in range(16)]
        for tt in range(16):
            S.dma("sp", blkf(tt, 1), dram["x"][tt * 128:(tt + 1) * 128, :], writes=[XS[tt]])
        for g in range(4):
            for c in range(8):
                pb = nb_()
                for j in range(4):
                    tt = g * 4 + j
                    S.op("pe", L("transpose", bank(pb)[:, j * 128:(j + 1) * 128], blkf(tt, 1)[:, c * 128:(c + 1) * 128], ident[:]),
                        reads=[XS[tt], B_const], writes=[PB[pb]], track=(j == 3))
                eng = "act" if c % 2 == 0 else "dve"
                if eng == "act":
                    S.op("act", L("activation", xT[:, c, tgs(g)], bank(pb), AF.Copy),
                         reads=[PB[pb]], writes=[XT[c][g]])
                else:
                    S.op("dve", L("tensor_copy", xT[:, c, tgs(g)], bank(pb)),
                         reads=[PB[pb]], writes=[XT[c][g]])
        S.barrier()

        def ffn(layer, goff):
            norm_stage(goff)
            U = [Buf("u%d" % i) for i in range(11)]
            TB = [Buf("tb%d" % i) for i in range(2)]
            cp = layer * 88
            for rnd in range(2):
                for cc in range(11):
                    c = rnd * 11 + cc

                    slot, wb = wtile(("gv", layer, c))
                    wv = wview(slot, 8, 256)
                    for g in range(4):
                        for k in range(8):
                            mm(bank(g), wv[:, k, 0:128], hT[:, k, tgs(g)], k == 0, k == 7, [wb, HT[k][g]], [PB[g]])
                    for g in range(4):
                        for k in range(8):
                            mm(bank(4 + g), wv[:, k, 128:256], hT[:, k, tgs(g)], k == 0, k == 7,
                               [wb, HT[k][g]], [PB[4 + g]])
                    tb = cc % 2
                    tv = blkf(11 + 2 * tb)
                    G = PS[:, 0:2048]
                    gb = [PB[0], PB[1], PB[2], PB[3]]
                    w0 = convp[:, cp + c:cp + c + 1]
                    w1 = convp[:, cp + 22 + c:cp + 22 + c + 1]
                    w2 = convp[:, cp + 44 + c:cp + 44 + c + 1]
                    bb = convp[:, cp + 66 + c:cp + 66 + c + 1]
                    S.op("act", L("activation", tv, G, AF.Identity, scale=w1, bias=bb),
                         reads=gb + [cb1], writes=[TB[tb]])
                    S.op("dve", L("scalar_tensor_tensor", tv[:, 1:T], PS[:, 0:T - 1], w0, tv[:, 1:T], ALU.mult, ALU.add),
                        reads=gb + [TB[tb]], writes=[TB[tb]])
                    S.op("dve", L("scalar_tensor_tensor", tv[:, 0:T - 1], PS[:, 1:T], w2, tv[:, 0:T - 1], ALU.mult, ALU.add),
                        reads=gb + [TB[tb]], writes=[TB[tb]])
                    S.op("act", L("activation", tv, tv, AF.Gelu), reads=[TB[tb]], writes=[TB[tb]])
                    for hh in range(2):
                        S.op("dve", L("tensor_tensor", blk(cc)[:, hh * 1024:(hh + 1) * 1024], tv[:, hh * 1024:(hh + 1) * 1024],
                            PS[:, 2048 + hh * 1024:2048 + (hh + 1) * 1024], ALU.mult),
                            reads=[TB[tb], PB[4 + 2 * hh], PB[5 + 2 * hh]], writes=[U[cc]])
                for n in range(8):
                    slot, wb = wtile(wreq_cols("w_down", 11, n * 128, 128, lidx=layer, r0=rnd * 1408))
                    wv = wview(slot, 11, 128)
                    for g in range(4):
                        pb = proj_fm(wv, wb, 0, lambda k, g: blk(k)[:, tgs(g)], lambda k, g: [U[k]], 11, g)
                        resid_add(pb, n, g)
            S.barrier()

        def softmax_epilogue_diff(h, I, pO, pD, MIX):
            r1, r2, t2 = blkf(13, 2)[:, 0:512], blkf(13, 2)[:, 512:1024], blkf(13, 2)[:, 1024:1536]
            t1 = blkf(15, 2)[:, I * 512:(I + 1) * 512]
            EB = state.setdefault("EB", [Buf("eb%d" % i) for i in range(3)])
            A1 = state.setdefault("A1", [Buf("a1_%d" % i) for i in range(4)])
            S.op("dve", L("tensor_copy", r1, bank(pD[0])), reads=[PB[pD[0]]], writes=[EB[0]])
            S.op("dve", L("tensor_copy", r2, bank(pD[1])), reads=[PB[pD[1]]], writes=[EB[1]])
            S.op("dve", L("tensor_copy", t1, bank(pO[0])), reads=[PB[pO[0]]], writes=[A1[I]])
            S.op("dve", L("tensor_copy", t2, bank(pO[1])), reads=[PB[pO[1]]], writes=[EB[2]])
            S.op("dve", L("reciprocal", r1, r1), reads=[EB[0]], writes=[EB[0]])
            S.op("dve", L("reciprocal", r2, r2), reads=[EB[1]], writes=[EB[1]])
            S.op("dve", L("tensor_tensor", t1, t1, r1, ALU.mult), reads=[EB[0], A1[I]], writes=[A1[I]])
            S.op("dve", L("tensor_tensor", t2, t2, r2, ALU.mult), reads=[EB[1], EB[2]], writes=[EB[2]])
            S.op("dve", L("scalar_tensor_tensor", t1, t2, small[:, 1:2], t1, ALU.mult, ALU.add),
                 reads=[A1[I], EB[2], B_small], writes=[A1[I]])

        def subln_head(h, MIX):
            A1 = state["A1"]
            RS = state.setdefault("RS", [Buf("rs_%d" % i) for i in range(4)])
            pbs = []
            for I in range(4):
                t1 = blkf(15, 2)[:, I * 512:(I + 1) * 512]
                S.op("act", L("activation", sqr[:, I, :], t1, AF.Square), reads=[A1[I]], writes=[SQ[I]])
                pb = 4 + state["sb"] % 4
                state["sb"] += 1
                mm(bank(pb), onesb[:], sqr[:, I, :], True, True, [SQ[I], B_const], [PB[pb]])
                pbs.append(pb)
            for I in range(4):
                rsd = blkf(17, 2)[:, I * 512:(I + 1) * 512]
                S.op("act", L("activation", rsd, bank(pbs[I]), AF.Sqrt, scale=1.0 / 128.0, bias=small[:, 0:1]),
                     reads=[PB[pbs[I]], B_small], writes=[RS[I]])
            for I in range(4):
                t1 = blkf(15, 2)[:, I * 512:(I + 1) * 512]
                rsd = blkf(17, 2)[:, I * 512:(I + 1) * 512]
                S.op("dve", L("reciprocal", rsd, rsd), reads=[RS[I]], writes=[RS[I]])
                S.op("dve", L("scalar_tensor_tensor", blk(h)[:, tgs(I)], t1, small[:, 2:3], rsd, ALU.mult, ALU.mult),
                     reads=[A1[I], RS[I], B_small], writes=[MIX[h]])

        def diff_phase():
            MIX = [Buf("mix%d" % i) for i in range(4)]
            QB = [Buf("dq%d" % i) for i in range(2)]
            KB = [[Buf("dk%d%d" % (i, v)) for v in range(2)] for i in range(2)]
            VB = Buf("dv")
            PT = [Buf("pt%d" % i) for i in range(8)]
            Qt = [blk(4), blk(5)]
            Kt = [[blk(6), blk(7)], [blk(8), blk(9)]]
            Vt = blk(10).rearrange("p (t e) -> p t e", t=16)
            ptv = lambda i: blk(11, 2)[:, i * 512:(i + 1) * 512]
            pti = [0]
            for n in range(2):
                for v in range(2):
                    S.dma("pool", Kt[n][v][64:68, :], dram["augk"][v], writes=[KB[n][v]])
            for h in range(int(os.environ.get("KH0", "0")), int(os.environ.get("KH1", "4")) if KDBG in (0, 5) else 1):
                if KDBG == 1:
                    break
                if os.environ.get("KBAR", "0") == "1":
                    S.barrier()
                slot, wb = wtile(wreq_cols("w_in", 8, h * 128, 128))
                wq = wview(slot, 8, 128)
                for n in range(2):
                    S.dma("pool", Qt[n][64:68, :], dram["augq"][h], writes=[QB[n]])
                for g in range(4):
                    pb = proj_fm(wq, wb, 0, hT_rhs, hT_bufs, 8, g)
                    S.op("dve", L("tensor_scalar", Qt[0][0:64, tgs(g)], bank(pb)[0:64, :], 0.125, None, ALU.mult),
                         reads=[PB[pb]], writes=[QB[0]])
                    S.op("dve", L("tensor_scalar", Qt[1][0:64, tgs(g)], bank(pb)[64:128, :], 0.125, None, ALU.mult),
                         reads=[PB[pb]], writes=[QB[1]])
                slot, wb = wtile(wreq_cols("w_in", 8, 512 + h * 128, 128))
                wk = wview(slot, 8, 128)
                for g in range(4):
                    pb = proj_fm(wk, wb, 0, hT_rhs, hT_bufs, 8, g)
                    for n in range(2):
                        S.op("act", L("activation", Kt[n][0][0:64, tgs(g)], bank(pb)[64 * n:64 * n + 64, :], AF.Copy),
                             reads=[PB[pb]], writes=[KB[n][0]])
                        S.op("dve", L("tensor_copy", Kt[n][1][0:64, tgs(g)], bank(pb)[64 * n:64 * n + 64, :]),
                             reads=[PB[pb]], writes=[KB[n][1]])
                slot, wb = wtile(wreq_cols("w_in", 8, 1024 + h * 128, 128))
                wvv = wview(slot, 8, 128)
                for g in range(4):
                    pb = nb_()
                    for j in range(4):
                        tt = g * 4 + j
                        for k in range(8):
                            mm(bank(pb)[:, j * 128:(j + 1) * 128], hT[:, k, tt * 128:(tt + 1) * 128], wvv[:, k, :],
                               k == 0, k == 7, [wb, HT[k][g]], [PB[pb]], track=(k == 7 and j == 3))
                    S.op("act", L("activation", Vt[:, g * 4:(g + 1) * 4, :], bank(pb).rearrange("p (t e) -> p t e", t=4), AF.Copy),
                        reads=[PB[pb]], writes=[VB])
                pO, pD = [0, 1], [2, 3]
                tiles = [(I, J, n) for I in range(4) for J in range(16) for n in range(2)]
                info = {}

                def stage1(t):
                    I, J, n = t
                    ps = 4 + state["sb"] % 4
                    state["sb"] += 1
                    if J < 4 * I or J >= 4 * I + 4:
                        v = 0 if J < 4 * I else 1
                        mm(bank(ps), Kt[n][v][0:68, J * 128:(J + 1) * 128], Qt[n][0:68, tgs(I)], True, True,
                           [KB[n][v], QB[n]], [PB[ps]])
                    else:
                        for b in range(4):
                            gb_ = 4 * I + b
                            v = 0 if gb_ >= J else 1
                            qs = slice(gb_ * 128, (gb_ + 1) * 128)
                            osl = bank(ps)[:, b * 128:(b + 1) * 128]
                            diag = gb_ == J
                            mm(osl, Kt[n][v][0:68, J * 128:(J + 1) * 128], Qt[n][0:68, qs], True, not diag,
                               [KB[n][v], QB[n]], [PB[ps]], track=(b == 3 and not diag))
                            if diag:
                                mm(osl, identb[:], cdiag[:, h * 128:(h + 1) * 128], False, True,
                                   [B_const, cb3], [PB[ps]], track=(b == 3))
                    pi = pti[0]
                    pti[0] = (pi + 1) % 8
                    S.op("act", L("activation", ptv(pi), bank(ps), AF.Exp), reads=[PB[ps]], writes=[PT[pi]])
                    info[t] = pi

                def stage2(t):
                    I, J, n = t
                    pi = info.pop(t)
                    mm(bank(pO[n]), Vt[:, J, :], ptv(pi), J == 0, J == 15, [VB, PT[pi]], [PB[pO[n]]], track=True)
                    mm(bank(pD[n]), onesb[:], ptv(pi), J == 0, J == 15, [B_const, PT[pi]], [PB[pD[n]]], track=True)
                    if J == 15 and n == 1:
                        softmax_epilogue_diff(h, I, pO, pD, MIX)
                        if I == 3:
                            pending.append((state["tick"] + 24, lambda h=h: subln_head(h, MIX)))

                DEPTH = 2
                for i in range(len(tiles) + DEPTH):
                    state["tick"] += 1
                    while pending and pending[0][0] <= state["tick"]:
                        pending.pop(0)[1]()
                    if i < len(tiles):
                        stage1(tiles[i])
                    if i >= DEPTH:
                        stage2(tiles[i - DEPTH])
            while pending:
                pending.pop(0)[1]()
            if KDBG == 0:
                out_proj("w_out", lambda k, g: blk(k)[:, tgs(g)], lambda k, g: [MIX[k]], 0, 4)
            S.barrier()

        def na_phase():
            lists, _ = _na_tile_lists()
            MIX = [Buf("nmix%d" % i) for i in range(4)]
            NQb, NKb = Buf("nq"), Buf("nk")
            VAb = [Buf("va0"), Buf("va1")]
            NBb = Buf("nbias")
            PT = [Buf("npt%d" % i) for i in range(6)]
            RD = [Buf("nrd%d" % i) for i in range(2)]
            NQ, NK = blk(4), blk(5)
            VA = [blk(6).rearrange("p (t e) -> p t e", t=16), blk(7).rearrange("p (t e) -> p t e", t=16)]
            NB2 = blk(15, 6)
            layout, _nc = _na_kb_layout()
            ptv = lambda i: blk(8, 3)[:, i * 768:(i + 1) * 768]
            rdv = lambda i: blkf(13, 2)[:, i * 512:(i + 1) * 512]
            for a in range(2):
                S.op("dve", L("memset", VA[a][:, :, 64:128], 1.0), writes=[VAb[a]])
            pti = [0]
            for j in range(4):
                slot, wb = wtile(wreq_cols("w_in", 8, 1536 + j * 128, 128))
                wq = wview(slot, 8, 128)
                for g in range(4):
                    pb = proj_fm(wq, wb, 0, hT_rhs, hT_bufs, 8, g)
                    S.op("dve", L("tensor_scalar", NQ[:, tgs(g)], bank(pb), 0.125, None, ALU.mult),
                         reads=[PB[pb]], writes=[NQb])
                slot, wb = wtile(wreq_cols("w_in", 8, 2048 + j * 128, 128))
                wk = wview(slot, 8, 128)
                for g in range(4):
                    pb = proj_fm(wk, wb, 0, hT_rhs, hT_bufs, 8, g)
                    S.op("dve", L("tensor_copy", NK[:, tgs(g)], bank(pb)), reads=[PB[pb]], writes=[NKb])
                slot, wb = wtile(wreq_cols("w_in", 8, 2560 + j * 128, 128))
                wvv = wview(slot, 8, 128)
                for g in range(4):
                    pb = nb_()
                    for jj in range(4):
                        tt = g * 4 + jj
                        for k in range(8):
                            mm(bank(pb)[:, jj * 128:(jj + 1) * 128], hT[:, k, tt * 128:(tt + 1) * 128], wvv[:, k, :],
                               k == 0, k == 7, [wb, HT[k][g]], [PB[pb]], track=(k == 7 and jj == 3))
                    pv = bank(pb).rearrange("p (t e) -> p t e", t=4)
                    S.op("act", L("activation", VA[0][:, g * 4:(g + 1) * 4, 0:64], pv[:, :, 0:64], AF.Copy),
                         reads=[PB[pb]], writes=[VAb[0]])
                    S.op("dve", L("tensor_copy", VA[1][:, g * 4:(g + 1) * 4, 0:64], pv[:, :, 64:128]),
                         reads=[PB[pb]], writes=[VAb[1]])
                for a in range(2):
                    S.dma("pool", NB2[:, a * 5248:(a + 1) * 5248], dram["nbias"][2 * j + a], writes=[NBb])
                for a in range(2):
                    p0 = 64 * a
                    nbase = a * 5248
                    for b4 in range(4):
                        mm(bank(b4), zerob[:], NK[:, b4 * 512:(b4 + 1) * 512], True, True, [B_const, NKb], [PB[b4]], sgc=True)
                    info = {}

                    def stage1(kbi, a=a, p0=p0, nbase=nbase):
                        kb, o, ms, tl = layout[kbi]
                        W = 128 * len(ms)
                        q0 = 128 * ms[0]
                        b0 = 4 + 2 * (state["sb"] % 2)
                        state["sb"] += 1
                        pieces = [(0, min(W, 512))] + ([(512, W)] if W > 512 else [])
                        for (c0, c1) in pieces:
                            bb = b0 + c0 // 512
                            osl = PS[:, b0 * 512 + c0:b0 * 512 + c1]
                            mm(osl, NK[p0:p0 + 64, kb * 128:(kb + 1) * 128], NQ[p0:p0 + 64, q0 + c0:q0 + c1],
                               True, False, [NKb, NQb], [PB[bb]], track=False)
                            mm(osl, identb[:], NB2[:, nbase + o + c0:nbase + o + c1], False, True,
                               [B_const, NBb], [PB[bb]], track=True)
                        pi = pti[0]
                        pti[0] = (pi + 1) % 3
                        rb = [PB[b0]] + ([PB[b0 + 1]] if W > 512 else [])
                        S.op("act", L("activation", ptv(pi)[:, 0:W], PS[:, b0 * 512:b0 * 512 + W], AF.Exp),
                             reads=rb, writes=[PT[pi]])
                        info[kbi] = pi

                    def stage2(kbi, a=a, p0=p0, j=j):
                        kb, o, ms, tl = layout[kbi]
                        W = 128 * len(ms)
                        q0 = 128 * ms[0]
                        pi = info.pop(kbi)
                        g0 = q0
                        while g0 < q0 + W:
                            g1 = min(q0 + W, (g0 // 512 + 1) * 512)
                            mm(PS[:, g0:g1], VA[a][:, kb, :], ptv(pi)[:, g0 - q0:g1 - q0], False, True,
                               [VAb[a], PT[pi]], [PB[g0 // 512]], track=True, sgc=True)
                            g0 = g1
                        if kbi == 15:
                            ev = blkf(13, 2)
                            S.op("dve", L("tensor_copy", ev, PS[:, 0:2048]), reads=[PB[0], PB[1], PB[2], PB[3]], writes=[RD[0]])
                            rd = blkf(11, 2)
                            for g in range(4):
                                S.op("dve", L("reciprocal", rd[0:64, tgs(g)], ev[64:128, tgs(g)]), reads=[RD[0]], writes=[RD[1]])
                            for g in range(4):
                                S.op("dve", L("tensor_tensor", blk(j)[p0:p0 + 64, tgs(g)], ev[0:64, tgs(g)], rd[0:64, tgs(g)], ALU.mult),
                                     reads=[RD[0], RD[1]], writes=[MIX[j]])

                    DEPTH = 1
                    for i in range(16 + DEPTH):
                        if i < 16:
                            stage1(i)
                        if i >= DEPTH:
                            stage2(i - DEPTH)
            out_proj("w_out", lambda k, g: blk(k)[:, tgs(g)], lambda k, g: [MIX[k]], 512, 4)
            S.barrier()

        def mla_phase():
            CQ = [Buf("cq%d" % i) for i in range(4)]
            CKV = [Buf("ckv%d" % i) for i in range(2)]
            KRb = Buf("kr")
            RTb = Buf("ropetab")
            QHb = [Buf("qh0"), Buf("qh1")]
            KHb = [Buf("kh0"), Buf("kh1")]
            VAb = [Buf("mva0"), Buf("mva1")]
            MIX = [Buf("mmix%d" % i) for i in range(4)]
            PT = [Buf("mpt%d" % i) for i in range(8)]
            TMP = [Buf("mtmp%d" % i) for i in range(2)]
            cqv = lambda c: blk(4 + c)
            ckvv = lambda c: blk(8 + c)
            KR = blk(10)
            rt = blkf(11, 2)
            QH = [blk(13), blk(14)]
            KH = [blk(15), blk(16)]
            VA = [blk(17).rearrange("p (t e) -> p t e", t=16), blk(18).rearrange("p (t e) -> p t e", t=16)]
            ptv = lambda i: blk(10 if i < 4 else 19)[:, (i % 4) * 512:(i % 4 + 1) * 512]
            tmpv = lambda i: blkf(20, 1)[:, i * 512:(i + 1) * 512]
            scale = 96.0 ** -0.5
            S.dma("sp", rt, dram["ropetab"], writes=[RTb])
            for a in range(2):
                S.op("dve", L("memset", VA[a][:, :, 64:128], 1.0), writes=[VAb[a]])

            def rope(dst, dstbuf, pb, g):
                S.op("dve", L("tensor_tensor", tmpv(0)[64:96, :], bank(pb)[64:96, :], rt[64:96, tgs(g)], ALU.mult),
                     reads=[PB[pb], RTb], writes=[TMP[0]])
                S.op("dve", L("tensor_tensor", tmpv(1)[64:96, :], bank(pb)[96:128, :], rt[96:128, tgs(g)], ALU.mult),
                     reads=[PB[pb], RTb], writes=[TMP[1]])
                S.op("dve", L("tensor_tensor", dst[64:96, tgs(g)], tmpv(0)[64:96, :], tmpv(1)[64:96, :], ALU.add),
                     reads=[TMP[0], TMP[1]], writes=[dstbuf])

            def latent(wtiles, nch, goff, dstv, dstb, nfeat, extra=None):
                for g in range(4):
                    pbs = []
                    for (wv, wb, ncol) in wtiles:
                        for cc in range(ncol // 128):
                            pbs.append(proj_fm(wv, wb, cc * 128, hT_rhs, hT_bufs, 8, g))
                    pss = nb_()
                    while pss in pbs:
                        pss = nb_()
                    for c in range(nch):
                        q = state["sq"]
                        state["sq"] = (q + 1) % 4
                        S.op("act", L("activation", sqr[:, q, :], bank(pbs[c]), AF.Square),
                             reads=[PB[pbs[c]]], writes=[SQ[q]])
                        mm(bank(pss), onesb[:], sqr[:, q, :], c == 0, c == nch - 1, [SQ[q], B_const], [PB[pss]], track=True)
                    rstd_from_psum(pss, nfeat, rs_b[:], B_rsb)
                    for c in range(nch):
                        S.op("dve", L("scalar_tensor_tensor", dstv(c)[:, tgs(g)], bank(pbs[c]), gains[:, goff + c:goff + c + 1], rs_b[:], ALU.mult, ALU.mult),
                            reads=[PB[pbs[c]], B_rsb], writes=[dstb[c]])
                    if extra is not None:
                        extra(pbs[nch], g)

            s0, b0 = wtile(wreq_cols("w_dq", 8, 0, 256))
            s1, b1 = wtile(wreq_cols("w_dq", 8, 256, 256))
            latent([(wview(s0, 8, 256), b0, 256), (wview(s1, 8, 256), b1, 256)], 4, 40, cqv, CQ, 512.0)
            s0, b0 = wtile(wreq_cols("w_dkv", 8, 0, 256))
            s1, b1 = wtile(wreq_cols("w_dkv", 8, 256, 128))
            latent([(wview(s0, 8, 256), b0, 256), (wview(s1, 8, 128), b1, 128)], 2, 44, ckvv, CKV, 256.0,
                   extra=lambda pb, g: rope(KH[0], KHb[0], pb, g))
            S.op("dve", L("tensor_copy", KH[1][64:96, :], KH[0][64:96, :]), reads=[KHb[0]], writes=[KHb[1]])

            def sbank():
                ps = 4 + state["sb"] % 4
                state["sb"] += 1
                return ps

            def ibank():
                state["ib"] = state.get("ib", 0) + 1
                return 2 + state["ib"] % 2

            def head_items(hd):
                s_ = hd % 2
                st = {}
                items = []

                def q_item(g):
                    if g == 0:
                        sl, st["bq"] = wtile(wreq_cols("w_uq", 4, hd * 128, 128))
                        st["wq"] = wview(sl, 4, 128)
                    pb = ibank()
                    for k in range(4):
                        mm(bank(pb), st["wq"][:, k, :], cqv(k)[:, tgs(g)], k == 0, k == 3, [st["bq"], CQ[k]], [PB[pb]])
                    S.op("act", L("activation", QH[s_][0:64, tgs(g)], bank(pb)[0:64, :], AF.Copy),
                         reads=[PB[pb]], writes=[QHb[s_]])
                    rope(QH[s_], QHb[s_], pb, g)

                def k_item(g):
                    if g == 0:
                        sl, st["bk"] = wtile(wreq_cols("w_uk", 2, hd * 64, 64))
                        st["wk"] = wview(sl, 2, 64)
                    pb = ibank()
                    for k in range(2):
                        mm(bank(pb)[0:64, :], st["wk"][:, k, :], ckvv(k)[:, tgs(g)], k == 0, k == 1, [st["bk"], CKV[k]], [PB[pb]])
                    S.op("dve", L("tensor_copy", KH[s_][0:64, tgs(g)], bank(pb)[0:64, :]), reads=[PB[pb]], writes=[KHb[s_]])

                def v_item(g):
                    if g == 0:
                        sl, st["bv"] = wtile(wreq_cols("w_uv", 2, hd * 64, 64))
                        st["wv"] = wview(sl, 2, 64)
                    pb = ibank()
                    for jj in range(4):
                        tt = g * 4 + jj
                        for k in range(2):
                            mm(bank(pb)[:, jj * 64:(jj + 1) * 64], ckvv(k)[:, tt * 128:(tt + 1) * 128],
                               st["wv"][:, k, :], k == 0, k == 1, [st["bv"], CKV[k]], [PB[pb]],
                               track=(k == 1 and jj == 3))
                    pv = bank(pb)[:, 0:256].rearrange("p (t e) -> p t e", t=4)
                    S.op("dve", L("tensor_copy", VA[s_][:, g * 4:(g + 1) * 4, 0:64], pv), reads=[PB[pb]], writes=[VAb[s_]])

                for g in range(4):
                    items.append(lambda g=g: q_item(g))
                for g in range(4):
                    items.append(lambda g=g: k_item(g))
                for g in range(4):
                    items.append(lambda g=g: v_item(g))
                return items

            pti = state.setdefault("mpti", [0])
            for it in head_items(0):
                it()
            for hd in range(16):
                s_ = hd % 2
                pr = (hd // 2) % 4
                a = hd % 2
                nxt = head_items(hd + 1) if hd < 15 else []
                tiles = [(I, J) for I in range(4) for J in range(16)]
                info = {}

                def stage1(t):
                    I, J = t
                    ps = sbank()
                    mm(bank(ps), KH[s_][0:96, J * 128:(J + 1) * 128], QH[s_][0:96, tgs(I)], True, True,
                       [KHb[s_], QHb[s_]], [PB[ps]])
                    pi = pti[0]
                    pti[0] = (pi + 1) % 8
                    S.op("act", L("activation", ptv(pi), bank(ps), AF.Exp, scale=scale),
                         reads=[PB[ps]], writes=[PT[pi]])
                    info[t] = pi

                def stage2(t):
                    I, J = t
                    pi = info.pop(t)
                    po = I % 2
                    mm(bank(po), VA[s_][:, J, :], ptv(pi), J == 0, J == 15, [VAb[s_], PT[pi]], [PB[po]], track=True)
                    if J == 15:
                        rsv = rs_a if po == 0 else rs_b
                        rsb_ = B_rsa if po == 0 else B_rsb
                        S.op("dve", L("reciprocal", rsv[0:64, :], bank(po)[64:128, :]),
                             reads=[PB[po]], writes=[rsb_])
                        S.op("dve", L("tensor_tensor", blk(pr)[64 * a:64 * a + 64, tgs(I)], bank(po)[0:64, :], rsv[0:64, :], ALU.mult),
                             reads=[PB[po], rsb_], writes=[MIX[pr]])

                DEPTH = 3
                for i in range(len(tiles) + DEPTH):
                    if i < len(tiles):
                        stage1(tiles[i])
                    if i >= DEPTH:
                        stage2(tiles[i - DEPTH])
                    if nxt and i >= 2 and (i - 2) % 5 == 0:
                        nxt.pop(0)()
                while nxt:
                    nxt.pop(0)()
                if hd % 8 == 7:
                    half = hd // 8
                    out_proj("w_o", lambda k, g: blk(k)[:, tgs(g)], lambda k, g: [MIX[k]], half * 512, 4)
            S.barrier()

        def final_stage(goff):
            YT = [Buf("yt%d" % i) for i in range(8)]
            OS = [Buf("os%d" % i) for i in range(2)]
            ytv = lambda c: blkf(2 * c, 2)[:, 0:512]
            osv = lambda i: blkf(16 + 2 * i, 2)[:, 0:1024]
            for g in range(4):
                pb = nb_()
                for c in range(8):
                    q = state["sq"]
                    state["sq"] = (q + 1) % 4
                    S.op("act", L("activation", sqr[:, q, :], xT[:, c, tgs(g)], AF.Square),
                         reads=[XT[c][g]], writes=[SQ[q]])
                    mm(bank(pb), onesb[:], sqr[:, q, :], c == 0, c == 7, [SQ[q], B_const], [PB[pb]], track=True)
                rstd_from_psum(pb, 1024.0, rs_b[:], B_rsb)
                for c in range(8):
                    if goff is None:
                        S.op("dve", L("tensor_copy", ytv(c), xT[:, c, tgs(g)]), reads=[XT[c][g]], writes=[YT[c]])
                    else:
                        S.op("dve", L("scalar_tensor_tensor", ytv(c), xT[:, c, tgs(g)], gains[:, goff + c:goff + c + 1], rs_b[:], ALU.mult, ALU.mult),
                            reads=[XT[c][g], B_rsb], writes=[YT[c]])
                for j in range(4):
                    tt = g * 4 + j
                    oi = tt % 2
                    p0, p1 = nb_(), nb_()
                    for c in range(8):
                        pbx = p0 if c < 4 else p1
                        S.op("pe", L("transpose", bank(pbx)[:, (c % 4) * 128:(c % 4 + 1) * 128], ytv(c)[:, j * 128:(j + 1) * 128], ident[:]),
                            reads=[YT[c], B_const], writes=[PB[pbx]], track=(c % 4 == 3))
                    S.op("act", L("activation", osv(oi)[:, 0:512], bank(p0), AF.Copy),
                         reads=[PB[p0]], writes=[OS[oi]])
                    S.op("dve", L("tensor_copy", osv(oi)[:, 512:1024], bank(p1)),
                         reads=[PB[p1]], writes=[OS[oi]])
                    S.dma("sp", y_out[tt * 128:(tt + 1) * 128, :], osv(oi), reads=[OS[oi]])

        if KDBG == 10:
            for _ in range(30):
                norm_stage(0)
        elif KDBG == 11:
            norm_stage(0)
            for i in range(40):
                slot, wb = wtile(wreq_cols("w_in", 8, (i % 24) * 128, 128))
                wq = wview(slot, 8, 128)
                pb = proj_fm(wq, wb, 0, hT_rhs, hT_bufs, 8, i % 4)
                S.op("dve", L("tensor_copy", rs_b[:], bank(pb)), reads=[PB[pb]], writes=[B_rsb])
        elif nstage >= 1:
            norm_stage(0)
            S.barrier()
            diff_phase()
        if nstage >= 2:
            na_phase()
        if nstage >= 3:
            ffn(0, 8)
        if nstage >= 4:
            norm_stage(16)
            S.barrier()
            mla_phase()
        if nstage >= 5:
            ffn(1, 24)
        final_stage(32 if nstage >= 5 else None)
        S.final_wait("sp")
        S.emit()
    return nc, requests


_CACHE = {}


def _get_program(nstage=99):
    if nstage not in _CACHE:
        _, reqs = build(None, nstage)
        nc, _ = build(reqs, nstage)
        _CACHE[nstage] = (nc, reqs)
    return _CACHE[nstage]


def kernel(**inputs):
    nstage = int(inputs.pop("_nstage", 99))
    cores = inputs.pop("_cores", None)
    shared = _prep_shared(inputs)
    x = np.asarray(inputs["x"], np.float32)
    nb = x.shape[0] if cores is None else cores
    nc, plan = _get_program(nstage)
    dev = {k: shared[k] for k in SHAPES if k != "x"}
    dev["wpack"] = _pack_weights(shared, plan)
    in_maps = []
    for b in range(nb):
        m = dict(dev)
        m["x"] = np.ascontiguousarray(x[b])
        in_maps.append(m)
    res = run_bass_kernel_spmd(nc, in_maps, core_ids=list(range(nb)))
    out = np.stack([np.asarray(r["y"], np.float32) for r in res.results], axis=0)
    return out
```

```python
import math
import os
KDBG = int(os.environ.get("KDBG", "0"))
from contextlib import ExitStack
import numpy as np
import ml_dtypes
import concourse.bass as bass
import concourse.mybir as mybir
from concourse.bass_utils import run_bass_kernel_spmd

F32 = mybir.dt.float32
BF16 = mybir.dt.bfloat16
AF = mybir.ActivationFunctionType
ALU = mybir.AluOpType

T = 2048
D = 1024
DFF = 2816
NFC = 22
EPS = 1e-6
NEG = -30000.0
NSLOT = 4
LOOKAHEAD = 2
NBLK = 21


class Buf:
    __slots__ = ("name", "w", "r", "dsem", "dcnt", "excl")

    def __init__(self, name, excl=False):
        self.name = name
        self.excl = excl
        self.w = None
        self.r = {}
        self.dsem = None
        self.dcnt = 0


class Sched:
    ENG = ["pe", "act", "dve", "pool", "sp"]
    COMPUTE = ["pe", "act", "dve", "pool"]

    def __init__(self, nc, stack):
        self.nc = nc
        self.stack = stack
        self.ops = {e: [] for e in self.ENG}
        self.cnt = {e: 0 for e in self.ENG}
        self.sem = {e: stack.enter_context(nc.semaphore("s_" + e)) for e in self.ENG}
        self.seen = {e: {} for e in self.ENG}
        self.semobj = {e: self.sem[e] for e in self.ENG}
        self.nd = 0
        self.dma_bufs = []

    def _deps(self, eng, reads, writes):
        deps = {}

        def add(d):
            if d is None:
                return
            k, v = d
            if deps.get(k, 0) < v:
                deps[k] = v
        for b in reads:
            add(b.w)
        for b in writes:
            add(b.w)
            for d in b.r.items():
                add(d)
        waits = []
        seen = self.seen[eng]
        for k, v in deps.items():
            if k == "pe" and eng == "pe":
                continue
            if seen.get(k, 0) >= v:
                continue
            seen[k] = v
            waits.append((self.semobj[k], v))
        return waits

    def op(self, eng, fn, reads=(), writes=(), track=True):
        ex = [b for b in reads if b.excl]
        if ex:
            reads = [b for b in reads if not b.excl]
            writes = list(writes) + [b for b in ex if b not in writes]
        waits = self._deps(eng, reads, writes)
        idx = self.cnt[eng] + 1
        if track:
            self.cnt[eng] = idx
        ev = (eng, idx)
        for b in reads:
            if b.r.get(eng, 0) < idx:
                b.r[eng] = idx
        for b in writes:
            b.w = ev
            b.r = {}
        self.ops[eng].append((waits, fn, track, None))

    def dma(self, q, out, in_, reads=(), writes=()):
        waits = self._deps(q, reads, writes)
        bufs = list(writes) + list(reads)
        assert len(bufs) == 1
        b = bufs[0]
        if b.dsem is None:
            self.nd += 1
            key = "d%d" % self.nd
            b.dsem = key
            self.semobj[key] = self.stack.enter_context(self.nc.semaphore(key))
            self.dma_bufs.append(b)
        b.dcnt += 16
        ev = (b.dsem, b.dcnt)
        if writes:
            b.w = ev
            b.r = {}
        else:
            b.r[b.dsem] = b.dcnt
        self.ops[q].append((waits, L("dma_start", out=out, in_=in_), False,
                            (self.semobj[b.dsem], 16)))

    def barrier(self, dma=False):
        for e in self.ENG:
            waits = []
            if dma:
                for b in self.dma_bufs:
                    if self.seen[e].get(b.dsem, 0) < b.dcnt:
                        self.seen[e][b.dsem] = b.dcnt
                        waits.append((self.semobj[b.dsem], b.dcnt))
            for k in self.COMPUTE:
                v = self.cnt[k]
                if v > 0 and self.seen[e].get(k, 0) < v:
                    self.seen[e][k] = v
                    waits.append((self.semobj[k], v))
            if waits:
                self.ops[e].append((waits, None, False, None))

    def final_wait(self, eng):
        waits = [(self.semobj[b.dsem], b.dcnt) for b in self.dma_bufs]
        for k in self.COMPUTE:
            if self.cnt[k] > 0:
                waits.append((self.semobj[k], self.cnt[k]))
        self.ops[eng].append((waits, None, False, None))

    def emit(self):
        nc = self.nc
        handles = {"pe": "tensor", "act": "scalar", "dve": "vector", "pool": "gpsimd", "sp": "sync"}
        with nc.Block() as block:
            for e in self.ENG:
                ops = self.ops[e]
                own = self.sem[e]

                def body(engh, ops=ops, own=own):
                    for waits, fn, track, dinc in ops:
                        for s, v in waits:
                            engh.wait_ge(s, v)
                        if fn is None:
                            continue
                        ins = fn(engh)
                        if dinc is not None:
                            ins.then_inc(dinc[0], dinc[1])
                        elif track:
                            ins.then_inc(own, 1)
                getattr(block, handles[e])(body)


def L(method, *args, **kw):
    return lambda e: getattr(e, method)(*args, **kw)


def _na_tile_lists():
    lists = []
    tid = {}
    for m in range(16):
        if m <= 1:
            kbs = [0, 1, 2, 3]
        elif m >= 14:
            kbs = [12, 13, 14, 15]
        else:
            kbs = [m - 2, m - 1, m, m + 1, m + 2]
        cur = []
        for kb in kbs:
            key = ("int", kb - m) if 2 <= m <= 13 else ("edge", m, kb)
            if key not in tid:
                tid[key] = len(tid)
            cur.append((kb, tid[key]))
        lists.append(cur)
    return lists, tid


def _na_index_tables():
    lists, tid = _na_tile_lists()
    ntile = len(tid)
    dr_i = np.zeros((ntile, 128, 128), np.int64)
    dc_i = np.zeros((ntile, 128, 128), np.int64)
    valid = np.zeros((ntile, 128, 128), bool)
    done = set()
    for m in range(16):
        for kb, t in lists[m]:
            if t in done:
                continue
            done.add(t)
            kk = np.arange(128)
            kr = (2 * kb + kk // 64)[:, None]
            kc = (kk % 64)[:, None]
            qr = (2 * m + kk // 64)[None, :]
            qc = (kk % 64)[None, :]
            rs = np.clip(qr - 4, 0, 24)
            ws = np.clip(qc - 8, 0, 48)
            v = (kr >= rs) & (kr < rs + 8) & (kc >= ws) & (kc < ws + 16)
            dr = np.clip(kr - qr + 7, 0, 14)
            dc = np.clip(kc - qc + 15, 0, 30)
            dr_i[t] = np.broadcast_to(dr, (128, 128))
            dc_i[t] = np.broadcast_to(dc, (128, 128))
            valid[t] = v
    return dr_i, dc_i, valid


def _na_kb_layout():
    lists, tid = _na_tile_lists()
    layout = []
    off = 0
    shared = None
    for kb in range(16):
        ms = [m for m in range(16) if kb in [k for k, _ in lists[m]]]
        tl = [dict(lists[m])[kb] for m in ms]
        if 4 <= kb <= 11:
            if shared is None:
                shared = off
                off += 128 * len(ms)
            o = shared
        else:
            o = off
            off += 128 * len(ms)
        layout.append((kb, o, ms, tl))
    return layout, off


def _const_tables():
    c = {}
    c["ident"] = np.eye(128, dtype=np.float32)
    slopes = 2.0 ** (-8.0 * np.arange(1, 5) / 4)
    i = np.arange(T)
    a = (i // 128).astype(np.float32)
    r = (i % 128).astype(np.float32)
    augq = np.zeros((4, 4, T), np.float32)
    for h in range(4):
        s = slopes[h]
        augq[h, 0] = -s * 128.0 * a
        augq[h, 1] = -s * r
        augq[h, 2] = s
        augq[h, 3] = s
    c["augq"] = augq
    augk = np.zeros((2, 4, T), np.float32)
    for v, sg in enumerate([1.0, -1.0]):
        augk[v, 0] = sg
        augk[v, 1] = sg
        augk[v, 2] = sg * 128.0 * a
        augk[v, 3] = sg * r
    c["augk"] = augk
    jj = np.arange(128)[:, None]
    ii = np.arange(128)[None, :]
    cd = np.zeros((128, 4 * 128), np.float32)
    for h in range(4):
        cd[:, h * 128:(h + 1) * 128] = -2.0 * slopes[h] * np.maximum(jj - ii, 0)
    c["cdiag"] = cd
    inv = (10000.0 ** (-np.arange(0, 32, 2, dtype=np.float32) / np.float32(32))).astype(np.float32)
    ang = (np.arange(T, dtype=np.float32)[:, None] * inv[None, :]).astype(np.float32)
    cos = np.cos(ang.astype(np.float64)).astype(np.float32).T
    sin = np.sin(ang.astype(np.float64)).astype(np.float32).T
    rt = np.zeros((128, T), np.float32)
    rt[64:80] = cos
    rt[80:96] = cos
    rt[96:112] = -sin
    rt[112:128] = sin
    c["ropetab"] = rt
    return c


def _pm(v, nch):
    return np.ascontiguousarray(np.asarray(v, np.float32).reshape(nch, 128).T)


def _prep_shared(inp):
    g = {}
    c = _const_tables()
    g.update(c)
    f = lambda k: np.asarray(inp[k], np.float32)
    gains = np.zeros((128, 48), np.float32)
    gains[:, 0:8] = _pm(f("mix_norm_e")[0], 8)
    gains[:, 8:16] = _pm(f("ffn_norm_g")[0], 8)
    gains[:, 16:24] = _pm(f("mix_norm_o")[0], 8)
    gains[:, 24:32] = _pm(f("ffn_norm_g")[1], 8)
    gains[:, 32:40] = _pm(f("final_norm_g"), 8)
    gains[:, 40:44] = _pm(f("q_norm_g")[0], 4)
    gains[:, 44:46] = _pm(f("kv_norm_g")[0], 2)
    gains[:, 46] = f("diff_subln_g")[0]
    g["gains"] = gains
    cv = np.zeros((128, 2 * NFC * 4), np.float32)
    for l in range(2):
        for k in range(3):
            cv[:, l * 88 + k * 22: l * 88 + (k + 1) * 22] = _pm(f("ffn_conv_w")[l, k], NFC)
        cv[:, l * 88 + 66: l * 88 + 88] = _pm(f("ffn_conv_b")[l], NFC)
    g["convp"] = cv
    lam = np.zeros((128, 256), np.float32)
    for k, nm in enumerate(["diff_lq1", "diff_lk1", "diff_lq2", "diff_lk2"]):
        lam[:, k * 64:(k + 1) * 64] = np.broadcast_to(f(nm)[0][None, :], (128, 64))
    g["lamv"] = lam
    dr_i, dc_i, valid = _na_index_tables()
    rpb = f("na_rpb")[0]
    nb = np.empty((8, dr_i.shape[0], 128, 128), np.float32)
    for h in range(8):
        nb[h] = np.where(valid, rpb[h][dr_i, dc_i], np.float32(NEG))
    layout, ncols = _na_kb_layout()
    nb2 = np.empty((8, 128, ncols), np.float32)
    for kb, o, ms, tl in layout:
        for i, t in enumerate(tl):
            nb2[:, :, o + i * 128:o + (i + 1) * 128] = nb[:, t]
    g["nbias"] = nb2
    g["w_in"] = f("w_in_e")[0]
    g["w_out"] = f("w_out_e")[0]
    g["w_gate"] = f("w_ffn_gate")
    g["w_val"] = f("w_ffn_val")
    g["w_down"] = f("w_ffn_down")
    g["w_dq"] = f("w_dq")[0]
    wuq = f("w_uq")[0]
    cols = []
    for h in range(16):
        b = h * 96
        cols += list(range(b, b + 96)) + list(range(b + 80, b + 96)) + list(range(b + 64, b + 80))
    g["w_uq"] = np.ascontiguousarray(wuq[:, cols])
    wdkv = f("w_dkv")[0]
    wd = np.zeros((1024, 384), np.float32)
    wd[:, 0:256] = wdkv[:, 0:256]
    wd[:, 320:352] = wdkv[:, 256:288]
    wd[:, 352:368] = wdkv[:, 272:288]
    wd[:, 368:384] = wdkv[:, 256:272]
    g["w_dkv"] = wd
    wukv = f("w_ukv")[0].reshape(256, 16, 128)
    g["w_uk"] = np.ascontiguousarray(wukv[:, :, 0:64].reshape(256, 1024))
    g["w_uv"] = np.ascontiguousarray(wukv[:, :, 64:128].reshape(256, 1024))
    g["w_o"] = f("w_o_mla")[0]
    return g


BF16_IN = ()
SHAPES = {
    "x": [T, D], "ident": [128, 128], "augq": [4, 4, T], "augk": [2, 4, T], "cdiag": [128, 512],
    "ropetab": [128, T], "gains": [128, 48], "convp": [128, 176], "lamv": [128, 256],
    "nbias": [8, 128, 5248],
}


def _desc_size(d):
    if d[0] == "gv":
        return 2048
    return d[2] * d[4]


def _pack_weights(g, plan):
    pack = np.zeros((len(plan), 128, 2048), np.float32)
    for i, d in enumerate(plan):
        if d[0] == "gv":
            _, layer, c = d
            wg = g["w_gate"][layer][:, c * 128:(c + 1) * 128].reshape(8, 128, 128)
            wv = g["w_val"][layer][:, c * 128:(c + 1) * 128].reshape(8, 128, 128)
            t = np.concatenate([wg, wv], axis=2)
            pack[i] = t.transpose(1, 0, 2).reshape(128, 2048)
        else:
            _, name, kc, c0, ncols, lidx, r0 = d
            w = g[name] if lidx is None else g[name][lidx]
            t = w[r0:r0 + kc * 128, c0:c0 + ncols].reshape(kc, 128, ncols)
            pack[i, :, 0:kc * ncols] = t.transpose(1, 0, 2).reshape(128, kc * ncols)
    return pack


def build(plan=None, nstage=99):
    nc = bass.Bass("TRN2", target_bir_lowering=False)
    dram = {k: nc.dram_tensor(k, shp, BF16 if k in BF16_IN else F32, kind="ExternalInput").ap() for k, shp in SHAPES.items()}
    if plan is not None:
        dram["wpack"] = nc.dram_tensor("wpack", [len(plan), 128, 2048], F32, kind="ExternalInput").ap()
    y_out = nc.dram_tensor("y", [T, D], F32, kind="ExternalOutput").ap()
    requests = []
    with ExitStack() as st:
        S = Sched(nc, st)
        sb = lambda n, shp, dt: st.enter_context(nc.sbuf_tensor("sb_" + n, shp, dt))
        xT = sb("xT", [128, 8, T], F32)
        hT = sb("hT", [128, 8, T], BF16)
        wring = sb("wring", [128, NSLOT, 2048], BF16)
        AR = sb("arena", [128, NBLK * 2048], BF16)
        ident = sb("ident", [128, 128], F32)
        identb = sb("identb", [128, 128], BF16)
        onesb = sb("onesb", [128, 128], BF16)
        zerob = sb("zerob", [128, 128], BF16)
        gains = sb("gains", [128, 48], F32)
        convp = sb("convp", [128, 176], F32)
        small = sb("small", [128, 16], F32)
        cdiag = sb("cdiag", [128, 512], BF16)
        rs_a = sb("rs_a", [128, 512], F32)
        rs_b = sb("rs_b", [128, 512], F32)
        sqr = sb("sqr", [128, 4, 512], BF16)
        PS = st.enter_context(nc.psum_tensor("PS", [128, 8 * 512], F32))

        XT = [[Buf("xT%d_%d" % (c, g)) for g in range(4)] for c in range(8)]
        HT = [[Buf("hT%d_%d" % (c, g)) for g in range(4)] for c in range(8)]
        WS = [Buf("ws%d" % i) for i in range(NSLOT)]
        PB = [Buf("pb%d" % i, excl=True) for i in range(8)]
        B_const = Buf("const")
        B_small = Buf("small")
        B_rsa, B_rsb = Buf("rsa"), Buf("rsb")
        SQ = [Buf("sq%d" % i) for i in range(4)]

        def bank(i):
            return PS[:, i * 512:(i + 1) * 512]

        def blk(k, n=1):
            return AR[:, k * 2048:(k + n) * 2048]

        def blkf(k, n=2):
            return AR[:, k * 2048:(k + n) * 2048].bitcast(F32)

        state = {"bank": 0, "wi": 0, "wissued": 0, "sq": 0, "sb": 0, "tick": 0}
        pending = []
        lamv = blkf(0, 1)[:, 0:256]

        def nb_():
            b = state["bank"]
            state["bank"] = (b + 1) % 8
            return b

        def tgs(g):
            return slice(g * 512, (g + 1) * 512)

        def wtile(desc):
            i = state["wi"]
            state["wi"] = i + 1
            requests.append(desc)
            if plan is not None:
                hi = min(len(plan), i + 1 + LOOKAHEAD)
                while state["wissued"] < hi:
                    j = state["wissued"]
                    slot = j % NSLOT
                    X = _desc_size(plan[j])
                    S.dma("pool", wring[:, slot, 0:X], dram["wpack"][j][:, 0:X], writes=[WS[slot]])
                    state["wissued"] = j + 1
            return wring[:, i % NSLOT, :], WS[i % NSLOT]

        def wreq_cols(name, kc, c0, ncols, lidx=None, r0=0):
            return ("cols", name, kc, c0, ncols, lidx, r0)

        def wview(slot, kc, ncols):
            return slot[:, 0:kc * ncols].rearrange("p (k n) -> p k n", k=kc)

        S.dma("sp", ident[:], dram["ident"], writes=[B_const])
        gb_l = Buf("gains_l")
        S.dma("sp", gains[:], dram["gains"], writes=[gb_l])
        cb1, cb2, cb3 = Buf("c1"), Buf("c2"), Buf("c3")
        S.dma("sp", convp[:], dram["convp"], writes=[cb1])
        S.dma("sp", lamv, dram["lamv"], writes=[cb2])
        S.dma("pool", cdiag[:], dram["cdiag"], writes=[cb3])
        S.op("dve", L("tensor_copy", identb[:], ident[:]), reads=[B_const], writes=[B_const])
        S.op("dve", L("memset", onesb[:], 1.0), writes=[B_const])
        S.op("dve", L("memset", zerob[:], 0.0), writes=[B_const])
        S.op("dve", L("memset", small[:, 0:1], EPS), writes=[B_small])
        lam_init = 0.8 - 0.6 * math.exp(-0.3 * 0)
        S.op("dve", L("tensor_tensor", lamv[:, 0:64], lamv[:, 0:64], lamv[:, 64:128], ALU.mult),
             reads=[cb2], writes=[cb2])
        S.op("dve", L("tensor_tensor", lamv[:, 128:192], lamv[:, 128:192], lamv[:, 192:256], ALU.mult),
             reads=[cb2], writes=[cb2])
        S.op("dve", L("reduce_sum", small[:, 4:5], lamv[:, 0:64], mybir.AxisListType.X),
             reads=[cb2], writes=[B_small])
        S.op("dve", L("reduce_sum", small[:, 5:6], lamv[:, 128:192], mybir.AxisListType.X),
             reads=[cb2], writes=[B_small])
        S.op("act", L("activation", small[:, 6:8], small[:, 4:6], AF.Exp), reads=[B_small], writes=[B_small])
        S.op("dve", L("scalar_tensor_tensor", small[:, 1:2], small[:, 7:8], -lam_init, small[:, 6:7],
                                                     ALU.add, ALU.subtract), reads=[B_small], writes=[B_small])
        S.op("dve", L("tensor_scalar", small[:, 2:3], gains[:, 46:47], 1.0 - lam_init, None, ALU.mult),
             reads=[B_small, gb_l], writes=[B_small])
        S.barrier(dma=True)

        def mm(out, lhsT, rhs, start, stop, reads, writes, track=None, sgc=False):
            if track is None:
                track = stop
            if sgc:
                S.op("pe", L("matmul", out, lhsT, rhs, start=start, stop=stop, skip_group_check=True),
                     reads=reads, writes=writes, track=track)
            else:
                S.op("pe", L("matmul", out, lhsT, rhs, start=start, stop=stop), reads=reads, writes=writes,
                     track=track)

        def rstd_from_psum(pb, nfeat, dst, dstbuf):
            S.op("act", L("activation", rs_a[:], bank(pb), AF.Sqrt, scale=1.0 / nfeat, bias=small[:, 0:1]),
                 reads=[PB[pb], B_small], writes=[B_rsa])
            S.op("dve", L("reciprocal", dst, rs_a[:]), reads=[B_rsa], writes=[dstbuf])

        def norm_stage(goff):
            for g in range(4):
                pb = nb_()
                for c in range(8):
                    q = state["sq"]
                    state["sq"] = (q + 1) % 4
                    S.op("act", L("activation", sqr[:, q, :], xT[:, c, tgs(g)], AF.Square),
                         reads=[XT[c][g]], writes=[SQ[q]])
                    mm(bank(pb), onesb[:], sqr[:, q, :], c == 0, c == 7, [SQ[q], B_const], [PB[pb]], track=True)
                rstd_from_psum(pb, 1024.0, rs_b[:], B_rsb)
                for c in range(8):
                    S.op("dve", L("scalar_tensor_tensor", hT[:, c, tgs(g)], xT[:, c, tgs(g)], gains[:, goff + c:goff + c + 1], rs_b[:],
                        ALU.mult, ALU.mult), reads=[XT[c][g], B_rsb], writes=[HT[c][g]])

        def proj_fm(wv, wbuf, col0, rhs_fn, rhs_bufs_fn, nk, g, M=128):
            pb = nb_()
            for k in range(nk):
                mm(bank(pb)[0:M, :], wv[:, k, col0:col0 + M], rhs_fn(k, g), k == 0, k == nk - 1,
                   [wbuf] + rhs_bufs_fn(k, g), [PB[pb]])
            return pb

        hT_rhs = lambda k, g: hT[:, k, tgs(g)]
        hT_bufs = lambda k, g: [HT[k][g]]

        def resid_add(pb, n, g, eng="dve"):
            S.op(eng, L("tensor_tensor", xT[:, n, tgs(g)], bank(pb), xT[:, n, tgs(g)], ALU.add),
                 reads=[PB[pb], XT[n][g]], writes=[XT[n][g]])

        def out_proj(name, mix_fn, mix_bufs, krow0, nkc):
            for nt in range(4):
                slot, wb = wtile(wreq_cols(name, nkc, nt * 256, 256, r0=krow0))
                wv = wview(slot, nkc, 256)
                for nn in range(2):
                    n = nt * 2 + nn
                    for g in range(4):
                        pb = proj_fm(wv, wb, nn * 128, mix_fn, mix_bufs, nkc, g)
                        resid_add(pb, n, g)

        XS = [Buf("xs%d" % i) for i in range(16)]
        for tt in range(16):
            S.dma("sp", blkf(tt, 1), dram["x"][tt * 128:(tt + 1) * 128, :], writes=[XS[tt]])
        for g in range(4):
            for c in range(8):
                pb = nb_()
                for j in range(4):
                    tt = g * 4 + j
                    S.op("pe", L("transpose", bank(pb)[:, j * 128:(j + 1) * 128], blkf(tt, 1)[:, c * 128:(c + 1) * 128], ident[:]),
                        reads=[XS[tt], B_const], writes=[PB[pb]], track=(j == 3))
                eng = "act" if c % 2 == 0 else "dve"
                if eng == "act":
                    S.op("act", L("activation", xT[:, c, tgs(g)], bank(pb), AF.Copy),
                         reads=[PB[pb]], writes=[XT[c][g]])
                else:
                    S.op("dve", L("tensor_copy", xT[:, c, tgs(g)], bank(pb)),
                         reads=[PB[pb]], writes=[XT[c][g]])
        S.barrier()

        def ffn(layer, goff):
            norm_stage(goff)
            U = [Buf("u%d" % i) for i in range(11)]
            TB = [Buf("tb%d" % i) for i in range(2)]
            cp = layer * 88
            for rnd in range(2):
                for cc in range(11):
                    c = rnd * 11 + cc

                    slot, wb = wtile(("gv", layer, c))
                    wv = wview(slot, 8, 256)
                    for g in range(4):
                        for k in range(8):
                            mm(bank(g), wv[:, k, 0:128], hT[:, k, tgs(g)], k == 0, k == 7, [wb, HT[k][g]], [PB[g]])
                    for g in range(4):
                        for k in range(8):
                            mm(bank(4 + g), wv[:, k, 128:256], hT[:, k, tgs(g)], k == 0, k == 7,
                               [wb, HT[k][g]], [PB[4 + g]])
                    tb = cc % 2
                    tv = blkf(11 + 2 * tb)
                    G = PS[:, 0:2048]
                    gb = [PB[0], PB[1], PB[2], PB[3]]
                    w0 = convp[:, cp + c:cp + c + 1]
                    w1 = convp[:, cp + 22 + c:cp + 22 + c + 1]
                    w2 = convp[:, cp + 44 + c:cp + 44 + c + 1]
                    bb = convp[:, cp + 66 + c:cp + 66 + c + 1]
                    S.op("act", L("activation", tv, G, AF.Identity, scale=w1, bias=bb),
                         reads=gb + [cb1], writes=[TB[tb]])
                    S.op("dve", L("scalar_tensor_tensor", tv[:, 1:T], PS[:, 0:T - 1], w0, tv[:, 1:T], ALU.mult, ALU.add),
                        reads=gb + [TB[tb]], writes=[TB[tb]])
                    S.op("dve", L("scalar_tensor_tensor", tv[:, 0:T - 1], PS[:, 1:T], w2, tv[:, 0:T - 1], ALU.mult, ALU.add),
                        reads=gb + [TB[tb]], writes=[TB[tb]])
                    S.op("act", L("activation", tv, tv, AF.Gelu), reads=[TB[tb]], writes=[TB[tb]])
                    for hh in range(2):
                        S.op("dve", L("tensor_tensor", blk(cc)[:, hh * 1024:(hh + 1) * 1024], tv[:, hh * 1024:(hh + 1) * 1024],
                            PS[:, 2048 + hh * 1024:2048 + (hh + 1) * 1024], ALU.mult),
                            reads=[TB[tb], PB[4 + 2 * hh], PB[5 + 2 * hh]], writes=[U[cc]])
                for n in range(8):
                    slot, wb = wtile(wreq_cols("w_down", 11, n * 128, 128, lidx=layer, r0=rnd * 1408))
                    wv = wview(slot, 11, 128)
                    for g in range(4):
                        pb = proj_fm(wv, wb, 0, lambda k, g: blk(k)[:, tgs(g)], lambda k, g: [U[k]], 11, g)
                        resid_add(pb, n, g)
            S.barrier()

        def softmax_epilogue_diff(h, I, pO, pD, MIX):
            r1, r2, t2 = blkf(13, 2)[:, 0:512], blkf(13, 2)[:, 512:1024], blkf(13, 2)[:, 1024:1536]
            t1 = blkf(15, 2)[:, I * 512:(I + 1) * 512]
            EB = state.setdefault("EB", [Buf("eb%d" % i) for i in range(3)])
            A1 = state.setdefault("A1", [Buf("a1_%d" % i) for i in range(4)])
            S.op("dve", L("tensor_copy", r1, bank(pD[0])), reads=[PB[pD[0]]], writes=[EB[0]])
            S.op("dve", L("tensor_copy", r2, bank(pD[1])), reads=[PB[pD[1]]], writes=[EB[1]])
            S.op("dve", L("tensor_copy", t1, bank(pO[0])), reads=[PB[pO[0]]], writes=[A1[I]])
            S.op("dve", L("tensor_copy", t2, bank(pO[1])), reads=[PB[pO[1]]], writes=[EB[2]])
            S.op("dve", L("reciprocal", r1, r1), reads=[EB[0]], writes=[EB[0]])
            S.op("dve", L("reciprocal", r2, r2), reads=[EB[1]], writes=[EB[1]])
            S.op("dve", L("tensor_tensor", t1, t1, r1, ALU.mult), reads=[EB[0], A1[I]], writes=[A1[I]])
            S.op("dve", L("tensor_tensor", t2, t2, r2, ALU.mult), reads=[EB[1], EB[2]], writes=[EB[2]])
            S.op("dve", L("scalar_tensor_tensor", t1, t2, small[:, 1:2], t1, ALU.mult, ALU.add),
                 reads=[A1[I], EB[2], B_small], writes=[A1[I]])

        def subln_head(h, MIX):
            A1 = state["A1"]
            RS = state.setdefault("RS", [Buf("rs_%d" % i) for i in range(4)])
            pbs = []
            for I in range(4):
                t1 = blkf(15, 2)[:, I * 512:(I + 1) * 512]
                S.op("act", L("activation", sqr[:, I, :], t1, AF.Square), reads=[A1[I]], writes=[SQ[I]])
                pb = 4 + state["sb"] % 4
                state["sb"] += 1
                mm(bank(pb), onesb[:], sqr[:, I, :], True, True, [SQ[I], B_const], [PB[pb]])
                pbs.append(pb)
            for I in range(4):
                rsd = blkf(17, 2)[:, I * 512:(I + 1) * 512]
                S.op("act", L("activation", rsd, bank(pbs[I]), AF.Sqrt, scale=1.0 / 128.0, bias=small[:, 0:1]),
                     reads=[PB[pbs[I]], B_small], writes=[RS[I]])
            for I in range(4):
                t1 = blkf(15, 2)[:, I * 512:(I + 1) * 512]
                rsd = blkf(17, 2)[:, I * 512:(I + 1) * 512]
                S.op("dve", L("reciprocal", rsd, rsd), reads=[RS[I]], writes=[RS[I]])
                S.op("dve", L("scalar_tensor_tensor", blk(h)[:, tgs(I)], t1, small[:, 2:3], rsd, ALU.mult, ALU.mult),
                     reads=[A1[I], RS[I], B_small], writes=[MIX[h]])

        def diff_phase():
            MIX = [Buf("mix%d" % i) for i in range(4)]
            QB = [Buf("dq%d" % i) for i in range(2)]
            KB = [[Buf("dk%d%d" % (i, v)) for v in range(2)] for i in range(2)]
            VB = Buf("dv")
            PT = [Buf("pt%d" % i) for i in range(8)]
            Qt = [blk(4), blk(5)]
            Kt = [[blk(6), blk(7)], [blk(8), blk(9)]]
            Vt = blk(10).rearrange("p (t e) -> p t e", t=16)
            ptv = lambda i: blk(11, 2)[:, i * 512:(i + 1) * 512]
            pti = [0]
            for n in range(2):
                for v in range(2):
                    S.dma("pool", Kt[n][v][64:68, :], dram["augk"][v], writes=[KB[n][v]])
            for h in range(int(os.environ.get("KH0", "0")), int(os.environ.get("KH1", "4")) if KDBG in (0, 5) else 1):
                if KDBG == 1:
                    break
                if os.environ.get("KBAR", "0") == "1":
                    S.barrier()
                slot, wb = wtile(wreq_cols("w_in", 8, h * 128, 128))
                wq = wview(slot, 8, 128)
                for n in range(2):
                    S.dma("pool", Qt[n][64:68, :], dram["augq"][h], writes=[QB[n]])
                for g in range(4):
                    pb = proj_fm(wq, wb, 0, hT_rhs, hT_bufs, 8, g)
                    S.op("dve", L("tensor_scalar", Qt[0][0:64, tgs(g)], bank(pb)[0:64, :], 0.125, None, ALU.mult),
                         reads=[PB[pb]], writes=[QB[0]])
                    S.op("dve", L("tensor_scalar", Qt[1][0:64, tgs(g)], bank(pb)[64:128, :], 0.125, None, ALU.mult),
                         reads=[PB[pb]], writes=[QB[1]])
                slot, wb = wtile(wreq_cols("w_in", 8, 512 + h * 128, 128))
                wk = wview(slot, 8, 128)
                for g in range(4):
                    pb = proj_fm(wk, wb, 0, hT_rhs, hT_bufs, 8, g)
                    for n in range(2):
                        S.op("act", L("activation", Kt[n][0][0:64, tgs(g)], bank(pb)[64 * n:64 * n + 64, :], AF.Copy),
                             reads=[PB[pb]], writes=[KB[n][0]])
                        S.op("dve", L("tensor_copy", Kt[n][1][0:64, tgs(g)], bank(pb)[64 * n:64 * n + 64, :]),
                             reads=[PB[pb]], writes=[KB[n][1]])
                slot, wb = wtile(wreq_cols("w_in", 8, 1024 + h * 128, 128))
                wvv = wview(slot, 8, 128)
                for g in range(4):
                    pb = nb_()
                    for j in range(4):
                        tt = g * 4 + j
                        for k in range(8):
                            mm(bank(pb)[:, j * 128:(j + 1) * 128], hT[:, k, tt * 128:(tt + 1) * 128], wvv[:, k, :],
                               k == 0, k == 7, [wb, HT[k][g]], [PB[pb]], track=(k == 7 and j == 3))
                    S.op("act", L("activation", Vt[:, g * 4:(g + 1) * 4, :], bank(pb).rearrange("p (t e) -> p t e", t=4), AF.Copy),
                        reads=[PB[pb]], writes=[VB])
                pO, pD = [0, 1], [2, 3]
                tiles = [(I, J, n) for I in range(4) for J in range(16) for n in range(2)]
                info = {}

                def stage1(t):
                    I, J, n = t
                    ps = 4 + state["sb"] % 4
                    state["sb"] += 1
                    if J < 4 * I or J >= 4 * I + 4:
                        v = 0 if J < 4 * I else 1
                        mm(bank(ps), Kt[n][v][0:68, J * 128:(J + 1) * 128], Qt[n][0:68, tgs(I)], True, True,
                           [KB[n][v], QB[n]], [PB[ps]])
                    else:
                        for b in range(4):
                            gb_ = 4 * I + b
                            v = 0 if gb_ >= J else 1
                            qs = slice(gb_ * 128, (gb_ + 1) * 128)
                            osl = bank(ps)[:, b * 128:(b + 1) * 128]
                            diag = gb_ == J
                            mm(osl, Kt[n][v][0:68, J * 128:(J + 1) * 128], Qt[n][0:68, qs], True, not diag,
                               [KB[n][v], QB[n]], [PB[ps]], track=(b == 3 and not diag))
                            if diag:
                                mm(osl, identb[:], cdiag[:, h * 128:(h + 1) * 128], False, True,
                                   [B_const, cb3], [PB[ps]], track=(b == 3))
                    pi = pti[0]
                    pti[0] = (pi + 1) % 8
                    S.op("act", L("activation", ptv(pi), bank(ps), AF.Exp), reads=[PB[ps]], writes=[PT[pi]])
                    info[t] = pi

                def stage2(t):
                    I, J, n = t
                    pi = info.pop(t)
                    mm(bank(pO[n]), Vt[:, J, :], ptv(pi), J == 0, J == 15, [VB, PT[pi]], [PB[pO[n]]], track=True)
                    mm(bank(pD[n]), onesb[:], ptv(pi), J == 0, J == 15, [B_const, PT[pi]], [PB[pD[n]]], track=True)
                    if J == 15 and n == 1:
                        softmax_epilogue_diff(h, I, pO, pD, MIX)
                        if I == 3:
                            pending.append((state["tick"] + 24, lambda h=h: subln_head(h, MIX)))

                DEPTH = 2
                for i in range(len(tiles) + DEPTH):
                    state["tick"] += 1
                    while pending and pending[0][0] <= state["tick"]:
                        pending.pop(0)[1]()
                    if i < len(tiles):
                        stage1(tiles[i])
                    if i >= DEPTH:
                        stage2(tiles[i - DEPTH])
            while pending:
                pending.pop(0)[1]()
            if KDBG == 0:
                out_proj("w_out", lambda k, g: blk(k)[:, tgs(g)], lambda k, g: [MIX[k]], 0, 4)
            S.barrier()

        def na_phase():
            lists, _ = _na_tile_lists()
            MIX = [Buf("nmix%d" % i) for i in range(4)]
            NQb, NKb = Buf("nq"), Buf("nk")
            VAb = [Buf("va0"), Buf("va1")]
            NBb = Buf("nbias")
            PT = [Buf("npt%d" % i) for i in range(6)]
            RD = [Buf("nrd%d" % i) for i in range(2)]
            NQ, NK = blk(4), blk(5)
            VA = [blk(6).rearrange("p (t e) -> p t e", t=16), blk(7).rearrange("p (t e) -> p t e", t=16)]
            NB2 = blk(15, 6)
            layout, _nc = _na_kb_layout()
            ptv = lambda i: blk(8, 3)[:, i * 768:(i + 1) * 768]
            rdv = lambda i: blkf(13, 2)[:, i * 512:(i + 1) * 512]
            for a in range(2):
                S.op("dve", L("memset", VA[a][:, :, 64:128], 1.0), writes=[VAb[a]])
            pti = [0]
            for j in range(4):
                slot, wb = wtile(wreq_cols("w_in", 8, 1536 + j * 128, 128))
                wq = wview(slot, 8, 128)
                for g in range(4):
                    pb = proj_fm(wq, wb, 0, hT_rhs, hT_bufs, 8, g)
                    S.op("dve", L("tensor_scalar", NQ[:, tgs(g)], bank(pb), 0.125, None, ALU.mult),
                         reads=[PB[pb]], writes=[NQb])
                slot, wb = wtile(wreq_cols("w_in", 8, 2048 + j * 128, 128))
                wk = wview(slot, 8, 128)
                for g in range(4):
                    pb = proj_fm(wk, wb, 0, hT_rhs, hT_bufs, 8, g)
                    S.op("dve", L("tensor_copy", NK[:, tgs(g)], bank(pb)), reads=[PB[pb]], writes=[NKb])
                slot, wb = wtile(wreq_cols("w_in", 8, 2560 + j * 128, 128))
                wvv = wview(slot, 8, 128)
                for g in range(4):
                    pb = nb_()
                    for jj in range(4):
                        tt = g * 4 + jj
                        for k in range(8):
                            mm(bank(pb)[:, jj * 128:(jj + 1) * 128], hT[:, k, tt * 128:(tt + 1) * 128], wvv[:, k, :],
                               k == 0, k == 7, [wb, HT[k][g]], [PB[pb]], track=(k == 7 and jj == 3))
                    pv = bank(pb).rearrange("p (t e) -> p t e", t=4)
                    S.op("act", L("activation", VA[0][:, g * 4:(g + 1) * 4, 0:64], pv[:, :, 0:64], AF.Copy),
                         reads=[PB[pb]], writes=[VAb[0]])
                    S.op("dve", L("tensor_copy", VA[1][:, g * 4:(g + 1) * 4, 0:64], pv[:, :, 64:128]),
                         reads=[PB[pb]], writes=[VAb[1]])
                for a in range(2):
                    S.dma("pool", NB2[:, a * 5248:(a + 1) * 5248], dram["nbias"][2 * j + a], writes=[NBb])
                for a in range(2):
                    p0 = 64 * a
                    nbase = a * 5248
                    for b4 in range(4):
                        mm(bank(b4), zerob[:], NK[:, b4 * 512:(b4 + 1) * 512], True, True, [B_const, NKb], [PB[b4]], sgc=True)
                    info = {}

                    def stage1(kbi, a=a, p0=p0, nbase=nbase):
                        kb, o, ms, tl = layout[kbi]
                        W = 128 * len(ms)
                        q0 = 128 * ms[0]
                        b0 = 4 + 2 * (state["sb"] % 2)
                        state["sb"] += 1
                        pieces = [(0, min(W, 512))] + ([(512, W)] if W > 512 else [])
                        for (c0, c1) in pieces:
                            bb = b0 + c0 // 512
                            osl = PS[:, b0 * 512 + c0:b0 * 512 + c1]
                            mm(osl, NK[p0:p0 + 64, kb * 128:(kb + 1) * 128], NQ[p0:p0 + 64, q0 + c0:q0 + c1],
                               True, False, [NKb, NQb], [PB[bb]], track=False)
                            mm(osl, identb[:], NB2[:, nbase + o + c0:nbase + o + c1], False, True,
                               [B_const, NBb], [PB[bb]], track=True)
                        pi = pti[0]
                        pti[0] = (pi + 1) % 3
                        rb = [PB[b0]] + ([PB[b0 + 1]] if W > 512 else [])
                        S.op("act", L("activation", ptv(pi)[:, 0:W], PS[:, b0 * 512:b0 * 512 + W], AF.Exp),
                             reads=rb, writes=[PT[pi]])
                        info[kbi] = pi

                    def stage2(kbi, a=a, p0=p0, j=j):
                        kb, o, ms, tl = layout[kbi]
                        W = 128 * len(ms)
                        q0 = 128 * ms[0]
                        pi = info.pop(kbi)
                        g0 = q0
                        while g0 < q0 + W:
                            g1 = min(q0 + W, (g0 // 512 + 1) * 512)
                            mm(PS[:, g0:g1], VA[a][:, kb, :], ptv(pi)[:, g0 - q0:g1 - q0], False, True,
                               [VAb[a], PT[pi]], [PB[g0 // 512]], track=True, sgc=True)
                            g0 = g1
                        if kbi == 15:
                            ev = blkf(13, 2)
                            S.op("dve", L("tensor_copy", ev, PS[:, 0:2048]), reads=[PB[0], PB[1], PB[2], PB[3]], writes=[RD[0]])
                            rd = blkf(11, 2)
                            for g in range(4):
                                S.op("dve", L("reciprocal", rd[0:64, tgs(g)], ev[64:128, tgs(g)]), reads=[RD[0]], writes=[RD[1]])
                            for g in range(4):
                                S.op("dve", L("tensor_tensor", blk(j)[p0:p0 + 64, tgs(g)], ev[0:64, tgs(g)], rd[0:64, tgs(g)], ALU.mult),
                                     reads=[RD[0], RD[1]], writes=[MIX[j]])

                    DEPTH = 1
                    for i in range(16 + DEPTH):
                        if i < 16:
                            stage1(i)
                        if i >= DEPTH:
                            stage2(i - DEPTH)
            out_proj("w_out", lambda k, g: blk(k)[:, tgs(g)], lambda k, g: [MIX[k]], 512, 4)
            S.barrier()

        def mla_phase():
            CQ = [Buf("cq%d" % i) for i in range(4)]
            CKV = [Buf("ckv%d" % i) for i in range(2)]
            KRb = Buf("kr")
            RTb = Buf("ropetab")
            QHb = [Buf("qh0"), Buf("qh1")]
            KHb = [Buf("kh0"), Buf("kh1")]
            VAb = [Buf("mva0"), Buf("mva1")]
            MIX = [Buf("mmix%d" % i) for i in range(4)]
            PT = [Buf("mpt%d" % i) for i in range(8)]
            TMP = [Buf("mtmp%d" % i) for i in range(2)]
            cqv = lambda c: blk(4 + c)
            ckvv = lambda c: blk(8 + c)
            KR = blk(10)
            rt = blkf(11, 2)
            QH = [blk(13), blk(14)]
            KH = [blk(15), blk(16)]
            VA = [blk(17).rearrange("p (t e) -> p t e", t=16), blk(18).rearrange("p (t e) -> p t e", t=16)]
            ptv = lambda i: blk(10 if i < 4 else 19)[:, (i % 4) * 512:(i % 4 + 1) * 512]
            tmpv = lambda i: blkf(20, 1)[:, i * 512:(i + 1) * 512]
            scale = 96.0 ** -0.5
            S.dma("sp", rt, dram["ropetab"], writes=[RTb])
            for a in range(2):
                S.op("dve", L("memset", VA[a][:, :, 64:128], 1.0), writes=[VAb[a]])

            def rope(dst, dstbuf, pb, g):
                S.op("dve", L("tensor_tensor", tmpv(0)[64:96, :], bank(pb)[64:96, :], rt[64:96, tgs(g)], ALU.mult),
                     reads=[PB[pb], RTb], writes=[TMP[0]])
                S.op("dve", L("tensor_tensor", tmpv(1)[64:96, :], bank(pb)[96:128, :], rt[96:128, tgs(g)], ALU.mult),
                     reads=[PB[pb], RTb], writes=[TMP[1]])
                S.op("dve", L("tensor_tensor", dst[64:96, tgs(g)], tmpv(0)[64:96, :], tmpv(1)[64:96, :], ALU.add),
                     reads=[TMP[0], TMP[1]], writes=[dstbuf])

            def latent(wtiles, nch, goff, dstv, dstb, nfeat, extra=None):
                for g in range(4):
                    pbs = []
                    for (wv, wb, ncol) in wtiles:
                        for cc in range(ncol // 128):
                            pbs.append(proj_fm(wv, wb, cc * 128, hT_rhs, hT_bufs, 8, g))
                    pss = nb_()
                    while pss in pbs:
                        pss = nb_()
                    for c in range(nch):
                        q = state["sq"]
                        state["sq"] = (q + 1) % 4
                        S.op("act", L("activation", sqr[:, q, :], bank(pbs[c]), AF.Square),
                             reads=[PB[pbs[c]]], writes=[SQ[q]])
                        mm(bank(pss), onesb[:], sqr[:, q, :], c == 0, c == nch - 1, [SQ[q], B_const], [PB[pss]], track=True)
                    rstd_from_psum(pss, nfeat, rs_b[:], B_rsb)
                    for c in range(nch):
                        S.op("dve", L("scalar_tensor_tensor", dstv(c)[:, tgs(g)], bank(pbs[c]), gains[:, goff + c:goff + c + 1], rs_b[:], ALU.mult, ALU.mult),
                            reads=[PB[pbs[c]], B_rsb], writes=[dstb[c]])
                    if extra is not None:
                        extra(pbs[nch], g)

            s0, b0 = wtile(wreq_cols("w_dq", 8, 0, 256))
            s1, b1 = wtile(wreq_cols("w_dq", 8, 256, 256))
            latent([(wview(s0, 8, 256), b0, 256), (wview(s1, 8, 256), b1, 256)], 4, 40, cqv, CQ, 512.0)
            s0, b0 = wtile(wreq_cols("w_dkv", 8, 0, 256))
            s1, b1 = wtile(wreq_cols("w_dkv", 8, 256, 128))
            latent([(wview(s0, 8, 256), b0, 256), (wview(s1, 8, 128), b1, 128)], 2, 44, ckvv, CKV, 256.0,
                   extra=lambda pb, g: rope(KH[0], KHb[0], pb, g))
            S.op("dve", L("tensor_copy", KH[1][64:96, :], KH[0][64:96, :]), reads=[KHb[0]], writes=[KHb[1]])

            def sbank():
                ps = 4 + state["sb"] % 4
                state["sb"] += 1
                return ps

            def ibank():
                state["ib"] = state.get("ib", 0) + 1
                return 2 + state["ib"] % 2

            def head_items(hd):
                s_ = hd % 2
                st = {}
                items = []

                def q_item(g):
                    if g == 0:
                        sl, st["bq"] = wtile(wreq_cols("w_uq", 4, hd * 128, 128))
                        st["wq"] = wview(sl, 4, 128)
                    pb = ibank()
                    for k in range(4):
                        mm(bank(pb), st["wq"][:, k, :], cqv(k)[:, tgs(g)], k == 0, k == 3, [st["bq"], CQ[k]], [PB[pb]])
                    S.op("act", L("activation", QH[s_][0:64, tgs(g)], bank(pb)[0:64, :], AF.Copy),
                         reads=[PB[pb]], writes=[QHb[s_]])
                    rope(QH[s_], QHb[s_], pb, g)

                def k_item(g):
                    if g == 0:
                        sl, st["bk"] = wtile(wreq_cols("w_uk", 2, hd * 64, 64))
                        st["wk"] = wview(sl, 2, 64)
                    pb = ibank()
                    for k in range(2):
                        mm(bank(pb)[0:64, :], st["wk"][:, k, :], ckvv(k)[:, tgs(g)], k == 0, k == 1, [st["bk"], CKV[k]], [PB[pb]])
                    S.op("dve", L("tensor_copy", KH[s_][0:64, tgs(g)], bank(pb)[0:64, :]), reads=[PB[pb]], writes=[KHb[s_]])

                def v_item(g):
                    if g == 0:
                        sl, st["bv"] = wtile(wreq_cols("w_uv", 2, hd * 64, 64))
                        st["wv"] = wview(sl, 2, 64)
                    pb = ibank()
                    for jj in range(4):
                        tt = g * 4 + jj
                        for k in range(2):
                            mm(bank(pb)[:, jj * 64:(jj + 1) * 64], ckvv(k)[:, tt * 128:(tt + 1) * 128],
                               st["wv"][:, k, :], k == 0, k == 1, [st["bv"], CKV[k]], [PB[pb]],
                               track=(k == 1 and jj == 3))
                    pv = bank(pb)[:, 0:256].rearrange("p (t e) -> p t e", t=4)
                    S.op("dve", L("tensor_copy", VA[s_][:, g * 4:(g + 1) * 4, 0:64], pv), reads=[PB[pb]], writes=[VAb[s_]])

                for g in range(4):
                    items.append(lambda g=g: q_item(g))
                for g in range(4):
                    items.append(lambda g=g: k_item(g))
                for g in range(4):
                    items.append(lambda g=g: v_item(g))
                return items

            pti = state.setdefault("mpti", [0])
            for it in head_items(0):
                it()
            for hd in range(16):
                s_ = hd % 2
                pr = (hd // 2) % 4
                a = hd % 2
                nxt = head_items(hd + 1) if hd < 15 else []
                tiles = [(I, J) for I in range(4) for J in range(16)]
                info = {}

                def stage1(t):
                    I, J = t
                    ps = sbank()
                    mm(bank(ps), KH[s_][0:96, J * 128:(J + 1) * 128], QH[s_][0:96, tgs(I)], True, True,
                       [KHb[s_], QHb[s_]], [PB[ps]])
                    pi = pti[0]
                    pti[0] = (pi + 1) % 8
                    S.op("act", L("activation", ptv(pi), bank(ps), AF.Exp, scale=scale),
                         reads=[PB[ps]], writes=[PT[pi]])
                    info[t] = pi

                def stage2(t):
                    I, J = t
                    pi = info.pop(t)
                    po = I % 2
                    mm(bank(po), VA[s_][:, J, :], ptv(pi), J == 0, J == 15, [VAb[s_], PT[pi]], [PB[po]], track=True)
                    if J == 15:
                        rsv = rs_a if po == 0 else rs_b
                        rsb_ = B_rsa if po == 0 else B_rsb
                        S.op("dve", L("reciprocal", rsv[0:64, :], bank(po)[64:128, :]),
                             reads=[PB[po]], writes=[rsb_])
                        S.op("dve", L("tensor_tensor", blk(pr)[64 * a:64 * a + 64, tgs(I)], bank(po)[0:64, :], rsv[0:64, :], ALU.mult),
                             reads=[PB[po], rsb_], writes=[MIX[pr]])

                DEPTH = 3
                for i in range(len(tiles) + DEPTH):
                    if i < len(tiles):
                        stage1(tiles[i])
                    if i >= DEPTH:
                        stage2(tiles[i - DEPTH])
                    if nxt and i >= 2 and (i - 2) % 5 == 0:
                        nxt.pop(0)()
                while nxt:
                    nxt.pop(0)()
                if hd % 8 == 7:
                    half = hd // 8
                    out_proj("w_o", lambda k, g: blk(k)[:, tgs(g)], lambda k, g: [MIX[k]], half * 512, 4)
            S.barrier()

        def final_stage(goff):
            YT = [Buf("yt%d" % i) for i in range(8)]
            OS = [Buf("os%d" % i) for i in range(2)]
            RSF = [Buf("rsf%d" % i) for i in range(4)]
            ytv = lambda c: blkf(2 * c, 2)[:, 0:512]
            osv = lambda i: blkf(16 + 2 * i, 2)[:, 0:1024]
            rsf = lambda g: blkf(2 * g, 2)[:, 1024:1536]
            pbs = []
            for g in range(4):
                pb = nb_()
                for c in range(8):
                    q = state["sq"]
                    state["sq"] = (q + 1) % 4
                    S.op("act", L("activation", sqr[:, q, :], xT[:, c, tgs(g)], AF.Square),
                         reads=[XT[c][g]], writes=[SQ[q]])
                    mm(bank(pb), onesb[:], sqr[:, q, :], c == 0, c == 7, [SQ[q], B_const], [PB[pb]], track=True)
                pbs.append(pb)
            for g in range(4):
                S.op("act", L("activation", rsf(g), bank(pbs[g]), AF.Sqrt, scale=1.0 / 1024.0, bias=small[:, 0:1]),
                     reads=[PB[pbs[g]], B_small], writes=[RSF[g]])
                S.op("dve", L("reciprocal", rsf(g), rsf(g)), reads=[RSF[g]], writes=[RSF[g]])
            for g in range(4):
                for c in range(8):
                    eng = "dve"
                    if goff is None:
                        S.op(eng, L("tensor_copy", ytv(c), xT[:, c, tgs(g)]), reads=[XT[c][g]], writes=[YT[c]])
                    else:
                        S.op(eng, L("scalar_tensor_tensor", ytv(c), xT[:, c, tgs(g)], gains[:, goff + c:goff + c + 1],
                                    rsf(g), ALU.mult, ALU.mult), reads=[XT[c][g], RSF[g]], writes=[YT[c]])
                for j in range(4):
                    tt = g * 4 + j
                    oi = tt % 2
                    p0, p1 = nb_(), nb_()
                    for c in range(8):
                        pbx = p0 if c < 4 else p1
                        S.op("pe", L("transpose", bank(pbx)[:, (c % 4) * 128:(c % 4 + 1) * 128], ytv(c)[:, j * 128:(j + 1) * 128], ident[:]),
                            reads=[YT[c], B_const], writes=[PB[pbx]], track=(c % 4 == 3))
                    S.op("act", L("activation", osv(oi)[:, 0:512], bank(p0), AF.Copy),
                         reads=[PB[p0]], writes=[OS[oi]])
                    S.op("act", L("activation", osv(oi)[:, 512:1024], bank(p1), AF.Copy),
                         reads=[PB[p1]], writes=[OS[oi]])
                    S.dma("sp", y_out[tt * 128:(tt + 1) * 128, :], osv(oi), reads=[OS[oi]])

        if KDBG == 10:
            for _ in range(30):
                norm_stage(0)
        elif KDBG == 11:
            norm_stage(0)
            for i in range(40):
                slot, wb = wtile(wreq_cols("w_in", 8, (i % 24) * 128, 128))
                wq = wview(slot, 8, 128)
                pb = proj_fm(wq, wb, 0, hT_rhs, hT_bufs, 8, i % 4)
                S.op("dve", L("tensor_copy", rs_b[:], bank(pb)), reads=[PB[pb]], writes=[B_rsb])
        elif nstage >= 1:
            norm_stage(0)
            diff_phase()
        if nstage >= 2:
            na_phase()
        if nstage >= 3:
            ffn(0, 8)
        if nstage >= 4:
            norm_stage(16)
            mla_phase()
        if nstage >= 5:
            ffn(1, 24)
        final_stage(32 if nstage >= 5 else None)
        S.final_wait("sp")
        S.emit()
    return nc, requests


_CACHE = {}


def _get_program(nstage=99):
    if nstage not in _CACHE:
        _, reqs = build(None, nstage)
        nc, _ = build(reqs, nstage)
        _CACHE[nstage] = (nc, reqs)
    return _CACHE[nstage]


def kernel(**inputs):
    nstage = int(inputs.pop("_nstage", 99))
    cores = inputs.pop("_cores", None)
    shared = _prep_shared(inputs)
    x = np.asarray(inputs["x"], np.float32)
    nb = x.shape[0] if cores is None else cores
    nc, plan = _get_program(nstage)
    dev = {k: shared[k] for k in SHAPES if k != "x"}
    dev["wpack"] = _pack_weights(shared, plan)
    in_maps = []
    for b in range(nb):
        m = dict(dev)
        m["x"] = np.ascontiguousarray(x[b])
        in_maps.append(m)
    res = run_bass_kernel_spmd(nc, in_maps, core_ids=list(range(nb)))
    out = np.stack([np.asarray(r["y"], np.float32) for r in res.results], axis=0)
    return out
```

```python
import math
import os
KDBG = int(os.environ.get("KDBG", "0"))
from contextlib import ExitStack
import numpy as np
import ml_dtypes
import concourse.bass as bass
import concourse.mybir as mybir
from concourse.bass_utils import run_bass_kernel_spmd

F32 = mybir.dt.float32
BF16 = mybir.dt.bfloat16
AF = mybir.ActivationFunctionType
ALU = mybir.AluOpType

T = 2048
D = 1024
DFF = 2816
NFC = 22
EPS = 1e-6
NEG = -30000.0
NSLOT = 4
LOOKAHEAD = 2
NBLK = 21


class Buf:
    __slots__ = ("name", "w", "r", "dsem", "dcnt", "excl")

    def __init__(self, name, excl=False):
        self.name = name
        self.excl = excl
        self.w = None
        self.r = {}
        self.dsem = None
        self.dcnt = 0


class Sched:
    ENG = ["pe", "act", "dve", "pool", "sp"]
    COMPUTE = ["pe", "act", "dve", "pool"]

    def __init__(self, nc, stack):
        self.nc = nc
        self.stack = stack
        self.ops = {e: [] for e in self.ENG}
        self.cnt = {e: 0 for e in self.ENG}
        self.sem = {e: stack.enter_context(nc.semaphore("s_" + e)) for e in self.ENG}
        self.seen = {e: {} for e in self.ENG}
        self.semobj = {e: self.sem[e] for e in self.ENG}
        self.nd = 0
        self.dma_bufs = []

    def _deps(self, eng, reads, writes):
        deps = {}

        def add(d):
            if d is None:
                return
            k, v = d
            if deps.get(k, 0) < v:
                deps[k] = v
        for b in reads:
            add(b.w)
        for b in writes:
            add(b.w)
            for d in b.r.items():
                add(d)
        waits = []
        seen = self.seen[eng]
        for k, v in deps.items():
            if k == "pe" and eng == "pe":
                continue
            if seen.get(k, 0) >= v:
                continue
            seen[k] = v
            waits.append((self.semobj[k], v))
        return waits

    def op(self, eng, fn, reads=(), writes=(), track=True):
        ex = [b for b in reads if b.excl]
        if ex:
            reads = [b for b in reads if not b.excl]
            writes = list(writes) + [b for b in ex if b not in writes]
        waits = self._deps(eng, reads, writes)
        idx = self.cnt[eng] + 1
        if track:
            self.cnt[eng] = idx
        ev = (eng, idx)
        for b in reads:
            if b.r.get(eng, 0) < idx:
                b.r[eng] = idx
        for b in writes:
            b.w = ev
            b.r = {}
        self.ops[eng].append((waits, fn, track, None))

    def dma(self, q, out, in_, reads=(), writes=()):
        waits = self._deps(q, reads, writes)
        bufs = list(writes) + list(reads)
        assert len(bufs) == 1
        b = bufs[0]
        if b.dsem is None:
            self.nd += 1
            key = "d%d" % self.nd
            b.dsem = key
            self.semobj[key] = self.stack.enter_context(self.nc.semaphore(key))
            self.dma_bufs.append(b)
        b.dcnt += 16
        ev = (b.dsem, b.dcnt)
        if writes:
            b.w = ev
            b.r = {}
        else:
            b.r[b.dsem] = b.dcnt
        self.ops[q].append((waits, L("dma_start", out=out, in_=in_), False,
                            (self.semobj[b.dsem], 16)))

    def barrier(self, dma=False):
        for e in self.ENG:
            waits = []
            if dma:
                for b in self.dma_bufs:
                    if self.seen[e].get(b.dsem, 0) < b.dcnt:
                        self.seen[e][b.dsem] = b.dcnt
                        waits.append((self.semobj[b.dsem], b.dcnt))
            for k in self.COMPUTE:
                v = self.cnt[k]
                if v > 0 and self.seen[e].get(k, 0) < v:
                    self.seen[e][k] = v
                    waits.append((self.semobj[k], v))
            if waits:
                self.ops[e].append((waits, None, False, None))

    def final_wait(self, eng):
        waits = [(self.semobj[b.dsem], b.dcnt) for b in self.dma_bufs]
        for k in self.COMPUTE:
            if self.cnt[k] > 0:
                waits.append((self.semobj[k], self.cnt[k]))
        self.ops[eng].append((waits, None, False, None))

    def emit(self):
        nc = self.nc
        handles = {"pe": "tensor", "act": "scalar", "dve": "vector", "pool": "gpsimd", "sp": "sync"}
        with nc.Block() as block:
            for e in self.ENG:
                ops = self.ops[e]
                own = self.sem[e]

                def body(engh, ops=ops, own=own):
                    for waits, fn, track, dinc in ops:
                        for s, v in waits:
                            engh.wait_ge(s, v)
                        if fn is None:
                            continue
                        ins = fn(engh)
                        if dinc is not None:
                            ins.then_inc(dinc[0], dinc[1])
                        elif track:
                            ins.then_inc(own, 1)
                getattr(block, handles[e])(body)


def L(method, *args, **kw):
    return lambda e: getattr(e, method)(*args, **kw)


def _na_tile_lists():
    lists = []
    tid = {}
    for m in range(16):
        if m <= 1:
            kbs = [0, 1, 2, 3]
        elif m >= 14:
            kbs = [12, 13, 14, 15]
        else:
            kbs = [m - 2, m - 1, m, m + 1, m + 2]
        cur = []
        for kb in kbs:
            key = ("int", kb - m) if 2 <= m <= 13 else ("edge", m, kb)
            if key not in tid:
                tid[key] = len(tid)
            cur.append((kb, tid[key]))
        lists.append(cur)
    return lists, tid


def _na_index_tables():
    lists, tid = _na_tile_lists()
    ntile = len(tid)
    dr_i = np.zeros((ntile, 128, 128), np.int64)
    dc_i = np.zeros((ntile, 128, 128), np.int64)
    valid = np.zeros((ntile, 128, 128), bool)
    done = set()
    for m in range(16):
        for kb, t in lists[m]:
            if t in done:
                continue
            done.add(t)
            kk = np.arange(128)
            kr = (2 * kb + kk // 64)[:, None]
            kc = (kk % 64)[:, None]
            qr = (2 * m + kk // 64)[None, :]
            qc = (kk % 64)[None, :]
            rs = np.clip(qr - 4, 0, 24)
            ws = np.clip(qc - 8, 0, 48)
            v = (kr >= rs) & (kr < rs + 8) & (kc >= ws) & (kc < ws + 16)
            dr = np.clip(kr - qr + 7, 0, 14)
            dc = np.clip(kc - qc + 15, 0, 30)
            dr_i[t] = np.broadcast_to(dr, (128, 128))
            dc_i[t] = np.broadcast_to(dc, (128, 128))
            valid[t] = v
    return dr_i, dc_i, valid


def _na_kb_layout():
    lists, tid = _na_tile_lists()
    layout = []
    off = 0
    shared = None
    for kb in range(16):
        ms = [m for m in range(16) if kb in [k for k, _ in lists[m]]]
        tl = [dict(lists[m])[kb] for m in ms]
        if 4 <= kb <= 11:
            if shared is None:
                shared = off
                off += 128 * len(ms)
            o = shared
        else:
            o = off
            off += 128 * len(ms)
        layout.append((kb, o, ms, tl))
    return layout, off


def _const_tables():
    c = {}
    c["ident"] = np.eye(128, dtype=np.float32)
    slopes = 2.0 ** (-8.0 * np.arange(1, 5) / 4)
    i = np.arange(T)
    a = (i // 128).astype(np.float32)
    r = (i % 128).astype(np.float32)
    augq = np.zeros((4, 4, T), np.float32)
    for h in range(4):
        s = slopes[h]
        augq[h, 0] = -s * 128.0 * a
        augq[h, 1] = -s * r
        augq[h, 2] = s
        augq[h, 3] = s
    c["augq"] = augq
    augk = np.zeros((2, 4, T), np.float32)
    for v, sg in enumerate([1.0, -1.0]):
        augk[v, 0] = sg
        augk[v, 1] = sg
        augk[v, 2] = sg * 128.0 * a
        augk[v, 3] = sg * r
    c["augk"] = augk
    jj = np.arange(128)[:, None]
    ii = np.arange(128)[None, :]
    cd = np.zeros((128, 4 * 128), np.float32)
    for h in range(4):
        cd[:, h * 128:(h + 1) * 128] = -2.0 * slopes[h] * np.maximum(jj - ii, 0)
    c["cdiag"] = cd
    inv = (10000.0 ** (-np.arange(0, 32, 2, dtype=np.float32) / np.float32(32))).astype(np.float32)
    ang = (np.arange(T, dtype=np.float32)[:, None] * inv[None, :]).astype(np.float32)
    cos = np.cos(ang.astype(np.float64)).astype(np.float32).T
    sin = np.sin(ang.astype(np.float64)).astype(np.float32).T
    rt = np.zeros((128, T), np.float32)
    rt[64:80] = cos
    rt[80:96] = cos
    rt[96:112] = -sin
    rt[112:128] = sin
    c["ropetab"] = rt
    return c


def _pm(v, nch):
    return np.ascontiguousarray(np.asarray(v, np.float32).reshape(nch, 128).T)


def _prep_shared(inp):
    g = {}
    c = _const_tables()
    g.update(c)
    f = lambda k: np.asarray(inp[k], np.float32)
    gains = np.zeros((128, 48), np.float32)
    gains[:, 0:8] = _pm(f("mix_norm_e")[0], 8)
    gains[:, 8:16] = _pm(f("ffn_norm_g")[0], 8)
    gains[:, 16:24] = _pm(f("mix_norm_o")[0], 8)
    gains[:, 24:32] = _pm(f("ffn_norm_g")[1], 8)
    gains[:, 32:40] = _pm(f("final_norm_g"), 8)
    gains[:, 40:44] = _pm(f("q_norm_g")[0], 4)
    gains[:, 44:46] = _pm(f("kv_norm_g")[0], 2)
    gains[:, 46] = f("diff_subln_g")[0]
    g["gains"] = gains
    cv = np.zeros((128, 2 * NFC * 4), np.float32)
    for l in range(2):
        for k in range(3):
            cv[:, l * 88 + k * 22: l * 88 + (k + 1) * 22] = _pm(f("ffn_conv_w")[l, k], NFC)
        cv[:, l * 88 + 66: l * 88 + 88] = _pm(f("ffn_conv_b")[l], NFC)
    g["convp"] = cv
    lam = np.zeros((128, 256), np.float32)
    for k, nm in enumerate(["diff_lq1", "diff_lk1", "diff_lq2", "diff_lk2"]):
        lam[:, k * 64:(k + 1) * 64] = np.broadcast_to(f(nm)[0][None, :], (128, 64))
    g["lamv"] = lam
    dr_i, dc_i, valid = _na_index_tables()
    rpb = f("na_rpb")[0]
    nb = np.empty((8, dr_i.shape[0], 128, 128), np.float32)
    for h in range(8):
        nb[h] = np.where(valid, rpb[h][dr_i, dc_i], np.float32(NEG))
    layout, ncols = _na_kb_layout()
    nb2 = np.empty((8, 128, ncols), np.float32)
    for kb, o, ms, tl in layout:
        for i, t in enumerate(tl):
            nb2[:, :, o + i * 128:o + (i + 1) * 128] = nb[:, t]
    g["nbias"] = nb2
    g["w_in"] = f("w_in_e")[0]
    g["w_out"] = f("w_out_e")[0]
    g["w_gate"] = f("w_ffn_gate")
    g["w_val"] = f("w_ffn_val")
    g["w_down"] = f("w_ffn_down")
    g["w_dq"] = f("w_dq")[0]
    wuq = f("w_uq")[0]
    cols = []
    for h in range(16):
        b = h * 96
        cols += list(range(b, b + 96)) + list(range(b + 80, b + 96)) + list(range(b + 64, b + 80))
    g["w_uq"] = np.ascontiguousarray(wuq[:, cols])
    wdkv = f("w_dkv")[0]
    wd = np.zeros((1024, 384), np.float32)
    wd[:, 0:256] = wdkv[:, 0:256]
    wd[:, 320:352] = wdkv[:, 256:288]
    wd[:, 352:368] = wdkv[:, 272:288]
    wd[:, 368:384] = wdkv[:, 256:272]
    g["w_dkv"] = wd
    wukv = f("w_ukv")[0].reshape(256, 16, 128)
    g["w_uk"] = np.ascontiguousarray(wukv[:, :, 0:64].reshape(256, 1024))
    g["w_uv"] = np.ascontiguousarray(wukv[:, :, 64:128].reshape(256, 1024))
    g["w_o"] = f("w_o_mla")[0]
    return g


BF16_IN = ()
SHAPES = {
    "x": [T, D], "ident": [128, 128], "augq": [4, 4, T], "augk": [2, 4, T], "cdiag": [128, 512],
    "ropetab": [128, T], "gains": [128, 48], "convp": [128, 176], "lamv": [128, 256],
    "nbias": [8, 128, 5248],
}


def _desc_size(d):
    if d[0] == "gv":
        return 2048
    return d[2] * d[4]


def _pack_weights(g, plan):
    pack = np.zeros((len(plan), 128, 2048), np.float32)
    for i, d in enumerate(plan):
        if d[0] == "gv":
            _, layer, c = d
            wg = g["w_gate"][layer][:, c * 128:(c + 1) * 128].reshape(8, 128, 128)
            wv = g["w_val"][layer][:, c * 128:(c + 1) * 128].reshape(8, 128, 128)
            t = np.concatenate([wg, wv], axis=2)
            pack[i] = t.transpose(1, 0, 2).reshape(128, 2048)
        else:
            _, name, kc, c0, ncols, lidx, r0 = d
            w = g[name] if lidx is None else g[name][lidx]
            t = w[r0:r0 + kc * 128, c0:c0 + ncols].reshape(kc, 128, ncols)
            pack[i, :, 0:kc * ncols] = t.transpose(1, 0, 2).reshape(128, kc * ncols)
    return pack


def build(plan=None, nstage=99):
    nc = bass.Bass("TRN2", target_bir_lowering=False)
    dram = {k: nc.dram_tensor(k, shp, BF16 if k in BF16_IN else F32, kind="ExternalInput").ap() for k, shp in SHAPES.items()}
    if plan is not None:
        dram["wpack"] = nc.dram_tensor("wpack", [len(plan), 128, 2048], F32, kind="ExternalInput").ap()
    y_out = nc.dram_tensor("y", [T, D], F32, kind="ExternalOutput").ap()
    requests = []
    with ExitStack() as st:
        S = Sched(nc, st)
        sb = lambda n, shp, dt: st.enter_context(nc.sbuf_tensor("sb_" + n, shp, dt))
        xT = sb("xT", [128, 8, T], F32)
        hT = sb("hT", [128, 8, T], BF16)
        wring = sb("wring", [128, NSLOT, 2048], BF16)
        AR = sb("arena", [128, NBLK * 2048], BF16)
        ident = sb("ident", [128, 128], F32)
        identb = sb("identb", [128, 128], BF16)
        onesb = sb("onesb", [128, 128], BF16)
        zerob = sb("zerob", [128, 128], BF16)
        gains = sb("gains", [128, 48], F32)
        convp = sb("convp", [128, 176], F32)
        small = sb("small", [128, 16], F32)
        cdiag = sb("cdiag", [128, 512], BF16)
        rs_a = sb("rs_a", [128, 512], F32)
        rs_b = sb("rs_b", [128, 512], F32)
        sqr = sb("sqr", [128, 4, 512], BF16)
        PS = st.enter_context(nc.psum_tensor("PS", [128, 8 * 512], F32))

        XT = [[Buf("xT%d_%d" % (c, g)) for g in range(4)] for c in range(8)]
        HT = [[Buf("hT%d_%d" % (c, g)) for g in range(4)] for c in range(8)]
        WS = [Buf("ws%d" % i) for i in range(NSLOT)]
        PB = [Buf("pb%d" % i, excl=True) for i in range(8)]
        B_const = Buf("const")
        B_small = Buf("small")
        B_rsa, B_rsb = Buf("rsa"), Buf("rsb")
        SQ = [Buf("sq%d" % i) for i in range(4)]

        def bank(i):
            return PS[:, i * 512:(i + 1) * 512]

        def blk(k, n=1):
            return AR[:, k * 2048:(k + n) * 2048]

        def blkf(k, n=2):
            return AR[:, k * 2048:(k + n) * 2048].bitcast(F32)

        state = {"bank": 0, "wi": 0, "wissued": 0, "sq": 0, "sb": 0, "tick": 0}
        pending = []
        lamv = blkf(0, 1)[:, 0:256]

        def nb_():
            b = state["bank"]
            state["bank"] = (b + 1) % 8
            return b

        def tgs(g):
            return slice(g * 512, (g + 1) * 512)

        def wtile(desc):
            i = state["wi"]
            state["wi"] = i + 1
            requests.append(desc)
            if plan is not None:
                hi = min(len(plan), i + 1 + LOOKAHEAD)
                while state["wissued"] < hi:
                    j = state["wissued"]
                    slot = j % NSLOT
                    X = _desc_size(plan[j])
                    S.dma("pool", wring[:, slot, 0:X], dram["wpack"][j][:, 0:X], writes=[WS[slot]])
                    state["wissued"] = j + 1
            return wring[:, i % NSLOT, :], WS[i % NSLOT]

        def wreq_cols(name, kc, c0, ncols, lidx=None, r0=0):
            return ("cols", name, kc, c0, ncols, lidx, r0)

        def wview(slot, kc, ncols):
            return slot[:, 0:kc * ncols].rearrange("p (k n) -> p k n", k=kc)

        S.dma("sp", ident[:], dram["ident"], writes=[B_const])
        gb_l = Buf("gains_l")
        S.dma("sp", gains[:], dram["gains"], writes=[gb_l])
        cb1, cb2, cb3 = Buf("c1"), Buf("c2"), Buf("c3")
        S.dma("sp", convp[:], dram["convp"], writes=[cb1])
        S.dma("sp", lamv, dram["lamv"], writes=[cb2])
        S.dma("pool", cdiag[:], dram["cdiag"], writes=[cb3])
        S.op("dve", L("tensor_copy", identb[:], ident[:]), reads=[B_const], writes=[B_const])
        S.op("dve", L("memset", onesb[:], 1.0), writes=[B_const])
        S.op("dve", L("memset", zerob[:], 0.0), writes=[B_const])
        S.op("dve", L("memset", small[:, 0:1], EPS), writes=[B_small])
        lam_init = 0.8 - 0.6 * math.exp(-0.3 * 0)
        S.op("dve", L("tensor_tensor", lamv[:, 0:64], lamv[:, 0:64], lamv[:, 64:128], ALU.mult),
             reads=[cb2], writes=[cb2])
        S.op("dve", L("tensor_tensor", lamv[:, 128:192], lamv[:, 128:192], lamv[:, 192:256], ALU.mult),
             reads=[cb2], writes=[cb2])
        S.op("dve", L("reduce_sum", small[:, 4:5], lamv[:, 0:64], mybir.AxisListType.X),
             reads=[cb2], writes=[B_small])
        S.op("dve", L("reduce_sum", small[:, 5:6], lamv[:, 128:192], mybir.AxisListType.X),
             reads=[cb2], writes=[B_small])
        S.op("act", L("activation", small[:, 6:8], small[:, 4:6], AF.Exp), reads=[B_small], writes=[B_small])
        S.op("dve", L("scalar_tensor_tensor", small[:, 1:2], small[:, 7:8], -lam_init, small[:, 6:7],
                                                     ALU.add, ALU.subtract), reads=[B_small], writes=[B_small])
        S.op("dve", L("tensor_scalar", small[:, 2:3], gains[:, 46:47], 1.0 - lam_init, None, ALU.mult),
             reads=[B_small, gb_l], writes=[B_small])
        S.barrier(dma=True)

        def mm(out, lhsT, rhs, start, stop, reads, writes, track=None, sgc=False):
            if track is None:
                track = stop
            if sgc:
                S.op("pe", L("matmul", out, lhsT, rhs, start=start, stop=stop, skip_group_check=True),
                     reads=reads, writes=writes, track=track)
            else:
                S.op("pe", L("matmul", out, lhsT, rhs, start=start, stop=stop), reads=reads, writes=writes,
                     track=track)

        def rstd_from_psum(pb, nfeat, dst, dstbuf):
            S.op("act", L("activation", rs_a[:], bank(pb), AF.Sqrt, scale=1.0 / nfeat, bias=small[:, 0:1]),
                 reads=[PB[pb], B_small], writes=[B_rsa])
            S.op("dve", L("reciprocal", dst, rs_a[:]), reads=[B_rsa], writes=[dstbuf])

        def norm_stage(goff):
            for g in range(4):
                pb = nb_()
                for c in range(8):
                    q = state["sq"]
                    state["sq"] = (q + 1) % 4
                    S.op("act", L("activation", sqr[:, q, :], xT[:, c, tgs(g)], AF.Square),
                         reads=[XT[c][g]], writes=[SQ[q]])
                    mm(bank(pb), onesb[:], sqr[:, q, :], c == 0, c == 7, [SQ[q], B_const], [PB[pb]], track=True)
                rstd_from_psum(pb, 1024.0, rs_b[:], B_rsb)
                for c in range(8):
                    S.op("dve", L("scalar_tensor_tensor", hT[:, c, tgs(g)], xT[:, c, tgs(g)], gains[:, goff + c:goff + c + 1], rs_b[:],
                        ALU.mult, ALU.mult), reads=[XT[c][g], B_rsb], writes=[HT[c][g]])

        def proj_fm(wv, wbuf, col0, rhs_fn, rhs_bufs_fn, nk, g, M=128):
            pb = nb_()
            for k in range(nk):
                mm(bank(pb)[0:M, :], wv[:, k, col0:col0 + M], rhs_fn(k, g), k == 0, k == nk - 1,
                   [wbuf] + rhs_bufs_fn(k, g), [PB[pb]])
            return pb

        hT_rhs = lambda k, g: hT[:, k, tgs(g)]
        hT_bufs = lambda k, g: [HT[k][g]]

        def resid_add(pb, n, g, eng="dve"):
            S.op(eng, L("tensor_tensor", xT[:, n, tgs(g)], bank(pb), xT[:, n, tgs(g)], ALU.add),
                 reads=[PB[pb], XT[n][g]], writes=[XT[n][g]])

        def out_proj(name, mix_fn, mix_bufs, krow0, nkc):
            assert nkc == 4
            tiles_ = []
            for nt in range(2):
                slot, wb = wtile(wreq_cols(name, nkc, nt * 512, 512, r0=krow0))
                tiles_.append((wview(slot, nkc, 512), wb))
            for g in range(4):
                for n in range(8):
                    wv, wb = tiles_[n // 4]
                    pb = proj_fm(wv, wb, (n % 4) * 128, mix_fn, mix_bufs, nkc, g)
                    resid_add(pb, n, g)

        XS = [Buf("xs%d" % i) for i in range(16)]
        for tt in range(16):
            S.dma("sp", blkf(tt, 1), dram["x"][tt * 128:(tt + 1) * 128, :], writes=[XS[tt]])
        for g in range(4):
            for c in range(8):
                pb = nb_()
                for j in range(4):
                    tt = g * 4 + j
                    S.op("pe", L("transpose", bank(pb)[:, j * 128:(j + 1) * 128], blkf(tt, 1)[:, c * 128:(c + 1) * 128], ident[:]),
                        reads=[XS[tt], B_const], writes=[PB[pb]], track=(j == 3))
                eng = "act" if c % 2 == 0 else "dve"
                if eng == "act":
                    S.op("act", L("activation", xT[:, c, tgs(g)], bank(pb), AF.Copy),
                         reads=[PB[pb]], writes=[XT[c][g]])
                else:
                    S.op("dve", L("tensor_copy", xT[:, c, tgs(g)], bank(pb)),
                         reads=[PB[pb]], writes=[XT[c][g]])
        S.barrier()

        def ffn(layer, goff):
            norm_stage(goff)
            U = [Buf("u%d" % i) for i in range(11)]
            TB = [Buf("tb%d" % i) for i in range(2)]
            cp = layer * 88
            for rnd in range(2):
                for cc in range(11):
                    c = rnd * 11 + cc

                    slot, wb = wtile(("gv", layer, c))
                    wv = wview(slot, 8, 256)
                    for g in range(4):
                        for k in range(8):
                            mm(bank(g), wv[:, k, 0:128], hT[:, k, tgs(g)], k == 0, k == 7, [wb, HT[k][g]], [PB[g]])
                    for g in range(4):
                        for k in range(8):
                            mm(bank(4 + g), wv[:, k, 128:256], hT[:, k, tgs(g)], k == 0, k == 7,
                               [wb, HT[k][g]], [PB[4 + g]])
                    tb = cc % 2
                    tv = blkf(11 + 2 * tb)
                    G = PS[:, 0:2048]
                    gb = [PB[0], PB[1], PB[2], PB[3]]
                    w0 = convp[:, cp + c:cp + c + 1]
                    w1 = convp[:, cp + 22 + c:cp + 22 + c + 1]
                    w2 = convp[:, cp + 44 + c:cp + 44 + c + 1]
                    bb = convp[:, cp + 66 + c:cp + 66 + c + 1]
                    S.op("act", L("activation", tv, G, AF.Identity, scale=w1, bias=bb),
                         reads=gb + [cb1], writes=[TB[tb]])
                    S.op("dve", L("scalar_tensor_tensor", tv[:, 1:T], PS[:, 0:T - 1], w0, tv[:, 1:T], ALU.mult, ALU.add),
                        reads=gb + [TB[tb]], writes=[TB[tb]])
                    S.op("dve", L("scalar_tensor_tensor", tv[:, 0:T - 1], PS[:, 1:T], w2, tv[:, 0:T - 1], ALU.mult, ALU.add),
                        reads=gb + [TB[tb]], writes=[TB[tb]])
                    S.op("act", L("activation", tv, tv, AF.Gelu), reads=[TB[tb]], writes=[TB[tb]])
                    for hh in range(2):
                        S.op("dve", L("tensor_tensor", blk(cc)[:, hh * 1024:(hh + 1) * 1024], tv[:, hh * 1024:(hh + 1) * 1024],
                            PS[:, 2048 + hh * 1024:2048 + (hh + 1) * 1024], ALU.mult),
                            reads=[TB[tb], PB[4 + 2 * hh], PB[5 + 2 * hh]], writes=[U[cc]])
                for n in range(8):
                    slot, wb = wtile(wreq_cols("w_down", 11, n * 128, 128, lidx=layer, r0=rnd * 1408))
                    wv = wview(slot, 11, 128)
                    for g in range(4):
                        pb = proj_fm(wv, wb, 0, lambda k, g: blk(k)[:, tgs(g)], lambda k, g: [U[k]], 11, g)
                        resid_add(pb, n, g)
            S.barrier()

        def softmax_epilogue_diff(h, I, pO, pD, MIX):
            r1, r2, t2 = blkf(13, 2)[:, 0:512], blkf(13, 2)[:, 512:1024], blkf(13, 2)[:, 1024:1536]
            t1 = blkf(15, 2)[:, I * 512:(I + 1) * 512]
            EB = state.setdefault("EB", [Buf("eb%d" % i) for i in range(3)])
            A1 = state.setdefault("A1", [Buf("a1_%d" % i) for i in range(4)])
            S.op("dve", L("tensor_copy", t1, bank(pO[0])), reads=[PB[pO[0]]], writes=[A1[I]])
            S.op("dve", L("tensor_copy", r1, bank(pD[0])), reads=[PB[pD[0]]], writes=[EB[0]])
            S.op("dve", L("tensor_copy", t2, bank(pO[1])), reads=[PB[pO[1]]], writes=[EB[2]])
            S.op("dve", L("tensor_copy", r2, bank(pD[1])), reads=[PB[pD[1]]], writes=[EB[1]])
            S.op("dve", L("reciprocal", r1, r1), reads=[EB[0]], writes=[EB[0]])
            S.op("dve", L("reciprocal", r2, r2), reads=[EB[1]], writes=[EB[1]])
            S.op("dve", L("tensor_tensor", t1, t1, r1, ALU.mult), reads=[EB[0], A1[I]], writes=[A1[I]])
            S.op("dve", L("tensor_tensor", t2, t2, r2, ALU.mult), reads=[EB[1], EB[2]], writes=[EB[2]])
            S.op("dve", L("scalar_tensor_tensor", t1, t2, small[:, 1:2], t1, ALU.mult, ALU.add),
                 reads=[A1[I], EB[2], B_small], writes=[A1[I]])

        def subln_head(h, MIX):
            A1 = state["A1"]
            RS = state.setdefault("RS", [Buf("rs_%d" % i) for i in range(4)])
            pbs = []
            for I in range(4):
                t1 = blkf(15, 2)[:, I * 512:(I + 1) * 512]
                S.op("act", L("activation", sqr[:, I, :], t1, AF.Square), reads=[A1[I]], writes=[SQ[I]])
                pb = 4 + state["sb"] % 4
                state["sb"] += 1
                mm(bank(pb), onesb[:], sqr[:, I, :], True, True, [SQ[I], B_const], [PB[pb]])
                pbs.append(pb)
            for I in range(4):
                rsd = blkf(17, 2)[:, I * 512:(I + 1) * 512]
                S.op("act", L("activation", rsd, bank(pbs[I]), AF.Sqrt, scale=1.0 / 128.0, bias=small[:, 0:1]),
                     reads=[PB[pbs[I]], B_small], writes=[RS[I]])
            for I in range(4):
                t1 = blkf(15, 2)[:, I * 512:(I + 1) * 512]
                rsd = blkf(17, 2)[:, I * 512:(I + 1) * 512]
                S.op("dve", L("reciprocal", rsd, rsd), reads=[RS[I]], writes=[RS[I]])
                S.op("dve", L("scalar_tensor_tensor", blk(h)[:, tgs(I)], t1, small[:, 2:3], rsd, ALU.mult, ALU.mult),
                     reads=[A1[I], RS[I], B_small], writes=[MIX[h]])

        def diff_phase():
            MIX = [Buf("mix%d" % i) for i in range(4)]
            QB = [Buf("dq%d" % i) for i in range(2)]
            KB = [[Buf("dk%d%d" % (i, v)) for v in range(2)] for i in range(2)]
            VB = Buf("dv")
            PT = [Buf("pt%d" % i) for i in range(8)]
            Qt = [blk(4), blk(5)]
            Kt = [[blk(6), blk(7)], [blk(8), blk(9)]]
            Vt = blk(10).rearrange("p (t e) -> p t e", t=16)
            ptv = lambda i: blk(11, 2)[:, i * 512:(i + 1) * 512]
            pti = [0]
            for n in range(2):
                for v in range(2):
                    S.dma("pool", Kt[n][v][64:68, :], dram["augk"][v], writes=[KB[n][v]])
            for h in range(int(os.environ.get("KH0", "0")), int(os.environ.get("KH1", "4")) if KDBG in (0, 5) else 1):
                if KDBG == 1:
                    break
                if os.environ.get("KBAR", "0") == "1":
                    S.barrier()
                slot, wb = wtile(wreq_cols("w_in", 8, h * 128, 128))
                wq = wview(slot, 8, 128)
                for n in range(2):
                    S.dma("pool", Qt[n][64:68, :], dram["augq"][h], writes=[QB[n]])
                for g in range(4):
                    pb = proj_fm(wq, wb, 0, hT_rhs, hT_bufs, 8, g)
                    S.op("dve", L("tensor_scalar", Qt[0][0:64, tgs(g)], bank(pb)[0:64, :], 0.125, None, ALU.mult),
                         reads=[PB[pb]], writes=[QB[0]])
                    S.op("dve", L("tensor_scalar", Qt[1][0:64, tgs(g)], bank(pb)[64:128, :], 0.125, None, ALU.mult),
                         reads=[PB[pb]], writes=[QB[1]])
                slot, wb = wtile(wreq_cols("w_in", 8, 512 + h * 128, 128))
                wk = wview(slot, 8, 128)
                for g in range(4):
                    pb = proj_fm(wk, wb, 0, hT_rhs, hT_bufs, 8, g)
                    for n in range(2):
                        S.op("act", L("activation", Kt[n][0][0:64, tgs(g)], bank(pb)[64 * n:64 * n + 64, :], AF.Copy),
                             reads=[PB[pb]], writes=[KB[n][0]])
                        S.op("dve", L("tensor_copy", Kt[n][1][0:64, tgs(g)], bank(pb)[64 * n:64 * n + 64, :]),
                             reads=[PB[pb]], writes=[KB[n][1]])
                slot, wb = wtile(wreq_cols("w_in", 8, 1024 + h * 128, 128))
                wvv = wview(slot, 8, 128)
                for g in range(4):
                    pb = nb_()
                    for j in range(4):
                        tt = g * 4 + j
                        for k in range(8):
                            mm(bank(pb)[:, j * 128:(j + 1) * 128], hT[:, k, tt * 128:(tt + 1) * 128], wvv[:, k, :],
                               k == 0, k == 7, [wb, HT[k][g]], [PB[pb]], track=(k == 7 and j == 3))
                    S.op("act", L("activation", Vt[:, g * 4:(g + 1) * 4, :], bank(pb).rearrange("p (t e) -> p t e", t=4), AF.Copy),
                        reads=[PB[pb]], writes=[VB])
                pO, pD = [0, 1], [2, 3]
                tiles = [(I, J, n) for I in range(4) for J in range(16) for n in range(2)]
                info = {}

                def stage1(t):
                    I, J, n = t
                    ps = 4 + state["sb"] % 4
                    state["sb"] += 1
                    if J < 4 * I or J >= 4 * I + 4:
                        v = 0 if J < 4 * I else 1
                        mm(bank(ps), Kt[n][v][0:68, J * 128:(J + 1) * 128], Qt[n][0:68, tgs(I)], True, True,
                           [KB[n][v], QB[n]], [PB[ps]])
                    else:
                        for b in range(4):
                            gb_ = 4 * I + b
                            v = 0 if gb_ >= J else 1
                            qs = slice(gb_ * 128, (gb_ + 1) * 128)
                            osl = bank(ps)[:, b * 128:(b + 1) * 128]
                            diag = gb_ == J
                            mm(osl, Kt[n][v][0:68, J * 128:(J + 1) * 128], Qt[n][0:68, qs], True, not diag,
                               [KB[n][v], QB[n]], [PB[ps]], track=(b == 3 and not diag))
                            if diag:
                                mm(osl, identb[:], cdiag[:, h * 128:(h + 1) * 128], False, True,
                                   [B_const, cb3], [PB[ps]], track=(b == 3))
                    pi = pti[0]
                    pti[0] = (pi + 1) % 8
                    S.op("act", L("activation", ptv(pi), bank(ps), AF.Exp), reads=[PB[ps]], writes=[PT[pi]])
                    info[t] = pi

                def stage2(t):
                    I, J, n = t
                    pi = info.pop(t)
                    mm(bank(pO[n]), Vt[:, J, :], ptv(pi), J == 0, J == 15, [VB, PT[pi]], [PB[pO[n]]], track=True)
                    mm(bank(pD[n]), onesb[:], ptv(pi), J == 0, J == 15, [B_const, PT[pi]], [PB[pD[n]]], track=True)
                    if J == 15 and n == 1:
                        softmax_epilogue_diff(h, I, pO, pD, MIX)
                        if I == 3:
                            pending.append((state["tick"] + 24, lambda h=h: subln_head(h, MIX)))

                DEPTH = 2
                for i in range(len(tiles) + DEPTH):
                    state["tick"] += 1
                    while pending and pending[0][0] <= state["tick"]:
                        pending.pop(0)[1]()
                    if i < len(tiles):
                        stage1(tiles[i])
                    if i >= DEPTH:
                        stage2(tiles[i - DEPTH])
            while pending:
                pending.pop(0)[1]()
            if KDBG == 0:
                out_proj("w_out", lambda k, g: blk(k)[:, tgs(g)], lambda k, g: [MIX[k]], 0, 4)
            S.barrier()

        def na_phase():
            lists, _ = _na_tile_lists()
            MIX = [Buf("nmix%d" % i) for i in range(4)]
            NQb, NKb = Buf("nq"), Buf("nk")
            VAb = [Buf("va0"), Buf("va1")]
            NBb = Buf("nbias")
            PT = [Buf("npt%d" % i) for i in range(6)]
            RD = [Buf("nrd%d" % i) for i in range(2)]
            NQ, NK = blk(4), blk(5)
            VA = [blk(6).rearrange("p (t e) -> p t e", t=16), blk(7).rearrange("p (t e) -> p t e", t=16)]
            NB2 = blk(15, 6)
            layout, _nc = _na_kb_layout()
            ptv = lambda i: blk(8, 3)[:, i * 768:(i + 1) * 768]
            rdv = lambda i: blkf(13, 2)[:, i * 512:(i + 1) * 512]
            for a in range(2):
                S.op("dve", L("memset", VA[a][:, :, 64:128], 1.0), writes=[VAb[a]])
            pti = [0]
            for j in range(4):
                slot, wb = wtile(wreq_cols("w_in", 8, 1536 + j * 128, 128))
                wq = wview(slot, 8, 128)
                for g in range(4):
                    pb = proj_fm(wq, wb, 0, hT_rhs, hT_bufs, 8, g)
                    S.op("dve", L("tensor_scalar", NQ[:, tgs(g)], bank(pb), 0.125, None, ALU.mult),
                         reads=[PB[pb]], writes=[NQb])
                slot, wb = wtile(wreq_cols("w_in", 8, 2048 + j * 128, 128))
                wk = wview(slot, 8, 128)
                for g in range(4):
                    pb = proj_fm(wk, wb, 0, hT_rhs, hT_bufs, 8, g)
                    S.op("dve", L("tensor_copy", NK[:, tgs(g)], bank(pb)), reads=[PB[pb]], writes=[NKb])
                slot, wb = wtile(wreq_cols("w_in", 8, 2560 + j * 128, 128))
                wvv = wview(slot, 8, 128)
                for g in range(4):
                    pb = nb_()
                    for jj in range(4):
                        tt = g * 4 + jj
                        for k in range(8):
                            mm(bank(pb)[:, jj * 128:(jj + 1) * 128], hT[:, k, tt * 128:(tt + 1) * 128], wvv[:, k, :],
                               k == 0, k == 7, [wb, HT[k][g]], [PB[pb]], track=(k == 7 and jj == 3))
                    pv = bank(pb).rearrange("p (t e) -> p t e", t=4)
                    S.op("act", L("activation", VA[0][:, g * 4:(g + 1) * 4, 0:64], pv[:, :, 0:64], AF.Copy),
                         reads=[PB[pb]], writes=[VAb[0]])
                    S.op("dve", L("tensor_copy", VA[1][:, g * 4:(g + 1) * 4, 0:64], pv[:, :, 64:128]),
                         reads=[PB[pb]], writes=[VAb[1]])
                for a in range(2):
                    S.dma("pool", NB2[:, a * 5248:(a + 1) * 5248], dram["nbias"][2 * j + a], writes=[NBb])
                for a in range(2):
                    p0 = 64 * a
                    nbase = a * 5248
                    for b4 in range(4):
                        mm(bank(b4), zerob[:], NK[:, b4 * 512:(b4 + 1) * 512], True, True, [B_const, NKb], [PB[b4]], sgc=True)
                    info = {}

                    def stage1(kbi, a=a, p0=p0, nbase=nbase):
                        kb, o, ms, tl = layout[kbi]
                        W = 128 * len(ms)
                        q0 = 128 * ms[0]
                        b0 = 4 + 2 * (state["sb"] % 2)
                        state["sb"] += 1
                        pieces = [(0, min(W, 512))] + ([(512, W)] if W > 512 else [])
                        for (c0, c1) in pieces:
                            bb = b0 + c0 // 512
                            osl = PS[:, b0 * 512 + c0:b0 * 512 + c1]
                            mm(osl, NK[p0:p0 + 64, kb * 128:(kb + 1) * 128], NQ[p0:p0 + 64, q0 + c0:q0 + c1],
                               True, False, [NKb, NQb], [PB[bb]], track=False)
                            mm(osl, identb[:], NB2[:, nbase + o + c0:nbase + o + c1], False, True,
                               [B_const, NBb], [PB[bb]], track=True)
                        pi = pti[0]
                        pti[0] = (pi + 1) % 3
                        rb = [PB[b0]] + ([PB[b0 + 1]] if W > 512 else [])
                        S.op("act", L("activation", ptv(pi)[:, 0:W], PS[:, b0 * 512:b0 * 512 + W], AF.Exp),
                             reads=rb, writes=[PT[pi]])
                        info[kbi] = pi

                    def stage2(kbi, a=a, p0=p0, j=j):
                        kb, o, ms, tl = layout[kbi]
                        W = 128 * len(ms)
                        q0 = 128 * ms[0]
                        pi = info.pop(kbi)
                        g0 = q0
                        while g0 < q0 + W:
                            g1 = min(q0 + W, (g0 // 512 + 1) * 512)
                            mm(PS[:, g0:g1], VA[a][:, kb, :], ptv(pi)[:, g0 - q0:g1 - q0], False, True,
                               [VAb[a], PT[pi]], [PB[g0 // 512]], track=True, sgc=True)
                            g0 = g1
                        if kbi == 15:
                            ev = blkf(13, 2)
                            for b4 in range(4):
                                eng4 = "dve" if b4 % 2 == 0 else "act"
                                if eng4 == "dve":
                                    S.op("dve", L("tensor_copy", ev[:, b4 * 512:(b4 + 1) * 512], bank(b4)), reads=[PB[b4]], writes=[RD[0]])
                                else:
                                    S.op("act", L("activation", ev[:, b4 * 512:(b4 + 1) * 512], bank(b4), AF.Copy), reads=[PB[b4]], writes=[RD[0]])
                            rd = blkf(11, 2)
                            for g in range(4):
                                S.op("dve", L("reciprocal", rd[0:64, tgs(g)], ev[64:128, tgs(g)]), reads=[RD[0]], writes=[RD[1]])
                            for g in range(4):
                                S.op("dve", L("tensor_tensor", blk(j)[p0:p0 + 64, tgs(g)], ev[0:64, tgs(g)], rd[0:64, tgs(g)], ALU.mult),
                                     reads=[RD[0], RD[1]], writes=[MIX[j]])

                    DEPTH = 1
                    for i in range(16 + DEPTH):
                        if i < 16:
                            stage1(i)
                        if i >= DEPTH:
                            stage2(i - DEPTH)
            out_proj("w_out", lambda k, g: blk(k)[:, tgs(g)], lambda k, g: [MIX[k]], 512, 4)
            S.barrier()

        def mla_phase():
            CQ = [Buf("cq%d" % i) for i in range(4)]
            CKV = [Buf("ckv%d" % i) for i in range(2)]
            KRb = Buf("kr")
            RTb = Buf("ropetab")
            QHb = [Buf("qh0"), Buf("qh1")]
            KHb = [Buf("kh0"), Buf("kh1")]
            VAb = [Buf("mva0"), Buf("mva1")]
            MIX = [Buf("mmix%d" % i) for i in range(4)]
            PT = [Buf("mpt%d" % i) for i in range(8)]
            TMP = [Buf("mtmp%d" % i) for i in range(2)]
            cqv = lambda c: blk(4 + c)
            ckvv = lambda c: blk(8 + c)
            KR = blk(10)
            rt = blkf(11, 2)
            QH = [blk(13), blk(14)]
            KH = [blk(15), blk(16)]
            VA = [blk(17).rearrange("p (t e) -> p t e", t=16), blk(18).rearrange("p (t e) -> p t e", t=16)]
            ptv = lambda i: blk(10 if i < 4 else 19)[:, (i % 4) * 512:(i % 4 + 1) * 512]
            tmpv = lambda i: blkf(20, 1)[:, i * 512:(i + 1) * 512]
            scale = 96.0 ** -0.5
            S.dma("sp", rt, dram["ropetab"], writes=[RTb])
            for a in range(2):
                S.op("dve", L("memset", VA[a][:, :, 64:128], 1.0), writes=[VAb[a]])

            def rope(dst, dstbuf, pb, g):
                S.op("dve", L("tensor_tensor", tmpv(0)[64:96, :], bank(pb)[64:96, :], rt[64:96, tgs(g)], ALU.mult),
                     reads=[PB[pb], RTb], writes=[TMP[0]])
                S.op("dve", L("tensor_tensor", tmpv(1)[64:96, :], bank(pb)[96:128, :], rt[96:128, tgs(g)], ALU.mult),
                     reads=[PB[pb], RTb], writes=[TMP[1]])
                S.op("dve", L("tensor_tensor", dst[64:96, tgs(g)], tmpv(0)[64:96, :], tmpv(1)[64:96, :], ALU.add),
                     reads=[TMP[0], TMP[1]], writes=[dstbuf])

            def latent(wtiles, nch, goff, dstv, dstb, nfeat, extra=None):
                for g in range(4):
                    pbs = []
                    for (wv, wb, ncol) in wtiles:
                        for cc in range(ncol // 128):
                            pbs.append(proj_fm(wv, wb, cc * 128, hT_rhs, hT_bufs, 8, g))
                    pss = nb_()
                    while pss in pbs:
                        pss = nb_()
                    for c in range(nch):
                        q = state["sq"]
                        state["sq"] = (q + 1) % 4
                        S.op("act", L("activation", sqr[:, q, :], bank(pbs[c]), AF.Square),
                             reads=[PB[pbs[c]]], writes=[SQ[q]])
                        mm(bank(pss), onesb[:], sqr[:, q, :], c == 0, c == nch - 1, [SQ[q], B_const], [PB[pss]], track=True)
                    rstd_from_psum(pss, nfeat, rs_b[:], B_rsb)
                    for c in range(nch):
                        S.op("dve", L("scalar_tensor_tensor", dstv(c)[:, tgs(g)], bank(pbs[c]), gains[:, goff + c:goff + c + 1], rs_b[:], ALU.mult, ALU.mult),
                            reads=[PB[pbs[c]], B_rsb], writes=[dstb[c]])
                    if extra is not None:
                        extra(pbs[nch], g)

            s0, b0 = wtile(wreq_cols("w_dq", 8, 0, 256))
            s1, b1 = wtile(wreq_cols("w_dq", 8, 256, 256))
            latent([(wview(s0, 8, 256), b0, 256), (wview(s1, 8, 256), b1, 256)], 4, 40, cqv, CQ, 512.0)
            s0, b0 = wtile(wreq_cols("w_dkv", 8, 0, 256))
            s1, b1 = wtile(wreq_cols("w_dkv", 8, 256, 128))
            latent([(wview(s0, 8, 256), b0, 256), (wview(s1, 8, 128), b1, 128)], 2, 44, ckvv, CKV, 256.0,
                   extra=lambda pb, g: rope(KH[0], KHb[0], pb, g))
            S.op("dve", L("tensor_copy", KH[1][64:96, :], KH[0][64:96, :]), reads=[KHb[0]], writes=[KHb[1]])

            def sbank():
                ps = 4 + state["sb"] % 4
                state["sb"] += 1
                return ps

            def ibank():
                state["ib"] = state.get("ib", 0) + 1
                return 2 + state["ib"] % 2

            def head_items(hd):
                s_ = hd % 2
                st = {}
                items = []

                def q_item(g):
                    if g == 0:
                        sl, st["bq"] = wtile(wreq_cols("w_uq", 4, hd * 128, 128))
                        st["wq"] = wview(sl, 4, 128)
                    pb = ibank()
                    for k in range(4):
                        mm(bank(pb), st["wq"][:, k, :], cqv(k)[:, tgs(g)], k == 0, k == 3, [st["bq"], CQ[k]], [PB[pb]])
                    S.op("act", L("activation", QH[s_][0:64, tgs(g)], bank(pb)[0:64, :], AF.Copy),
                         reads=[PB[pb]], writes=[QHb[s_]])
                    rope(QH[s_], QHb[s_], pb, g)

                def k_item(g):
                    if g == 0:
                        sl, st["bk"] = wtile(wreq_cols("w_uk", 2, hd * 64, 64))
                        st["wk"] = wview(sl, 2, 64)
                    pb = ibank()
                    for k in range(2):
                        mm(bank(pb)[0:64, :], st["wk"][:, k, :], ckvv(k)[:, tgs(g)], k == 0, k == 1, [st["bk"], CKV[k]], [PB[pb]])
                    S.op("dve", L("tensor_copy", KH[s_][0:64, tgs(g)], bank(pb)[0:64, :]), reads=[PB[pb]], writes=[KHb[s_]])

                def v_item(g):
                    if g == 0:
                        sl, st["bv"] = wtile(wreq_cols("w_uv", 2, hd * 64, 64))
                        st["wv"] = wview(sl, 2, 64)
                    pb = ibank()
                    for jj in range(4):
                        tt = g * 4 + jj
                        for k in range(2):
                            mm(bank(pb)[:, jj * 64:(jj + 1) * 64], ckvv(k)[:, tt * 128:(tt + 1) * 128],
                               st["wv"][:, k, :], k == 0, k == 1, [st["bv"], CKV[k]], [PB[pb]],
                               track=(k == 1 and jj == 3))
                    pv = bank(pb)[:, 0:256].rearrange("p (t e) -> p t e", t=4)
                    S.op("dve", L("tensor_copy", VA[s_][:, g * 4:(g + 1) * 4, 0:64], pv), reads=[PB[pb]], writes=[VAb[s_]])

                for g in range(4):
                    items.append(lambda g=g: q_item(g))
                for g in range(4):
                    items.append(lambda g=g: k_item(g))
                for g in range(4):
                    items.append(lambda g=g: v_item(g))
                return items

            pti = state.setdefault("mpti", [0])
            for it in head_items(0):
                it()
            info = {}

            def stage1(t):
                hd, I, J = t
                s_ = hd % 2
                ps = sbank()
                mm(bank(ps), KH[s_][0:96, J * 128:(J + 1) * 128], QH[s_][0:96, tgs(I)], True, True,
                   [KHb[s_], QHb[s_]], [PB[ps]])
                pi = pti[0]
                pti[0] = (pi + 1) % 8
                S.op("act", L("activation", ptv(pi), bank(ps), AF.Exp, scale=scale),
                     reads=[PB[ps]], writes=[PT[pi]])
                info[t] = pi

            def stage2(t):
                hd, I, J = t
                s_ = hd % 2
                pr = (hd // 2) % 4
                a = hd % 2
                pi = info.pop(t)
                po = I % 2
                mm(bank(po), VA[s_][:, J, :], ptv(pi), J == 0, J == 15, [VAb[s_], PT[pi]], [PB[po]], track=True)
                if J == 15:
                    rsv = rs_a if po == 0 else rs_b
                    rsb_ = B_rsa if po == 0 else B_rsb
                    S.op("dve", L("reciprocal", rsv[0:64, :], bank(po)[64:128, :]),
                         reads=[PB[po]], writes=[rsb_])
                    S.op("dve", L("tensor_tensor", blk(pr)[64 * a:64 * a + 64, tgs(I)], bank(po)[0:64, :], rsv[0:64, :], ALU.mult),
                         reads=[PB[po], rsb_], writes=[MIX[pr]])

            DEPTH = 3
            for half in range(2):
                tiles = [(hd, I, J) for hd in range(half * 8, half * 8 + 8) for I in range(4) for J in range(16)]
                nxt = []
                for i in range(len(tiles) + DEPTH):
                    local = -1
                    if i < len(tiles):
                        hd, I, J = tiles[i]
                        local = I * 16 + J
                        if local == 0:
                            while nxt:
                                nxt.pop(0)()
                            nxt = head_items(hd + 1) if hd < 15 else []
                        stage1(tiles[i])
                    if i >= DEPTH:
                        stage2(tiles[i - DEPTH])
                    if nxt and local >= 2 and (local - 2) % 5 == 0:
                        nxt.pop(0)()
                while nxt:
                    nxt.pop(0)()
                out_proj("w_o", lambda k, g: blk(k)[:, tgs(g)], lambda k, g: [MIX[k]], half * 512, 4)
            S.barrier()

        def final_stage(goff):
            YT = [Buf("yt%d" % i) for i in range(8)]
            OS = [Buf("os%d" % i) for i in range(2)]
            RSF = [Buf("rsf%d" % i) for i in range(4)]
            ytv = lambda c: blkf(2 * c, 2)[:, 0:512]
            osv = lambda i: blkf(16 + 2 * i, 2)[:, 0:1024]
            rsf = lambda g: blkf(2 * g, 2)[:, 1024:1536]
            pbs = []
            for g in range(4):
                pb = nb_()
                for c in range(8):
                    q = state["sq"]
                    state["sq"] = (q + 1) % 4
                    S.op("act", L("activation", sqr[:, q, :], xT[:, c, tgs(g)], AF.Square),
                         reads=[XT[c][g]], writes=[SQ[q]])
                    mm(bank(pb), onesb[:], sqr[:, q, :], c == 0, c == 7, [SQ[q], B_const], [PB[pb]], track=True)
                pbs.append(pb)
            for g in range(4):
                S.op("act", L("activation", rsf(g), bank(pbs[g]), AF.Sqrt, scale=1.0 / 1024.0, bias=small[:, 0:1]),
                     reads=[PB[pbs[g]], B_small], writes=[RSF[g]])
                S.op("dve", L("reciprocal", rsf(g), rsf(g)), reads=[RSF[g]], writes=[RSF[g]])
            for g in range(4):
                for c in range(8):
                    eng = "dve"
                    if goff is None:
                        S.op(eng, L("tensor_copy", ytv(c), xT[:, c, tgs(g)]), reads=[XT[c][g]], writes=[YT[c]])
                    else:
                        S.op(eng, L("scalar_tensor_tensor", ytv(c), xT[:, c, tgs(g)], gains[:, goff + c:goff + c + 1],
                                    rsf(g), ALU.mult, ALU.mult), reads=[XT[c][g], RSF[g]], writes=[YT[c]])
                for j in range(4):
                    tt = g * 4 + j
                    oi = tt % 2
                    p0, p1 = nb_(), nb_()
                    for c in range(8):
                        pbx = p0 if c < 4 else p1
                        S.op("pe", L("transpose", bank(pbx)[:, (c % 4) * 128:(c % 4 + 1) * 128], ytv(c)[:, j * 128:(j + 1) * 128], ident[:]),
                            reads=[YT[c], B_const], writes=[PB[pbx]], track=(c % 4 == 3))
                    S.op("act", L("activation", osv(oi)[:, 0:512], bank(p0), AF.Copy),
                         reads=[PB[p0]], writes=[OS[oi]])
                    S.op("act", L("activation", osv(oi)[:, 512:1024], bank(p1), AF.Copy),
                         reads=[PB[p1]], writes=[OS[oi]])
                    S.dma("sp", y_out[tt * 128:(tt + 1) * 128, :], osv(oi), reads=[OS[oi]])

        if KDBG == 10:
            for _ in range(30):
                norm_stage(0)
        elif KDBG == 11:
            norm_stage(0)
            for i in range(40):
                slot, wb = wtile(wreq_cols("w_in", 8, (i % 24) * 128, 128))
                wq = wview(slot, 8, 128)
                pb = proj_fm(wq, wb, 0, hT_rhs, hT_bufs, 8, i % 4)
                S.op("dve", L("tensor_copy", rs_b[:], bank(pb)), reads=[PB[pb]], writes=[B_rsb])
        elif nstage >= 1:
            norm_stage(0)
            diff_phase()
        if nstage >= 2:
            na_phase()
        if nstage >= 3:
            ffn(0, 8)
        if nstage >= 4:
            norm_stage(16)
            mla_phase()
        if nstage >= 5:
            ffn(1, 24)
        final_stage(32 if nstage >= 5 else None)
        S.final_wait("sp")
        S.emit()
    return nc, requests


_CACHE = {}


def _get_program(nstage=99):
    if nstage not in _CACHE:
        _, reqs = build(None, nstage)
        nc, _ = build(reqs, nstage)
        _CACHE[nstage] = (nc, reqs)
    return _CACHE[nstage]


def kernel(**inputs):
    nstage = int(inputs.pop("_nstage", 99))
    cores = inputs.pop("_cores", None)
    shared = _prep_shared(inputs)
    x = np.asarray(inputs["x"], np.float32)
    nb = x.shape[0] if cores is None else cores
    nc, plan = _get_program(nstage)
    dev = {k: shared[k] for k in SHAPES if k != "x"}
    dev["wpack"] = _pack_weights(shared, plan)
    in_maps = []
    for b in range(nb):
        m = dict(dev)
        m["x"] = np.ascontiguousarray(x[b])
        in_maps.append(m)
    res = run_bass_kernel_spmd(nc, in_maps, core_ids=list(range(nb)))
    out = np.stack([np.asarray(r["y"], np.float32) for r in res.results], axis=0)
    return out
```

```python
import math
import os
KDBG = int(os.environ.get("KDBG", "0"))
from contextlib import ExitStack
import numpy as np
import ml_dtypes
import concourse.bass as bass
import concourse.mybir as mybir
from concourse.bass_utils import run_bass_kernel_spmd

F32 = mybir.dt.float32
BF16 = mybir.dt.bfloat16
AF = mybir.ActivationFunctionType
ALU = mybir.AluOpType

T = 2048
D = 1024
DFF = 2816
NFC = 22
EPS = 1e-6
NEG = -30000.0
NSLOT = 4
LOOKAHEAD = 2
NBLK = 21


class Buf:
    __slots__ = ("name", "w", "r", "dsem", "dcnt", "excl")

    def __init__(self, name, excl=False):
        self.name = name
        self.excl = excl
        self.w = None
        self.r = {}
        self.dsem = None
        self.dcnt = 0


class Sched:
    ENG = ["pe", "act", "dve", "pool", "sp"]
    COMPUTE = ["pe", "act", "dve", "pool"]

    def __init__(self, nc, stack):
        self.nc = nc
        self.stack = stack
        self.ops = {e: [] for e in self.ENG}
        self.cnt = {e: 0 for e in self.ENG}
        self.sem = {e: stack.enter_context(nc.semaphore("s_" + e)) for e in self.ENG}
        self.seen = {e: {} for e in self.ENG}
        self.semobj = {e: self.sem[e] for e in self.ENG}
        self.nd = 0
        self.dma_bufs = []

    def _deps(self, eng, reads, writes):
        deps = {}

        def add(d):
            if d is None:
                return
            k, v = d
            if deps.get(k, 0) < v:
                deps[k] = v
        for b in reads:
            add(b.w)
        for b in writes:
            add(b.w)
            for d in b.r.items():
                add(d)
        waits = []
        seen = self.seen[eng]
        for k, v in deps.items():
            if k == "pe" and eng == "pe":
                continue
            if seen.get(k, 0) >= v:
                continue
            seen[k] = v
            waits.append((self.semobj[k], v))
        return waits

    def op(self, eng, fn, reads=(), writes=(), track=True):
        ex = [b for b in reads if b.excl]
        if ex:
            reads = [b for b in reads if not b.excl]
            writes = list(writes) + [b for b in ex if b not in writes]
        waits = self._deps(eng, reads, writes)
        idx = self.cnt[eng] + 1
        if track:
            self.cnt[eng] = idx
        ev = (eng, idx)
        for b in reads:
            if b.r.get(eng, 0) < idx:
                b.r[eng] = idx
        for b in writes:
            b.w = ev
            b.r = {}
        self.ops[eng].append((waits, fn, track, None))

    def dma(self, q, out, in_, reads=(), writes=()):
        waits = self._deps(q, reads, writes)
        bufs = list(writes) + list(reads)
        assert len(bufs) == 1
        b = bufs[0]
        if b.dsem is None:
            self.nd += 1
            key = "d%d" % self.nd
            b.dsem = key
            self.semobj[key] = self.stack.enter_context(self.nc.semaphore(key))
            self.dma_bufs.append(b)
        b.dcnt += 16
        ev = (b.dsem, b.dcnt)
        if writes:
            b.w = ev
            b.r = {}
        else:
            b.r[b.dsem] = b.dcnt
        self.ops[q].append((waits, L("dma_start", out=out, in_=in_), False,
                            (self.semobj[b.dsem], 16)))

    def barrier(self, dma=False):
        for e in self.ENG:
            waits = []
            if dma:
                for b in self.dma_bufs:
                    if self.seen[e].get(b.dsem, 0) < b.dcnt:
                        self.seen[e][b.dsem] = b.dcnt
                        waits.append((self.semobj[b.dsem], b.dcnt))
            for k in self.COMPUTE:
                v = self.cnt[k]
                if v > 0 and self.seen[e].get(k, 0) < v:
                    self.seen[e][k] = v
                    waits.append((self.semobj[k], v))
            if waits:
                self.ops[e].append((waits, None, False, None))

    def final_wait(self, eng):
        waits = [(self.semobj[b.dsem], b.dcnt) for b in self.dma_bufs]
        for k in self.COMPUTE:
            if self.cnt[k] > 0:
                waits.append((self.semobj[k], self.cnt[k]))
        self.ops[eng].append((waits, None, False, None))

    def emit(self):
        nc = self.nc
        handles = {"pe": "tensor", "act": "scalar", "dve": "vector", "pool": "gpsimd", "sp": "sync"}
        with nc.Block() as block:
            for e in self.ENG:
                ops = self.ops[e]
                own = self.sem[e]

                def body(engh, ops=ops, own=own):
                    for waits, fn, track, dinc in ops:
                        for s, v in waits:
                            engh.wait_ge(s, v)
                        if fn is None:
                            continue
                        ins = fn(engh)
                        if dinc is not None:
                            ins.then_inc(dinc[0], dinc[1])
                        elif track:
                            ins.then_inc(own, 1)
                getattr(block, handles[e])(body)


def L(method, *args, **kw):
    return lambda e: getattr(e, method)(*args, **kw)


def _na_tile_lists():
    lists = []
    tid = {}
    for m in range(16):
        if m <= 1:
            kbs = [0, 1, 2, 3]
        elif m >= 14:
            kbs = [12, 13, 14, 15]
        else:
            kbs = [m - 2, m - 1, m, m + 1, m + 2]
        cur = []
        for kb in kbs:
            key = ("int", kb - m) if 2 <= m <= 13 else ("edge", m, kb)
            if key not in tid:
                tid[key] = len(tid)
            cur.append((kb, tid[key]))
        lists.append(cur)
    return lists, tid


def _na_index_tables():
    lists, tid = _na_tile_lists()
    ntile = len(tid)
    dr_i = np.zeros((ntile, 128, 128), np.int64)
    dc_i = np.zeros((ntile, 128, 128), np.int64)
    valid = np.zeros((ntile, 128, 128), bool)
    done = set()
    for m in range(16):
        for kb, t in lists[m]:
            if t in done:
                continue
            done.add(t)
            kk = np.arange(128)
            kr = (2 * kb + kk // 64)[:, None]
            kc = (kk % 64)[:, None]
            qr = (2 * m + kk // 64)[None, :]
            qc = (kk % 64)[None, :]
            rs = np.clip(qr - 4, 0, 24)
            ws = np.clip(qc - 8, 0, 48)
            v = (kr >= rs) & (kr < rs + 8) & (kc >= ws) & (kc < ws + 16)
            dr = np.clip(kr - qr + 7, 0, 14)
            dc = np.clip(kc - qc + 15, 0, 30)
            dr_i[t] = np.broadcast_to(dr, (128, 128))
            dc_i[t] = np.broadcast_to(dc, (128, 128))
            valid[t] = v
    return dr_i, dc_i, valid


def _na_kb_layout():
    lists, tid = _na_tile_lists()
    layout = []
    off = 0
    shared = None
    for kb in range(16):
        ms = [m for m in range(16) if kb in [k for k, _ in lists[m]]]
        tl = [dict(lists[m])[kb] for m in ms]
        if 4 <= kb <= 11:
            if shared is None:
                shared = off
                off += 128 * len(ms)
            o = shared
        else:
            o = off
            off += 128 * len(ms)
        layout.append((kb, o, ms, tl))
    return layout, off


def _const_tables():
    c = {}
    c["ident"] = np.eye(128, dtype=np.float32)
    slopes = 2.0 ** (-8.0 * np.arange(1, 5) / 4)
    i = np.arange(T)
    a = (i // 128).astype(np.float32)
    r = (i % 128).astype(np.float32)
    augq = np.zeros((4, 4, T), np.float32)
    for h in range(4):
        s = slopes[h]
        augq[h, 0] = -s * 128.0 * a
        augq[h, 1] = -s * r
        augq[h, 2] = s
        augq[h, 3] = s
    c["augq"] = augq
    augk = np.zeros((2, 4, T), np.float32)
    for v, sg in enumerate([1.0, -1.0]):
        augk[v, 0] = sg
        augk[v, 1] = sg
        augk[v, 2] = sg * 128.0 * a
        augk[v, 3] = sg * r
    c["augk"] = augk
    jj = np.arange(128)[:, None]
    ii = np.arange(128)[None, :]
    cd = np.zeros((128, 4 * 128), np.float32)
    for h in range(4):
        cd[:, h * 128:(h + 1) * 128] = -2.0 * slopes[h] * np.maximum(jj - ii, 0)
    c["cdiag"] = cd
    inv = (10000.0 ** (-np.arange(0, 32, 2, dtype=np.float32) / np.float32(32))).astype(np.float32)
    ang = (np.arange(T, dtype=np.float32)[:, None] * inv[None, :]).astype(np.float32)
    cos = np.cos(ang.astype(np.float64)).astype(np.float32).T
    sin = np.sin(ang.astype(np.float64)).astype(np.float32).T
    rt = np.zeros((128, T), np.float32)
    rt[64:80] = cos
    rt[80:96] = cos
    rt[96:112] = -sin
    rt[112:128] = sin
    c["ropetab"] = rt
    return c


def _pm(v, nch):
    return np.ascontiguousarray(np.asarray(v, np.float32).reshape(nch, 128).T)


def _prep_shared(inp):
    g = {}
    c = _const_tables()
    g.update(c)
    f = lambda k: np.asarray(inp[k], np.float32)
    gains = np.zeros((128, 48), np.float32)
    gains[:, 0:8] = _pm(f("mix_norm_e")[0], 8)
    gains[:, 8:16] = _pm(f("ffn_norm_g")[0], 8)
    gains[:, 16:24] = _pm(f("mix_norm_o")[0], 8)
    gains[:, 24:32] = _pm(f("ffn_norm_g")[1], 8)
    gains[:, 32:40] = _pm(f("final_norm_g"), 8)
    gains[:, 40:44] = _pm(f("q_norm_g")[0], 4)
    gains[:, 44:46] = _pm(f("kv_norm_g")[0], 2)
    gains[:, 46] = f("diff_subln_g")[0]
    g["gains"] = gains
    cv = np.zeros((128, 2 * NFC * 4), np.float32)
    for l in range(2):
        for k in range(3):
            cv[:, l * 88 + k * 22: l * 88 + (k + 1) * 22] = _pm(f("ffn_conv_w")[l, k], NFC)
        cv[:, l * 88 + 66: l * 88 + 88] = _pm(f("ffn_conv_b")[l], NFC)
    g["convp"] = cv
    lam = np.zeros((128, 256), np.float32)
    for k, nm in enumerate(["diff_lq1", "diff_lk1", "diff_lq2", "diff_lk2"]):
        lam[:, k * 64:(k + 1) * 64] = np.broadcast_to(f(nm)[0][None, :], (128, 64))
    g["lamv"] = lam
    dr_i, dc_i, valid = _na_index_tables()
    rpb = f("na_rpb")[0]
    nb = np.empty((8, dr_i.shape[0], 128, 128), np.float32)
    for h in range(8):
        nb[h] = np.where(valid, rpb[h][dr_i, dc_i], np.float32(NEG))
    layout, ncols = _na_kb_layout()
    nb2 = np.empty((8, 128, ncols), np.float32)
    for kb, o, ms, tl in layout:
        for i, t in enumerate(tl):
            nb2[:, :, o + i * 128:o + (i + 1) * 128] = nb[:, t]
    g["nbias"] = nb2
    g["w_in"] = f("w_in_e")[0]
    g["w_out"] = f("w_out_e")[0]
    g["w_gate"] = f("w_ffn_gate")
    g["w_val"] = f("w_ffn_val")
    g["w_down"] = f("w_ffn_down")
    g["w_dq"] = f("w_dq")[0]
    wuq = f("w_uq")[0]
    cols = []
    for h in range(16):
        b = h * 96
        cols += list(range(b, b + 96)) + list(range(b + 80, b + 96)) + list(range(b + 64, b + 80))
    g["w_uq"] = np.ascontiguousarray(wuq[:, cols])
    wdkv = f("w_dkv")[0]
    wd = np.zeros((1024, 384), np.float32)
    wd[:, 0:256] = wdkv[:, 0:256]
    wd[:, 320:352] = wdkv[:, 256:288]
    wd[:, 352:368] = wdkv[:, 272:288]
    wd[:, 368:384] = wdkv[:, 256:272]
    g["w_dkv"] = wd
    wukv = f("w_ukv")[0].reshape(256, 16, 128)
    g["w_uk"] = np.ascontiguousarray(wukv[:, :, 0:64].reshape(256, 1024))
    g["w_uv"] = np.ascontiguousarray(wukv[:, :, 64:128].reshape(256, 1024))
    g["w_o"] = f("w_o_mla")[0]
    return g


BF16_IN = ()
SHAPES = {
    "x": [T, D], "ident": [128, 128], "augq": [4, 4, T], "augk": [2, 4, T], "cdiag": [128, 512],
    "ropetab": [128, T], "gains": [128, 48], "convp": [128, 176], "lamv": [128, 256],
    "nbias": [8, 128, 5248],
}


def _desc_size(d):
    if d[0] == "gv":
        return 2048
    return d[2] * d[4]


def _pack_weights(g, plan):
    pack = np.zeros((len(plan), 128, 2048), np.float32)
    for i, d in enumerate(plan):
        if d[0] == "gv":
            _, layer, c = d
            wg = g["w_gate"][layer][:, c * 128:(c + 1) * 128].reshape(8, 128, 128)
            wv = g["w_val"][layer][:, c * 128:(c + 1) * 128].reshape(8, 128, 128)
            t = np.concatenate([wg, wv], axis=2)
            pack[i] = t.transpose(1, 0, 2).reshape(128, 2048)
        else:
            _, name, kc, c0, ncols, lidx, r0 = d
            w = g[name] if lidx is None else g[name][lidx]
            t = w[r0:r0 + kc * 128, c0:c0 + ncols].reshape(kc, 128, ncols)
            pack[i, :, 0:kc * ncols] = t.transpose(1, 0, 2).reshape(128, kc * ncols)
    return pack


def build(plan=None, nstage=99):
    nc = bass.Bass("TRN2", target_bir_lowering=False)
    dram = {k: nc.dram_tensor(k, shp, BF16 if k in BF16_IN else F32, kind="ExternalInput").ap() for k, shp in SHAPES.items()}
    if plan is not None:
        dram["wpack"] = nc.dram_tensor("wpack", [len(plan), 128, 2048], F32, kind="ExternalInput").ap()
    y_out = nc.dram_tensor("y", [T, D], F32, kind="ExternalOutput").ap()
    requests = []
    with ExitStack() as st:
        S = Sched(nc, st)
        sb = lambda n, shp, dt: st.enter_context(nc.sbuf_tensor("sb_" + n, shp, dt))
        xT = sb("xT", [128, 8, T], F32)
        hT = sb("hT", [128, 8, T], BF16)
        wring = sb("wring", [128, NSLOT, 2048], BF16)
        AR = sb("arena", [128, NBLK * 2048], BF16)
        ident = sb("ident", [128, 128], F32)
        identb = sb("identb", [128, 128], BF16)
        onesb = sb("onesb", [128, 128], BF16)
        zerob = sb("zerob", [128, 128], BF16)
        gains = sb("gains", [128, 48], F32)
        convp = sb("convp", [128, 176], F32)
        small = sb("small", [128, 16], F32)
        cdiag = sb("cdiag", [128, 512], BF16)
        rs_a = sb("rs_a", [128, 512], F32)
        rs_b = sb("rs_b", [128, 512], F32)
        sqr = sb("sqr", [128, 4, 512], BF16)
        PS = st.enter_context(nc.psum_tensor("PS", [128, 8 * 512], F32))

        XT = [[Buf("xT%d_%d" % (c, g)) for g in range(4)] for c in range(8)]
        HT = [[Buf("hT%d_%d" % (c, g)) for g in range(4)] for c in range(8)]
        WS = [Buf("ws%d" % i) for i in range(NSLOT)]
        PB = [Buf("pb%d" % i, excl=True) for i in range(8)]
        B_const = Buf("const")
        B_small = Buf("small")
        B_rsa, B_rsb = Buf("rsa"), Buf("rsb")
        SQ = [Buf("sq%d" % i) for i in range(4)]

        def bank(i):
            return PS[:, i * 512:(i + 1) * 512]

        def blk(k, n=1):
            return AR[:, k * 2048:(k + n) * 2048]

        def blkf(k, n=2):
            return AR[:, k * 2048:(k + n) * 2048].bitcast(F32)

        state = {"bank": 0, "wi": 0, "wissued": 0, "sq": 0, "sb": 0, "tick": 0}
        pending = []
        lamv = blkf(0, 1)[:, 0:256]

        def nb_():
            b = state["bank"]
            state["bank"] = (b + 1) % 8
            return b

        def tgs(g):
            return slice(g * 512, (g + 1) * 512)

        def wtile(desc):
            i = state["wi"]
            state["wi"] = i + 1
            requests.append(desc)
            if plan is not None:
                hi = min(len(plan), i + 1 + LOOKAHEAD)
                while state["wissued"] < hi:
                    j = state["wissued"]
                    slot = j % NSLOT
                    X = _desc_size(plan[j])
                    S.dma("pool", wring[:, slot, 0:X], dram["wpack"][j][:, 0:X], writes=[WS[slot]])
                    state["wissued"] = j + 1
            return wring[:, i % NSLOT, :], WS[i % NSLOT]

        def wreq_cols(name, kc, c0, ncols, lidx=None, r0=0):
            return ("cols", name, kc, c0, ncols, lidx, r0)

        def wview(slot, kc, ncols):
            return slot[:, 0:kc * ncols].rearrange("p (k n) -> p k n", k=kc)

        S.dma("sp", ident[:], dram["ident"], writes=[B_const])
        gb_l = Buf("gains_l")
        S.dma("sp", gains[:], dram["gains"], writes=[gb_l])
        cb1, cb2, cb3 = Buf("c1"), Buf("c2"), Buf("c3")
        S.dma("sp", convp[:], dram["convp"], writes=[cb1])
        S.dma("sp", lamv, dram["lamv"], writes=[cb2])
        S.dma("pool", cdiag[:], dram["cdiag"], writes=[cb3])
        S.op("dve", L("tensor_copy", identb[:], ident[:]), reads=[B_const], writes=[B_const])
        S.op("dve", L("memset", onesb[:], 1.0), writes=[B_const])
        S.op("dve", L("memset", zerob[:], 0.0), writes=[B_const])
        S.op("dve", L("memset", small[:, 0:1], EPS), writes=[B_small])
        lam_init = 0.8 - 0.6 * math.exp(-0.3 * 0)
        S.op("dve", L("tensor_tensor", lamv[:, 0:64], lamv[:, 0:64], lamv[:, 64:128], ALU.mult),
             reads=[cb2], writes=[cb2])
        S.op("dve", L("tensor_tensor", lamv[:, 128:192], lamv[:, 128:192], lamv[:, 192:256], ALU.mult),
             reads=[cb2], writes=[cb2])
        S.op("dve", L("reduce_sum", small[:, 4:5], lamv[:, 0:64], mybir.AxisListType.X),
             reads=[cb2], writes=[B_small])
        S.op("dve", L("reduce_sum", small[:, 5:6], lamv[:, 128:192], mybir.AxisListType.X),
             reads=[cb2], writes=[B_small])
        S.op("act", L("activation", small[:, 6:8], small[:, 4:6], AF.Exp), reads=[B_small], writes=[B_small])
        S.op("dve", L("scalar_tensor_tensor", small[:, 1:2], small[:, 7:8], -lam_init, small[:, 6:7],
                                                     ALU.add, ALU.subtract), reads=[B_small], writes=[B_small])
        S.op("dve", L("tensor_scalar", small[:, 2:3], gains[:, 46:47], 1.0 - lam_init, None, ALU.mult),
             reads=[B_small, gb_l], writes=[B_small])
        S.barrier(dma=True)

        def mm(out, lhsT, rhs, start, stop, reads, writes, track=None, sgc=False):
            if track is None:
                track = stop
            if sgc:
                S.op("pe", L("matmul", out, lhsT, rhs, start=start, stop=stop, skip_group_check=True),
                     reads=reads, writes=writes, track=track)
            else:
                S.op("pe", L("matmul", out, lhsT, rhs, start=start, stop=stop), reads=reads, writes=writes,
                     track=track)

        def rstd_from_psum(pb, nfeat, dst, dstbuf):
            S.op("act", L("activation", rs_a[:], bank(pb), AF.Sqrt, scale=1.0 / nfeat, bias=small[:, 0:1]),
                 reads=[PB[pb], B_small], writes=[B_rsa])
            S.op("dve", L("reciprocal", dst, rs_a[:]), reads=[B_rsa], writes=[dstbuf])

        def norm_stage(goff):
            for g in range(4):
                pb = nb_()
                for c in range(8):
                    q = state["sq"]
                    state["sq"] = (q + 1) % 4
                    S.op("act", L("activation", sqr[:, q, :], xT[:, c, tgs(g)], AF.Square),
                         reads=[XT[c][g]], writes=[SQ[q]])
                    mm(bank(pb), onesb[:], sqr[:, q, :], c == 0, c == 7, [SQ[q], B_const], [PB[pb]], track=True)
                rstd_from_psum(pb, 1024.0, rs_b[:], B_rsb)
                for c in range(8):
                    S.op("dve", L("scalar_tensor_tensor", hT[:, c, tgs(g)], xT[:, c, tgs(g)], gains[:, goff + c:goff + c + 1], rs_b[:],
                        ALU.mult, ALU.mult), reads=[XT[c][g], B_rsb], writes=[HT[c][g]])

        def proj_fm(wv, wbuf, col0, rhs_fn, rhs_bufs_fn, nk, g, M=128):
            pb = nb_()
            for k in range(nk):
                mm(bank(pb)[0:M, :], wv[:, k, col0:col0 + M], rhs_fn(k, g), k == 0, k == nk - 1,
                   [wbuf] + rhs_bufs_fn(k, g), [PB[pb]])
            return pb

        hT_rhs = lambda k, g: hT[:, k, tgs(g)]
        hT_bufs = lambda k, g: [HT[k][g]]

        def resid_add(pb, n, g, eng="dve"):
            S.op(eng, L("tensor_tensor", xT[:, n, tgs(g)], bank(pb), xT[:, n, tgs(g)], ALU.add),
                 reads=[PB[pb], XT[n][g]], writes=[XT[n][g]])

        def out_proj(name, mix_fn, mix_bufs, krow0, nkc):
            assert nkc == 4
            tiles_ = []
            for nt in range(2):
                slot, wb = wtile(wreq_cols(name, nkc, nt * 512, 512, r0=krow0))
                tiles_.append((wview(slot, nkc, 512), wb))
            for g in range(4):
                for n in range(8):
                    wv, wb = tiles_[n // 4]
                    pb = proj_fm(wv, wb, (n % 4) * 128, mix_fn, mix_bufs, nkc, g)
                    resid_add(pb, n, g)

        XS = [Buf("xs%d" % i) for i in range(16)]
        for tt in range(16):
            S.dma("sp", blkf(tt, 1), dram["x"][tt * 128:(tt + 1) * 128, :], writes=[XS[tt]])
        for g in range(4):
            for c in range(8):
                pb = nb_()
                for j in range(4):
                    tt = g * 4 + j
                    S.op("pe", L("transpose", bank(pb)[:, j * 128:(j + 1) * 128], blkf(tt, 1)[:, c * 128:(c + 1) * 128], ident[:]),
                        reads=[XS[tt], B_const], writes=[PB[pb]], track=(j == 3))
                eng = "act" if c % 2 == 0 else "dve"
                if eng == "act":
                    S.op("act", L("activation", xT[:, c, tgs(g)], bank(pb), AF.Copy),
                         reads=[PB[pb]], writes=[XT[c][g]])
                else:
                    S.op("dve", L("tensor_copy", xT[:, c, tgs(g)], bank(pb)),
                         reads=[PB[pb]], writes=[XT[c][g]])
        S.barrier()

        def ffn(layer, goff):
            norm_stage(goff)
            U = [Buf("u%d" % i) for i in range(11)]
            TB = [Buf("tb%d" % i) for i in range(2)]
            cp = layer * 88
            for rnd in range(2):
                for cc in range(11):
                    c = rnd * 11 + cc

                    slot, wb = wtile(("gv", layer, c))
                    wv = wview(slot, 8, 256)
                    for g in range(4):
                        for k in range(8):
                            mm(bank(g), wv[:, k, 0:128], hT[:, k, tgs(g)], k == 0, k == 7, [wb, HT[k][g]], [PB[g]])
                    for g in range(4):
                        for k in range(8):
                            mm(bank(4 + g), wv[:, k, 128:256], hT[:, k, tgs(g)], k == 0, k == 7,
                               [wb, HT[k][g]], [PB[4 + g]])
                    tb = cc % 2
                    tv = blkf(11 + 2 * tb)
                    G = PS[:, 0:2048]
                    gb = [PB[0], PB[1], PB[2], PB[3]]
                    w0 = convp[:, cp + c:cp + c + 1]
                    w1 = convp[:, cp + 22 + c:cp + 22 + c + 1]
                    w2 = convp[:, cp + 44 + c:cp + 44 + c + 1]
                    bb = convp[:, cp + 66 + c:cp + 66 + c + 1]
                    S.op("act", L("activation", tv, G, AF.Identity, scale=w1, bias=bb),
                         reads=gb + [cb1], writes=[TB[tb]])
                    S.op("dve", L("scalar_tensor_tensor", tv[:, 1:T], PS[:, 0:T - 1], w0, tv[:, 1:T], ALU.mult, ALU.add),
                        reads=gb + [TB[tb]], writes=[TB[tb]])
                    S.op("dve", L("scalar_tensor_tensor", tv[:, 0:T - 1], PS[:, 1:T], w2, tv[:, 0:T - 1], ALU.mult, ALU.add),
                        reads=gb + [TB[tb]], writes=[TB[tb]])
                    S.op("act", L("activation", tv, tv, AF.Gelu), reads=[TB[tb]], writes=[TB[tb]])
                    for hh in range(2):
                        S.op("dve", L("tensor_tensor", blk(cc)[:, hh * 1024:(hh + 1) * 1024], tv[:, hh * 1024:(hh + 1) * 1024],
                            PS[:, 2048 + hh * 1024:2048 + (hh + 1) * 1024], ALU.mult),
                            reads=[TB[tb], PB[4 + 2 * hh], PB[5 + 2 * hh]], writes=[U[cc]])
                for n in range(8):
                    slot, wb = wtile(wreq_cols("w_down", 11, n * 128, 128, lidx=layer, r0=rnd * 1408))
                    wv = wview(slot, 11, 128)
                    for g in range(4):
                        pb = proj_fm(wv, wb, 0, lambda k, g: blk(k)[:, tgs(g)], lambda k, g: [U[k]], 11, g)
                        resid_add(pb, n, g)
            S.barrier()

        def softmax_epilogue_diff(h, I, pO, pD, MIX):
            r1, r2, t2 = blkf(13, 2)[:, 0:512], blkf(13, 2)[:, 512:1024], blkf(13, 2)[:, 1024:1536]
            t1 = blkf(15, 2)[:, I * 512:(I + 1) * 512]
            EB = state.setdefault("EB", [Buf("eb%d" % i) for i in range(3)])
            A1 = state.setdefault("A1", [Buf("a1_%d" % i) for i in range(4)])
            S.op("dve", L("tensor_copy", t1, bank(pO[0])), reads=[PB[pO[0]]], writes=[A1[I]])
            S.op("dve", L("tensor_copy", r1, bank(pD[0])), reads=[PB[pD[0]]], writes=[EB[0]])
            S.op("dve", L("tensor_copy", t2, bank(pO[1])), reads=[PB[pO[1]]], writes=[EB[2]])
            S.op("dve", L("tensor_copy", r2, bank(pD[1])), reads=[PB[pD[1]]], writes=[EB[1]])
            S.op("dve", L("reciprocal", r1, r1), reads=[EB[0]], writes=[EB[0]])
            S.op("dve", L("reciprocal", r2, r2), reads=[EB[1]], writes=[EB[1]])
            S.op("dve", L("tensor_tensor", t1, t1, r1, ALU.mult), reads=[EB[0], A1[I]], writes=[A1[I]])
            S.op("dve", L("tensor_tensor", t2, t2, r2, ALU.mult), reads=[EB[1], EB[2]], writes=[EB[2]])
            S.op("dve", L("scalar_tensor_tensor", t1, t2, small[:, 1:2], t1, ALU.mult, ALU.add),
                 reads=[A1[I], EB[2], B_small], writes=[A1[I]])

        def subln_head(h, MIX):
            A1 = state["A1"]
            RS = state.setdefault("RS", [Buf("rs_%d" % i) for i in range(4)])
            pbs = []
            for I in range(4):
                t1 = blkf(15, 2)[:, I * 512:(I + 1) * 512]
                S.op("act", L("activation", sqr[:, I, :], t1, AF.Square), reads=[A1[I]], writes=[SQ[I]])
                pb = 4 + state["sb"] % 4
                state["sb"] += 1
                mm(bank(pb), onesb[:], sqr[:, I, :], True, True, [SQ[I], B_const], [PB[pb]])
                pbs.append(pb)
            for I in range(4):
                rsd = blkf(17, 2)[:, I * 512:(I + 1) * 512]
                S.op("act", L("activation", rsd, bank(pbs[I]), AF.Sqrt, scale=1.0 / 128.0, bias=small[:, 0:1]),
                     reads=[PB[pbs[I]], B_small], writes=[RS[I]])
            for I in range(4):
                t1 = blkf(15, 2)[:, I * 512:(I + 1) * 512]
                rsd = blkf(17, 2)[:, I * 512:(I + 1) * 512]
                S.op("dve", L("reciprocal", rsd, rsd), reads=[RS[I]], writes=[RS[I]])
                S.op("dve", L("scalar_tensor_tensor", blk(h)[:, tgs(I)], t1, small[:, 2:3], rsd, ALU.mult, ALU.mult),
                     reads=[A1[I], RS[I], B_small], writes=[MIX[h]])

        def diff_phase():
            MIX = [Buf("mix%d" % i) for i in range(4)]
            QB = [Buf("dq%d" % i) for i in range(2)]
            KB = [[Buf("dk%d%d" % (i, v)) for v in range(2)] for i in range(2)]
            VB = Buf("dv")
            PT = [Buf("pt%d" % i) for i in range(8)]
            Qt = [blk(4), blk(5)]
            Kt = [[blk(6), blk(7)], [blk(8), blk(9)]]
            Vt = blk(10).rearrange("p (t e) -> p t e", t=16)
            ptv = lambda i: blk(11, 2)[:, i * 512:(i + 1) * 512]
            pti = [0]
            for n in range(2):
                for v in range(2):
                    S.dma("pool", Kt[n][v][64:68, :], dram["augk"][v], writes=[KB[n][v]])
            for h in range(int(os.environ.get("KH0", "0")), int(os.environ.get("KH1", "4")) if KDBG in (0, 5) else 1):
                if KDBG == 1:
                    break
                if os.environ.get("KBAR", "0") == "1":
                    S.barrier()
                slot, wb = wtile(wreq_cols("w_in", 8, h * 128, 128))
                wq = wview(slot, 8, 128)
                for n in range(2):
                    S.dma("pool", Qt[n][64:68, :], dram["augq"][h], writes=[QB[n]])
                for g in range(4):
                    pb = proj_fm(wq, wb, 0, hT_rhs, hT_bufs, 8, g)
                    S.op("dve", L("tensor_scalar", Qt[0][0:64, tgs(g)], bank(pb)[0:64, :], 0.125, None, ALU.mult),
                         reads=[PB[pb]], writes=[QB[0]])
                    S.op("dve", L("tensor_scalar", Qt[1][0:64, tgs(g)], bank(pb)[64:128, :], 0.125, None, ALU.mult),
                         reads=[PB[pb]], writes=[QB[1]])
                slot, wb = wtile(wreq_cols("w_in", 8, 512 + h * 128, 128))
                wk = wview(slot, 8, 128)
                for g in range(4):
                    pb = proj_fm(wk, wb, 0, hT_rhs, hT_bufs, 8, g)
                    for n in range(2):
                        S.op("act", L("activation", Kt[n][0][0:64, tgs(g)], bank(pb)[64 * n:64 * n + 64, :], AF.Copy),
                             reads=[PB[pb]], writes=[KB[n][0]])
                        S.op("dve", L("tensor_copy", Kt[n][1][0:64, tgs(g)], bank(pb)[64 * n:64 * n + 64, :]),
                             reads=[PB[pb]], writes=[KB[n][1]])
                slot, wb = wtile(wreq_cols("w_in", 8, 1024 + h * 128, 128))
                wvv = wview(slot, 8, 128)
                for g in range(4):
                    pb = nb_()
                    for j in range(4):
                        tt = g * 4 + j
                        for k in range(8):
                            mm(bank(pb)[:, j * 128:(j + 1) * 128], hT[:, k, tt * 128:(tt + 1) * 128], wvv[:, k, :],
                               k == 0, k == 7, [wb, HT[k][g]], [PB[pb]], track=(k == 7 and j == 3))
                    S.op("act", L("activation", Vt[:, g * 4:(g + 1) * 4, :], bank(pb).rearrange("p (t e) -> p t e", t=4), AF.Copy),
                        reads=[PB[pb]], writes=[VB])
                pO, pD = [0, 1], [2, 3]
                tiles = [(I, J, n) for I in range(4) for J in range(16) for n in range(2)]
                info = {}

                def stage1(t):
                    I, J, n = t
                    ps = 4 + state["sb"] % 4
                    state["sb"] += 1
                    if J < 4 * I or J >= 4 * I + 4:
                        v = 0 if J < 4 * I else 1
                        mm(bank(ps), Kt[n][v][0:68, J * 128:(J + 1) * 128], Qt[n][0:68, tgs(I)], True, True,
                           [KB[n][v], QB[n]], [PB[ps]])
                    else:
                        for b in range(4):
                            gb_ = 4 * I + b
                            v = 0 if gb_ >= J else 1
                            qs = slice(gb_ * 128, (gb_ + 1) * 128)
                            osl = bank(ps)[:, b * 128:(b + 1) * 128]
                            diag = gb_ == J
                            mm(osl, Kt[n][v][0:68, J * 128:(J + 1) * 128], Qt[n][0:68, qs], True, not diag,
                               [KB[n][v], QB[n]], [PB[ps]], track=(b == 3 and not diag))
                            if diag:
                                mm(osl, identb[:], cdiag[:, h * 128:(h + 1) * 128], False, True,
                                   [B_const, cb3], [PB[ps]], track=(b == 3))
                    pi = pti[0]
                    pti[0] = (pi + 1) % 8
                    S.op("act", L("activation", ptv(pi), bank(ps), AF.Exp), reads=[PB[ps]], writes=[PT[pi]])
                    info[t] = pi

                def stage2(t):
                    I, J, n = t
                    pi = info.pop(t)
                    mm(bank(pO[n]), Vt[:, J, :], ptv(pi), J == 0, J == 15, [VB, PT[pi]], [PB[pO[n]]], track=True)
                    mm(bank(pD[n]), onesb[:], ptv(pi), J == 0, J == 15, [B_const, PT[pi]], [PB[pD[n]]], track=True)
                    if J == 15 and n == 1:
                        softmax_epilogue_diff(h, I, pO, pD, MIX)
                        if I == 3:
                            pending.append((state["tick"] + 24, lambda h=h: subln_head(h, MIX)))

                DEPTH = 2
                for i in range(len(tiles) + DEPTH):
                    state["tick"] += 1
                    while pending and pending[0][0] <= state["tick"]:
                        pending.pop(0)[1]()
                    if i < len(tiles):
                        stage1(tiles[i])
                    if i >= DEPTH:
                        stage2(tiles[i - DEPTH])
            while pending:
                pending.pop(0)[1]()
            if KDBG == 0:
                out_proj("w_out", lambda k, g: blk(k)[:, tgs(g)], lambda k, g: [MIX[k]], 0, 4)
            S.barrier()

        def na_phase():
            lists, _ = _na_tile_lists()
            MIX = [Buf("nmix%d" % i) for i in range(4)]
            NQb, NKb = Buf("nq"), Buf("nk")
            VAb = [Buf("va0"), Buf("va1")]
            NBb = Buf("nbias")
            PT = [Buf("npt%d" % i) for i in range(6)]
            RD = [Buf("nrd%d" % i) for i in range(2)]
            NQ, NK = blk(4), blk(5)
            VA = [blk(6).rearrange("p (t e) -> p t e", t=16), blk(7).rearrange("p (t e) -> p t e", t=16)]
            NB2 = blk(15, 6)
            layout, _nc = _na_kb_layout()
            ptv = lambda i: blk(8, 3)[:, i * 768:(i + 1) * 768]
            rdv = lambda i: blkf(13, 2)[:, i * 512:(i + 1) * 512]
            for a in range(2):
                S.op("dve", L("memset", VA[a][:, :, 64:128], 1.0), writes=[VAb[a]])
            pti = [0]
            for j in range(4):
                slot, wb = wtile(wreq_cols("w_in", 8, 1536 + j * 128, 128))
                wq = wview(slot, 8, 128)
                for g in range(4):
                    pb = proj_fm(wq, wb, 0, hT_rhs, hT_bufs, 8, g)
                    S.op("dve", L("tensor_scalar", NQ[:, tgs(g)], bank(pb), 0.125, None, ALU.mult),
                         reads=[PB[pb]], writes=[NQb])
                slot, wb = wtile(wreq_cols("w_in", 8, 2048 + j * 128, 128))
                wk = wview(slot, 8, 128)
                for g in range(4):
                    pb = proj_fm(wk, wb, 0, hT_rhs, hT_bufs, 8, g)
                    S.op("dve", L("tensor_copy", NK[:, tgs(g)], bank(pb)), reads=[PB[pb]], writes=[NKb])
                slot, wb = wtile(wreq_cols("w_in", 8, 2560 + j * 128, 128))
                wvv = wview(slot, 8, 128)
                for g in range(4):
                    pb = nb_()
                    for jj in range(4):
                        tt = g * 4 + jj
                        for k in range(8):
                            mm(bank(pb)[:, jj * 128:(jj + 1) * 128], hT[:, k, tt * 128:(tt + 1) * 128], wvv[:, k, :],
                               k == 0, k == 7, [wb, HT[k][g]], [PB[pb]], track=(k == 7 and jj == 3))
                    pv = bank(pb).rearrange("p (t e) -> p t e", t=4)
                    S.op("act", L("activation", VA[0][:, g * 4:(g + 1) * 4, 0:64], pv[:, :, 0:64], AF.Copy),
                         reads=[PB[pb]], writes=[VAb[0]])
                    S.op("dve", L("tensor_copy", VA[1][:, g * 4:(g + 1) * 4, 0:64], pv[:, :, 64:128]),
                         reads=[PB[pb]], writes=[VAb[1]])
                for a in range(2):
                    S.dma("pool", NB2[:, a * 5248:(a + 1) * 5248], dram["nbias"][2 * j + a], writes=[NBb])
                if True:
                    info = {}

                    def stage1(t):
                        a, kbi = t
                        p0 = 64 * a
                        nbase = a * 5248
                        kb, o, ms, tl = layout[kbi]
                        W = 128 * len(ms)
                        q0 = 128 * ms[0]
                        b0 = 4 + 2 * (state["sb"] % 2)
                        state["sb"] += 1
                        pieces = [(0, min(W, 512))] + ([(512, W)] if W > 512 else [])
                        for (c0, c1) in pieces:
                            bb = b0 + c0 // 512
                            osl = PS[:, b0 * 512 + c0:b0 * 512 + c1]
                            mm(osl, NK[p0:p0 + 64, kb * 128:(kb + 1) * 128], NQ[p0:p0 + 64, q0 + c0:q0 + c1],
                               True, False, [NKb, NQb], [PB[bb]], track=False)
                            mm(osl, identb[:], NB2[:, nbase + o + c0:nbase + o + c1], False, True,
                               [B_const, NBb], [PB[bb]], track=True)
                        pi = pti[0]
                        pti[0] = (pi + 1) % 3
                        rb = [PB[b0]] + ([PB[b0 + 1]] if W > 512 else [])
                        S.op("act", L("activation", ptv(pi)[:, 0:W], PS[:, b0 * 512:b0 * 512 + W], AF.Exp),
                             reads=rb, writes=[PT[pi]])
                        info[t] = pi

                    def stage2(t, j=j):
                        a, kbi = t
                        p0 = 64 * a
                        kb, o, ms, tl = layout[kbi]
                        W = 128 * len(ms)
                        q0 = 128 * ms[0]
                        pi = info.pop(t)
                        if kbi == 0:
                            for b4 in range(4):
                                mm(bank(b4), zerob[:], NK[:, b4 * 512:(b4 + 1) * 512], True, True, [B_const, NKb], [PB[b4]], sgc=True)
                        g0 = q0
                        while g0 < q0 + W:
                            g1 = min(q0 + W, (g0 // 512 + 1) * 512)
                            mm(PS[:, g0:g1], VA[a][:, kb, :], ptv(pi)[:, g0 - q0:g1 - q0], False, True,
                               [VAb[a], PT[pi]], [PB[g0 // 512]], track=True, sgc=True)
                            g0 = g1
                        if kbi == 15:
                            ev = blkf(13, 2)
                            for b4 in range(4):
                                eng4 = "dve" if b4 % 2 == 0 else "act"
                                if eng4 == "dve":
                                    S.op("dve", L("tensor_copy", ev[:, b4 * 512:(b4 + 1) * 512], bank(b4)), reads=[PB[b4]], writes=[RD[0]])
                                else:
                                    S.op("act", L("activation", ev[:, b4 * 512:(b4 + 1) * 512], bank(b4), AF.Copy), reads=[PB[b4]], writes=[RD[0]])
                            rd = blkf(11, 2)
                            for g in range(4):
                                S.op("dve", L("reciprocal", rd[0:64, tgs(g)], ev[64:128, tgs(g)]), reads=[RD[0]], writes=[RD[1]])
                            for g in range(4):
                                S.op("dve", L("tensor_tensor", blk(j)[p0:p0 + 64, tgs(g)], ev[0:64, tgs(g)], rd[0:64, tgs(g)], ALU.mult),
                                     reads=[RD[0], RD[1]], writes=[MIX[j]])

                    DEPTH = 1
                    tiles = [(a, kbi) for a in range(2) for kbi in range(16)]
                    for i in range(len(tiles) + DEPTH):
                        if i < len(tiles):
                            stage1(tiles[i])
                        if i >= DEPTH:
                            stage2(tiles[i - DEPTH])
            out_proj("w_out", lambda k, g: blk(k)[:, tgs(g)], lambda k, g: [MIX[k]], 512, 4)
            S.barrier()

        def mla_phase():
            CQ = [Buf("cq%d" % i) for i in range(4)]
            CKV = [Buf("ckv%d" % i) for i in range(2)]
            KRb = Buf("kr")
            RTb = Buf("ropetab")
            QHb = [Buf("qh0"), Buf("qh1")]
            KHb = [Buf("kh0"), Buf("kh1")]
            VAb = [Buf("mva0"), Buf("mva1")]
            MIX = [Buf("mmix%d" % i) for i in range(4)]
            PT = [Buf("mpt%d" % i) for i in range(8)]
            TMP = [Buf("mtmp%d" % i) for i in range(2)]
            cqv = lambda c: blk(4 + c)
            ckvv = lambda c: blk(8 + c)
            KR = blk(10)
            rt = blkf(11, 2)
            QH = [blk(13), blk(14)]
            KH = [blk(15), blk(16)]
            VA = [blk(17).rearrange("p (t e) -> p t e", t=16), blk(18).rearrange("p (t e) -> p t e", t=16)]
            ptv = lambda i: blk(10 if i < 4 else 19)[:, (i % 4) * 512:(i % 4 + 1) * 512]
            tmpv = lambda i: blkf(20, 1)[:, i * 512:(i + 1) * 512]
            scale = 96.0 ** -0.5
            S.dma("sp", rt, dram["ropetab"], writes=[RTb])
            for a in range(2):
                S.op("dve", L("memset", VA[a][:, :, 64:128], 1.0), writes=[VAb[a]])

            def rope(dst, dstbuf, pb, g):
                S.op("dve", L("tensor_tensor", tmpv(0)[64:96, :], bank(pb)[64:96, :], rt[64:96, tgs(g)], ALU.mult),
                     reads=[PB[pb], RTb], writes=[TMP[0]])
                S.op("dve", L("tensor_tensor", tmpv(1)[64:96, :], bank(pb)[96:128, :], rt[96:128, tgs(g)], ALU.mult),
                     reads=[PB[pb], RTb], writes=[TMP[1]])
                S.op("dve", L("tensor_tensor", dst[64:96, tgs(g)], tmpv(0)[64:96, :], tmpv(1)[64:96, :], ALU.add),
                     reads=[TMP[0], TMP[1]], writes=[dstbuf])

            def latent(wtiles, nch, goff, dstv, dstb, nfeat, extra=None):
                for g in range(4):
                    pbs = []
                    for (wv, wb, ncol) in wtiles:
                        for cc in range(ncol // 128):
                            pbs.append(proj_fm(wv, wb, cc * 128, hT_rhs, hT_bufs, 8, g))
                    pss = nb_()
                    while pss in pbs:
                        pss = nb_()
                    for c in range(nch):
                        q = state["sq"]
                        state["sq"] = (q + 1) % 4
                        S.op("act", L("activation", sqr[:, q, :], bank(pbs[c]), AF.Square),
                             reads=[PB[pbs[c]]], writes=[SQ[q]])
                        mm(bank(pss), onesb[:], sqr[:, q, :], c == 0, c == nch - 1, [SQ[q], B_const], [PB[pss]], track=True)
                    rstd_from_psum(pss, nfeat, rs_b[:], B_rsb)
                    for c in range(nch):
                        S.op("dve", L("scalar_tensor_tensor", dstv(c)[:, tgs(g)], bank(pbs[c]), gains[:, goff + c:goff + c + 1], rs_b[:], ALU.mult, ALU.mult),
                            reads=[PB[pbs[c]], B_rsb], writes=[dstb[c]])
                    if extra is not None:
                        extra(pbs[nch], g)

            s0, b0 = wtile(wreq_cols("w_dq", 8, 0, 256))
            s1, b1 = wtile(wreq_cols("w_dq", 8, 256, 256))
            latent([(wview(s0, 8, 256), b0, 256), (wview(s1, 8, 256), b1, 256)], 4, 40, cqv, CQ, 512.0)
            s0, b0 = wtile(wreq_cols("w_dkv", 8, 0, 256))
            s1, b1 = wtile(wreq_cols("w_dkv", 8, 256, 128))
            latent([(wview(s0, 8, 256), b0, 256), (wview(s1, 8, 128), b1, 128)], 2, 44, ckvv, CKV, 256.0,
                   extra=lambda pb, g: rope(KH[0], KHb[0], pb, g))
            S.op("dve", L("tensor_copy", KH[1][64:96, :], KH[0][64:96, :]), reads=[KHb[0]], writes=[KHb[1]])

            def sbank():
                ps = 4 + state["sb"] % 4
                state["sb"] += 1
                return ps

            def ibank():
                state["ib"] = state.get("ib", 0) + 1
                return 2 + state["ib"] % 2

            def head_items(hd):
                s_ = hd % 2
                st = {}
                items = []

                def q_item(g):
                    if g == 0:
                        sl, st["bq"] = wtile(wreq_cols("w_uq", 4, hd * 128, 128))
                        st["wq"] = wview(sl, 4, 128)
                    pb = ibank()
                    for k in range(4):
                        mm(bank(pb), st["wq"][:, k, :], cqv(k)[:, tgs(g)], k == 0, k == 3, [st["bq"], CQ[k]], [PB[pb]])
                    S.op("act", L("activation", QH[s_][0:64, tgs(g)], bank(pb)[0:64, :], AF.Copy),
                         reads=[PB[pb]], writes=[QHb[s_]])
                    rope(QH[s_], QHb[s_], pb, g)

                def k_item(g):
                    if g == 0:
                        sl, st["bk"] = wtile(wreq_cols("w_uk", 2, hd * 64, 64))
                        st["wk"] = wview(sl, 2, 64)
                    pb = ibank()
                    for k in range(2):
                        mm(bank(pb)[0:64, :], st["wk"][:, k, :], ckvv(k)[:, tgs(g)], k == 0, k == 1, [st["bk"], CKV[k]], [PB[pb]])
                    S.op("dve", L("tensor_copy", KH[s_][0:64, tgs(g)], bank(pb)[0:64, :]), reads=[PB[pb]], writes=[KHb[s_]])

                def v_item(g):
                    if g == 0:
                        sl, st["bv"] = wtile(wreq_cols("w_uv", 2, hd * 64, 64))
                        st["wv"] = wview(sl, 2, 64)
                    pb = ibank()
                    for jj in range(4):
                        tt = g * 4 + jj
                        for k in range(2):
                            mm(bank(pb)[:, jj * 64:(jj + 1) * 64], ckvv(k)[:, tt * 128:(tt + 1) * 128],
                               st["wv"][:, k, :], k == 0, k == 1, [st["bv"], CKV[k]], [PB[pb]],
                               track=(k == 1 and jj == 3))
                    pv = bank(pb)[:, 0:256].rearrange("p (t e) -> p t e", t=4)
                    S.op("dve", L("tensor_copy", VA[s_][:, g * 4:(g + 1) * 4, 0:64], pv), reads=[PB[pb]], writes=[VAb[s_]])

                for g in range(4):
                    items.append(lambda g=g: q_item(g))
                for g in range(4):
                    items.append(lambda g=g: k_item(g))
                for g in range(4):
                    items.append(lambda g=g: v_item(g))
                return items

            pti = state.setdefault("mpti", [0])
            for it in head_items(0):
                it()
            info = {}

            def stage1(t):
                hd, I, J = t
                s_ = hd % 2
                ps = sbank()
                mm(bank(ps), KH[s_][0:96, J * 128:(J + 1) * 128], QH[s_][0:96, tgs(I)], True, True,
                   [KHb[s_], QHb[s_]], [PB[ps]])
                pi = pti[0]
                pti[0] = (pi + 1) % 8
                S.op("act", L("activation", ptv(pi), bank(ps), AF.Exp, scale=scale),
                     reads=[PB[ps]], writes=[PT[pi]])
                info[t] = pi

            def stage2(t):
                hd, I, J = t
                s_ = hd % 2
                pr = (hd // 2) % 4
                a = hd % 2
                pi = info.pop(t)
                po = I % 2
                mm(bank(po), VA[s_][:, J, :], ptv(pi), J == 0, J == 15, [VAb[s_], PT[pi]], [PB[po]], track=True)
                if J == 15:
                    rsv = rs_a if po == 0 else rs_b
                    rsb_ = B_rsa if po == 0 else B_rsb
                    S.op("dve", L("reciprocal", rsv[0:64, :], bank(po)[64:128, :]),
                         reads=[PB[po]], writes=[rsb_])
                    S.op("dve", L("tensor_tensor", blk(pr)[64 * a:64 * a + 64, tgs(I)], bank(po)[0:64, :], rsv[0:64, :], ALU.mult),
                         reads=[PB[po], rsb_], writes=[MIX[pr]])

            DEPTH = 3
            for half in range(2):
                tiles = [(hd, I, J) for hd in range(half * 8, half * 8 + 8) for I in range(4) for J in range(16)]
                nxt = []
                for i in range(len(tiles) + DEPTH):
                    local = -1
                    if i < len(tiles):
                        hd, I, J = tiles[i]
                        local = I * 16 + J
                        if local == 0:
                            while nxt:
                                nxt.pop(0)()
                            nxt = head_items(hd + 1) if hd < 15 else []
                        stage1(tiles[i])
                    if i >= DEPTH:
                        stage2(tiles[i - DEPTH])
                    if nxt and local >= 2 and (local - 2) % 5 == 0:
                        nxt.pop(0)()
                while nxt:
                    nxt.pop(0)()
                out_proj("w_o", lambda k, g: blk(k)[:, tgs(g)], lambda k, g: [MIX[k]], half * 512, 4)
            S.barrier()

        def final_stage(goff):
            YT = [Buf("yt%d" % i) for i in range(8)]
            OS = [Buf("os%d" % i) for i in range(2)]
            RSF = [Buf("rsf%d" % i) for i in range(4)]
            ytv = lambda c: blkf(2 * c, 2)[:, 0:512]
            osv = lambda i: blkf(16 + 2 * i, 2)[:, 0:1024]
            rsf = lambda g: blkf(2 * g, 2)[:, 1024:1536]
            pbs = []
            for g in range(4):
                pb = nb_()
                for c in range(8):
                    q = state["sq"]
                    state["sq"] = (q + 1) % 4
                    S.op("act", L("activation", sqr[:, q, :], xT[:, c, tgs(g)], AF.Square),
                         reads=[XT[c][g]], writes=[SQ[q]])
                    mm(bank(pb), onesb[:], sqr[:, q, :], c == 0, c == 7, [SQ[q], B_const], [PB[pb]], track=True)
                pbs.append(pb)
            for g in range(4):
                S.op("act", L("activation", rsf(g), bank(pbs[g]), AF.Sqrt, scale=1.0 / 1024.0, bias=small[:, 0:1]),
                     reads=[PB[pbs[g]], B_small], writes=[RSF[g]])
                S.op("dve", L("reciprocal", rsf(g), rsf(g)), reads=[RSF[g]], writes=[RSF[g]])
            for g in range(4):
                for c in range(8):
                    eng = "dve"
                    if goff is None:
                        S.op(eng, L("tensor_copy", ytv(c), xT[:, c, tgs(g)]), reads=[XT[c][g]], writes=[YT[c]])
                    else:
                        S.op(eng, L("scalar_tensor_tensor", ytv(c), xT[:, c, tgs(g)], gains[:, goff + c:goff + c + 1],
                                    rsf(g), ALU.mult, ALU.mult), reads=[XT[c][g], RSF[g]], writes=[YT[c]])
                for j in range(4):
                    tt = g * 4 + j
                    oi = tt % 2
                    p0, p1 = nb_(), nb_()
                    for c in range(8):
                        pbx = p0 if c < 4 else p1
                        S.op("pe", L("transpose", bank(pbx)[:, (c % 4) * 128:(c % 4 + 1) * 128], ytv(c)[:, j * 128:(j + 1) * 128], ident[:]),
                            reads=[YT[c], B_const], writes=[PB[pbx]], track=(c % 4 == 3))
                    S.op("act", L("activation", osv(oi)[:, 0:512], bank(p0), AF.Copy),
                         reads=[PB[p0]], writes=[OS[oi]])
                    S.op("act", L("activation", osv(oi)[:, 512:1024], bank(p1), AF.Copy),
                         reads=[PB[p1]], writes=[OS[oi]])
                    S.dma("sp", y_out[tt * 128:(tt + 1) * 128, :], osv(oi), reads=[OS[oi]])

        if KDBG == 10:
            for _ in range(30):
                norm_stage(0)
        elif KDBG == 11:
            norm_stage(0)
            for i in range(40):
                slot, wb = wtile(wreq_cols("w_in", 8, (i % 24) * 128, 128))
                wq = wview(slot, 8, 128)
                pb = proj_fm(wq, wb, 0, hT_rhs, hT_bufs, 8, i % 4)
                S.op("dve", L("tensor_copy", rs_b[:], bank(pb)), reads=[PB[pb]], writes=[B_rsb])
        elif nstage >= 1:
            norm_stage(0)
            diff_phase()
        if nstage >= 2:
            na_phase()
        if nstage >= 3:
            ffn(0, 8)
        if nstage >= 4:
            norm_stage(16)
            mla_phase()
        if nstage >= 5:
            ffn(1, 24)
        final_stage(32 if nstage >= 5 else None)
        S.final_wait("sp")
        S.emit()
    return nc, requests


_CACHE = {}


def _get_program(nstage=99):
    if nstage not in _CACHE:
        _, reqs = build(None, nstage)
        nc, _ = build(reqs, nstage)
        _CACHE[nstage] = (nc, reqs)
    return _CACHE[nstage]


def kernel(**inputs):
    nstage = int(inputs.pop("_nstage", 99))
    cores = inputs.pop("_cores", None)
    shared = _prep_shared(inputs)
    x = np.asarray(inputs["x"], np.float32)
    nb = x.shape[0] if cores is None else cores
    nc, plan = _get_program(nstage)
    dev = {k: shared[k] for k in SHAPES if k != "x"}
    dev["wpack"] = _pack_weights(shared, plan)
    in_maps = []
    for b in range(nb):
        m = dict(dev)
        m["x"] = np.ascontiguousarray(x[b])
        in_maps.append(m)
    res = run_bass_kernel_spmd(nc, in_maps, core_ids=list(range(nb)))
    out = np.stack([np.asarray(r["y"], np.float32) for r in res.results], axis=0)
    return out
```

```python
import math
import os
KDBG = int(os.environ.get("KDBG", "0"))
from contextlib import ExitStack
import numpy as np
import ml_dtypes
import concourse.bass as bass
import concourse.mybir as mybir
from concourse.bass_utils import run_bass_kernel_spmd

F32 = mybir.dt.float32
BF16 = mybir.dt.bfloat16
AF = mybir.ActivationFunctionType
ALU = mybir.AluOpType

T = 2048
D = 1024
DFF = 2816
NFC = 22
EPS = 1e-6
NEG = -30000.0
NSLOT = 4
LOOKAHEAD = 2
NBLK = 21


class Buf:
    __slots__ = ("name", "w", "r", "dsem", "dcnt", "excl")

    def __init__(self, name, excl=False):
        self.name = name
        self.excl = excl
        self.w = None
        self.r = {}
        self.dsem = None
        self.dcnt = 0


class Sched:
    ENG = ["pe", "act", "dve", "pool", "sp"]
    COMPUTE = ["pe", "act", "dve", "pool"]

    def __init__(self, nc, stack):
        self.nc = nc
        self.stack = stack
        self.ops = {e: [] for e in self.ENG}
        self.cnt = {e: 0 for e in self.ENG}
        self.sem = {e: stack.enter_context(nc.semaphore("s_" + e)) for e in self.ENG}
        self.seen = {e: {} for e in self.ENG}
        self.semobj = {e: self.sem[e] for e in self.ENG}
        self.nd = 0
        self.dma_bufs = []

    def _deps(self, eng, reads, writes):
        deps = {}

        def add(d):
            if d is None:
                return
            k, v = d
            if deps.get(k, 0) < v:
                deps[k] = v
        for b in reads:
            add(b.w)
        for b in writes:
            add(b.w)
            for d in b.r.items():
                add(d)
        waits = []
        seen = self.seen[eng]
        for k, v in deps.items():
            if k == "pe" and eng == "pe":
                continue
            if seen.get(k, 0) >= v:
                continue
            seen[k] = v
            waits.append((self.semobj[k], v))
        return waits

    def op(self, eng, fn, reads=(), writes=(), track=True):
        ex = [b for b in reads if b.excl]
        if ex:
            reads = [b for b in reads if not b.excl]
            writes = list(writes) + [b for b in ex if b not in writes]
        waits = self._deps(eng, reads, writes)
        idx = self.cnt[eng] + 1
        if track:
            self.cnt[eng] = idx
        ev = (eng, idx)
        for b in reads:
            if b.r.get(eng, 0) < idx:
                b.r[eng] = idx
        for b in writes:
            b.w = ev
            b.r = {}
        self.ops[eng].append((waits, fn, track, None))

    def dma(self, q, out, in_, reads=(), writes=()):
        waits = self._deps(q, reads, writes)
        bufs = list(writes) + list(reads)
        assert len(bufs) == 1
        b = bufs[0]
        if b.dsem is None:
            self.nd += 1
            key = "d%d" % self.nd
            b.dsem = key
            self.semobj[key] = self.stack.enter_context(self.nc.semaphore(key))
            self.dma_bufs.append(b)
        b.dcnt += 16
        ev = (b.dsem, b.dcnt)
        if writes:
            b.w = ev
            b.r = {}
        else:
            b.r[b.dsem] = b.dcnt
        self.ops[q].append((waits, L("dma_start", out=out, in_=in_), False,
                            (self.semobj[b.dsem], 16)))

    def barrier(self, dma=False):
        for e in self.ENG:
            waits = []
            if dma:
                for b in self.dma_bufs:
                    if self.seen[e].get(b.dsem, 0) < b.dcnt:
                        self.seen[e][b.dsem] = b.dcnt
                        waits.append((self.semobj[b.dsem], b.dcnt))
            for k in self.COMPUTE:
                v = self.cnt[k]
                if v > 0 and self.seen[e].get(k, 0) < v:
                    self.seen[e][k] = v
                    waits.append((self.semobj[k], v))
            if waits:
                self.ops[e].append((waits, None, False, None))

    def final_wait(self, eng):
        waits = [(self.semobj[b.dsem], b.dcnt) for b in self.dma_bufs]
        for k in self.COMPUTE:
            if self.cnt[k] > 0:
                waits.append((self.semobj[k], self.cnt[k]))
        self.ops[eng].append((waits, None, False, None))

    def emit(self):
        nc = self.nc
        handles = {"pe": "tensor", "act": "scalar", "dve": "vector", "pool": "gpsimd", "sp": "sync"}
        with nc.Block() as block:
            for e in self.ENG:
                ops = self.ops[e]
                own = self.sem[e]

                def body(engh, ops=ops, own=own):
                    for waits, fn, track, dinc in ops:
                        for s, v in waits:
                            engh.wait_ge(s, v)
                        if fn is None:
                            continue
                        ins = fn(engh)
                        if dinc is not None:
                            ins.then_inc(dinc[0], dinc[1])
                        elif track:
                            ins.then_inc(own, 1)
                getattr(block, handles[e])(body)


def L(method, *args, **kw):
    return lambda e: getattr(e, method)(*args, **kw)


def _na_tile_lists():
    lists = []
    tid = {}
    for m in range(16):
        if m <= 1:
            kbs = [0, 1, 2, 3]
        elif m >= 14:
            kbs = [12, 13, 14, 15]
        else:
            kbs = [m - 2, m - 1, m, m + 1, m + 2]
        cur = []
        for kb in kbs:
            key = ("int", kb - m) if 2 <= m <= 13 else ("edge", m, kb)
            if key not in tid:
                tid[key] = len(tid)
            cur.append((kb, tid[key]))
        lists.append(cur)
    return lists, tid


def _na_index_tables():
    lists, tid = _na_tile_lists()
    ntile = len(tid)
    dr_i = np.zeros((ntile, 128, 128), np.int64)
    dc_i = np.zeros((ntile, 128, 128), np.int64)
    valid = np.zeros((ntile, 128, 128), bool)
    done = set()
    for m in range(16):
        for kb, t in lists[m]:
            if t in done:
                continue
            done.add(t)
            kk = np.arange(128)
            kr = (2 * kb + kk // 64)[:, None]
            kc = (kk % 64)[:, None]
            qr = (2 * m + kk // 64)[None, :]
            qc = (kk % 64)[None, :]
            rs = np.clip(qr - 4, 0, 24)
            ws = np.clip(qc - 8, 0, 48)
            v = (kr >= rs) & (kr < rs + 8) & (kc >= ws) & (kc < ws + 16)
            dr = np.clip(kr - qr + 7, 0, 14)
            dc = np.clip(kc - qc + 15, 0, 30)
            dr_i[t] = np.broadcast_to(dr, (128, 128))
            dc_i[t] = np.broadcast_to(dc, (128, 128))
            valid[t] = v
    return dr_i, dc_i, valid


def _na_kb_layout():
    lists, tid = _na_tile_lists()
    layout = []
    off = 0
    shared = None
    for kb in range(16):
        ms = [m for m in range(16) if kb in [k for k, _ in lists[m]]]
        tl = [dict(lists[m])[kb] for m in ms]
        if 4 <= kb <= 11:
            if shared is None:
                shared = off
                off += 128 * len(ms)
            o = shared
        else:
            o = off
            off += 128 * len(ms)
        layout.append((kb, o, ms, tl))
    return layout, off


def _const_tables():
    c = {}
    c["ident"] = np.eye(128, dtype=np.float32)
    slopes = 2.0 ** (-8.0 * np.arange(1, 5) / 4)
    i = np.arange(T)
    a = (i // 128).astype(np.float32)
    r = (i % 128).astype(np.float32)
    augq = np.zeros((4, 4, T), np.float32)
    for h in range(4):
        s = slopes[h]
        augq[h, 0] = -s * 128.0 * a
        augq[h, 1] = -s * r
        augq[h, 2] = s
        augq[h, 3] = s
    c["augq"] = augq
    augk = np.zeros((2, 4, T), np.float32)
    for v, sg in enumerate([1.0, -1.0]):
        augk[v, 0] = sg
        augk[v, 1] = sg
        augk[v, 2] = sg * 128.0 * a
        augk[v, 3] = sg * r
    c["augk"] = augk
    jj = np.arange(128)[:, None]
    ii = np.arange(128)[None, :]
    cd = np.zeros((128, 4 * 128), np.float32)
    for h in range(4):
        cd[:, h * 128:(h + 1) * 128] = -2.0 * slopes[h] * np.maximum(jj - ii, 0)
    c["cdiag"] = cd
    inv = (10000.0 ** (-np.arange(0, 32, 2, dtype=np.float32) / np.float32(32))).astype(np.float32)
    ang = (np.arange(T, dtype=np.float32)[:, None] * inv[None, :]).astype(np.float32)
    cos = np.cos(ang.astype(np.float64)).astype(np.float32).T
    sin = np.sin(ang.astype(np.float64)).astype(np.float32).T
    rt = np.zeros((128, T), np.float32)
    rt[64:80] = cos
    rt[80:96] = cos
    rt[96:112] = -sin
    rt[112:128] = sin
    c["ropetab"] = rt
    return c


def _pm(v, nch):
    return np.ascontiguousarray(np.asarray(v, np.float32).reshape(nch, 128).T)


def _prep_shared(inp):
    g = {}
    c = _const_tables()
    g.update(c)
    f = lambda k: np.asarray(inp[k], np.float32)
    gains = np.zeros((128, 48), np.float32)
    gains[:, 0:8] = _pm(f("mix_norm_e")[0], 8)
    gains[:, 8:16] = _pm(f("ffn_norm_g")[0], 8)
    gains[:, 16:24] = _pm(f("mix_norm_o")[0], 8)
    gains[:, 24:32] = _pm(f("ffn_norm_g")[1], 8)
    gains[:, 32:40] = _pm(f("final_norm_g"), 8)
    gains[:, 40:44] = _pm(f("q_norm_g")[0], 4)
    gains[:, 44:46] = _pm(f("kv_norm_g")[0], 2)
    gains[:, 46] = f("diff_subln_g")[0]
    g["gains"] = gains
    cv = np.zeros((128, 2 * NFC * 4), np.float32)
    for l in range(2):
        for k in range(3):
            cv[:, l * 88 + k * 22: l * 88 + (k + 1) * 22] = _pm(f("ffn_conv_w")[l, k], NFC)
        cv[:, l * 88 + 66: l * 88 + 88] = _pm(f("ffn_conv_b")[l], NFC)
    g["convp"] = cv
    lam = np.zeros((128, 256), np.float32)
    for k, nm in enumerate(["diff_lq1", "diff_lk1", "diff_lq2", "diff_lk2"]):
        lam[:, k * 64:(k + 1) * 64] = np.broadcast_to(f(nm)[0][None, :], (128, 64))
    g["lamv"] = lam
    dr_i, dc_i, valid = _na_index_tables()
    rpb = f("na_rpb")[0]
    nb = np.empty((8, dr_i.shape[0], 128, 128), np.float32)
    for h in range(8):
        nb[h] = np.where(valid, rpb[h][dr_i, dc_i], np.float32(NEG))
    layout, ncols = _na_kb_layout()
    nb2 = np.empty((8, 128, ncols), np.float32)
    for kb, o, ms, tl in layout:
        for i, t in enumerate(tl):
            nb2[:, :, o + i * 128:o + (i + 1) * 128] = nb[:, t]
    g["nbias"] = nb2
    g["w_in"] = f("w_in_e")[0]
    g["w_out"] = f("w_out_e")[0]
    g["w_gate"] = f("w_ffn_gate")
    g["w_val"] = f("w_ffn_val")
    g["w_down"] = f("w_ffn_down")
    g["w_dq"] = f("w_dq")[0]
    wuq = f("w_uq")[0]
    cols = []
    for h in range(16):
        b = h * 96
        cols += list(range(b, b + 96)) + list(range(b + 80, b + 96)) + list(range(b + 64, b + 80))
    g["w_uq"] = np.ascontiguousarray(wuq[:, cols])
    wdkv = f("w_dkv")[0]
    wd = np.zeros((1024, 384), np.float32)
    wd[:, 0:256] = wdkv[:, 0:256]
    wd[:, 320:352] = wdkv[:, 256:288]
    wd[:, 352:368] = wdkv[:, 272:288]
    wd[:, 368:384] = wdkv[:, 256:272]
    g["w_dkv"] = wd
    wukv = f("w_ukv")[0].reshape(256, 16, 128)
    g["w_uk"] = np.ascontiguousarray(wukv[:, :, 0:64].reshape(256, 1024))
    g["w_uv"] = np.ascontiguousarray(wukv[:, :, 64:128].reshape(256, 1024))
    g["w_o"] = f("w_o_mla")[0]
    return g


BF16_IN = ()
SHAPES = {
    "x": [T, D], "ident": [128, 128], "augq": [4, 4, T], "augk": [2, 4, T], "cdiag": [128, 512],
    "ropetab": [128, T], "gains": [128, 48], "convp": [128, 176], "lamv": [128, 256],
    "nbias": [8, 128, 5248],
}


def _desc_size(d):
    if d[0] == "gv":
        return 2048
    return d[2] * d[4]


def _pack_weights(g, plan):
    pack = np.zeros((len(plan), 128, 2048), np.float32)
    for i, d in enumerate(plan):
        if d[0] == "gv":
            _, layer, c = d
            wg = g["w_gate"][layer][:, c * 128:(c + 1) * 128].reshape(8, 128, 128)
            wv = g["w_val"][layer][:, c * 128:(c + 1) * 128].reshape(8, 128, 128)
            t = np.concatenate([wg, wv], axis=2)
            pack[i] = t.transpose(1, 0, 2).reshape(128, 2048)
        else:
            _, name, kc, c0, ncols, lidx, r0 = d
            w = g[name] if lidx is None else g[name][lidx]
            t = w[r0:r0 + kc * 128, c0:c0 + ncols].reshape(kc, 128, ncols)
            pack[i, :, 0:kc * ncols] = t.transpose(1, 0, 2).reshape(128, kc * ncols)
    return pack


def build(plan=None, nstage=99):
    nc = bass.Bass("TRN2", target_bir_lowering=False)
    dram = {k: nc.dram_tensor(k, shp, BF16 if k in BF16_IN else F32, kind="ExternalInput").ap() for k, shp in SHAPES.items()}
    if plan is not None:
        dram["wpack"] = nc.dram_tensor("wpack", [len(plan), 128, 2048], F32, kind="ExternalInput").ap()
    y_out = nc.dram_tensor("y", [T, D], F32, kind="ExternalOutput").ap()
    requests = []
    with ExitStack() as st:
        S = Sched(nc, st)
        sb = lambda n, shp, dt: st.enter_context(nc.sbuf_tensor("sb_" + n, shp, dt))
        xT = sb("xT", [128, 8, T], F32)
        hT = sb("hT", [128, 8, T], BF16)
        wring = sb("wring", [128, NSLOT, 2048], BF16)
        AR = sb("arena", [128, NBLK * 2048], BF16)
        ident = sb("ident", [128, 128], F32)
        identb = sb("identb", [128, 128], BF16)
        onesb = sb("onesb", [128, 128], BF16)
        zerob = sb("zerob", [128, 128], BF16)
        gains = sb("gains", [128, 48], F32)
        convp = sb("convp", [128, 176], F32)
        small = sb("small", [128, 16], F32)
        cdiag = sb("cdiag", [128, 512], BF16)
        rs_a = sb("rs_a", [128, 512], F32)
        rs_b = sb("rs_b", [128, 512], F32)
        sqr = sb("sqr", [128, 4, 512], BF16)
        PS = st.enter_context(nc.psum_tensor("PS", [128, 8 * 512], F32))

        XT = [[Buf("xT%d_%d" % (c, g)) for g in range(4)] for c in range(8)]
        HT = [[Buf("hT%d_%d" % (c, g)) for g in range(4)] for c in range(8)]
        WS = [Buf("ws%d" % i) for i in range(NSLOT)]
        PB = [Buf("pb%d" % i, excl=True) for i in range(8)]
        B_const = Buf("const")
        B_small = Buf("small")
        B_rsa, B_rsb = Buf("rsa"), Buf("rsb")
        SQ = [Buf("sq%d" % i) for i in range(4)]

        def bank(i):
            return PS[:, i * 512:(i + 1) * 512]

        def blk(k, n=1):
            return AR[:, k * 2048:(k + n) * 2048]

        def blkf(k, n=2):
            return AR[:, k * 2048:(k + n) * 2048].bitcast(F32)

        state = {"bank": 0, "wi": 0, "wissued": 0, "sq": 0, "sb": 0, "tick": 0}
        pending = []
        lamv = blkf(0, 1)[:, 0:256]

        def nb_():
            b = state["bank"]
            state["bank"] = (b + 1) % 8
            return b

        def tgs(g):
            return slice(g * 512, (g + 1) * 512)

        def wtile(desc):
            i = state["wi"]
            state["wi"] = i + 1
            requests.append(desc)
            if plan is not None:
                hi = min(len(plan), i + 1 + LOOKAHEAD)
                while state["wissued"] < hi:
                    j = state["wissued"]
                    slot = j % NSLOT
                    X = _desc_size(plan[j])
                    S.dma("pool", wring[:, slot, 0:X], dram["wpack"][j][:, 0:X], writes=[WS[slot]])
                    state["wissued"] = j + 1
            return wring[:, i % NSLOT, :], WS[i % NSLOT]

        def wreq_cols(name, kc, c0, ncols, lidx=None, r0=0):
            return ("cols", name, kc, c0, ncols, lidx, r0)

        def wview(slot, kc, ncols):
            return slot[:, 0:kc * ncols].rearrange("p (k n) -> p k n", k=kc)

        S.dma("sp", ident[:], dram["ident"], writes=[B_const])
        gb_l = Buf("gains_l")
        S.dma("sp", gains[:], dram["gains"], writes=[gb_l])
        cb1, cb2, cb3 = Buf("c1"), Buf("c2"), Buf("c3")
        S.dma("sp", convp[:], dram["convp"], writes=[cb1])
        S.dma("sp", lamv, dram["lamv"], writes=[cb2])
        S.dma("pool", cdiag[:], dram["cdiag"], writes=[cb3])
        S.op("dve", L("tensor_copy", identb[:], ident[:]), reads=[B_const], writes=[B_const])
        S.op("dve", L("memset", onesb[:], 1.0), writes=[B_const])
        S.op("dve", L("memset", zerob[:], 0.0), writes=[B_const])
        S.op("dve", L("memset", small[:, 0:1], EPS), writes=[B_small])
        lam_init = 0.8 - 0.6 * math.exp(-0.3 * 0)
        S.op("dve", L("tensor_tensor", lamv[:, 0:64], lamv[:, 0:64], lamv[:, 64:128], ALU.mult),
             reads=[cb2], writes=[cb2])
        S.op("dve", L("tensor_tensor", lamv[:, 128:192], lamv[:, 128:192], lamv[:, 192:256], ALU.mult),
             reads=[cb2], writes=[cb2])
        S.op("dve", L("reduce_sum", small[:, 4:5], lamv[:, 0:64], mybir.AxisListType.X),
             reads=[cb2], writes=[B_small])
        S.op("dve", L("reduce_sum", small[:, 5:6], lamv[:, 128:192], mybir.AxisListType.X),
             reads=[cb2], writes=[B_small])
        S.op("act", L("activation", small[:, 6:8], small[:, 4:6], AF.Exp), reads=[B_small], writes=[B_small])
        S.op("dve", L("scalar_tensor_tensor", small[:, 1:2], small[:, 7:8], -lam_init, small[:, 6:7],
                                                     ALU.add, ALU.subtract), reads=[B_small], writes=[B_small])
        S.op("dve", L("tensor_scalar", small[:, 2:3], gains[:, 46:47], 1.0 - lam_init, None, ALU.mult),
             reads=[B_small, gb_l], writes=[B_small])
        S.barrier(dma=True)

        def mm(out, lhsT, rhs, start, stop, reads, writes, track=None, sgc=False):
            if track is None:
                track = stop
            if sgc:
                S.op("pe", L("matmul", out, lhsT, rhs, start=start, stop=stop, skip_group_check=True),
                     reads=reads, writes=writes, track=track)
            else:
                S.op("pe", L("matmul", out, lhsT, rhs, start=start, stop=stop), reads=reads, writes=writes,
                     track=track)

        def rstd_from_psum(pb, nfeat, dst, dstbuf):
            S.op("act", L("activation", rs_a[:], bank(pb), AF.Sqrt, scale=1.0 / nfeat, bias=small[:, 0:1]),
                 reads=[PB[pb], B_small], writes=[B_rsa])
            S.op("dve", L("reciprocal", dst, rs_a[:]), reads=[B_rsa], writes=[dstbuf])

        def norm_stage(goff):
            for g in range(4):
                pb = nb_()
                for c in range(8):
                    q = state["sq"]
                    state["sq"] = (q + 1) % 4
                    S.op("act", L("activation", sqr[:, q, :], xT[:, c, tgs(g)], AF.Square),
                         reads=[XT[c][g]], writes=[SQ[q]])
                    mm(bank(pb), onesb[:], sqr[:, q, :], c == 0, c == 7, [SQ[q], B_const], [PB[pb]], track=True)
                rstd_from_psum(pb, 1024.0, rs_b[:], B_rsb)
                for c in range(8):
                    S.op("dve", L("scalar_tensor_tensor", hT[:, c, tgs(g)], xT[:, c, tgs(g)], gains[:, goff + c:goff + c + 1], rs_b[:],
                        ALU.mult, ALU.mult), reads=[XT[c][g], B_rsb], writes=[HT[c][g]])

        def proj_fm(wv, wbuf, col0, rhs_fn, rhs_bufs_fn, nk, g, M=128):
            pb = nb_()
            for k in range(nk):
                mm(bank(pb)[0:M, :], wv[:, k, col0:col0 + M], rhs_fn(k, g), k == 0, k == nk - 1,
                   [wbuf] + rhs_bufs_fn(k, g), [PB[pb]])
            return pb

        hT_rhs = lambda k, g: hT[:, k, tgs(g)]
        hT_bufs = lambda k, g: [HT[k][g]]

        def resid_add(pb, n, g, eng="dve"):
            S.op(eng, L("tensor_tensor", xT[:, n, tgs(g)], bank(pb), xT[:, n, tgs(g)], ALU.add),
                 reads=[PB[pb], XT[n][g]], writes=[XT[n][g]])

        def out_proj(name, mix_fn, mix_bufs, krow0, nkc):
            assert nkc == 4
            tiles_ = []
            for nt in range(2):
                slot, wb = wtile(wreq_cols(name, nkc, nt * 512, 512, r0=krow0))
                tiles_.append((wview(slot, nkc, 512), wb))
            for g in range(4):
                for n in range(8):
                    wv, wb = tiles_[n // 4]
                    pb = proj_fm(wv, wb, (n % 4) * 128, mix_fn, mix_bufs, nkc, g)
                    resid_add(pb, n, g)

        XS = [Buf("xs%d" % i) for i in range(16)]
        for tt in range(16):
            S.dma("sp", blkf(tt, 1), dram["x"][tt * 128:(tt + 1) * 128, :], writes=[XS[tt]])
        for g in range(4):
            for c in range(8):
                pb = nb_()
                for j in range(4):
                    tt = g * 4 + j
                    S.op("pe", L("transpose", bank(pb)[:, j * 128:(j + 1) * 128], blkf(tt, 1)[:, c * 128:(c + 1) * 128], ident[:]),
                        reads=[XS[tt], B_const], writes=[PB[pb]], track=(j == 3))
                eng = "act" if c % 2 == 0 else "dve"
                if eng == "act":
                    S.op("act", L("activation", xT[:, c, tgs(g)], bank(pb), AF.Copy),
                         reads=[PB[pb]], writes=[XT[c][g]])
                else:
                    S.op("dve", L("tensor_copy", xT[:, c, tgs(g)], bank(pb)),
                         reads=[PB[pb]], writes=[XT[c][g]])

        def ffn(layer, goff):
            norm_stage(goff)
            U = [Buf("u%d" % i) for i in range(11)]
            TB = [Buf("tb%d" % i) for i in range(2)]
            cp = layer * 88
            for rnd in range(2):
                for cc in range(11):
                    c = rnd * 11 + cc

                    slot, wb = wtile(("gv", layer, c))
                    wv = wview(slot, 8, 256)
                    for g in range(4):
                        for k in range(8):
                            mm(bank(g), wv[:, k, 0:128], hT[:, k, tgs(g)], k == 0, k == 7, [wb, HT[k][g]], [PB[g]])
                    for g in range(4):
                        for k in range(8):
                            mm(bank(4 + g), wv[:, k, 128:256], hT[:, k, tgs(g)], k == 0, k == 7,
                               [wb, HT[k][g]], [PB[4 + g]])
                    tb = cc % 2
                    tv = blkf(11 + 2 * tb)
                    G = PS[:, 0:2048]
                    gb = [PB[0], PB[1], PB[2], PB[3]]
                    w0 = convp[:, cp + c:cp + c + 1]
                    w1 = convp[:, cp + 22 + c:cp + 22 + c + 1]
                    w2 = convp[:, cp + 44 + c:cp + 44 + c + 1]
                    bb = convp[:, cp + 66 + c:cp + 66 + c + 1]
                    S.op("act", L("activation", tv, G, AF.Identity, scale=w1, bias=bb),
                         reads=gb + [cb1], writes=[TB[tb]])
                    S.op("dve", L("scalar_tensor_tensor", tv[:, 1:T], PS[:, 0:T - 1], w0, tv[:, 1:T], ALU.mult, ALU.add),
                        reads=gb + [TB[tb]], writes=[TB[tb]])
                    S.op("dve", L("scalar_tensor_tensor", tv[:, 0:T - 1], PS[:, 1:T], w2, tv[:, 0:T - 1], ALU.mult, ALU.add),
                        reads=gb + [TB[tb]], writes=[TB[tb]])
                    S.op("act", L("activation", tv, tv, AF.Gelu), reads=[TB[tb]], writes=[TB[tb]])
                    for hh in range(2):
                        S.op("dve", L("tensor_tensor", blk(cc)[:, hh * 1024:(hh + 1) * 1024], tv[:, hh * 1024:(hh + 1) * 1024],
                            PS[:, 2048 + hh * 1024:2048 + (hh + 1) * 1024], ALU.mult),
                            reads=[TB[tb], PB[4 + 2 * hh], PB[5 + 2 * hh]], writes=[U[cc]])
                for n in range(8):
                    slot, wb = wtile(wreq_cols("w_down", 11, n * 128, 128, lidx=layer, r0=rnd * 1408))
                    wv = wview(slot, 11, 128)
                    for g in range(4):
                        pb = proj_fm(wv, wb, 0, lambda k, g: blk(k)[:, tgs(g)], lambda k, g: [U[k]], 11, g)
                        resid_add(pb, n, g)
            S.barrier()

        def softmax_epilogue_diff(h, I, pO, pD, MIX):
            r1, r2, t2 = blkf(13, 2)[:, 0:512], blkf(13, 2)[:, 512:1024], blkf(13, 2)[:, 1024:1536]
            t1 = blkf(15, 2)[:, I * 512:(I + 1) * 512]
            EB = state.setdefault("EB", [Buf("eb%d" % i) for i in range(3)])
            A1 = state.setdefault("A1", [Buf("a1_%d" % i) for i in range(4)])
            S.op("dve", L("tensor_copy", t1, bank(pO[0])), reads=[PB[pO[0]]], writes=[A1[I]])
            S.op("dve", L("tensor_copy", r1, bank(pD[0])), reads=[PB[pD[0]]], writes=[EB[0]])
            S.op("dve", L("tensor_copy", t2, bank(pO[1])), reads=[PB[pO[1]]], writes=[EB[2]])
            S.op("dve", L("tensor_copy", r2, bank(pD[1])), reads=[PB[pD[1]]], writes=[EB[1]])
            S.op("dve", L("reciprocal", r1, r1), reads=[EB[0]], writes=[EB[0]])
            S.op("dve", L("reciprocal", r2, r2), reads=[EB[1]], writes=[EB[1]])
            S.op("dve", L("tensor_tensor", t1, t1, r1, ALU.mult), reads=[EB[0], A1[I]], writes=[A1[I]])
            S.op("dve", L("tensor_tensor", t2, t2, r2, ALU.mult), reads=[EB[1], EB[2]], writes=[EB[2]])
            S.op("dve", L("scalar_tensor_tensor", t1, t2, small[:, 1:2], t1, ALU.mult, ALU.add),
                 reads=[A1[I], EB[2], B_small], writes=[A1[I]])

        def subln_head(h, MIX):
            A1 = state["A1"]
            RS = state.setdefault("RS", [Buf("rs_%d" % i) for i in range(4)])
            pbs = []
            for I in range(4):
                t1 = blkf(15, 2)[:, I * 512:(I + 1) * 512]
                S.op("act", L("activation", sqr[:, I, :], t1, AF.Square), reads=[A1[I]], writes=[SQ[I]])
                pb = 4 + state["sb"] % 4
                state["sb"] += 1
                mm(bank(pb), onesb[:], sqr[:, I, :], True, True, [SQ[I], B_const], [PB[pb]])
                pbs.append(pb)
            for I in range(4):
                rsd = blkf(17, 2)[:, I * 512:(I + 1) * 512]
                S.op("act", L("activation", rsd, bank(pbs[I]), AF.Sqrt, scale=1.0 / 128.0, bias=small[:, 0:1]),
                     reads=[PB[pbs[I]], B_small], writes=[RS[I]])
            for I in range(4):
                t1 = blkf(15, 2)[:, I * 512:(I + 1) * 512]
                rsd = blkf(17, 2)[:, I * 512:(I + 1) * 512]
                S.op("dve", L("reciprocal", rsd, rsd), reads=[RS[I]], writes=[RS[I]])
                S.op("dve", L("scalar_tensor_tensor", blk(h)[:, tgs(I)], t1, small[:, 2:3], rsd, ALU.mult, ALU.mult),
                     reads=[A1[I], RS[I], B_small], writes=[MIX[h]])

        def diff_phase():
            MIX = [Buf("mix%d" % i) for i in range(4)]
            QB = [Buf("dq%d" % i) for i in range(2)]
            KB = [[Buf("dk%d%d" % (i, v)) for v in range(2)] for i in range(2)]
            VB = Buf("dv")
            PT = [Buf("pt%d" % i) for i in range(8)]
            Qt = [blk(4), blk(5)]
            Kt = [[blk(6), blk(7)], [blk(8), blk(9)]]
            Vt = blk(10).rearrange("p (t e) -> p t e", t=16)
            ptv = lambda i: blk(11, 2)[:, i * 512:(i + 1) * 512]
            pti = [0]
            for n in range(2):
                for v in range(2):
                    S.dma("pool", Kt[n][v][64:68, :], dram["augk"][v], writes=[KB[n][v]])
            for h in range(int(os.environ.get("KH0", "0")), int(os.environ.get("KH1", "4")) if KDBG in (0, 5) else 1):
                if KDBG == 1:
                    break
                if os.environ.get("KBAR", "0") == "1":
                    S.barrier()
                slot, wb = wtile(wreq_cols("w_in", 8, h * 128, 128))
                wq = wview(slot, 8, 128)
                for n in range(2):
                    S.dma("pool", Qt[n][64:68, :], dram["augq"][h], writes=[QB[n]])
                for g in range(4):
                    pb = proj_fm(wq, wb, 0, hT_rhs, hT_bufs, 8, g)
                    S.op("dve", L("tensor_scalar", Qt[0][0:64, tgs(g)], bank(pb)[0:64, :], 0.125, None, ALU.mult),
                         reads=[PB[pb]], writes=[QB[0]])
                    S.op("dve", L("tensor_scalar", Qt[1][0:64, tgs(g)], bank(pb)[64:128, :], 0.125, None, ALU.mult),
                         reads=[PB[pb]], writes=[QB[1]])
                slot, wb = wtile(wreq_cols("w_in", 8, 512 + h * 128, 128))
                wk = wview(slot, 8, 128)
                for g in range(4):
                    pb = proj_fm(wk, wb, 0, hT_rhs, hT_bufs, 8, g)
                    for n in range(2):
                        S.op("act", L("activation", Kt[n][0][0:64, tgs(g)], bank(pb)[64 * n:64 * n + 64, :], AF.Copy),
                             reads=[PB[pb]], writes=[KB[n][0]])
                        S.op("dve", L("tensor_copy", Kt[n][1][0:64, tgs(g)], bank(pb)[64 * n:64 * n + 64, :]),
                             reads=[PB[pb]], writes=[KB[n][1]])
                slot, wb = wtile(wreq_cols("w_in", 8, 1024 + h * 128, 128))
                wvv = wview(slot, 8, 128)
                for g in range(4):
                    pb = nb_()
                    for j in range(4):
                        tt = g * 4 + j
                        for k in range(8):
                            mm(bank(pb)[:, j * 128:(j + 1) * 128], hT[:, k, tt * 128:(tt + 1) * 128], wvv[:, k, :],
                               k == 0, k == 7, [wb, HT[k][g]], [PB[pb]], track=(k == 7 and j == 3))
                    S.op("act", L("activation", Vt[:, g * 4:(g + 1) * 4, :], bank(pb).rearrange("p (t e) -> p t e", t=4), AF.Copy),
                        reads=[PB[pb]], writes=[VB])
                pO, pD = [0, 1], [2, 3]
                tiles = [(I, J, n) for I in range(4) for J in range(16) for n in range(2)]
                info = {}

                def stage1(t):
                    I, J, n = t
                    ps = 4 + state["sb"] % 4
                    state["sb"] += 1
                    if J < 4 * I or J >= 4 * I + 4:
                        v = 0 if J < 4 * I else 1
                        mm(bank(ps), Kt[n][v][0:68, J * 128:(J + 1) * 128], Qt[n][0:68, tgs(I)], True, True,
                           [KB[n][v], QB[n]], [PB[ps]])
                    else:
                        for b in range(4):
                            gb_ = 4 * I + b
                            v = 0 if gb_ >= J else 1
                            qs = slice(gb_ * 128, (gb_ + 1) * 128)
                            osl = bank(ps)[:, b * 128:(b + 1) * 128]
                            diag = gb_ == J
                            mm(osl, Kt[n][v][0:68, J * 128:(J + 1) * 128], Qt[n][0:68, qs], True, not diag,
                               [KB[n][v], QB[n]], [PB[ps]], track=(b == 3 and not diag))
                            if diag:
                                mm(osl, identb[:], cdiag[:, h * 128:(h + 1) * 128], False, True,
                                   [B_const, cb3], [PB[ps]], track=(b == 3))
                    pi = pti[0]
                    pti[0] = (pi + 1) % 8
                    S.op("act", L("activation", ptv(pi), bank(ps), AF.Exp), reads=[PB[ps]], writes=[PT[pi]])
                    info[t] = pi

                def stage2(t):
                    I, J, n = t
                    pi = info.pop(t)
                    mm(bank(pO[n]), Vt[:, J, :], ptv(pi), J == 0, J == 15, [VB, PT[pi]], [PB[pO[n]]], track=True)
                    mm(bank(pD[n]), onesb[:], ptv(pi), J == 0, J == 15, [B_const, PT[pi]], [PB[pD[n]]], track=True)
                    if J == 15 and n == 1:
                        softmax_epilogue_diff(h, I, pO, pD, MIX)
                        if I == 3:
                            pending.append((state["tick"] + 24, lambda h=h: subln_head(h, MIX)))

                DEPTH = 2
                for i in range(len(tiles) + DEPTH):
                    state["tick"] += 1
                    while pending and pending[0][0] <= state["tick"]:
                        pending.pop(0)[1]()
                    if i < len(tiles):
                        stage1(tiles[i])
                    if i >= DEPTH:
                        stage2(tiles[i - DEPTH])
            while pending:
                pending.pop(0)[1]()
            if KDBG == 0:
                out_proj("w_out", lambda k, g: blk(k)[:, tgs(g)], lambda k, g: [MIX[k]], 0, 4)
            S.barrier()

        def na_phase():
            lists, _ = _na_tile_lists()
            MIX = [Buf("nmix%d" % i) for i in range(4)]
            NQb, NKb = Buf("nq"), Buf("nk")
            VAb = [Buf("va0"), Buf("va1")]
            NBb = Buf("nbias")
            PT = [Buf("npt%d" % i) for i in range(6)]
            RD = [Buf("nrd%d" % i) for i in range(2)]
            NQ, NK = blk(4), blk(5)
            VA = [blk(6).rearrange("p (t e) -> p t e", t=16), blk(7).rearrange("p (t e) -> p t e", t=16)]
            NB2 = blk(15, 6)
            layout, _nc = _na_kb_layout()
            ptv = lambda i: blk(8, 3)[:, i * 768:(i + 1) * 768]
            rdv = lambda i: blkf(13, 2)[:, i * 512:(i + 1) * 512]
            for a in range(2):
                S.op("dve", L("memset", VA[a][:, :, 64:128], 1.0), writes=[VAb[a]])
            pti = [0]
            for j in range(4):
                slot, wb = wtile(wreq_cols("w_in", 8, 1536 + j * 128, 128))
                wq = wview(slot, 8, 128)
                for g in range(4):
                    pb = proj_fm(wq, wb, 0, hT_rhs, hT_bufs, 8, g)
                    S.op("dve", L("tensor_scalar", NQ[:, tgs(g)], bank(pb), 0.125, None, ALU.mult),
                         reads=[PB[pb]], writes=[NQb])
                slot, wb = wtile(wreq_cols("w_in", 8, 2048 + j * 128, 128))
                wk = wview(slot, 8, 128)
                for g in range(4):
                    pb = proj_fm(wk, wb, 0, hT_rhs, hT_bufs, 8, g)
                    S.op("dve", L("tensor_copy", NK[:, tgs(g)], bank(pb)), reads=[PB[pb]], writes=[NKb])
                slot, wb = wtile(wreq_cols("w_in", 8, 2560 + j * 128, 128))
                wvv = wview(slot, 8, 128)
                for g in range(4):
                    pb = nb_()
                    for jj in range(4):
                        tt = g * 4 + jj
                        for k in range(8):
                            mm(bank(pb)[:, jj * 128:(jj + 1) * 128], hT[:, k, tt * 128:(tt + 1) * 128], wvv[:, k, :],
                               k == 0, k == 7, [wb, HT[k][g]], [PB[pb]], track=(k == 7 and jj == 3))
                    pv = bank(pb).rearrange("p (t e) -> p t e", t=4)
                    S.op("act", L("activation", VA[0][:, g * 4:(g + 1) * 4, 0:64], pv[:, :, 0:64], AF.Copy),
                         reads=[PB[pb]], writes=[VAb[0]])
                    S.op("dve", L("tensor_copy", VA[1][:, g * 4:(g + 1) * 4, 0:64], pv[:, :, 64:128]),
                         reads=[PB[pb]], writes=[VAb[1]])
                for a in range(2):
                    S.dma("pool", NB2[:, a * 5248:(a + 1) * 5248], dram["nbias"][2 * j + a], writes=[NBb])
                if True:
                    info = {}

                    def stage1(t):
                        a, kbi = t
                        p0 = 64 * a
                        nbase = a * 5248
                        kb, o, ms, tl = layout[kbi]
                        W = 128 * len(ms)
                        q0 = 128 * ms[0]
                        b0 = 4 + 2 * (state["sb"] % 2)
                        state["sb"] += 1
                        pieces = [(0, min(W, 512))] + ([(512, W)] if W > 512 else [])
                        for (c0, c1) in pieces:
                            bb = b0 + c0 // 512
                            osl = PS[:, b0 * 512 + c0:b0 * 512 + c1]
                            mm(osl, NK[p0:p0 + 64, kb * 128:(kb + 1) * 128], NQ[p0:p0 + 64, q0 + c0:q0 + c1],
                               True, False, [NKb, NQb], [PB[bb]], track=False)
                            mm(osl, identb[:], NB2[:, nbase + o + c0:nbase + o + c1], False, True,
                               [B_const, NBb], [PB[bb]], track=True)
                        pi = pti[0]
                        pti[0] = (pi + 1) % 3
                        rb = [PB[b0]] + ([PB[b0 + 1]] if W > 512 else [])
                        S.op("act", L("activation", ptv(pi)[:, 0:W], PS[:, b0 * 512:b0 * 512 + W], AF.Exp),
                             reads=rb, writes=[PT[pi]])
                        info[t] = pi

                    def stage2(t, j=j):
                        a, kbi = t
                        p0 = 64 * a
                        kb, o, ms, tl = layout[kbi]
                        W = 128 * len(ms)
                        q0 = 128 * ms[0]
                        pi = info.pop(t)
                        if kbi == 0:
                            for b4 in range(4):
                                mm(bank(b4), zerob[:], NK[:, b4 * 512:(b4 + 1) * 512], True, True, [B_const, NKb], [PB[b4]], sgc=True)
                        g0 = q0
                        while g0 < q0 + W:
                            g1 = min(q0 + W, (g0 // 512 + 1) * 512)
                            mm(PS[:, g0:g1], VA[a][:, kb, :], ptv(pi)[:, g0 - q0:g1 - q0], False, True,
                               [VAb[a], PT[pi]], [PB[g0 // 512]], track=True, sgc=True)
                            g0 = g1
                        if kbi == 15:
                            ev = blkf(13, 2)
                            for b4 in range(4):
                                eng4 = "dve" if b4 % 2 == 0 else "act"
                                if eng4 == "dve":
                                    S.op("dve", L("tensor_copy", ev[:, b4 * 512:(b4 + 1) * 512], bank(b4)), reads=[PB[b4]], writes=[RD[0]])
                                else:
                                    S.op("act", L("activation", ev[:, b4 * 512:(b4 + 1) * 512], bank(b4), AF.Copy), reads=[PB[b4]], writes=[RD[0]])
                            rd = blkf(11, 2)
                            for g in range(4):
                                S.op("dve", L("reciprocal", rd[0:64, tgs(g)], ev[64:128, tgs(g)]), reads=[RD[0]], writes=[RD[1]])
                            for g in range(4):
                                S.op("dve", L("tensor_tensor", blk(j)[p0:p0 + 64, tgs(g)], ev[0:64, tgs(g)], rd[0:64, tgs(g)], ALU.mult),
                                     reads=[RD[0], RD[1]], writes=[MIX[j]])

                    DEPTH = 1
                    tiles = [(a, kbi) for a in range(2) for kbi in range(16)]
                    for i in range(len(tiles) + DEPTH):
                        if i < len(tiles):
                            stage1(tiles[i])
                        if i >= DEPTH:
                            stage2(tiles[i - DEPTH])
            out_proj("w_out", lambda k, g: blk(k)[:, tgs(g)], lambda k, g: [MIX[k]], 512, 4)
            S.barrier()

        def mla_phase():
            CQ = [Buf("cq%d" % i) for i in range(4)]
            CKV = [Buf("ckv%d" % i) for i in range(2)]
            KRb = Buf("kr")
            RTb = Buf("ropetab")
            QHb = [Buf("qh0"), Buf("qh1")]
            KHb = [Buf("kh0"), Buf("kh1")]
            VAb = [Buf("mva0"), Buf("mva1")]
            MIX = [Buf("mmix%d" % i) for i in range(4)]
            PT = [Buf("mpt%d" % i) for i in range(8)]
            TMP = [Buf("mtmp%d" % i) for i in range(2)]
            cqv = lambda c: blk(4 + c)
            ckvv = lambda c: blk(8 + c)
            KR = blk(10)
            rt = blkf(11, 2)
            QH = [blk(13), blk(14)]
            KH = [blk(15), blk(16)]
            VA = [blk(17).rearrange("p (t e) -> p t e", t=16), blk(18).rearrange("p (t e) -> p t e", t=16)]
            ptv = lambda i: blk(10 if i < 4 else 19)[:, (i % 4) * 512:(i % 4 + 1) * 512]
            tmpv = lambda i: blkf(20, 1)[:, i * 512:(i + 1) * 512]
            scale = 96.0 ** -0.5
            S.dma("sp", rt, dram["ropetab"], writes=[RTb])
            for a in range(2):
                S.op("dve", L("memset", VA[a][:, :, 64:128], 1.0), writes=[VAb[a]])

            def rope(dst, dstbuf, pb, g):
                S.op("dve", L("tensor_tensor", tmpv(0)[64:96, :], bank(pb)[64:96, :], rt[64:96, tgs(g)], ALU.mult),
                     reads=[PB[pb], RTb], writes=[TMP[0]])
                S.op("dve", L("tensor_tensor", tmpv(1)[64:96, :], bank(pb)[96:128, :], rt[96:128, tgs(g)], ALU.mult),
                     reads=[PB[pb], RTb], writes=[TMP[1]])
                S.op("dve", L("tensor_tensor", dst[64:96, tgs(g)], tmpv(0)[64:96, :], tmpv(1)[64:96, :], ALU.add),
                     reads=[TMP[0], TMP[1]], writes=[dstbuf])

            def latent(wtiles, nch, goff, dstv, dstb, nfeat, extra=None):
                for g in range(4):
                    pbs = []
                    for (wv, wb, ncol) in wtiles:
                        for cc in range(ncol // 128):
                            pbs.append(proj_fm(wv, wb, cc * 128, hT_rhs, hT_bufs, 8, g))
                    pss = nb_()
                    while pss in pbs:
                        pss = nb_()
                    for c in range(nch):
                        q = state["sq"]
                        state["sq"] = (q + 1) % 4
                        S.op("act", L("activation", sqr[:, q, :], bank(pbs[c]), AF.Square),
                             reads=[PB[pbs[c]]], writes=[SQ[q]])
                        mm(bank(pss), onesb[:], sqr[:, q, :], c == 0, c == nch - 1, [SQ[q], B_const], [PB[pss]], track=True)
                    rstd_from_psum(pss, nfeat, rs_b[:], B_rsb)
                    for c in range(nch):
                        S.op("dve", L("scalar_tensor_tensor", dstv(c)[:, tgs(g)], bank(pbs[c]), gains[:, goff + c:goff + c + 1], rs_b[:], ALU.mult, ALU.mult),
                            reads=[PB[pbs[c]], B_rsb], writes=[dstb[c]])
                    if extra is not None:
                        extra(pbs[nch], g)

            s0, b0 = wtile(wreq_cols("w_dq", 8, 0, 256))
            s1, b1 = wtile(wreq_cols("w_dq", 8, 256, 256))
            latent([(wview(s0, 8, 256), b0, 256), (wview(s1, 8, 256), b1, 256)], 4, 40, cqv, CQ, 512.0)
            s0, b0 = wtile(wreq_cols("w_dkv", 8, 0, 256))
            s1, b1 = wtile(wreq_cols("w_dkv", 8, 256, 128))
            latent([(wview(s0, 8, 256), b0, 256), (wview(s1, 8, 128), b1, 128)], 2, 44, ckvv, CKV, 256.0,
                   extra=lambda pb, g: rope(KH[0], KHb[0], pb, g))
            S.op("dve", L("tensor_copy", KH[1][64:96, :], KH[0][64:96, :]), reads=[KHb[0]], writes=[KHb[1]])

            def sbank():
                ps = 4 + state["sb"] % 4
                state["sb"] += 1
                return ps

            def ibank():
                state["ib"] = state.get("ib", 0) + 1
                return 2 + state["ib"] % 2

            def head_items(hd):
                s_ = hd % 2
                st = {}
                items = []

                def q_item(g):
                    if g == 0:
                        sl, st["bq"] = wtile(wreq_cols("w_uq", 4, hd * 128, 128))
                        st["wq"] = wview(sl, 4, 128)
                    pb = ibank()
                    for k in range(4):
                        mm(bank(pb), st["wq"][:, k, :], cqv(k)[:, tgs(g)], k == 0, k == 3, [st["bq"], CQ[k]], [PB[pb]])
                    S.op("act", L("activation", QH[s_][0:64, tgs(g)], bank(pb)[0:64, :], AF.Copy),
                         reads=[PB[pb]], writes=[QHb[s_]])
                    rope(QH[s_], QHb[s_], pb, g)

                def k_item(g):
                    if g == 0:
                        sl, st["bk"] = wtile(wreq_cols("w_uk", 2, hd * 64, 64))
                        st["wk"] = wview(sl, 2, 64)
                    pb = ibank()
                    for k in range(2):
                        mm(bank(pb)[0:64, :], st["wk"][:, k, :], ckvv(k)[:, tgs(g)], k == 0, k == 1, [st["bk"], CKV[k]], [PB[pb]])
                    S.op("dve", L("tensor_copy", KH[s_][0:64, tgs(g)], bank(pb)[0:64, :]), reads=[PB[pb]], writes=[KHb[s_]])

                def v_item(g):
                    if g == 0:
                        sl, st["bv"] = wtile(wreq_cols("w_uv", 2, hd * 64, 64))
                        st["wv"] = wview(sl, 2, 64)
                    pb = ibank()
                    for jj in range(4):
                        tt = g * 4 + jj
                        for k in range(2):
                            mm(bank(pb)[:, jj * 64:(jj + 1) * 64], ckvv(k)[:, tt * 128:(tt + 1) * 128],
                               st["wv"][:, k, :], k == 0, k == 1, [st["bv"], CKV[k]], [PB[pb]],
                               track=(k == 1 and jj == 3))
                    pv = bank(pb)[:, 0:256].rearrange("p (t e) -> p t e", t=4)
                    S.op("dve", L("tensor_copy", VA[s_][:, g * 4:(g + 1) * 4, 0:64], pv), reads=[PB[pb]], writes=[VAb[s_]])

                for g in range(4):
                    items.append(lambda g=g: q_item(g))
                for g in range(4):
                    items.append(lambda g=g: k_item(g))
                for g in range(4):
                    items.append(lambda g=g: v_item(g))
                return items

            pti = state.setdefault("mpti", [0])
            for it in head_items(0):
                it()
            info = {}

            def stage1(t):
                hd, I, J = t
                s_ = hd % 2
                ps = sbank()
                mm(bank(ps), KH[s_][0:96, J * 128:(J + 1) * 128], QH[s_][0:96, tgs(I)], True, True,
                   [KHb[s_], QHb[s_]], [PB[ps]])
                pi = pti[0]
                pti[0] = (pi + 1) % 8
                S.op("act", L("activation", ptv(pi), bank(ps), AF.Exp, scale=scale),
                     reads=[PB[ps]], writes=[PT[pi]])
                info[t] = pi

            def stage2(t):
                hd, I, J = t
                s_ = hd % 2
                pr = (hd // 2) % 4
                a = hd % 2
                pi = info.pop(t)
                po = I % 2
                mm(bank(po), VA[s_][:, J, :], ptv(pi), J == 0, J == 15, [VAb[s_], PT[pi]], [PB[po]], track=True)
                if J == 15:
                    rsv = rs_a if po == 0 else rs_b
                    rsb_ = B_rsa if po == 0 else B_rsb
                    S.op("dve", L("reciprocal", rsv[0:64, :], bank(po)[64:128, :]),
                         reads=[PB[po]], writes=[rsb_])
                    S.op("dve", L("tensor_tensor", blk(pr)[64 * a:64 * a + 64, tgs(I)], bank(po)[0:64, :], rsv[0:64, :], ALU.mult),
                         reads=[PB[po], rsb_], writes=[MIX[pr]])

            DEPTH = 3
            for half in range(2):
                tiles = [(hd, I, J) for hd in range(half * 8, half * 8 + 8) for I in range(4) for J in range(16)]
                nxt = []
                for i in range(len(tiles) + DEPTH):
                    local = -1
                    if i < len(tiles):
                        hd, I, J = tiles[i]
                        local = I * 16 + J
                        if local == 0:
                            while nxt:
                                nxt.pop(0)()
                            nxt = head_items(hd + 1) if hd < 15 else []
                        stage1(tiles[i])
                    if i >= DEPTH:
                        stage2(tiles[i - DEPTH])
                    if nxt and local >= 2 and (local - 2) % 5 == 0:
                        nxt.pop(0)()
                while nxt:
                    nxt.pop(0)()
                out_proj("w_o", lambda k, g: blk(k)[:, tgs(g)], lambda k, g: [MIX[k]], half * 512, 4)
            S.barrier()

        def final_stage(goff):
            YT = [Buf("yt%d" % i) for i in range(8)]
            OS = [Buf("os%d" % i) for i in range(2)]
            RSF = [Buf("rsf%d" % i) for i in range(4)]
            ytv = lambda c: blkf(2 * c, 2)[:, 0:512]
            osv = lambda i: blkf(16 + 2 * i, 2)[:, 0:1024]
            rsf = lambda g: blkf(2 * g, 2)[:, 1024:1536]
            pbs = []
            for g in range(4):
                pb = nb_()
                for c in range(8):
                    q = state["sq"]
                    state["sq"] = (q + 1) % 4
                    S.op("act", L("activation", sqr[:, q, :], xT[:, c, tgs(g)], AF.Square),
                         reads=[XT[c][g]], writes=[SQ[q]])
                    mm(bank(pb), onesb[:], sqr[:, q, :], c == 0, c == 7, [SQ[q], B_const], [PB[pb]], track=True)
                pbs.append(pb)
            for g in range(4):
                S.op("act", L("activation", rsf(g), bank(pbs[g]), AF.Sqrt, scale=1.0 / 1024.0, bias=small[:, 0:1]),
                     reads=[PB[pbs[g]], B_small], writes=[RSF[g]])
                S.op("dve", L("reciprocal", rsf(g), rsf(g)), reads=[RSF[g]], writes=[RSF[g]])
            for g in range(4):
                for c in range(8):
                    eng = "dve"
                    if goff is None:
                        S.op(eng, L("tensor_copy", ytv(c), xT[:, c, tgs(g)]), reads=[XT[c][g]], writes=[YT[c]])
                    else:
                        S.op(eng, L("scalar_tensor_tensor", ytv(c), xT[:, c, tgs(g)], gains[:, goff + c:goff + c + 1],
                                    rsf(g), ALU.mult, ALU.mult), reads=[XT[c][g], RSF[g]], writes=[YT[c]])
                for j in range(4):
                    tt = g * 4 + j
                    oi = tt % 2
                    p0, p1 = nb_(), nb_()
                    for c in range(8):
                        pbx = p0 if c < 4 else p1
                        S.op("pe", L("transpose", bank(pbx)[:, (c % 4) * 128:(c % 4 + 1) * 128], ytv(c)[:, j * 128:(j + 1) * 128], ident[:]),
                            reads=[YT[c], B_const], writes=[PB[pbx]], track=(c % 4 == 3))
                    S.op("act", L("activation", osv(oi)[:, 0:512], bank(p0), AF.Copy),
                         reads=[PB[p0]], writes=[OS[oi]])
                    S.op("act", L("activation", osv(oi)[:, 512:1024], bank(p1), AF.Copy),
                         reads=[PB[p1]], writes=[OS[oi]])
                    S.dma("sp", y_out[tt * 128:(tt + 1) * 128, :], osv(oi), reads=[OS[oi]])

        if nstage < 1 or KDBG in (10, 11):
            S.barrier()
        if KDBG == 10:
            for _ in range(30):
                norm_stage(0)
        elif KDBG == 11:
            norm_stage(0)
            for i in range(40):
                slot, wb = wtile(wreq_cols("w_in", 8, (i % 24) * 128, 128))
                wq = wview(slot, 8, 128)
                pb = proj_fm(wq, wb, 0, hT_rhs, hT_bufs, 8, i % 4)
                S.op("dve", L("tensor_copy", rs_b[:], bank(pb)), reads=[PB[pb]], writes=[B_rsb])
        elif nstage >= 1:
            norm_stage(0)
            S.barrier()
            diff_phase()
        if nstage >= 2:
            na_phase()
        if nstage >= 3:
            ffn(0, 8)
        if nstage >= 4:
            norm_stage(16)
            mla_phase()
        if nstage >= 5:
            ffn(1, 24)
        final_stage(32 if nstage >= 5 else None)
        S.final_wait("sp")
        S.emit()
    return nc, requests


_CACHE = {}


def _get_program(nstage=99):
    if nstage not in _CACHE:
        _, reqs = build(None, nstage)
        nc, _ = build(reqs, nstage)
        _CACHE[nstage] = (nc, reqs)
    return _CACHE[nstage]


def kernel(**inputs):
    nstage = int(inputs.pop("_nstage", 99))
    cores = inputs.pop("_cores", None)
    shared = _prep_shared(inputs)
    x = np.asarray(inputs["x"], np.float32)
    nb = x.shape[0] if cores is None else cores
    nc, plan = _get_program(nstage)
    dev = {k: shared[k] for k in SHAPES if k != "x"}
    dev["wpack"] = _pack_weights(shared, plan)
    in_maps = []
    for b in range(nb):
        m = dict(dev)
        m["x"] = np.ascontiguousarray(x[b])
        in_maps.append(m)
    res = run_bass_kernel_spmd(nc, in_maps, core_ids=list(range(nb)))
    out = np.stack([np.asarray(r["y"], np.float32) for r in res.results], axis=0)
    return out
```
